# Optimizing a Trainium2 kernel written in Bass

```python
import math
import jax, jax.numpy as jnp
from jax import lax
import numpy as np

D_MODEL = 1024
BATCH = 8
SEQ = 4096
DEPTH = 4

FOX_HEADS = 8
FOX_HEAD_DIM = 64
FOX_WIDTH = FOX_HEADS * FOX_HEAD_DIM
Q_BLOCK = 128
SSD_HEADS = 8
SSD_HEAD_DIM = 64
SSD_WIDTH = SSD_HEADS * SSD_HEAD_DIM
SSD_GROUPS = 2
SSD_STATE = 64
SSD_CONV = 4
SSD_CHUNK = 128
SSD_CONV_DIM = SSD_WIDTH + 2 * SSD_GROUPS * SSD_STATE
S5_GROUP = 16
S5_WIDTH = 512
S5_GROUPS = S5_WIDTH // S5_GROUP
S5_STATE = 64
D_FF = 2816
FFN_CONV = 3
N_BRANCH = 3
BRANCH_WIDTH = 512
ALPHA = (2 * DEPTH) ** 0.25
BETA = (8 * DEPTH) ** -0.25
LN_EPS = 1e-5
RMS_EPS = 1e-5
D_IN_PROJ = 3 * FOX_WIDTH + FOX_HEADS + SSD_WIDTH + SSD_CONV_DIM + SSD_HEADS + S5_WIDTH + N_BRANCH * D_MODEL

kernel_name = "fox_ssd_s5_gated_hybrid_deepnorm"


def _split_points():
    sizes = [FOX_WIDTH, FOX_WIDTH, FOX_WIDTH, FOX_HEADS, SSD_WIDTH, SSD_CONV_DIM,
             SSD_HEADS, S5_WIDTH]
    pts, acc = [], 0
    for s in sizes:
        acc += s
        pts.append(acc)
    return pts


def layer_norm(x, g, b):
    xf = x.astype(jnp.float32)
    mu = jnp.mean(xf, axis=-1, keepdims=True)
    var = jnp.mean(jnp.square(xf - mu), axis=-1, keepdims=True)
    return ((xf - mu) * lax.rsqrt(var + LN_EPS) * g + b).astype(x.dtype)


def causal_dwconv(x, w, b):
    k, c = w.shape
    y = lax.conv_general_dilated(
        x, w[:, None, :].astype(x.dtype), window_strides=(1,), padding=((k - 1, 0),),
        dimension_numbers=("NWC", "WIO", "NWC"), feature_group_count=c)
    return y + b.astype(x.dtype)


def fox_attention(q, k, v, f_logit):
    b, l, h, dh = q.shape
    cum = jnp.cumsum(jax.nn.log_sigmoid(f_logit.astype(jnp.float32)), axis=1)
    cum = cum.transpose(0, 2, 1)
    scale = dh ** -0.5
    outs = []
    for i in range(l // Q_BLOCK):
        q0, q1 = i * Q_BLOCK, (i + 1) * Q_BLOCK
        s = jnp.einsum('bqhd,bkhd->bhqk', q[:, q0:q1], k[:, :q1]).astype(jnp.float32) * scale
        s = s + cum[:, :, q0:q1, None] - cum[:, :, None, :q1]
        causal = jnp.arange(q1)[None, :] <= jnp.arange(q0, q1)[:, None]
        p = jax.nn.softmax(jnp.where(causal, s, -jnp.inf), axis=-1)
        outs.append(jnp.einsum('bhqk,bkhd->bqhd', p.astype(v.dtype), v[:, :q1]))
    return jnp.concatenate(outs, axis=1).reshape(b, l, h * dh)


def segsum(a):
    t = a.shape[-1]
    cs = jnp.cumsum(a, axis=-1)
    diff = cs[..., :, None] - cs[..., None, :]
    return jnp.where(jnp.tril(jnp.ones((t, t), dtype=bool)), diff, -jnp.inf)


def ssd_mixer(z, xbc, dt_raw, conv_w, conv_b, dt_bias, a_log, d_skip, norm_w):
    f32 = jnp.float32
    bsz, l, _ = z.shape
    g, e, n, p, q = SSD_GROUPS, SSD_HEADS // SSD_GROUPS, SSD_STATE, SSD_HEAD_DIM, SSD_CHUNK
    c = l // q
    xbc = jax.nn.silu(causal_dwconv(xbc, conv_w, conv_b))
    xs, bm, cm = jnp.split(xbc, [SSD_WIDTH, SSD_WIDTH + g * n], axis=-1)
    xh = xs.reshape(bsz, l, SSD_HEADS, p).astype(f32)
    dt = jax.nn.softplus(dt_raw.astype(f32) + dt_bias.astype(f32))
    a = -jnp.exp(a_log.astype(f32))
    xdt = (xh * dt[..., None]).reshape(bsz, c, q, g, e, p)
    adt = (dt * a).reshape(bsz, c, q, g, e).transpose(0, 3, 4, 1, 2)
    bc = bm.astype(f32).reshape(bsz, c, q, g, n)
    cc = cm.astype(f32).reshape(bsz, c, q, g, n)
    a_cs = jnp.cumsum(adt, axis=-1)
    decay_in = jnp.exp(segsum(adt))
    cb = jnp.einsum('bclgn,bcsgn->bgcls', cc, bc)
    y_diag = jnp.einsum('bgecls,bcsgep->bclgep', cb[:, :, None] * decay_in, xdt)
    decay_to_end = jnp.exp(a_cs[..., -1:] - a_cs).transpose(0, 3, 4, 1, 2)
    states = jnp.einsum('bclgn,bclgep->bcgepn', bc, xdt * decay_to_end[..., None])
    states = jnp.concatenate([jnp.zeros_like(states[:, :1]), states], axis=1)
    chunk_tot = jnp.pad(a_cs[..., -1], ((0, 0), (0, 0), (0, 0), (1, 0)))
    chunk_decay = jnp.exp(segsum(chunk_tot))
    states = jnp.einsum('bgezc,bcgepn->bzgepn', chunk_decay, states)[:, :-1]
    decay_out = jnp.exp(a_cs).transpose(0, 3, 4, 1, 2)[..., None]
    y_off = jnp.einsum('bclgn,bcgepn->bclgep', cc, states) * decay_out
    y = (y_diag + y_off).reshape(bsz, l, SSD_HEADS, p) + d_skip.astype(f32)[:, None] * xh
    y = y.reshape(bsz, l, SSD_WIDTH) * jax.nn.silu(z.astype(f32))
    yg = y.reshape(bsz, l, g, SSD_WIDTH // g)
    yg = yg * lax.rsqrt(jnp.mean(yg * yg, axis=-1, keepdims=True) + RMS_EPS)
    return (yg.reshape(bsz, l, SSD_WIDTH) * norm_w.astype(f32)).astype(z.dtype)


def _complex_linear_combine(earlier, later):
    ar1, ai1, br1, bi1 = earlier
    ar2, ai2, br2, bi2 = later
    return (ar2 * ar1 - ai2 * ai1,
            ar2 * ai1 + ai2 * ar1,
            ar2 * br1 - ai2 * bi1 + br2,
            ar2 * bi1 + ai2 * br1 + bi2)


def s5_mixer(u, a_re, a_im, b_re, b_im, c_re, c_im, log_step, d_skip, w_glu, b_glu):
    f32 = jnp.float32
    bsz, l, _ = u.shape
    lam_re = jnp.minimum(a_re.astype(f32), -1e-4)
    lam_im = a_im.astype(f32)
    step = jnp.exp(log_step.astype(f32))[:, None]
    mag = jnp.exp(lam_re * step)
    abar_re = mag * jnp.cos(lam_im * step)
    abar_im = mag * jnp.sin(lam_im * step)
    den = lam_re * lam_re + lam_im * lam_im
    num_re = abar_re - 1.0
    k_re = (num_re * lam_re + abar_im * lam_im) / den
    k_im = (abar_im * lam_re - num_re * lam_im) / den
    bre, bim = b_re.astype(f32), b_im.astype(f32)
    bb_re = k_re[..., None] * bre - k_im[..., None] * bim
    bb_im = k_re[..., None] * bim + k_im[..., None] * bre
    ug = u.astype(f32).reshape(bsz, l, S5_GROUPS, S5_GROUP)
    bu_re = jnp.einsum('blgh,gnh->lbgn', ug, bb_re)
    bu_im = jnp.einsum('blgh,gnh->lbgn', ug, bb_im)
    a_seq_re = jnp.broadcast_to(abar_re[None, None], (l, 1) + abar_re.shape)
    a_seq_im = jnp.broadcast_to(abar_im[None, None], (l, 1) + abar_im.shape)
    _, _, h_re, h_im = lax.associative_scan(
        _complex_linear_combine, (a_seq_re, a_seq_im, bu_re, bu_im), axis=0)
    y = (jnp.einsum('lbgn,ghn->blgh', h_re, c_re.astype(f32))
         - jnp.einsum('lbgn,ghn->blgh', h_im, c_im.astype(f32)))
    y = y.reshape(bsz, l, S5_WIDTH) + d_skip.astype(f32) * u.astype(f32)
    gy = jax.nn.gelu(y)
    out = gy * jax.nn.sigmoid(gy @ w_glu.astype(f32) + b_glu.astype(f32))
    return out.astype(u.dtype)


def mixer_sublayer(x, w_in, fox_f_bias, ssd_conv_w, ssd_conv_b, ssd_dt_bias, ssd_a_log,
                   ssd_d, ssd_norm_w, s5_a_re, s5_a_im, s5_b_re, s5_b_im, s5_c_re, s5_c_im,
                   s5_log_step, s5_d, s5_w_glu, s5_b_glu, w_branch, b_gate, w_out):
    bsz, l, _ = x.shape
    proj = x @ w_in
    q, k, v, f_logit, z, xbc, dt_raw, u, gate_logit = jnp.split(proj, _split_points(), axis=-1)
    hs = (bsz, l, FOX_HEADS, FOX_HEAD_DIM)
    y_a = fox_attention(q.reshape(hs), k.reshape(hs), v.reshape(hs), f_logit + fox_f_bias)
    y_b = ssd_mixer(z, xbc, dt_raw, ssd_conv_w, ssd_conv_b, ssd_dt_bias, ssd_a_log,
                    ssd_d, ssd_norm_w)
    y_c = s5_mixer(u, s5_a_re, s5_a_im, s5_b_re, s5_b_im, s5_c_re, s5_c_im,
                   s5_log_step, s5_d, s5_w_glu, s5_b_glu)
    branches = jnp.stack([y_a, y_b.astype(y_a.dtype), y_c.astype(y_a.dtype)], axis=2)
    proj_br = jnp.einsum('blrw,rwd->blrd', branches, w_branch)
    gates = jax.nn.sigmoid(gate_logit.reshape(bsz, l, N_BRANCH, D_MODEL) + b_gate)
    merged = jnp.sum(gates * proj_br, axis=2)
    return merged @ w_out


def conv_ffn(x, w_up, conv_w, conv_b, w_down):
    h = causal_dwconv(x @ w_up, conv_w, conv_b)
    val, gate = jnp.split(h, 2, axis=-1)
    return (jax.nn.silu(gate) * val) @ w_down


def setup_inputs(seed: int = 0) -> dict:
    key = jax.random.key(seed)
    ks = iter(jax.random.split(key, 40))
    nrm = lambda shape, s: jax.random.normal(next(ks), shape, jnp.float32) * s
    L_ = DEPTH
    dt0 = jnp.exp(jax.random.uniform(next(ks), (L_, SSD_HEADS), jnp.float32,
                                     math.log(1e-3), math.log(1e-1)))
    inputs = {
        "x": nrm((BATCH, SEQ, D_MODEL), 1.0),
        "w_in": nrm((L_, D_MODEL, D_IN_PROJ), D_MODEL ** -0.5),
        "fox_f_bias": 2.0 + nrm((L_, FOX_HEADS), 0.5),
        "ssd_conv_w": nrm((L_, SSD_CONV, SSD_CONV_DIM), SSD_CONV ** -0.5),
        "ssd_conv_b": nrm((L_, SSD_CONV_DIM), 0.01),
        "ssd_dt_bias": dt0 + jnp.log(-jnp.expm1(-dt0)),
        "ssd_a_log": jnp.log(jax.random.uniform(next(ks), (L_, SSD_HEADS), jnp.float32, 1.0, 16.0)),
        "ssd_d": 1.0 + nrm((L_, SSD_HEADS), 0.1),
        "ssd_norm_w": 1.0 + nrm((L_, SSD_WIDTH), 0.02),
        "s5_a_re": -0.5 + nrm((L_, S5_GROUPS, S5_STATE), 0.01),
        "s5_a_im": jnp.pi * jnp.arange(S5_STATE, dtype=jnp.float32) + nrm((L_, S5_GROUPS, S5_STATE), 0.01),
        "s5_b_re": nrm((L_, S5_GROUPS, S5_STATE, S5_GROUP), (2 * S5_GROUP) ** -0.5),
        "s5_b_im": nrm((L_, S5_GROUPS, S5_STATE, S5_GROUP), (2 * S5_GROUP) ** -0.5),
        "s5_c_re": nrm((L_, S5_GROUPS, S5_GROUP, S5_STATE), 0.5),
        "s5_c_im": nrm((L_, S5_GROUPS, S5_GROUP, S5_STATE), 0.5),
        "s5_log_step": jax.random.uniform(next(ks), (L_, S5_GROUPS), jnp.float32,
                                          math.log(1e-3), math.log(1e-1)),
        "s5_d": nrm((L_, S5_WIDTH), 1.0),
        "s5_w_glu": nrm((L_, S5_WIDTH, S5_WIDTH), S5_WIDTH ** -0.5),
        "s5_b_glu": nrm((L_, S5_WIDTH), 0.01),
        "w_branch": nrm((L_, N_BRANCH, BRANCH_WIDTH, D_MODEL), BRANCH_WIDTH ** -0.5),
        "b_gate": nrm((L_, N_BRANCH, D_MODEL), 0.01),
        "w_out": nrm((L_, D_MODEL, D_MODEL), BETA * D_MODEL ** -0.5),
        "ln1_g": 1.0 + nrm((L_, D_MODEL), 0.02),
        "ln1_b": nrm((L_, D_MODEL), 0.01),
        "ffn_w_up": nrm((L_, D_MODEL, 2 * D_FF), D_MODEL ** -0.5),
        "ffn_conv_w": nrm((L_, FFN_CONV, 2 * D_FF), FFN_CONV ** -0.5),
        "ffn_conv_b": nrm((L_, 2 * D_FF), 0.01),
        "ffn_w_down": nrm((L_, D_FF, D_MODEL), BETA * D_FF ** -0.5),
        "ln2_g": 1.0 + nrm((L_, D_MODEL), 0.02),
        "ln2_b": nrm((L_, D_MODEL), 0.01),
    }
    return inputs


def reference(x, w_in, fox_f_bias, ssd_conv_w, ssd_conv_b, ssd_dt_bias, ssd_a_log, ssd_d,
              ssd_norm_w, s5_a_re, s5_a_im, s5_b_re, s5_b_im, s5_c_re, s5_c_im, s5_log_step,
              s5_d, s5_w_glu, s5_b_glu, w_branch, b_gate, w_out, ln1_g, ln1_b, ffn_w_up,
              ffn_conv_w, ffn_conv_b, ffn_w_down, ln2_g, ln2_b):
    for i in range(DEPTH):
        mix = mixer_sublayer(x, w_in[i], fox_f_bias[i], ssd_conv_w[i], ssd_conv_b[i],
                             ssd_dt_bias[i], ssd_a_log[i], ssd_d[i], ssd_norm_w[i],
                             s5_a_re[i], s5_a_im[i], s5_b_re[i], s5_b_im[i], s5_c_re[i],
                             s5_c_im[i], s5_log_step[i], s5_d[i], s5_w_glu[i], s5_b_glu[i],
                             w_branch[i], b_gate[i], w_out[i])
        x = layer_norm(ALPHA * x + mix.astype(x.dtype), ln1_g[i], ln1_b[i])
        ffn = conv_ffn(x, ffn_w_up[i], ffn_conv_w[i], ffn_conv_b[i], ffn_w_down[i])
        x = layer_norm(ALPHA * x + ffn.astype(x.dtype), ln2_g[i], ln2_b[i])
    return x
```

```python
import numpy as np
import concourse.bass as bass
import concourse.mybir as mybir
from concourse.bass_utils import run_bass_kernel_spmd
from contextlib import ExitStack

F32 = mybir.dt.float32
BF16 = mybir.dt.bfloat16
I32 = mybir.dt.int32
AF = mybir.ActivationFunctionType
ALU = mybir.AluOpType

L_SEQ = 4096
D = 1024
DEPTH = 4
DFF = 2816
DIN = 6416
ALPHA = (2 * DEPTH) ** 0.25
NTT = L_SEQ // 128
NTB = L_SEQ // 512


class Trk:
    __slots__ = ("w", "rs", "name")

    def __init__(self, name=""):
        self.w = None
        self.rs = []
        self.name = name


class Fw:
    NDMA = 48

    def __init__(self, nc):
        self.nc = nc
        self.eng = {"pe": nc.tensor, "act": nc.scalar, "dve": nc.vector,
                    "pool": nc.gpsimd, "sp": nc.sync}
        self.sem = {k: nc.alloc_semaphore("s_" + k) for k in self.eng}
        self.cnt = {k: 0 for k in self.eng}
        self.seen = {k: {} for k in self.eng}
        self.dsem = [nc.alloc_semaphore("d%d" % i) for i in range(self.NDMA)]
        self.dval = [0] * self.NDMA
        self.dnext = {"sp": 0, "pool": 0}
        self.drange = {"sp": (0, 32), "pool": (32, self.NDMA)}
        self.nwait = 0
        self.ninst = 0

    def sb(self, name, shape, dt):
        return self.nc.alloc_sbuf_tensor(name, list(shape), dt).ap()

    def ps(self, name, shape, dt=F32):
        return self.nc.alloc_psum_tensor(name, list(shape), dt).ap()

    def dram(self, name, shape, dt, kind="Internal"):
        return self.nc.dram_tensor(name, list(shape), dt, kind=kind).ap()

    def _need(self, e, ev):
        if ev is None:
            return
        key, val = ev
        if key == "pe" and e == "pe":
            return
        if self.seen[e].get(key, 0) >= val:
            return
        sem = self.sem[key] if isinstance(key, str) else self.dsem[key]
        self.eng[e].wait_ge(sem, val)
        self.nwait += 1
        self.seen[e][key] = val

    def _deps(self, e, reads, writes):
        for t in reads:
            self._need(e, t.w)
        for t in writes:
            self._need(e, t.w)
            for r in t.rs:
                self._need(e, r)

    def _commit(self, ev, reads, writes):
        for t in reads:
            t.rs.append(ev)
            if len(t.rs) > 48:
                d = {}
                for k, v in t.rs:
                    if d.get(k, 0) < v:
                        d[k] = v
                t.rs = list(d.items())
        for t in writes:
            t.w = ev
            t.rs = []

    def op(self, e, fn, reads=(), writes=()):
        self._deps(e, reads, writes)
        ins = fn(self.eng[e])
        self.cnt[e] += 1
        ins.then_inc(self.sem[e], 1)
        ev = (e, self.cnt[e])
        self._commit(ev, reads, writes)
        self.ninst += 1
        return ev

    def dma(self, q, out, in_, reads=(), writes=(), **kw):
        lo, hi = self.drange[q]
        i = lo + self.dnext[q]
        self.dnext[q] = (self.dnext[q] + 1) % (hi - lo)
        if self.dval[i] > 0:
            self._need(q, (i, self.dval[i]))
        self._deps(q, reads, writes)
        self.dval[i] += 16
        self.eng[q].dma_start(out=out, in_=in_, **kw).then_inc(self.dsem[i], 16)
        ev = (i, self.dval[i])
        self._commit(ev, reads, writes)
        self.ninst += 1
        return ev

    def barrier(self):
        for i in range(self.NDMA):
            if self.dval[i]:
                self._need("sp", (i, self.dval[i]))
        ins = self.eng["sp"].sem_inc(self.sem["sp"], 1)
        self.cnt["sp"] += 1
        for e in ("pe", "act", "dve", "pool", "sp"):
            for k in ("pe", "act", "dve", "pool", "sp"):
                if k != e and self.cnt[k]:
                    if self.seen[e].get(k, 0) < self.cnt[k]:
                        self.eng[e].wait_ge(self.sem[k], self.cnt[k])
                        self.seen[e][k] = self.cnt[k]
                        self.nwait += 1

    def drain(self):
        for i in range(self.NDMA):
            if self.dval[i]:
                self._need("sp", (i, self.dval[i]))
        for k in ("pe", "act", "dve", "pool"):
            if self.cnt[k]:
                self._need("sp", (k, self.cnt[k]))


class Arena:
    def __init__(self, fw):
        self.fw = fw
        self.st = ExitStack()

    _uid = [0]

    def sb(self, name, shape, dt):
        Arena._uid[0] += 1
        return self.st.enter_context(self.fw.nc.sbuf_tensor("%s_%d" % (name, Arena._uid[0]), list(shape), dt)).ap()

    def ring(self, name, n, shape, dt):
        return Ring(self.fw, name, n, shape, dt, mk=self.sb)

    def close(self):
        self.fw.barrier()
        self.st.close()


class Ring:
    def __init__(self, fw, name, n, shape, dt, psum=False, mk=None):
        if mk is None:
            mk = fw.ps if psum else fw.sb
        self.b = [mk("%s%d" % (name, i), shape, dt) for i in range(n)]
        self.k = [Trk("%s%d" % (name, i)) for i in range(n)]
        self.i = 0
        self.n = n

    def get(self):
        i = self.i
        self.i = (i + 1) % self.n
        return self.b[i], self.k[i]


class K:
    def __init__(self, nlayers=DEPTH, dbg=None):
        self.nl = nlayers
        self.dbg = dbg or {}
        nc = self.nc = bass.Bass("TRN2", target_bir_lowering=False)
        fw = self.fw = Fw(nc)
        self.inp = {}
        self.ev_alt = 0

    def din(self, name, shape, dt=F32):
        ap = self.nc.dram_tensor(name, list(shape), dt, kind="ExternalInput").ap()
        self.inp[name] = ap
        return ap

    def alt(self):
        self.ev_alt ^= 1
        return "act" if self.ev_alt else "dve"

    def copy(self, e, out, in_, reads, writes):
        if e == "act":
            return self.fw.op("act", lambda g: g.activation(out=out, in_=in_, func=AF.Copy), reads, writes)
        return self.fw.op(e, lambda g: g.tensor_copy(out=out, in_=in_), reads, writes)

    def mm(self, out, lhsT, rhs, start, stop, reads, writes):
        return self.fw.op("pe", lambda g: g.matmul(out, lhsT=lhsT, rhs=rhs, start=start, stop=stop), reads, writes)

    def setup_common(self):
        fw = self.fw
        self.psum = Ring(fw, "psb", 6, [128, 512], F32, psum=True)
        self.psacc = Ring(fw, "psa", 2, [128, 512], F32, psum=True)
        self.identb = fw.sb("identb", [128, 128], BF16)
        self.identf = fw.sb("identf", [128, 128], F32)
        self.k_ident = Trk("ident")
        fw.op("pool", lambda g: g.memset(self.identf, 0.0), writes=[self.k_ident])
        fw.op("pool", lambda g: g.affine_select(out=self.identf, in_=self.identf, compare_op=ALU.not_equal, fill=1.0,
                                                base=0, pattern=[[-1, 128]], channel_multiplier=1),
              reads=[self.k_ident], writes=[self.k_ident])
        fw.op("pool", lambda g: g.tensor_copy(out=self.identb, in_=self.identf), reads=[self.k_ident], writes=[self.k_ident])
        self.utri_f = fw.sb("utri_f", [128, 128], F32)
        self.trib = fw.sb("trib", [128, 128], BF16)
        self.ones_f = fw.sb("ones_f", [128, 128], F32)
        self.mean_f = fw.sb("mean_f", [128, 128], F32)
        self.ones_b = fw.sb("ones_b", [128, 128], BF16)
        self.scanmask = fw.sb("scanmask", [128, 8, 32], F32)
        fw.op("pool", lambda g: g.memset(self.utri_f, 1.0), writes=[self.k_ident])
        fw.op("pool", lambda g: g.affine_select(out=self.utri_f, in_=self.utri_f, compare_op=ALU.is_ge, fill=0.0,
                                                base=0, pattern=[[1, 128]], channel_multiplier=-1),
              reads=[self.k_ident], writes=[self.k_ident])
        fw.op("pool", lambda g: g.tensor_copy(out=self.trib, in_=self.utri_f), reads=[self.k_ident], writes=[self.k_ident])
        fw.op("pool", lambda g: g.memset(self.ones_f, 1.0), writes=[self.k_ident])
        fw.op("pool", lambda g: g.memset(self.ones_b, 1.0), writes=[self.k_ident])
        fw.op("pool", lambda g: g.memset(self.mean_f, 1.0 / 128.0), writes=[self.k_ident])
        fw.op("pool", lambda g: g.memset(self.scanmask, 1.0), writes=[self.k_ident])
        fw.op("pool", lambda g: g.memset(self.scanmask[:, :, 0:1], 0.0), writes=[self.k_ident])
        self.nutri_f = fw.sb("nutri_f", [128, 128], F32)
        self.negmask4 = fw.sb("negmask4", [128, 4, 128], F32)
        fw.op("pool", lambda g: g.tensor_scalar(out=self.nutri_f, in0=self.utri_f, scalar1=-1.0, scalar2=0.0, op0=ALU.mult, op1=ALU.add),
              reads=[self.k_ident], writes=[self.k_ident])
        fw.op("pool", lambda g: g.memset(self.negmask4, -30000.0), writes=[self.k_ident])
        for r_ in range(4):
            fw.op("pool", lambda g: g.affine_select(out=self.negmask4[:, r_, :], in_=self.negmask4[:, r_, :], compare_op=ALU.is_gt, fill=0.0,
                                                    base=0, pattern=[[-1, 128]], channel_multiplier=1),
                  reads=[self.k_ident], writes=[self.k_ident])
        self.rowmask = fw.sb("rowmask", [128, 8], F32)
        fw.op("pool", lambda g: g.memset(self.rowmask, 1.0), writes=[self.k_ident])
        fw.op("pool", lambda g: g.affine_select(out=self.rowmask, in_=self.rowmask, compare_op=ALU.is_ge, fill=0.0,
                                                base=0, pattern=[[-16, 8]], channel_multiplier=1), reads=[self.k_ident], writes=[self.k_ident])
        fw.op("pool", lambda g: g.affine_select(out=self.rowmask, in_=self.rowmask, compare_op=ALU.is_ge, fill=0.0,
                                                base=15, pattern=[[16, 8]], channel_multiplier=-1), reads=[self.k_ident], writes=[self.k_ident])
        self.mT = fw.dram("mT", [8, 128, L_SEQ], BF16)
        self.k_mT = [[Trk() for _ in range(NTB)] for _ in range(8)]
        self.gyT = fw.dram("gyT", [4, 128, L_SEQ], BF16)
        self.k_gyT = [Trk() for _ in range(4)]
        self.zd = fw.dram("zd", [L_SEQ, 512], F32)
        self.k_zd = [Trk() for _ in range(NTT)]
        self.yT = fw.dram("yT", [3, 512, L_SEQ], BF16)
        self.k_yT = [[Trk() for _ in range(NTB)] for _ in range(3)]
        self.xT = fw.sb("xT", [128, 8, L_SEQ], BF16)
        self.k_xT = [Trk("xT%d" % i) for i in range(NTT)]
        self.xr = fw.dram("xr", [L_SEQ, D], F32)
        self.k_xr = [Trk("xr%d" % i) for i in range(NTT)]

    def alloc_ln(self, A):
        self.tok32 = A.ring("tok32", 3, [128, D], F32)
        self.tokbf = A.ring("tokbf", 2, [128, D], BF16)
        self.ln_g = A.sb("ln_g", [128, D], F32)
        self.ln_b = A.sb("ln_b", [128, D], F32)
        self.k_lnp = Trk("lnp")
        self.stat = A.ring("stat", 2, [128, 16], F32)

    def to_xT(self, src_bf, k_src, tt):
        fw = self.fw
        ps, kp = self.psum.get()
        psb = ps.bitcast(BF16)
        for c in range(8):
            fw.op("pe", lambda g: g.transpose(psb[:, c * 128:(c + 1) * 128], src_bf[:, c * 128:(c + 1) * 128], self.identb),
                  reads=[k_src, self.k_ident], writes=[kp])
        dst = self.xT[:, :, tt * 128:(tt + 1) * 128]
        self.copy(self.alt(), dst, psb.rearrange("p (c t) -> p c t", c=8), reads=[kp], writes=[self.k_xT[tt]])

    def phase0(self, x_in):
        fw = self.fw
        A = Arena(fw)
        self.alloc_ln(A)
        for tt in range(NTT):
            t32, k32 = self.tok32.get()
            fw.dma("sp", t32, x_in[tt * 128:(tt + 1) * 128, :], writes=[k32])
            tb, kb = self.tokbf.get()
            self.copy(self.alt(), tb, t32, reads=[k32], writes=[kb])
            self.to_xT(tb, kb, tt)
        A.close()

    def load_ln(self, g_ap, b_ap):
        fw = self.fw
        fw.dma("sp", self.ln_g, g_ap.partition_broadcast(128), writes=[self.k_lnp])
        fw.dma("sp", self.ln_b, b_ap.partition_broadcast(128), writes=[self.k_lnp])

    def resid_ln(self, tt, ps_halves, k_ps, res_src, k_res, out_dram, k_out):
        fw = self.fw
        r32, kr = self.tok32.get()
        fw.dma("sp", r32, res_src[tt * 128:(tt + 1) * 128, :], reads=[k_res], writes=[kr])
        t32, kt = self.tok32.get()
        for h in range(2):
            fw.op("dve", lambda g: g.scalar_tensor_tensor(out=t32[:, h * 512:(h + 1) * 512], in0=r32[:, h * 512:(h + 1) * 512],
                                                          scalar=float(ALPHA), in1=ps_halves[h], op0=ALU.mult, op1=ALU.add),
                  reads=[kr, k_ps[h]], writes=[kt])
        st, ks = self.stat.get()
        for h in range(2):
            fw.op("dve", lambda g: g.bn_stats(out=st[:, h * 6:(h + 1) * 6], in_=t32[:, h * 512:(h + 1) * 512]), reads=[kt], writes=[ks])
        fw.op("dve", lambda g: g.bn_aggr(out=st[:, 12:14], in_=st[:, 0:12]), reads=[ks], writes=[ks])
        fw.op("dve", lambda g: g.tensor_scalar(out=st[:, 14:15], in0=st[:, 13:14], scalar1=1e-5, scalar2=None, op0=ALU.add), reads=[ks], writes=[ks])
        fw.op("act", lambda g: g.activation(out=st[:, 14:15], in_=st[:, 14:15], func=AF.Sqrt), reads=[ks], writes=[ks])
        fw.op("dve", lambda g: g.reciprocal(out=st[:, 15:16], in_=st[:, 14:15]), reads=[ks], writes=[ks])
        fw.op("dve", lambda g: g.tensor_scalar(out=t32, in0=t32, scalar1=st[:, 12:13], scalar2=st[:, 15:16], op0=ALU.subtract, op1=ALU.mult),
              reads=[kt, ks], writes=[kt])
        fw.op("pool", lambda g: g.tensor_tensor(out=t32, in0=t32, in1=self.ln_g, op=ALU.mult), reads=[kt, self.k_lnp], writes=[kt])
        fw.op("pool", lambda g: g.tensor_tensor(out=r32, in0=t32, in1=self.ln_b, op=ALU.add), reads=[kt, self.k_lnp], writes=[kr])
        fw.dma("sp", out_dram[tt * 128:(tt + 1) * 128, :], r32, reads=[kr], writes=[k_out])
        tb, kb = self.tokbf.get()
        self.copy("act", tb, r32, reads=[kr], writes=[kb])
        self.to_xT(tb, kb, tt)


    def fox(self, l):
        fw = self.fw
        I = self.inp
        A = Arena(fw)
        kc_ = self.k_ident
        win = A.ring("win", 2, [128, 8, 528], BF16)
        vaug = A.sb("vaug", [128, 32, 8, 65], BF16)
        k_v = [Trk() for _ in range(NTT)]
        FL = A.sb("FL", [128, 32, 8], F32)
        k_FL = Trk()
        qkr = A.ring("qk", 2, [128, 2, L_SEQ], BF16)
        w_d = I["w_in"][l].rearrange("(kc p) n -> p kc n", p=128)
        fw.op("pool", lambda g: g.memset(vaug[:, :, :, 64:65], 1.0), writes=k_v)

        def proj_qk(hp):
            qk, _ = qkr.get()
            kq = [Trk() for _ in range(NTB)]
            kk = [Trk() for _ in range(NTB)]
            wb, kw = win.get()
            fw.dma("pool", wb[:, :, 0:128], w_d[:, :, hp * 128:(hp + 1) * 128], writes=[kw])
            fw.dma("pool", wb[:, :, 128:256], w_d[:, :, 512 + hp * 128:512 + (hp + 1) * 128], writes=[kw])
            for which, kd in ((0, kq), (1, kk)):
                for tb in range(NTB):
                    ps, kp = self.psum.get()
                    for kc in range(8):
                        self.mm(ps, wb[:, kc, which * 128:(which + 1) * 128], self.xT[:, kc, tb * 512:(tb + 1) * 512], kc == 0, kc == 7,
                                [kw] + self.k_xT[tb * 4:(tb + 1) * 4], [kp])
                    o = qk[:, which, tb * 512:(tb + 1) * 512]
                    if which == 1:
                        self.copy(self.alt(), o, ps, [kp], [kd[tb]])
                    elif self.alt() == "act":
                        fw.op("act", lambda g: g.activation(out=o, in_=ps, func=AF.Copy, scale=0.125), [kp], [kd[tb]])
                    else:
                        fw.op("dve", lambda g: g.tensor_scalar(out=o, in0=ps, scalar1=0.125, scalar2=None, op0=ALU.mult), [kp], [kd[tb]])
            return qk, kq, kk
        wb, kw = win.get()
        fw.dma("pool", wb[:, :, 0:520], w_d[:, :, 1024:1544], writes=[kw])
        for tt in range(NTT):
            ps, kp = self.psum.get()
            for kc in range(8):
                self.mm(ps, self.xT[:, kc, tt * 128:(tt + 1) * 128], wb[:, kc, 0:512], kc == 0, kc == 7, [kw, self.k_xT[tt]], [kp])
            ps2, kp2 = self.psum.get()
            for kc in range(8):
                self.mm(ps2[:, 0:8], self.xT[:, kc, tt * 128:(tt + 1) * 128], wb[:, kc, 512:520], kc == 0, kc == 7, [kw, self.k_xT[tt]], [kp2])
            self.copy(self.alt(), vaug[:, tt, :, 0:64], ps.rearrange("p (h d) -> p h d", h=8), [kp], [k_v[tt]])
            self.copy(self.alt(), FL[:, tt, :], ps2[:, 0:8], [kp2], [k_FL])
        fb = A.sb("fb", [128, 8], F32)
        k_t = Trk()
        fw.dma("sp", fb, I["fox_f_bias"][l:l + 1, :].partition_broadcast(128), writes=[k_t])
        nls = A.sb("nls", [128, 32, 8], F32)
        fw.op("dve", lambda g: g.tensor_tensor(out=nls, in0=FL, in1=fb.unsqueeze(1).to_broadcast([128, 32, 8]), op=ALU.add), [k_FL, k_t], [k_t])
        fw.op("act", lambda g: g.activation(out=nls, in_=nls, func=AF.Exp, scale=-1.0), [k_t], [k_t])
        fw.op("act", lambda g: g.activation(out=nls, in_=nls, func=AF.Ln, bias=1.0), [k_t], [k_t])
        nlsf = nls.rearrange("p j h -> p (j h)")
        ps, kp = self.psum.get()
        self.mm(ps[:, 0:256], self.utri_f, nlsf, True, True, [kc_, k_t], [kp])
        ps2, kp2 = self.psum.get()
        self.mm(ps2[:, 0:256], self.ones_f, nlsf, True, True, [kc_, k_t], [kp2])
        totT = A.sb("totT", [128, 8, 32], F32)
        pin = A.sb("pin", [128, 8, 32], F32)
        cumk = A.sb("cumk", [128, 8, 32], F32)
        refp = A.sb("refp", [128, 8, 32], F32)
        k_c = Trk()
        fw.op("dve", lambda g: g.tensor_copy(out=totT, in_=ps2[:, 0:256].rearrange("p (j h) -> p h j", h=8)), [kp2], [k_c])
        fw.op("dve", lambda g: g.tensor_tensor_scan(out=pin.rearrange("p h j -> p (h j)"), data0=self.scanmask.rearrange("p h j -> p (h j)"),
                                                    data1=totT.rearrange("p h j -> p (h j)"), initial=0.0, op0=ALU.mult, op1=ALU.add),
              [k_c, kc_], [k_c])
        fw.op("dve", lambda g: g.tensor_tensor(out=cumk, in0=pin, in1=totT, op=ALU.subtract), [k_c], [k_c])
        fw.op("dve", lambda g: g.tensor_tensor(out=cumk, in0=cumk, in1=ps[:, 0:256].rearrange("p (j h) -> p h j", h=8), op=ALU.add), [k_c, kp], [k_c])
        ps3, kp3 = self.psum.get()
        self.mm(ps3[:, 0:256], self.mean_f, cumk.rearrange("p h j -> p (h j)"), True, True, [kc_, k_c], [kp3])
        fw.op("dve", lambda g: g.tensor_copy(out=refp.rearrange("p h j -> p (h j)"), in_=ps3[:, 0:256]), [kp3], [k_c])
        biasr = A.ring("biasT", 4, [128, 4, 32], F32)
        ptr = A.ring("pt", 4, [128, 512], BF16)
        rcp = A.ring("rcp", 2, [128, 512], F32)
        ysb = A.ring("ysb", 2, [64, 512], F32)
        ybf = A.ring("ybf", 2, [64, 512], BF16)
        for hp in range(4):
            qk, k_q, k_k = proj_qk(hp)
            qT = qk[:, 0, :]
            kT = qk[:, 1, :]
            for i in range(NTB):
                for hh in range(2):
                    h = 2 * hp + hh
                    pr = slice(hh * 64, (hh + 1) * 64)
                    bt, kb = biasr.get()
                    fw.op("dve", lambda g: g.tensor_tensor(out=bt, in0=cumk[:, h:h + 1, :].to_broadcast([128, 4, 32]),
                                                           in1=refp[:, h, 4 * i:4 * i + 4].unsqueeze(2).to_broadcast([128, 4, 32]),
                                                           op=ALU.subtract), [k_c], [kb])
                    po, kpo = self.psacc.get()
                    nj = 4 * i + 4
                    for j in range(nj):
                        s0 = max(0, j - 4 * i)
                        c0 = s0 * 128
                        ps, kp = self.psum.get()
                        self.mm(ps[:, c0:512], kT[pr, j * 128:(j + 1) * 128], qT[pr, i * 512 + c0:(i + 1) * 512], True, True,
                                [k_k[j // 4], k_q[i]], [kp])
                        pt, kpt = ptr.get()
                        for sb_ in range(s0, 4):
                            fw.op("act", lambda g: g.activation(out=pt[:, sb_ * 128:(sb_ + 1) * 128], in_=ps[:, sb_ * 128:(sb_ + 1) * 128],
                                                                func=AF.Exp, bias=bt[:, sb_, j:j + 1]), [kp, kb], [kpt])
                        if j >= 4 * i:
                            fw.op("pool", lambda g: g.tensor_tensor(out=pt[:, c0:c0 + 128], in0=pt[:, c0:c0 + 128], in1=self.trib, op=ALU.mult),
                                  [kpt, kc_], [kpt])
                        self.mm(po[0:65, c0:512], vaug[:, j, h, :], pt[:, c0:512], j == 0, j == nj - 1, [k_v[j], kpt], [kpo])
                    rc, krc = rcp.get()
                    fw.op("dve", lambda g: g.reciprocal(out=rc[64:65, :], in_=po[64:65, :]), [kpo], [krc])
                    pb, kpb = self.psum.get()
                    self.mm(pb[0:64, :], self.ones_f[64:65, 0:64], rc[64:65, :], True, True, [kc_, krc], [kpb])
                    ys, kys = ysb.get()
                    self.copy("act", ys, po[0:64, :], [kpo], [kys])
                    yb, kyb = ybf.get()
                    fw.op("dve", lambda g: g.tensor_tensor(out=yb, in0=ys, in1=pb[0:64, :], op=ALU.mult), [kys, kpb], [kyb])
                    fw.dma("sp", self.yT[0, h * 64:(h + 1) * 64, i * 512:(i + 1) * 512], yb, reads=[kyb], writes=[self.k_yT[0][i]])
        A.close()


    def ssd(self, l):
        fw = self.fw
        I = self.inp
        A = Arena(fw)
        kc_ = self.k_ident
        w_d = I["w_in"][l].rearrange("(kc p) n -> p kc n", p=128)
        XBC = A.sb("xbc", [128, 6, 3 + L_SEQ], BF16)
        k_xbc = [[Trk() for _ in range(NTB)] for _ in range(6)]
        k_pad = Trk()
        fw.op("pool", lambda g: g.memset(XBC[:, :, 0:3], 0.0), writes=[k_pad])
        DT = A.sb("DT", [128, 32, 8], F32)
        k_DT = Trk()
        A2 = Arena(fw)
        win = A2.ring("win", 2, [128, 8, 512], BF16)
        wb, kw = win.get()
        fw.dma("pool", wb[:, :, 0:512], w_d[:, :, 1544:2056], writes=[kw])
        zst = A2.ring("zst", 2, [128, 512], F32)
        for tt in range(NTT):
            ps, kp = self.psum.get()
            for kc in range(8):
                self.mm(ps, self.xT[:, kc, tt * 128:(tt + 1) * 128], wb[:, kc, 0:512], kc == 0, kc == 7, [kw, self.k_xT[tt]], [kp])
            zs, kzs = zst.get()
            self.copy(self.alt(), zs, ps, [kp], [kzs])
            fw.dma("sp", self.zd[tt * 128:(tt + 1) * 128, :], zs, reads=[kzs], writes=[self.k_zd[tt]])
        for gi, (c0, nct, ctb, ncols) in enumerate(((2056, 4, 0, 512), (2568, 2, 4, 264))):
            wb, kw = win.get()
            fw.dma("pool", wb[:, :, 0:ncols], w_d[:, :, c0:c0 + ncols], writes=[kw])
            for m in range(nct):
                ct = ctb + m
                for tb in range(NTB):
                    ps, kp = self.psum.get()
                    for kc in range(8):
                        self.mm(ps, wb[:, kc, m * 128:(m + 1) * 128], self.xT[:, kc, tb * 512:(tb + 1) * 512], kc == 0, kc == 7,
                                [kw] + self.k_xT[tb * 4:(tb + 1) * 4], [kp])
                    self.copy(self.alt(), XBC[:, ct, 3 + tb * 512:3 + (tb + 1) * 512], ps, [kp], [k_xbc[ct][tb]])
            if gi == 1:
                for tt in range(NTT):
                    ps2, kp2 = self.psum.get()
                    for kc in range(8):
                        self.mm(ps2[:, 0:8], self.xT[:, kc, tt * 128:(tt + 1) * 128], wb[:, kc, 256:264], kc == 0, kc == 7, [kw, self.k_xT[tt]], [kp2])
                    self.copy(self.alt(), DT[:, tt, :], ps2[:, 0:8], [kp2], [k_DT])
        A2.close()
        cw = A.sb("cw", [128, 6, 4], F32)
        cb = A.sb("cb", [128, 6], F32)
        k_cp = Trk()
        fw.dma("sp", cw, I["ssd_cw"][l], writes=[k_cp])
        fw.dma("sp", cb, I["ssd_cb"][l], writes=[k_cp])
        dgc = A.sb("dgc", [128, 6, 4, 128], BF16)
        for ct in range(6):
            for t in range(4):
                fw.op("pool", lambda g: g.tensor_scalar(out=dgc[:, ct, t, :], in0=self.identf, scalar1=cw[:, ct, t:t + 1], scalar2=0.0,
                                                        op0=ALU.mult, op1=ALU.add), [kc_, k_cp], [k_cp])
        cbr = A.sb("cbr", [1, 768], F32)
        cbh = A.sb("cbh", [1, 768], BF16)
        cbl = A.sb("cbl", [1, 768], BF16)
        cbt = cbr
        fw.dma("sp", cbr, I["ssd_conv_b"][l:l + 1, :], writes=[k_cp])
        fw.op("dve", lambda g: g.tensor_copy(out=cbh, in_=cbr), [k_cp], [k_cp])
        fw.op("dve", lambda g: g.tensor_tensor(out=cbt, in0=cbr, in1=cbh, op=ALU.subtract), [k_cp], [k_cp])
        fw.op("dve", lambda g: g.tensor_copy(out=cbl, in_=cbt), [k_cp], [k_cp])
        BT = A.sb("BT", [128, 2, L_SEQ], BF16)
        CT = A.sb("CT", [128, L_SEQ], BF16)
        k_BT = [Trk() for _ in range(NTB)]
        k_CT = [Trk() for _ in range(NTB)]
        fw.op("pool", lambda g: g.memset(BT[64:128, 0, :], 0.0), writes=k_BT)
        fw.op("pool", lambda g: g.memset(BT[0:64, 1, :], 0.0), writes=k_BT)
        for ct in (4, 5):
            for tb in range(NTB):
                ps, kp = self.psum.get()
                rd = [k_cp, k_pad, k_xbc[ct][tb]] + ([k_xbc[ct][tb - 1]] if tb else [])
                for t in range(4):
                    self.mm(ps, dgc[:, ct, t, :], XBC[:, ct, tb * 512 + t:tb * 512 + t + 512], t == 0, t == 3, rd, [kp])
                ts_ = slice(tb * 512, (tb + 1) * 512)
                if ct == 5:
                    fw.op("act", lambda g: g.activation(out=CT[:, ts_], in_=ps, func=AF.Silu, bias=cb[:, ct:ct + 1]), [kp, k_cp], [k_CT[tb]])
                else:
                    fw.op("act", lambda g: g.activation(out=BT[0:64, 0, ts_], in_=ps[0:64, :], func=AF.Silu, bias=cb[0:64, ct:ct + 1]), [kp, k_cp], [k_BT[tb]])
                    fw.op("act", lambda g: g.activation(out=BT[64:128, 1, ts_], in_=ps[64:128, :], func=AF.Silu, bias=cb[64:128, ct:ct + 1]), [kp, k_cp], [k_BT[tb]])
        if self.dbg.get('ssd_stop') == 2:
            A.close()
            return
        dtb = A.sb("dtb", [128, 8], F32)
        alog = A.sb("alog", [128, 8], F32)
        dsk = A.sb("dsk", [128, 8], F32)
        nw = A.sb("nw", [128, 512], F32)
        k_p = Trk()
        fw.dma("sp", dtb, I["ssd_dt_bias"][l:l + 1, :].partition_broadcast(128), writes=[k_p])
        fw.dma("sp", alog, I["ssd_a_log"][l:l + 1, :].partition_broadcast(128), writes=[k_p])
        fw.dma("sp", dsk, I["ssd_d"][l:l + 1, :].partition_broadcast(128), writes=[k_p])
        fw.dma("sp", nw, I["ssd_norm_w"][l:l + 1, :].partition_broadcast(128), writes=[k_p])
        fw.op("act", lambda g: g.activation(out=alog, in_=alog, func=AF.Exp), [k_p], [k_p])
        fw.op("dve", lambda g: g.tensor_scalar(out=alog, in0=alog, scalar1=-1.0, scalar2=None, op0=ALU.mult), [k_p], [k_p])
        dt = A.sb("dt", [128, 32, 8], F32)
        adt = A.sb("adt", [128, 32, 8], F32)
        acs = A.sb("acs", [128, 32, 8], F32)
        eacs = A.sb("eacs", [128, 32, 8], F32)
        dtdec = A.sb("dtdec", [128, 32, 8], F32)
        eatot = A.sb("eatot", [128, 32, 8], F32)
        esel = A.sb("esel", [128, 32, 4], F32)
        k_d = Trk()
        fw.op("dve", lambda g: g.tensor_tensor(out=dt, in0=DT, in1=dtb.unsqueeze(1).to_broadcast([128, 32, 8]), op=ALU.add), [k_DT, k_p], [k_d])
        fw.op("act", lambda g: g.activation(out=dt, in_=dt, func=AF.Exp), [k_d], [k_d])
        fw.op("act", lambda g: g.activation(out=dt, in_=dt, func=AF.Ln, bias=1.0), [k_d], [k_d])
        fw.op("dve", lambda g: g.tensor_tensor(out=adt, in0=dt, in1=alog.unsqueeze(1).to_broadcast([128, 32, 8]), op=ALU.mult), [k_d, k_p], [k_d])
        fl = lambda t: t.rearrange("p c h -> p (c h)")
        ps, kp = self.psum.get()
        self.mm(ps[:, 0:256], self.utri_f, fl(adt), True, True, [kc_, k_d], [kp])
        ps2, kp2 = self.psum.get()
        self.mm(ps2[:, 0:256], self.ones_f, fl(adt), True, True, [kc_, k_d], [kp2])
        fw.op("dve", lambda g: g.tensor_copy(out=fl(acs), in_=ps[:, 0:256]), [kp], [k_d])
        fw.op("act", lambda g: g.activation(out=fl(eacs), in_=ps[:, 0:256], func=AF.Exp), [kp], [k_d])
        fw.op("dve", lambda g: g.tensor_tensor(out=fl(dtdec), in0=ps2[:, 0:256], in1=fl(acs), op=ALU.subtract), [kp2, k_d], [k_d])
        fw.op("act", lambda g: g.activation(out=dtdec, in_=dtdec, func=AF.Exp), [k_d], [k_d])
        fw.op("dve", lambda g: g.tensor_tensor(out=dtdec, in0=dtdec, in1=dt, op=ALU.mult), [k_d], [k_d])
        fw.op("act", lambda g: g.activation(out=fl(eatot), in_=ps2[:, 0:256], func=AF.Exp), [kp2], [k_d])
        fw.op("dve", lambda g: g.tensor_copy(out=esel[0:64], in_=eatot[0:64, :, 0:4]), [k_d], [k_d])
        fw.op("dve", lambda g: g.tensor_copy(out=esel[64:128], in_=eatot[64:128, :, 4:8]), [k_d], [k_d])
        if self.dbg.get('ssd_stop') == 3:
            A.close()
            return
        xsr = A.ring("xs", 2, [128, 8, 64], F32)
        bpr = A.ring("bp", 2, [128, 2, 128], BF16)
        for b_ in bpr.b:
            fw.op("pool", lambda g: g.memset(b_, 0.0), writes=bpr.k)
        xdtr = A.ring("xdt", 2, [128, 8, 64], BF16)
        xddr = A.ring("xdd", 2, [128, 8, 64], BF16)
        Ar = A.ring("Aall", 1, [128, 8, 128], F32)
        Dr = A.ring("Dg", 1, [128, 4, 128], F32)
        Mr = A.ring("Mg", 2, [128, 4, 128], BF16)
        S = A.sb("S", [128, 4, 64], F32)
        k_S = Trk()
        fw.op("pool", lambda g: g.memset(S, 0.0), writes=[k_S])
        Sbr = A.ring("Sb", 2, [128, 2, 4, 64], BF16)
        for b_ in Sbr.b:
            fw.op("pool", lambda g: g.memset(b_, 0.0), writes=Sbr.k)
        yr = A.ring("y", 2, [128, 8, 64], F32)
        tmr = A.ring("tm", 1, [128, 8, 64], F32)
        ztr = A.ring("zt", 2, [128, 512], F32)
        ssr = A.ring("ss", 2, [128, 4], F32)
        ybr = A.ring("yb", 2, [128, 512], BF16)
        ytr = A.ring("yt", 2, [128, 4, 128], BF16)
        Sb_prev = None
        for c in range(self.dbg.get('ssd_nchunk', NTT)):
            tb = c // 4
            rdx = lambda ct: [k_cp, k_pad, k_xbc[ct][tb]] + ([k_xbc[ct][tb - 1]] if (tb and c % 4 == 0) else [])
            ps, kp = self.psum.get()
            for ct in range(4):
                o = ps[:, ct * 128:(ct + 1) * 128]
                for t in range(4):
                    self.mm(o, XBC[:, ct, c * 128 + t:c * 128 + t + 128], dgc[:, ct, t, :], t == 0, False, rdx(ct), [kp])
                self.mm(o, self.ones_b[0:1, :], cbh[0:1, ct * 128:(ct + 1) * 128], False, False, [kc_, k_cp], [kp])
                self.mm(o, self.ones_b[0:1, :], cbl[0:1, ct * 128:(ct + 1) * 128], False, True, [kc_, k_cp], [kp])
            psB, kpB = self.psum.get()
            o = psB[:, 0:128]
            for t in range(4):
                self.mm(o, XBC[:, 4, c * 128 + t:c * 128 + t + 128], dgc[:, 4, t, :], t == 0, False, rdx(4), [kpB])
            self.mm(o, self.ones_b[0:1, :], cbh[0:1, 512:640], False, False, [kc_, k_cp], [kpB])
            self.mm(o, self.ones_b[0:1, :], cbl[0:1, 512:640], False, True, [kc_, k_cp], [kpB])
            xs, kxs = xsr.get()
            fw.op("act", lambda g: g.activation(out=xs.rearrange("p h d -> p (h d)"), in_=ps, func=AF.Silu), [kp], [kxs])
            bp, kbp = bpr.get()
            fw.op("act", lambda g: g.activation(out=bp[:, 0, 0:64], in_=psB[:, 0:64], func=AF.Silu), [kpB], [kbp])
            fw.op("act", lambda g: g.activation(out=bp[:, 1, 64:128], in_=psB[:, 64:128], func=AF.Silu), [kpB], [kbp])
            if self.dbg.get('ssd_stop') == 4:
                continue
            xdt, kxdt = xdtr.get()
            xdd, kxdd = xddr.get()
            fw.op("dve", lambda g: g.tensor_tensor(out=xdt, in0=xs, in1=dt[:, c, :].unsqueeze(2).to_broadcast([128, 8, 64]), op=ALU.mult), [kxs, k_d], [kxdt])
            fw.op("pool", lambda g: g.tensor_tensor(out=xdd, in0=xs, in1=dtdec[:, c, :].unsqueeze(2).to_broadcast([128, 8, 64]), op=ALU.mult), [kxs, k_d], [kxdd])
            if self.dbg.get('ssd_stop') == 5:
                continue
            psG, kpG = self.psum.get()
            for g_ in range(self.dbg.get('ssd_ng', 2)):
                self.mm(psG[:, g_ * 128:(g_ + 1) * 128], BT[:, g_, c * 128:(c + 1) * 128], CT[:, c * 128:(c + 1) * 128], True, True,
                        [k_BT[tb], k_CT[tb]], [kpG])
            sub = self.dbg.get('ssd_sub', 99)
            if sub < 1:
                continue
            Aa, kA = Ar.get()
            fw.op("pool", lambda g: g.tensor_tensor(out=Aa, in0=self.ones_f.unsqueeze(1).to_broadcast([128, 8, 128]),
                                                    in1=adt[:, c, :].unsqueeze(2).to_broadcast([128, 8, 128]), op=ALU.mult), [kc_, k_d], [kA])
            if sub < 2:
                continue
            yd, kyd = self.psacc.get()
            for g_ in range(2):
                psS, kpS = self.psum.get()
                for hh in range(4):
                    o_ = psS[:, hh * 128:(hh + 1) * 128]
                    self.mm(o_, Aa[:, 4 * g_ + hh, :], self.utri_f, True, False, [kA, kc_], [kpS])
                    self.mm(o_, self.nutri_f, Aa[:, 4 * g_ + hh, :], False, False, [kA, kc_], [kpS])
                    self.mm(o_, self.identf, self.negmask4[:, 0, :], False, True, [kc_], [kpS])
                if sub < 3:
                    continue
                Dg, kD = Dr.get()
                fw.op("act", lambda g: g.activation(out=Dg.rearrange("p h l -> p (h l)"), in_=psS, func=AF.Exp), [kpS], [kD])
                if sub < 4:
                    continue
                Mg, kM = Mr.get()
                fw.op("dve", lambda g: g.tensor_tensor(out=Mg, in0=Dg, in1=psG[:, g_ * 128:(g_ + 1) * 128].unsqueeze(1).to_broadcast([128, 4, 128]),
                                                       op=ALU.mult), [kD, kpG], [kM])
                if sub < 5:
                    continue
                for hh in range(4):
                    h = 4 * g_ + hh
                    self.mm(yd[:, h * 64:(h + 1) * 64], Mg[:, hh, :], xdt[:, h, :], True, True, [kM, kxdt], [kyd])
            if sub < 99:
                continue
            if self.dbg.get('ssd_stop') == 6:
                continue
            pst, kst = self.psum.get()
            self.mm(pst[:, 0:256], bp[:, 0, :], xdd[:, 0:4, :].rearrange("p h d -> p (h d)"), True, False, [kbp, kxdd], [kst])
            self.mm(pst[:, 0:256], bp[:, 1, :], xdd[:, 4:8, :].rearrange("p h d -> p (h d)"), False, True, [kbp, kxdd], [kst])
            y, ky = yr.get()
            yf = y.rearrange("p h d -> p (h d)")
            if c > 0:
                Sb, kSb = Sb_prev
                yo, kyo = self.psum.get()
                for g_ in range(2):
                    self.mm(yo[:, g_ * 256:(g_ + 1) * 256], CT[:, c * 128:(c + 1) * 128], Sb[:, g_].rearrange("p h d -> p (h d)"), True, True,
                            [k_CT[tb], kSb], [kyo])
                fw.op("dve", lambda g: g.tensor_tensor(out=y, in0=yo.rearrange("p (h d) -> p h d", h=8),
                                                       in1=eacs[:, c, :].unsqueeze(2).to_broadcast([128, 8, 64]), op=ALU.mult), [kyo, k_d], [ky])
                fw.op("dve", lambda g: g.tensor_tensor(out=yf, in0=yf, in1=yd, op=ALU.add), [ky, kyd], [ky])
            else:
                self.copy("dve", yf, yd, [kyd], [ky])
            fw.op("dve", lambda g: g.tensor_tensor(out=S, in0=S, in1=esel[:, c, :].unsqueeze(2).to_broadcast([128, 4, 64]), op=ALU.mult), [k_S, k_d], [k_S])
            fw.op("dve", lambda g: g.tensor_tensor(out=S.rearrange("p h d -> p (h d)"), in0=S.rearrange("p h d -> p (h d)"), in1=pst[:, 0:256], op=ALU.add),
                  [k_S, kst], [k_S])
            Sb, kSb = Sbr.get()
            self.copy("pool", Sb[0:64, 0], S[0:64], [k_S], [kSb])
            self.copy("pool", Sb[64:128, 1], S[64:128], [k_S], [kSb])
            Sb_prev = (Sb, kSb)
            if self.dbg.get('ssd_stop') == 7:
                continue
            tm, ktm = tmr.get()
            fw.op("pool", lambda g: g.tensor_tensor(out=tm, in0=xs, in1=dsk.unsqueeze(2).to_broadcast([128, 8, 64]), op=ALU.mult), [kxs, k_p], [ktm])
            fw.op("pool", lambda g: g.tensor_tensor(out=y, in0=y, in1=tm, op=ALU.add), [ky, ktm], [ky])
            zt, kzt = ztr.get()
            fw.dma("sp", zt, self.zd[c * 128:(c + 1) * 128, :], reads=[self.k_zd[c]], writes=[kzt])
            fw.op("act", lambda g: g.activation(out=zt, in_=zt, func=AF.Silu), [kzt], [kzt])
            fw.op("dve", lambda g: g.tensor_tensor(out=yf, in0=yf, in1=zt, op=ALU.mult), [ky, kzt], [ky])
            if self.dbg.get('ssd_stop') == 8:
                continue
            ss, kss = ssr.get()
            tmf = tm.rearrange("p h d -> p (h d)")
            for g_ in range(2):
                fw.op("act", lambda g: g.activation(out=tmf[:, g_ * 256:(g_ + 1) * 256], in_=yf[:, g_ * 256:(g_ + 1) * 256], func=AF.Square,
                                                    accum_out=ss[:, g_:g_ + 1]), [ky, ktm], [ktm, kss])
            fw.op("dve", lambda g: g.tensor_scalar(out=ss[:, 0:2], in0=ss[:, 0:2], scalar1=1.0 / 256.0, scalar2=1e-5, op0=ALU.mult, op1=ALU.add), [kss], [kss])
            fw.op("act", lambda g: g.activation(out=ss[:, 0:2], in_=ss[:, 0:2], func=AF.Sqrt), [kss], [kss])
            fw.op("dve", lambda g: g.reciprocal(out=ss[:, 2:4], in_=ss[:, 0:2]), [kss], [kss])
            for g_ in range(2):
                fw.op("dve", lambda g: g.tensor_scalar(out=yf[:, g_ * 256:(g_ + 1) * 256], in0=yf[:, g_ * 256:(g_ + 1) * 256],
                                                       scalar1=ss[:, 2 + g_:3 + g_], scalar2=None, op0=ALU.mult), [ky, kss], [ky])
            yb, kyb = ybr.get()
            fw.op("pool", lambda g: g.tensor_tensor(out=yb, in0=yf, in1=nw, op=ALU.mult), [ky, k_p], [kyb])
            if self.dbg.get('ssd_stop') == 9:
                continue
            pT, kpT = self.psum.get()
            pTb = pT.bitcast(BF16)
            for ct in range(4):
                fw.op("pe", lambda g: g.transpose(pTb[:, ct * 128:(ct + 1) * 128], yb[:, ct * 128:(ct + 1) * 128], self.identb), [kyb, kc_], [kpT])
            yt, kyt = ytr.get()
            self.copy("act", yt, pTb[:, 0:512].rearrange("p (c t) -> p c t", c=4), [kpT], [kyt])
            fw.dma("sp", self.yT[1].rearrange("(ct p) t -> p ct t", p=128)[:, :, c * 128:(c + 1) * 128], yt, reads=[kyt], writes=[self.k_yT[1][tb]])
        A.close()


    def cmul(self, e, ore, oim, are, aim, bre, bim, t1, t2, k, conj_b=False):
        fw = self.fw
        tt = lambda o, a, b, op: fw.op(e, lambda g: g.tensor_tensor(out=o, in0=a, in1=b, op=op), k, k)
        tt(t1, are, bre, ALU.mult)
        tt(t2, aim, bim, ALU.mult)
        tt(ore, t1, t2, ALU.add if conj_b else ALU.subtract)
        tt(t1, are, bim, ALU.mult)
        tt(t2, aim, bre, ALU.mult)
        if conj_b:
            tt(oim, t2, t1, ALU.subtract)
        else:
            tt(oim, t1, t2, ALU.add)

    def sincos(self, arg, osin, ocos, ki, kf, k):
        fw = self.fw
        TWO_PI = 2.0 * np.pi
        for r, shift in ((osin, 0.0), (ocos, np.pi / 2)):
            fw.op("dve", lambda g: g.tensor_scalar(out=kf, in0=arg, scalar1=float(shift), scalar2=float(1.0 / TWO_PI), op0=ALU.add, op1=ALU.mult), k, k)
            fw.op("dve", lambda g: g.tensor_copy(out=ki, in_=kf), k, k)
            fw.op("dve", lambda g: g.tensor_copy(out=kf, in_=ki), k, k)
            fw.op("dve", lambda g: g.tensor_scalar(out=kf, in0=kf, scalar1=float(-TWO_PI), scalar2=float(shift), op0=ALU.mult, op1=ALU.add), k, k)
            fw.op("dve", lambda g: g.tensor_tensor(out=r, in0=kf, in1=arg, op=ALU.add), k, k)
            fw.op("dve", lambda g: g.tensor_scalar(out=r, in0=r, scalar1=3.141592, scalar2=-3.141592, op0=ALU.min, op1=ALU.max), k, k)
            fw.op("act", lambda g: g.activation(out=r, in_=r, func=AF.Sin), k, k)

    def s5(self, l):
        fw = self.fw
        I = self.inp
        A = Arena(fw)
        kc_ = self.k_ident
        w_d = I["w_in"][l].rearrange("(kc p) n -> p kc n", p=128)
        U = A.sb("U", [128, 32, 32, 16], BF16)
        k_U = [Trk() for _ in range(32)]
        k_p = Trk()
        kp_ = [k_p]
        T = lambda nm, shp=(128, 16): A.sb(nm, list(shp), F32)
        abr_keep = {}
        bbr, bbi = T("bbr", (128, 16, 16)), T("bbi", (128, 16, 16))
        cre, cim = T("cre", (128, 16, 16)), T("cim", (128, 16, 16))
        pwr_, pwi_ = T("pwr_", (128, 16, 65)), T("pwi_", (128, 16, 65))
        R = T("R")
        cosT = A.sb("cosT", [128, 16, 128], F32)
        sinT = A.sb("sinT", [128, 16, 128], F32)
        drep = T("drep", (128, 512))
        A2 = Arena(fw)
        wb = A2.sb("wu", [128, 8, 512], BF16)
        kw = Trk()
        fw.dma("pool", wb, w_d[:, :, 2832:3344], writes=[kw])
        for tau in range(32):
            ps, kp = self.psum.get()
            for kc in range(8):
                lhsT = self.xT[:, kc, :].rearrange("p (c t) -> p t c", t=32)[:, tau, :]
                self.mm(ps, lhsT, wb[:, kc, :], kc == 0, kc == 7, [kw] + self.k_xT, [kp])
            self.copy(self.alt(), U[:, :, tau, :], ps.rearrange("p (g h) -> p g h", h=16), [kp], [k_U[tau]])
        T2 = lambda nm, shp=(128, 16): A2.sb(nm, list(shp), F32)

        def ld(t, src):
            fw.dma("sp", t, src, writes=kp_)
            return t
        are, aim, lst = ld(T2("are"), I["s5_are"][l]), ld(T2("aim"), I["s5_aim"][l]), ld(T2("lst"), I["s5_lst"][l])
        bre, bim = ld(T2("bre", (128, 16, 16)), I["s5_bre"][l]), ld(T2("bim", (128, 16, 16)), I["s5_bim"][l])
        ld(cre, I["s5_cre"][l]); ld(cim, I["s5_cim"][l])
        ld(drep, I["s5_d"][l:l + 1, :].partition_broadcast(128))
        step, lre, den, t1, t2, xr_, th, mag = [T2(n) for n in ("step", "lre", "den", "t1", "t2", "xr_", "th", "mag")]
        sn, cs, abr, abi, nre, kre, kim = [T2(n) for n in ("sn", "cs", "abr", "abi", "nre", "kre", "kim")]
        ki16, kf16 = A2.sb("ki16", [128, 16], I32), T2("kf16")
        V = lambda fn: fw.op("dve", fn, kp_, kp_)
        fw.op("act", lambda g: g.activation(out=step, in_=lst, func=AF.Exp), kp_, kp_)
        V(lambda g: g.tensor_scalar(out=lre, in0=are, scalar1=-1e-4, scalar2=None, op0=ALU.min))
        V(lambda g: g.tensor_tensor(out=xr_, in0=lre, in1=step, op=ALU.mult))
        V(lambda g: g.tensor_tensor(out=th, in0=aim, in1=step, op=ALU.mult))
        fw.op("act", lambda g: g.activation(out=mag, in_=xr_, func=AF.Exp), kp_, kp_)
        self.sincos(th, sn, cs, ki16, kf16, kp_)
        V(lambda g: g.tensor_tensor(out=abr, in0=mag, in1=cs, op=ALU.mult))
        V(lambda g: g.tensor_tensor(out=abi, in0=mag, in1=sn, op=ALU.mult))
        V(lambda g: g.tensor_tensor(out=t1, in0=lre, in1=lre, op=ALU.mult))
        V(lambda g: g.tensor_tensor(out=t2, in0=aim, in1=aim, op=ALU.mult))
        V(lambda g: g.tensor_tensor(out=den, in0=t1, in1=t2, op=ALU.add))
        V(lambda g: g.reciprocal(out=den, in_=den))
        V(lambda g: g.tensor_scalar(out=nre, in0=abr, scalar1=-1.0, scalar2=None, op0=ALU.add))
        self.cmul("dve", kre, kim, nre, abi, lre, aim, t1, t2, kp_, conj_b=True)
        V(lambda g: g.tensor_tensor(out=kre, in0=kre, in1=den, op=ALU.mult))
        V(lambda g: g.tensor_tensor(out=kim, in0=kim, in1=den, op=ALU.mult))
        sh3 = [128, 16, 16]
        t3a, t3b = T2("t3a", sh3), T2("t3b", sh3)
        bc3 = lambda t: t.unsqueeze(2).to_broadcast(sh3)
        self.cmul("dve", bbr, bbi, bc3(kre), bc3(kim), bre, bim, t3a, t3b, kp_)
        shp = [128, 16, 65]
        ioti = A2.sb("ioti", [128, 65], I32)
        iot = T2("iot", (128, 65))
        fw.op("pool", lambda g: g.iota(ioti[:, 0:33], pattern=[[1, 33]], base=0, channel_multiplier=0), kp_, kp_)
        fw.op("pool", lambda g: g.iota(ioti[:, 33:65], pattern=[[-1, 32]], base=31, channel_multiplier=0), kp_, kp_)
        V(lambda g: g.tensor_copy(out=iot, in_=ioti))
        marg, parg, psn, pcs, kf65 = [T2(n, shp) for n in ("marg", "parg", "psn", "pcs", "kf65")]
        ki65 = A2.sb("ki65", shp, I32)
        bcm = lambda t: t.unsqueeze(2).to_broadcast(shp)
        bci = iot.unsqueeze(1).to_broadcast(shp)
        V(lambda g: g.tensor_tensor(out=marg, in0=bcm(xr_), in1=bci, op=ALU.mult))
        V(lambda g: g.tensor_tensor(out=parg, in0=bcm(th), in1=bci, op=ALU.mult))
        fw.op("act", lambda g: g.activation(out=marg, in_=marg, func=AF.Exp), kp_, kp_)
        self.sincos(parg, psn, pcs, ki65, kf65, kp_)
        V(lambda g: g.tensor_tensor(out=pwr_, in0=marg, in1=pcs, op=ALU.mult))
        V(lambda g: g.tensor_tensor(out=pwi_, in0=marg, in1=psn, op=ALU.mult))
        V(lambda g: g.tensor_copy(out=R, in_=marg[:, :, 32]))
        wre, wim, u1, u2 = [T2(n) for n in ("wre", "wim", "u1", "u2")]
        V(lambda g: g.tensor_copy(out=wre, in_=pcs[:, :, 32]))
        V(lambda g: g.tensor_copy(out=wim, in_=psn[:, :, 32]))
        fw.op("pool", lambda g: g.memset(cosT[:, :, 0:1], 1.0), kp_, kp_)
        fw.op("pool", lambda g: g.memset(sinT[:, :, 0:1], 0.0), kp_, kp_)
        tA, tB = T2("tA", (128, 16, 64)), T2("tB", (128, 16, 64))
        n_ = 1
        while n_ < 128:
            shn = [128, 16, n_]
            bw = lambda t: t.unsqueeze(2).to_broadcast(shn)
            self.cmul("dve", cosT[:, :, n_:2 * n_], sinT[:, :, n_:2 * n_], cosT[:, :, 0:n_], sinT[:, :, 0:n_], bw(wre), bw(wim),
                      tA[:, :, 0:n_], tB[:, :, 0:n_], kp_)
            self.cmul("dve", u1, u2, wre, wim, wre, wim, t1, t2, kp_)
            V(lambda g: g.tensor_copy(out=wre, in_=u1))
            V(lambda g: g.tensor_copy(out=wim, in_=u2))
            n_ *= 2
        A2.close()
        Zr = A.ring("Z", 2, [128, 2, 32, 16], BF16)
        ABr = A.ring("ABp", 2, [128, 8, 2, 128], BF16)
        for b_ in ABr.b:
            fw.op("pool", lambda g: g.memset(b_, 0.0), writes=ABr.k)
        CAr = A.ring("CAP", 2, [128, 2, 33, 16], BF16)
        BBr = A.ring("BBrep", 2, [128, 2, 2, 8, 16], BF16)
        for b_ in BBr.b:
            fw.op("pool", lambda g: g.memset(b_, 0.0), writes=BBr.k)
        UTr = A.ring("UT", 3, [128, 4, 128], BF16)
        TBf = A.ring("TBf", 1, [128, 512], F32)
        TBr = A.ring("TB", 2, [128, 512], BF16)
        gsc = A.ring("gsc", 2, [128, 4, 128], F32)
        Rbr = A.ring("Rb", 2, [128, 128], F32)
        Spr = A.ring("Sprev", 2, [128, 2, 2, 128], BF16)
        for b_ in Spr.b:
            fw.op("pool", lambda g: g.memset(b_, 0.0), writes=Spr.k)
        ypr = A.ring("ypre", 2, [128, 32, 16], F32)
        y2r = A.ring("y2", 2, [128, 32, 16], F32)
        GYr = A.ring("GY", 1, [128, 32, 128], BF16)
        gts = A.ring("gts", 1, [128, L_SEQ], BF16)
        z3, z4 = T("z3", (128, 32, 16)), T("z4", (128, 32, 16))
        cA, cB = T("cA", (128, 33, 16)), T("cB", (128, 33, 16))
        k_z = Trk()
        GY, kGY = None, None
        for gp in range(16):
            Z, kZ = Zr.get()
            shz = [128, 32, 16]
            pa = lambda t: t[:, gp, 33:65].unsqueeze(2).to_broadcast(shz)
            pb = lambda t: t[:, gp, :].unsqueeze(1).to_broadcast(shz)
            self.cmul("pool", Z[:, 0], Z[:, 1], pa(pwr_), pa(pwi_), pb(bbr), pb(bbi), z3, z4, [k_p, k_z, kZ])
            pT, kpT = self.psum.get()
            pTb = pT.bitcast(BF16)
            for a in range(4):
                for ri in range(2):
                    j = a * 2 + ri
                    fw.op("pe", lambda g: g.transpose(pTb[:, j * 128:(j + 1) * 128], Z[:, ri, 8 * a:8 * a + 8, :].rearrange("p t h -> p (t h)"),
                                                      self.identb), [kZ, kc_], [kpT])
            AB, kAB = ABr.get()
            src = pTb.rearrange("p (j m) -> p j m", j=8)
            self.copy("act", AB[:, :, 0, 0:64], src[:, :, 0:64], [kpT], [kAB])
            self.copy("act", AB[:, :, 1, 64:128], src[:, :, 64:128], [kpT], [kAB])
            CAP, kCA = CAr.get()
            shc = [128, 33, 16]
            pa2 = lambda t: t[:, gp, 0:33].unsqueeze(2).to_broadcast(shc)
            pc2 = lambda t: t[:, gp, :].unsqueeze(1).to_broadcast(shc)
            kz2 = [k_p, k_z]
            tt = lambda o, a_, b_, op, wr: fw.op("dve", lambda g: g.tensor_tensor(out=o, in0=a_, in1=b_, op=op), kz2 + wr, [k_z] + wr)
            tt(cA, pa2(pwr_), pc2(cre), ALU.mult, [])
            tt(cB, pa2(pwi_), pc2(cim), ALU.mult, [])
            tt(CAP[:, 0], cA, cB, ALU.subtract, [kCA])
            tt(cA, pa2(pwr_), pc2(cim), ALU.mult, [])
            tt(cB, pa2(pwi_), pc2(cre), ALU.mult, [])
            tt(cA, cA, cB, ALU.add, [])
            fw.op("dve", lambda g: g.tensor_scalar(out=CAP[:, 1], in0=cA, scalar1=-1.0, scalar2=None, op0=ALU.mult), [k_z, kCA], [kCA])
            BB, kBB = BBr.get()
            for (half, gl) in ((slice(0, 64), 0), (slice(64, 128), 1)):
                fw.op("pool", lambda g: g.tensor_copy(out=BB[half, gl, 0], in_=bbr[half, gp, :].unsqueeze(1).to_broadcast([64, 8, 16])), [k_p], [kBB])
                fw.op("pool", lambda g: g.tensor_copy(out=BB[half, gl, 1], in_=bbi[half, gp, :].unsqueeze(1).to_broadcast([64, 8, 16])), [k_p], [kBB])
            UTs = []
            for gl in range(2):
                g_ = 2 * gp + gl
                pU, kpU = self.psum.get()
                pUb = pU.bitcast(BF16)
                UT, kUT = UTr.get()
                UTs.append((UT, kUT))
                for a in range(4):
                    fw.op("pe", lambda g: g.transpose(pUb[:, a * 128:(a + 1) * 128], U[:, g_, 8 * a:8 * a + 8, :].rearrange("p t h -> p (t h)"), self.identb),
                          k_U[8 * a:8 * a + 8] + [kc_], [kpU])
                self.copy(self.alt(), UT, pUb[:, 0:512].rearrange("p (a c) -> p a c", a=4), [kpU], [kUT])
            pW, kpW = self.psum.get()
            for ri in range(2):
                o = pW[:, ri * 128:(ri + 1) * 128]
                n_mm = 0
                for gl in range(2):
                    UT, kUT = UTs[gl]
                    for a in range(4):
                        self.mm(o, AB[:, a * 2 + ri, gl, :], UT[:, a, :], n_mm == 0, n_mm == 7, [kAB, kUT], [kpW])
                        n_mm += 1
            gs, kgs = gsc.get()
            cT_, sT_ = cosT[:, gp, :], sinT[:, gp, :]
            W0, W1 = pW[:, 0:128], pW[:, 128:256]
            dv = lambda o, a_, b_, op: fw.op("dve", lambda g: g.tensor_tensor(out=o, in0=a_, in1=b_, op=op), [kpW, k_p, kgs], [kgs])
            dv(gs[:, 2], W0, cT_, ALU.mult); dv(gs[:, 3], W1, sT_, ALU.mult); dv(gs[:, 0], gs[:, 2], gs[:, 3], ALU.add)
            dv(gs[:, 2], W1, cT_, ALU.mult); dv(gs[:, 3], W0, sT_, ALU.mult); dv(gs[:, 1], gs[:, 2], gs[:, 3], ALU.subtract)
            Rb, kRb = Rbr.get()
            fw.op("pool", lambda g: g.tensor_scalar(out=Rb, in0=self.ones_f, scalar1=R[:, gp:gp + 1], scalar2=0.0, op0=ALU.mult, op1=ALU.add),
                  [kc_, k_p], [kRb])
            for ri in range(2):
                fw.op("dve", lambda g: g.tensor_tensor_scan(out=gs[:, 2 + ri], data0=Rb, data1=gs[:, ri], initial=0.0, op0=ALU.mult, op1=ALU.add),
                      [kgs, kRb], [kgs])
            Sp, kSp = Spr.get()
            n1 = slice(0, 127)
            dv2 = lambda o, a_, b_, op, wr: fw.op("dve", lambda g: g.tensor_tensor(out=o, in0=a_, in1=b_, op=op), [k_p, kgs] + wr, [kgs] + wr)
            for (half, gl) in ((slice(0, 64), 0), (slice(64, 128), 1)):
                dv2(gs[half, 0, n1], gs[half, 2, n1], cT_[half, n1], ALU.mult, [])
                dv2(gs[half, 1, n1], gs[half, 3, n1], sT_[half, n1], ALU.mult, [])
                dv2(Sp[half, gl, 0, 1:128], gs[half, 0, n1], gs[half, 1, n1], ALU.subtract, [kSp])
                dv2(gs[half, 0, n1], gs[half, 2, n1], sT_[half, n1], ALU.mult, [])
                dv2(gs[half, 1, n1], gs[half, 3, n1], cT_[half, n1], ALU.mult, [])
                dv2(Sp[half, gl, 1, 1:128], gs[half, 0, n1], gs[half, 1, n1], ALU.add, [kSp])
            for gl in range(2):
                g_ = 2 * gp + gl
                rows = slice(gl * 64, (gl + 1) * 64)
                UT, kUT = UTs[gl]
                pK, kpK = self.psum.get()
                for ri in range(2):
                    lhs = BB[:, gl, ri].rearrange("p s h -> p (s h)")
                    self.mm(pK, lhs, CAP[:, ri, 0:32, :].rearrange("p m h -> p (m h)"), ri == 0, ri == 1, [kBB, kCA], [kpK])
                TBf_, kTBf = TBf.get()
                fw.op("dve", lambda g: g.tensor_scalar(out=TBf_, in0=pK, scalar1=self.rowmask[:, 0:1], scalar2=None, op0=ALU.mult), [kpK, kc_], [kTBf])
                for s_ in range(1, 8):
                    fw.op("dve", lambda g: g.scalar_tensor_tensor(out=TBf_[:, 16 * s_:512], in0=pK[:, 0:512 - 16 * s_], scalar=self.rowmask[:, s_:s_ + 1],
                                                                  in1=TBf_[:, 16 * s_:512], op0=ALU.mult, op1=ALU.add), [kpK, kc_, kTBf], [kTBf])
                TB_, kTB = TBr.get()
                self.copy("act", TB_, TBf_, [kTBf], [kTB])
                pY, kpY = self.psacc.get()
                for a in range(4):
                    self.mm(pY[:, 128 * a:512], UT[:, a, :], TB_[:, 0:512 - 128 * a], a == 0, False, [kUT, kTB], [kpY])
                for ri in range(2):
                    self.mm(pY, Sp[:, gl, ri, :], CAP[:, ri, 1:33, :].rearrange("p m h -> p (m h)"), False, ri == 1, [kSp, kCA], [kpY])
                yp, kyp = ypr.get()
                fw.op("pool", lambda g: g.tensor_tensor(out=yp, in0=U[:, g_, :, :],
                                                        in1=drep[:, 16 * g_:16 * g_ + 16].unsqueeze(1).to_broadcast([128, 32, 16]), op=ALU.mult),
                      k_U + [k_p], [kyp])
                fw.op("dve", lambda g: g.tensor_tensor(out=yp, in0=yp, in1=pY.rearrange("p (t h) -> p t h", h=16), op=ALU.add), [kyp, kpY], [kyp])
                y2, ky2 = y2r.get()
                fw.op("pool", lambda g: g.tensor_tensor(out=y2, in0=yp, in1=yp, op=ALU.mult), [kyp], [ky2])
                fw.op("pool", lambda g: g.tensor_scalar(out=y2, in0=y2, scalar1=0.044715, scalar2=1.0, op0=ALU.mult, op1=ALU.add), [ky2], [ky2])
                fw.op("pool", lambda g: g.tensor_tensor(out=y2, in0=y2, in1=yp, op=ALU.mult), [ky2, kyp], [ky2])
                fw.op("act", lambda g: g.activation(out=y2, in_=y2, func=AF.Sigmoid, scale=1.5957691216057308), [ky2], [ky2])
                if g_ % 8 == 0:
                    GY, kGY = GYr.get()
                fw.op("dve", lambda g: g.tensor_tensor(out=GY[:, :, (g_ % 8) * 16:(g_ % 8 + 1) * 16], in0=yp, in1=y2, op=ALU.mult), [kyp, ky2], [kGY])
                if g_ % 8 == 7:
                    kt = g_ // 8
                    gt, kgt = gts.get()
                    gtv = gt.rearrange("p (c t) -> p t c", t=32)
                    for b in range(4):
                        pT, kpT = self.psum.get()
                        pTb = pT.bitcast(BF16)
                        for t8 in range(8):
                            fw.op("pe", lambda g: g.transpose(pTb[:, t8 * 128:(t8 + 1) * 128], GY[:, 8 * b + t8, :], self.identb), [kGY, kc_], [kpT])
                        self.copy(self.alt(), gtv[:, 8 * b:8 * b + 8, :], pTb.rearrange("p (t c) -> p t c", t=8), [kpT], [kgt])
                    fw.dma("sp", self.gyT[kt], gt, reads=[kgt], writes=[self.k_gyT[kt]])
        A.close()
        A = Arena(fw)
        wg = A.sb("wglu", [128, 4, 512], BF16)
        bglu = A.sb("bglu", [128, 4], F32)
        k_wg = Trk()
        fw.dma("pool", wg, I["s5_w_glu"][l].rearrange("(kc p) n -> p kc n", p=128), writes=[k_wg])
        fw.dma("sp", bglu, I["s5_bglu"][l], writes=[k_wg])
        gbr = A.ring("gb", 2, [128, 4, 512], BF16)
        sgr = A.ring("sg", 2, [128, 512], F32)
        obr = A.ring("ob", 2, [128, 512], BF16)
        for tb in range(NTB):
            gb, kgb = gbr.get()
            fw.dma("sp", gb, self.gyT[:, :, tb * 512:(tb + 1) * 512].rearrange("k p t -> p k t"), reads=self.k_gyT, writes=[kgb])
            for oc in range(4):
                ps, kp = self.psum.get()
                for kc in range(4):
                    self.mm(ps, wg[:, kc, oc * 128:(oc + 1) * 128], gb[:, kc, :], kc == 0, kc == 3, [k_wg, kgb], [kp])
                sg, ksg = sgr.get()
                fw.op("act", lambda g: g.activation(out=sg, in_=ps, func=AF.Sigmoid, bias=bglu[:, oc:oc + 1]), [kp, k_wg], [ksg])
                ob, kob = obr.get()
                fw.op("dve", lambda g: g.tensor_tensor(out=ob, in0=gb[:, oc, :], in1=sg, op=ALU.mult), [kgb, ksg], [kob])
                fw.dma("sp", self.yT[2, oc * 128:(oc + 1) * 128, tb * 512:(tb + 1) * 512], ob, reads=[kob], writes=[self.k_yT[2][tb]])
        A.close()


    def merge(self, l, res_src, k_res):
        fw = self.fw
        I = self.inp
        A = Arena(fw)
        Y = A.sb("Y", [128, 3, 4, L_SEQ], BF16)
        k_Y = [Trk() for _ in range(3)]
        for r in range(3):
            fw.dma("sp", Y[:, r], self.yT[r].rearrange("(kc p) t -> p kc t", p=128), reads=self.k_yT[r], writes=[k_Y[r]])
        wgr = A.ring("wgt", 2, [128, 3, 8, 128], BF16)
        wbr = A.ring("wbr", 2, [128, 3, 4, 128], BF16)
        bg = A.sb("bg", [128, 3, 8], F32)
        k_bg = Trk()
        fw.dma("sp", bg, I["b_gate_l"][l], writes=[k_bg])
        w_d = I["w_in"][l].rearrange("(kc p) n -> p kc n", p=128)
        wbr_d = I["w_branch"][l].rearrange("r (kc p) d -> p r kc d", p=128)
        gsr = A.ring("gs", 3, [128, 512], F32)
        accr = A.ring("acc", 2, [128, 512], F32)
        mbr = A.ring("mb", 2, [128, 512], BF16)
        for dt in range(8):
            wg, kwg = wgr.get()
            for r in range(3):
                c0 = 3344 + r * 1024 + dt * 128
                fw.dma("pool", wg[:, r], w_d[:, :, c0:c0 + 128], writes=[kwg])
            wb, kwb = wbr.get()
            for r in range(3):
                fw.dma("pool", wb[:, r], wbr_d[:, r, :, dt * 128:(dt + 1) * 128], writes=[kwb])
            for tb in range(NTB):
                ts_ = slice(tb * 512, (tb + 1) * 512)
                acc, kacc = accr.get()
                for r in range(3):
                    pg, kpg = self.psum.get()
                    for kc in range(8):
                        self.mm(pg, wg[:, r, kc, :], self.xT[:, kc, ts_], kc == 0, kc == 7, [kwg] + self.k_xT[tb * 4:(tb + 1) * 4], [kpg])
                    gs, kgs = gsr.get()
                    fw.op("act", lambda g: g.activation(out=gs, in_=pg, func=AF.Sigmoid, bias=bg[:, r, dt:dt + 1]), [kpg, k_bg], [kgs])
                    pp, kpp = self.psum.get()
                    for kc in range(4):
                        self.mm(pp, wb[:, r, kc, :], Y[:, r, kc, ts_], kc == 0, kc == 3, [kwb, k_Y[r]], [kpp])
                    if r == 0:
                        fw.op("dve", lambda g: g.tensor_tensor(out=acc, in0=gs, in1=pp, op=ALU.mult), [kgs, kpp], [kacc])
                    else:
                        fw.op("dve", lambda g: g.tensor_tensor(out=gs, in0=gs, in1=pp, op=ALU.mult), [kgs, kpp], [kgs])
                        if r == 1:
                            fw.op("pool", lambda g: g.tensor_tensor(out=acc, in0=acc, in1=gs, op=ALU.add), [kacc, kgs], [kacc])
                        else:
                            mb, kmb = mbr.get()
                            fw.op("pool", lambda g: g.tensor_tensor(out=mb, in0=acc, in1=gs, op=ALU.add), [kacc, kgs], [kmb])
                            fw.dma("sp", self.mT[dt, :, ts_], mb, reads=[kmb], writes=[self.k_mT[dt][tb]])
        A.close()
        A = Arena(fw)
        self.alloc_ln(A)
        self.load_ln(I["ln1_g"][l:l + 1, :], I["ln1_b"][l:l + 1, :])
        wo = A.sb("wo", [128, 8, D], BF16)
        k_wo = Trk()
        fw.dma("pool", wo, I["w_out"][l].rearrange("(kc p) d -> p kc d", p=128), writes=[k_wo])
        mir = A.ring("mi", 2, [128, 8, 512], BF16)
        for tb in range(NTB):
            mi, kmi = mir.get()
            fw.dma("sp", mi, self.mT[:, :, tb * 512:(tb + 1) * 512].rearrange("k p t -> p k t"),
                   reads=[self.k_mT[dt][tb] for dt in range(8)], writes=[kmi])
            for ts in range(4):
                tt = tb * 4 + ts
                p0, k0 = self.psum.get()
                p1, k1 = self.psum.get()
                for h, (pp, kk) in enumerate(((p0, k0), (p1, k1))):
                    for kc in range(8):
                        self.mm(pp, mi[:, kc, ts * 128:(ts + 1) * 128], wo[:, kc, h * 512:(h + 1) * 512], kc == 0, kc == 7, [kmi, k_wo], [kk])
                self.resid_ln(tt, (p0, p1), (k0, k1), res_src, k_res[tt], self.xr, self.k_xr[tt])
        A.close()

    def setup_ffn(self):
        fw = self.fw
        self.actT = fw.dram("actT", [22, 128, L_SEQ], BF16)
        self.k_actT = [[Trk() for _ in range(NTB)] for _ in range(22)]

    def ffn(self, l, out_dram, k_out):
        fw = self.fw
        I = self.inp
        A = Arena(fw)
        self.alloc_ln(A)
        self.wdown = A.sb("wdown", [128, 22, D], BF16)
        self.k_wdown = Trk("wdown")
        self.wup = A.ring("wup", 2, [128, 8, 256], BF16)
        self.fcw = A.sb("fcw", [128, 44, 3], F32)
        self.fcb = A.sb("fcb", [128, 44], F32)
        self.k_fc = Trk("fc")
        self.sv = A.ring("sv", 3, [128, 514], BF16)
        self.sg = A.ring("sg", 3, [128, 514], BF16)
        self.dg = A.ring("dg", 2, [128, 6, 128], BF16)
        self.hg = A.ring("hg", 2, [128, 512], F32)
        self.actb = A.ring("actb", 3, [128, 512], BF16)
        self.actin = A.ring("actin", 2, [128, 22, 512], BF16)
        wup_d = I["ffn_w_up"][l].rearrange("(kc p) n -> p kc n", p=128)
        fw.dma("sp", self.fcw, I["ffn_cw"][l], writes=[self.k_fc])
        fw.dma("sp", self.fcb, I["ffn_cb"][l], writes=[self.k_fc])
        k_all_xT = self.k_xT
        for m in range(22):
            wb, kw = self.wup.get()
            fw.dma("pool", wb[:, :, 0:128], wup_d[:, :, m * 128:(m + 1) * 128], writes=[kw])
            fw.dma("pool", wb[:, :, 128:256], wup_d[:, :, DFF + m * 128:DFF + (m + 1) * 128], writes=[kw])
            dg, kdg = self.dg.get()
            for t in range(3):
                fw.op("pool", lambda g: g.tensor_scalar(out=dg[:, t, :], in0=self.identf, scalar1=self.fcw[:, m, t:t + 1], scalar2=0.0,
                                                        op0=ALU.mult, op1=ALU.add), reads=[self.k_ident, self.k_fc], writes=[kdg])
                fw.op("pool", lambda g: g.tensor_scalar(out=dg[:, 3 + t, :], in0=self.identf, scalar1=self.fcw[:, 22 + m, t:t + 1], scalar2=0.0,
                                                        op0=ALU.mult, op1=ALU.add), reads=[self.k_ident, self.k_fc], writes=[kdg])
            prev = None
            for tb in range(NTB):
                xk = k_all_xT[tb * 4:(tb + 1) * 4]
                pv, kpv = self.psum.get()
                for kc in range(8):
                    self.mm(pv, wb[:, kc, 0:128], self.xT[:, kc, tb * 512:(tb + 1) * 512], kc == 0, kc == 7, [kw] + xk, [kpv])
                pg, kpg = self.psum.get()
                for kc in range(8):
                    self.mm(pg, wb[:, kc, 128:256], self.xT[:, kc, tb * 512:(tb + 1) * 512], kc == 0, kc == 7, [kw] + xk, [kpg])
                sv, ksv = self.sv.get()
                sg, ksg = self.sg.get()
                self.copy("act", sv[:, 2:514], pv, [kpv], [ksv])
                self.copy("act", sg[:, 2:514], pg, [kpg], [ksg])
                if prev is None:
                    fw.op("pool", lambda g: g.memset(sv[:, 0:2], 0.0), writes=[ksv])
                    fw.op("pool", lambda g: g.memset(sg[:, 0:2], 0.0), writes=[ksg])
                else:
                    psv, pksv, psg, pksg = prev
                    fw.op("pool", lambda g: g.tensor_copy(out=sv[:, 0:2], in_=psv[:, 512:514]), reads=[pksv], writes=[ksv])
                    fw.op("pool", lambda g: g.tensor_copy(out=sg[:, 0:2], in_=psg[:, 512:514]), reads=[pksg], writes=[ksg])
                prev = (sv, ksv, sg, ksg)
                cv, kcv = self.psum.get()
                for t in range(3):
                    self.mm(cv, dg[:, t, :], sv[:, t:t + 512], t == 0, t == 2, [kdg, ksv], [kcv])
                cg, kcg = self.psum.get()
                for t in range(3):
                    self.mm(cg, dg[:, 3 + t, :], sg[:, t:t + 512], t == 0, t == 2, [kdg, ksg], [kcg])
                hg, khg = self.hg.get()
                fw.op("act", lambda g: g.activation(out=hg, in_=cg, func=AF.Silu, bias=self.fcb[:, 22 + m:23 + m]),
                      reads=[kcg, self.k_fc], writes=[khg])
                ab, kab = self.actb.get()
                fw.op("dve", lambda g: g.scalar_tensor_tensor(out=ab, in0=cv, scalar=self.fcb[:, m:m + 1], in1=hg,
                                                              op0=ALU.add, op1=ALU.mult), reads=[kcv, self.k_fc, khg], writes=[kab])
                fw.dma("sp", self.actT[m, :, tb * 512:(tb + 1) * 512], ab, reads=[kab], writes=[self.k_actT[m][tb]])
        fw.dma("pool", self.wdown, I["ffn_w_down"][l].rearrange("(kt p) d -> p kt d", p=128), writes=[self.k_wdown])
        self.load_ln(I["ln2_g"][l:l + 1, :], I["ln2_b"][l:l + 1, :])
        for tb in range(NTB):
            ai, kai = self.actin.get()
            fw.dma("sp", ai, self.actT[:, :, tb * 512:(tb + 1) * 512].rearrange("m p t -> p m t"),
                   reads=[self.k_actT[m][tb] for m in range(22)], writes=[kai])
            for ts in range(4):
                tt = tb * 4 + ts
                p0, k0 = self.psum.get()
                p1, k1 = self.psum.get()
                for h, (pp, kk) in enumerate(((p0, k0), (p1, k1))):
                    for m in range(22):
                        self.mm(pp, ai[:, m, ts * 128:(ts + 1) * 128], self.wdown[:, m, h * 512:(h + 1) * 512], m == 0, m == 21,
                                [kai, self.k_wdown], [kk])
                self.resid_ln(tt, (p0, p1), (k0, k1), self.xr, self.k_xr[tt], out_dram, k_out[tt])
        A.close()


def _host_layout(inputs):
    f = {}
    A = lambda a: np.ascontiguousarray(np.asarray(a, dtype=np.float32))
    for k in ("w_in", "ffn_w_up", "ffn_w_down", "ln1_g", "ln1_b", "ln2_g", "ln2_b", "w_out", "w_branch", "s5_w_glu"):
        f[k] = A(inputs[k])
    cw = A(inputs["ffn_conv_w"])
    sw = A(inputs["ssd_conv_w"])
    f["ssd_cw"] = A(sw.reshape(DEPTH, 4, 6, 128).transpose(0, 3, 2, 1))
    f["ssd_cb"] = A(A(inputs["ssd_conv_b"]).reshape(DEPTH, 6, 128).transpose(0, 2, 1))
    for k in ("ssd_conv_b", "ssd_dt_bias", "ssd_a_log", "ssd_d", "ssd_norm_w", "fox_f_bias"):
        f[k] = A(inputs[k])
    pl = lambda a: A(a.reshape(DEPTH, 16, 2, 64).transpose(0, 2, 3, 1).reshape(DEPTH, 128, 16))
    f["s5_are"] = pl(A(inputs["s5_a_re"]))
    f["s5_aim"] = pl(A(inputs["s5_a_im"]))
    f["s5_lst"] = pl(np.repeat(A(inputs["s5_log_step"])[:, :, None], 64, axis=2))
    pb_ = lambda a: A(a.reshape(DEPTH, 16, 2, 64, 16).transpose(0, 2, 3, 1, 4).reshape(DEPTH, 128, 16, 16))
    f["s5_bre"] = pb_(A(inputs["s5_b_re"]))
    f["s5_bim"] = pb_(A(inputs["s5_b_im"]))
    f["s5_cre"] = pb_(A(A(inputs["s5_c_re"]).transpose(0, 1, 3, 2)))
    f["s5_cim"] = pb_(A(A(inputs["s5_c_im"]).transpose(0, 1, 3, 2)))
    f["s5_d"] = A(inputs["s5_d"])
    f["s5_bglu"] = A(A(inputs["s5_b_glu"]).reshape(DEPTH, 4, 128).transpose(0, 2, 1))
    f["b_gate_l"] = A(A(inputs["b_gate"]).reshape(DEPTH, 3, 8, 128).transpose(0, 3, 1, 2))
    f["ffn_cw"] = A(cw.reshape(DEPTH, 3, 44, 128).transpose(0, 3, 2, 1))
    f["ffn_cb"] = A(A(inputs["ffn_conv_b"]).reshape(DEPTH, 44, 128).transpose(0, 2, 1))
    return f


def build_ffn_test():
    k = K()
    x = k.din("x", [L_SEQ, D])
    k.din("ffn_w_up", [DEPTH, D, 2 * DFF]); k.din("ffn_w_down", [DEPTH, DFF, D])
    k.din("ffn_cw", [DEPTH, 128, 44, 3]); k.din("ffn_cb", [DEPTH, 128, 44])
    k.din("ln2_g", [DEPTH, D]); k.din("ln2_b", [DEPTH, D])
    out = k.nc.dram_tensor("out", [L_SEQ, D], F32, kind="ExternalOutput").ap()
    k.setup_common()
    k.setup_ffn()
    k.phase0(x)
    k.xr = x
    k_out = [Trk() for _ in range(NTT)]
    k.ffn(0, out, k_out)
    k.fw.drain()
    return k


def build_fox_test():
    k = K()
    x = k.din("x", [L_SEQ, D])
    k.din("w_in", [DEPTH, D, DIN]); k.din("fox_f_bias", [DEPTH, 8])
    out = k.nc.dram_tensor("out", [512, L_SEQ], BF16, kind="ExternalOutput").ap()
    k.setup_common()
    k.phase0(x)
    k.fox(0)
    t = k.fw.sb("cp", [128, 4, L_SEQ], BF16)
    kt = Trk()
    k.fw.dma("sp", t, k.yT[0].rearrange("(c p) t -> p c t", p=128), reads=[x for r in k.k_yT[0] for x in [r]], writes=[kt])
    k.fw.dma("sp", out.rearrange("(c p) t -> p c t", p=128), t, reads=[kt], writes=[Trk()])
    k.fw.drain()
    return k


def build_ssd_test(dbg=None):
    k = K(dbg=dbg)
    x = k.din("x", [L_SEQ, D])
    k.din("w_in", [DEPTH, D, DIN])
    k.din("ssd_cw", [DEPTH, 128, 6, 4]); k.din("ssd_cb", [DEPTH, 128, 6]); k.din("ssd_conv_b", [DEPTH, 768])
    k.din("ssd_dt_bias", [DEPTH, 8]); k.din("ssd_a_log", [DEPTH, 8]); k.din("ssd_d", [DEPTH, 8]); k.din("ssd_norm_w", [DEPTH, 512])
    out = k.nc.dram_tensor("out", [512, L_SEQ], BF16, kind="ExternalOutput").ap()
    k.setup_common()
    k.phase0(x)
    k.ssd(0)
    t = k.fw.sb("cp", [128, 4, L_SEQ], BF16)
    kt = Trk()
    k.fw.dma("sp", t, k.yT[1].rearrange("(c p) t -> p c t", p=128), reads=list(k.k_yT[1]), writes=[kt])
    k.fw.dma("sp", out.rearrange("(c p) t -> p c t", p=128), t, reads=[kt], writes=[Trk()])
    k.fw.drain()
    return k


S5_INS = [("s5_are", [DEPTH, 128, 16]), ("s5_aim", [DEPTH, 128, 16]), ("s5_lst", [DEPTH, 128, 16]),
          ("s5_bre", [DEPTH, 128, 16, 16]), ("s5_bim", [DEPTH, 128, 16, 16]), ("s5_cre", [DEPTH, 128, 16, 16]),
          ("s5_cim", [DEPTH, 128, 16, 16]), ("s5_d", [DEPTH, 512]), ("s5_bglu", [DEPTH, 128, 4]), ("s5_w_glu", [DEPTH, 512, 512])]


def build_s5_test(dbg=None):
    k = K(dbg=dbg)
    x = k.din("x", [L_SEQ, D])
    k.din("w_in", [DEPTH, D, DIN])
    for n, shp in S5_INS:
        k.din(n, shp)
    out = k.nc.dram_tensor("out", [512, L_SEQ], BF16, kind="ExternalOutput").ap()
    k.setup_common()
    k.phase0(x)
    k.s5(0)
    t = k.fw.sb("cp", [128, 4, L_SEQ], BF16)
    kt = Trk()
    k.fw.dma("sp", t, k.yT[2].rearrange("(c p) t -> p c t", p=128), reads=list(k.k_yT[2]), writes=[kt])
    k.fw.dma("sp", out.rearrange("(c p) t -> p c t", p=128), t, reads=[kt], writes=[Trk()])
    k.fw.drain()
    return k


ALL_INS = [("w_in", [DEPTH, D, DIN]), ("fox_f_bias", [DEPTH, 8]),
           ("ssd_cw", [DEPTH, 128, 6, 4]), ("ssd_cb", [DEPTH, 128, 6]), ("ssd_conv_b", [DEPTH, 768]),
           ("ssd_dt_bias", [DEPTH, 8]), ("ssd_a_log", [DEPTH, 8]), ("ssd_d", [DEPTH, 8]), ("ssd_norm_w", [DEPTH, 512])] + S5_INS + [
           ("w_branch", [DEPTH, 3, 512, D]), ("b_gate_l", [DEPTH, 128, 3, 8]), ("w_out", [DEPTH, D, D]),
           ("ln1_g", [DEPTH, D]), ("ln1_b", [DEPTH, D]),
           ("ffn_w_up", [DEPTH, D, 2 * DFF]), ("ffn_w_down", [DEPTH, DFF, D]), ("ffn_cw", [DEPTH, 128, 44, 3]), ("ffn_cb", [DEPTH, 128, 44]),
           ("ln2_g", [DEPTH, D]), ("ln2_b", [DEPTH, D])]


def build_full(nl=DEPTH, stop=None):
    k = K()
    x = k.din("x", [L_SEQ, D])
    for n, shp in ALL_INS:
        k.din(n, shp)
    out = k.nc.dram_tensor("out", [L_SEQ, D], F32, kind="ExternalOutput").ap()
    k.setup_common()
    k.setup_ffn()
    k.phase0(x)
    k_dummy = [Trk() for _ in range(NTT)]
    k_out = [Trk() for _ in range(NTT)]
    for l in range(nl):
        k.fox(l)
        k.ssd(l)
        k.s5(l)
        k.merge(l, x if l == 0 else k.xr, k_dummy if l == 0 else k.k_xr)
        if stop == "merge":
            break
        last = (l == nl - 1)
        k.ffn(l, out if last else k.xr, k_out if last else k.k_xr)
    if stop == "merge":
        A = Arena(k.fw)
        r = A.ring("dump", 2, [128, D], F32)
        for tt in range(NTT):
            t, kt = r.get()
            k.fw.dma("sp", t, k.xr[tt * 128:(tt + 1) * 128, :], reads=[k.k_xr[tt]], writes=[kt])
            k.fw.dma("sp", out[tt * 128:(tt + 1) * 128, :], t, reads=[kt], writes=[k_out[tt]])
    k.fw.drain()
    return k


def build_l1_test():
    return build_full(1)


def build_m1_test():
    return build_full(1, stop="merge")


_CACHE = {}


def kernel(**inputs):
    f = _host_layout(inputs)
    if "k" not in _CACHE:
        _CACHE["k"] = build_full(DEPTH)
    k = _CACHE["k"]
    x = np.ascontiguousarray(np.asarray(inputs["x"], dtype=np.float32))
    nb = x.shape[0]
    in_maps = []
    for b in range(nb):
        m = {"x": x[b]}
        for n, _ in ALL_INS:
            m[n] = f[n]
        in_maps.append(m)
    res = run_bass_kernel_spmd(k.nc, in_maps, core_ids=list(range(nb)))
    return np.stack([np.asarray(r["out"], dtype=np.float32) for r in res.results], axis=0)


def build_l4_test():
    return build_full(4)


def build_l2_test():
    return build_full(2)
```

```python
import numpy as np
import concourse.bass as bass
import concourse.mybir as mybir
from concourse.bass_utils import run_bass_kernel_spmd
from contextlib import ExitStack

F32 = mybir.dt.float32
BF16 = mybir.dt.bfloat16
I32 = mybir.dt.int32
AF = mybir.ActivationFunctionType
ALU = mybir.AluOpType

L_SEQ = 4096
D = 1024
DEPTH = 4
DFF = 2816
DIN = 6416
ALPHA = (2 * DEPTH) ** 0.25
NTT = L_SEQ // 128
NTB = L_SEQ // 512


class Trk:
    __slots__ = ("w", "rs", "name")

    def __init__(self, name=""):
        self.w = None
        self.rs = []
        self.name = name


class Fw:
    NDMA = 48

    def __init__(self, nc):
        self.nc = nc
        self.eng = {"pe": nc.tensor, "act": nc.scalar, "dve": nc.vector,
                    "pool": nc.gpsimd, "sp": nc.sync}
        self.sem = {k: nc.alloc_semaphore("s_" + k) for k in self.eng}
        self.cnt = {k: 0 for k in self.eng}
        self.seen = {k: {} for k in self.eng}
        self.dsem = [nc.alloc_semaphore("d%d" % i) for i in range(self.NDMA)]
        self.dval = [0] * self.NDMA
        self.dnext = {"sp": 0, "pool": 0}
        self.drange = {"sp": (0, 32), "pool": (32, self.NDMA)}
        self.nwait = 0
        self.ninst = 0

    def sb(self, name, shape, dt):
        return self.nc.alloc_sbuf_tensor(name, list(shape), dt).ap()

    def ps(self, name, shape, dt=F32):
        return self.nc.alloc_psum_tensor(name, list(shape), dt).ap()

    def dram(self, name, shape, dt, kind="Internal"):
        return self.nc.dram_tensor(name, list(shape), dt, kind=kind).ap()

    def _need(self, e, ev):
        if ev is None:
            return
        key, val = ev
        if key == "pe" and e == "pe":
            return
        if self.seen[e].get(key, 0) >= val:
            return
        sem = self.sem[key] if isinstance(key, str) else self.dsem[key]
        self.eng[e].wait_ge(sem, val)
        self.nwait += 1
        self.seen[e][key] = val

    def _deps(self, e, reads, writes):
        for t in reads:
            self._need(e, t.w)
        for t in writes:
            self._need(e, t.w)
            for r in t.rs:
                self._need(e, r)

    def _commit(self, ev, reads, writes):
        for t in reads:
            t.rs.append(ev)
            if len(t.rs) > 48:
                d = {}
                for k, v in t.rs:
                    if d.get(k, 0) < v:
                        d[k] = v
                t.rs = list(d.items())
        for t in writes:
            t.w = ev
            t.rs = []

    def op(self, e, fn, reads=(), writes=()):
        self._deps(e, reads, writes)
        ins = fn(self.eng[e])
        self.cnt[e] += 1
        ins.then_inc(self.sem[e], 1)
        ev = (e, self.cnt[e])
        self._commit(ev, reads, writes)
        self.ninst += 1
        return ev

    def dma(self, q, out, in_, reads=(), writes=(), **kw):
        lo, hi = self.drange[q]
        i = lo + self.dnext[q]
        self.dnext[q] = (self.dnext[q] + 1) % (hi - lo)
        if self.dval[i] > 0:
            self._need(q, (i, self.dval[i]))
        self._deps(q, reads, writes)
        self.dval[i] += 16
        self.eng[q].dma_start(out=out, in_=in_, **kw).then_inc(self.dsem[i], 16)
        ev = (i, self.dval[i])
        self._commit(ev, reads, writes)
        self.ninst += 1
        return ev

    def barrier(self):
        for i in range(self.NDMA):
            if self.dval[i]:
                self._need("sp", (i, self.dval[i]))
        ins = self.eng["sp"].sem_inc(self.sem["sp"], 1)
        self.cnt["sp"] += 1
        for e in ("pe", "act", "dve", "pool", "sp"):
            for k in ("pe", "act", "dve", "pool", "sp"):
                if k != e and self.cnt[k]:
                    if self.seen[e].get(k, 0) < self.cnt[k]:
                        self.eng[e].wait_ge(self.sem[k], self.cnt[k])
                        self.seen[e][k] = self.cnt[k]
                        self.nwait += 1

    def drain(self):
        for i in range(self.NDMA):
            if self.dval[i]:
                self._need("sp", (i, self.dval[i]))
        for k in ("pe", "act", "dve", "pool"):
            if self.cnt[k]:
                self._need("sp", (k, self.cnt[k]))


class Arena:
    def __init__(self, fw):
        self.fw = fw
        self.st = ExitStack()

    _uid = [0]

    def sb(self, name, shape, dt):
        Arena._uid[0] += 1
        return self.st.enter_context(self.fw.nc.sbuf_tensor("%s_%d" % (name, Arena._uid[0]), list(shape), dt)).ap()

    def ring(self, name, n, shape, dt):
        return Ring(self.fw, name, n, shape, dt, mk=self.sb)

    def close(self):
        self.fw.barrier()
        self.st.close()


class Ring:
    def __init__(self, fw, name, n, shape, dt, psum=False, mk=None):
        if mk is None:
            mk = fw.ps if psum else fw.sb
        self.b = [mk("%s%d" % (name, i), shape, dt) for i in range(n)]
        self.k = [Trk("%s%d" % (name, i)) for i in range(n)]
        self.i = 0
        self.n = n

    def get(self):
        i = self.i
        self.i = (i + 1) % self.n
        return self.b[i], self.k[i]


class K:
    def __init__(self, nlayers=DEPTH, dbg=None):
        self.nl = nlayers
        self.dbg = dbg or {}
        nc = self.nc = bass.Bass("TRN2", target_bir_lowering=False)
        fw = self.fw = Fw(nc)
        self.inp = {}
        self.ev_alt = 0

    def din(self, name, shape, dt=F32):
        ap = self.nc.dram_tensor(name, list(shape), dt, kind="ExternalInput").ap()
        self.inp[name] = ap
        return ap

    def alt(self):
        self.ev_alt ^= 1
        return "act" if self.ev_alt else "dve"

    def copy(self, e, out, in_, reads, writes):
        if e == "act":
            return self.fw.op("act", lambda g: g.activation(out=out, in_=in_, func=AF.Copy), reads, writes)
        return self.fw.op(e, lambda g: g.tensor_copy(out=out, in_=in_), reads, writes)

    def mm(self, out, lhsT, rhs, start, stop, reads, writes):
        return self.fw.op("pe", lambda g: g.matmul(out, lhsT=lhsT, rhs=rhs, start=start, stop=stop), reads, writes)

    def setup_common(self):
        fw = self.fw
        self.psum = Ring(fw, "psb", 6, [128, 512], F32, psum=True)
        self.psacc = Ring(fw, "psa", 2, [128, 512], F32, psum=True)
        self.identb = fw.sb("identb", [128, 128], BF16)
        self.identf = fw.sb("identf", [128, 128], F32)
        self.k_ident = Trk("ident")
        fw.op("pool", lambda g: g.memset(self.identf, 0.0), writes=[self.k_ident])
        fw.op("pool", lambda g: g.affine_select(out=self.identf, in_=self.identf, compare_op=ALU.not_equal, fill=1.0,
                                                base=0, pattern=[[-1, 128]], channel_multiplier=1),
              reads=[self.k_ident], writes=[self.k_ident])
        fw.op("pool", lambda g: g.tensor_copy(out=self.identb, in_=self.identf), reads=[self.k_ident], writes=[self.k_ident])
        self.utri_f = fw.sb("utri_f", [128, 128], F32)
        self.trib = fw.sb("trib", [128, 128], BF16)
        self.ones_f = fw.sb("ones_f", [128, 128], F32)
        self.mean_f = fw.sb("mean_f", [128, 128], F32)
        self.ones_b = fw.sb("ones_b", [128, 128], BF16)
        self.scanmask = fw.sb("scanmask", [128, 8, 32], F32)
        fw.op("pool", lambda g: g.memset(self.utri_f, 1.0), writes=[self.k_ident])
        fw.op("pool", lambda g: g.affine_select(out=self.utri_f, in_=self.utri_f, compare_op=ALU.is_ge, fill=0.0,
                                                base=0, pattern=[[1, 128]], channel_multiplier=-1),
              reads=[self.k_ident], writes=[self.k_ident])
        fw.op("pool", lambda g: g.tensor_copy(out=self.trib, in_=self.utri_f), reads=[self.k_ident], writes=[self.k_ident])
        fw.op("pool", lambda g: g.memset(self.ones_f, 1.0), writes=[self.k_ident])
        fw.op("pool", lambda g: g.memset(self.ones_b, 1.0), writes=[self.k_ident])
        fw.op("pool", lambda g: g.memset(self.mean_f, 1.0 / 128.0), writes=[self.k_ident])
        fw.op("pool", lambda g: g.memset(self.scanmask, 1.0), writes=[self.k_ident])
        fw.op("pool", lambda g: g.memset(self.scanmask[:, :, 0:1], 0.0), writes=[self.k_ident])
        self.ones33 = fw.sb("ones33", [128, 128], BF16)
        fw.op("pool", lambda g: g.memset(self.ones33, 0.0), writes=[self.k_ident])
        for r_ in (0, 32, 64, 96):
            fw.op("pool", lambda g: g.memset(self.ones33[r_:r_ + 1, :], 1.0), writes=[self.k_ident])
        self.nutri_f = fw.sb("nutri_f", [128, 128], F32)
        self.negmask4 = fw.sb("negmask4", [128, 4, 128], F32)
        fw.op("pool", lambda g: g.tensor_scalar(out=self.nutri_f, in0=self.utri_f, scalar1=-1.0, scalar2=0.0, op0=ALU.mult, op1=ALU.add),
              reads=[self.k_ident], writes=[self.k_ident])
        fw.op("pool", lambda g: g.memset(self.negmask4, -30000.0), writes=[self.k_ident])
        for r_ in range(4):
            fw.op("pool", lambda g: g.affine_select(out=self.negmask4[:, r_, :], in_=self.negmask4[:, r_, :], compare_op=ALU.is_gt, fill=0.0,
                                                    base=0, pattern=[[-1, 128]], channel_multiplier=1),
                  reads=[self.k_ident], writes=[self.k_ident])
        self.rowmask = fw.sb("rowmask", [128, 8], F32)
        fw.op("pool", lambda g: g.memset(self.rowmask, 1.0), writes=[self.k_ident])
        fw.op("pool", lambda g: g.affine_select(out=self.rowmask, in_=self.rowmask, compare_op=ALU.is_ge, fill=0.0,
                                                base=0, pattern=[[-16, 8]], channel_multiplier=1), reads=[self.k_ident], writes=[self.k_ident])
        fw.op("pool", lambda g: g.affine_select(out=self.rowmask, in_=self.rowmask, compare_op=ALU.is_ge, fill=0.0,
                                                base=15, pattern=[[16, 8]], channel_multiplier=-1), reads=[self.k_ident], writes=[self.k_ident])
        self.mT = fw.dram("mT", [8, 128, L_SEQ], BF16)
        self.k_mT = [[Trk() for _ in range(NTB)] for _ in range(8)]
        self.gyT = fw.dram("gyT", [4, 128, L_SEQ], BF16)
        self.k_gyT = [Trk() for _ in range(4)]
        self.zd = fw.dram("zd", [L_SEQ, 512], F32)
        self.k_zd = [Trk() for _ in range(NTT)]
        self.yT = fw.dram("yT", [3, 512, L_SEQ], BF16)
        self.k_yT = [[Trk() for _ in range(NTB)] for _ in range(3)]
        self.xT = fw.sb("xT", [128, 8, L_SEQ], BF16)
        self.k_xT = [Trk("xT%d" % i) for i in range(NTT)]
        self.xr = fw.dram("xr", [L_SEQ, D], F32)
        self.k_xr = [Trk("xr%d" % i) for i in range(NTT)]

    def alloc_ln(self, A):
        self.tok32 = A.ring("tok32", 3, [128, D], F32)
        self.tokbf = A.ring("tokbf", 2, [128, D], BF16)
        self.ln_g = A.sb("ln_g", [128, D], F32)
        self.ln_b = A.sb("ln_b", [128, D], F32)
        self.k_lnp = Trk("lnp")
        self.stat = A.ring("stat", 2, [128, 16], F32)

    def to_xT(self, src_bf, k_src, tt):
        fw = self.fw
        ps, kp = self.psum.get()
        psb = ps.bitcast(BF16)
        for c in range(8):
            fw.op("pe", lambda g: g.transpose(psb[:, c * 128:(c + 1) * 128], src_bf[:, c * 128:(c + 1) * 128], self.identb),
                  reads=[k_src, self.k_ident], writes=[kp])
        dst = self.xT[:, :, tt * 128:(tt + 1) * 128]
        self.copy(self.alt(), dst, psb.rearrange("p (c t) -> p c t", c=8), reads=[kp], writes=[self.k_xT[tt]])

    def phase0(self, x_in):
        fw = self.fw
        A = Arena(fw)
        self.alloc_ln(A)
        for tt in range(NTT):
            t32, k32 = self.tok32.get()
            fw.dma("sp", t32, x_in[tt * 128:(tt + 1) * 128, :], writes=[k32])
            tb, kb = self.tokbf.get()
            self.copy(self.alt(), tb, t32, reads=[k32], writes=[kb])
            self.to_xT(tb, kb, tt)
        A.close()

    def load_ln(self, g_ap, b_ap):
        fw = self.fw
        fw.dma("sp", self.ln_g, g_ap.partition_broadcast(128), writes=[self.k_lnp])
        fw.dma("sp", self.ln_b, b_ap.partition_broadcast(128), writes=[self.k_lnp])

    def resid_ln(self, tt, ps_halves, k_ps, res_src, k_res, out_dram, k_out):
        fw = self.fw
        r32, kr = self.tok32.get()
        fw.dma("sp", r32, res_src[tt * 128:(tt + 1) * 128, :], reads=[k_res], writes=[kr])
        t32, kt = self.tok32.get()
        for h in range(2):
            fw.op("dve", lambda g: g.scalar_tensor_tensor(out=t32[:, h * 512:(h + 1) * 512], in0=r32[:, h * 512:(h + 1) * 512],
                                                          scalar=float(ALPHA), in1=ps_halves[h], op0=ALU.mult, op1=ALU.add),
                  reads=[kr, k_ps[h]], writes=[kt])
        st, ks = self.stat.get()
        for h in range(2):
            fw.op("dve", lambda g: g.bn_stats(out=st[:, h * 6:(h + 1) * 6], in_=t32[:, h * 512:(h + 1) * 512]), reads=[kt], writes=[ks])
        fw.op("dve", lambda g: g.bn_aggr(out=st[:, 12:14], in_=st[:, 0:12]), reads=[ks], writes=[ks])
        fw.op("dve", lambda g: g.tensor_scalar(out=st[:, 14:15], in0=st[:, 13:14], scalar1=1e-5, scalar2=None, op0=ALU.add), reads=[ks], writes=[ks])
        fw.op("act", lambda g: g.activation(out=st[:, 14:15], in_=st[:, 14:15], func=AF.Sqrt), reads=[ks], writes=[ks])
        fw.op("dve", lambda g: g.reciprocal(out=st[:, 15:16], in_=st[:, 14:15]), reads=[ks], writes=[ks])
        fw.op("dve", lambda g: g.tensor_scalar(out=t32, in0=t32, scalar1=st[:, 12:13], scalar2=st[:, 15:16], op0=ALU.subtract, op1=ALU.mult),
              reads=[kt, ks], writes=[kt])
        fw.op("pool", lambda g: g.tensor_tensor(out=t32, in0=t32, in1=self.ln_g, op=ALU.mult), reads=[kt, self.k_lnp], writes=[kt])
        fw.op("pool", lambda g: g.tensor_tensor(out=r32, in0=t32, in1=self.ln_b, op=ALU.add), reads=[kt, self.k_lnp], writes=[kr])
        fw.dma("sp", out_dram[tt * 128:(tt + 1) * 128, :], r32, reads=[kr], writes=[k_out])
        tb, kb = self.tokbf.get()
        self.copy("act", tb, r32, reads=[kr], writes=[kb])
        self.to_xT(tb, kb, tt)


    def fox(self, l):
        fw = self.fw
        I = self.inp
        A = Arena(fw)
        kc_ = self.k_ident
        win = A.ring("win", 2, [128, 8, 528], BF16)
        vaug = A.sb("vaug", [128, 32, 8, 65], BF16)
        k_v = [Trk() for _ in range(NTT)]
        FL = A.sb("FL", [128, 32, 8], F32)
        k_FL = Trk()
        qkr = A.ring("qk", 2, [128, 2, L_SEQ], BF16)
        w_d = I["w_in"][l].rearrange("(kc p) n -> p kc n", p=128)
        fw.op("pool", lambda g: g.memset(vaug[:, :, :, 64:65], 1.0), writes=k_v)

        def proj_qk(hp):
            qk, _ = qkr.get()
            kq = [Trk() for _ in range(NTB)]
            kk = [Trk() for _ in range(NTB)]
            wb, kw = win.get()
            fw.dma("pool", wb[:, :, 0:128], w_d[:, :, hp * 128:(hp + 1) * 128], writes=[kw])
            fw.dma("pool", wb[:, :, 128:256], w_d[:, :, 512 + hp * 128:512 + (hp + 1) * 128], writes=[kw])
            for which, kd in ((0, kq), (1, kk)):
                for tb in range(NTB):
                    ps, kp = self.psum.get()
                    for kc in range(8):
                        self.mm(ps, wb[:, kc, which * 128:(which + 1) * 128], self.xT[:, kc, tb * 512:(tb + 1) * 512], kc == 0, kc == 7,
                                [kw] + self.k_xT[tb * 4:(tb + 1) * 4], [kp])
                    o = qk[:, which, tb * 512:(tb + 1) * 512]
                    if which == 1:
                        self.copy(self.alt(), o, ps, [kp], [kd[tb]])
                    elif self.alt() == "act":
                        fw.op("act", lambda g: g.activation(out=o, in_=ps, func=AF.Copy, scale=0.125), [kp], [kd[tb]])
                    else:
                        fw.op("dve", lambda g: g.tensor_scalar(out=o, in0=ps, scalar1=0.125, scalar2=None, op0=ALU.mult), [kp], [kd[tb]])
            return qk, kq, kk
        wb, kw = win.get()
        fw.dma("pool", wb[:, :, 0:520], w_d[:, :, 1024:1544], writes=[kw])
        for tt in range(NTT):
            ps, kp = self.psum.get()
            for kc in range(8):
                self.mm(ps, self.xT[:, kc, tt * 128:(tt + 1) * 128], wb[:, kc, 0:512], kc == 0, kc == 7, [kw, self.k_xT[tt]], [kp])
            ps2, kp2 = self.psum.get()
            for kc in range(8):
                self.mm(ps2[:, 0:8], self.xT[:, kc, tt * 128:(tt + 1) * 128], wb[:, kc, 512:520], kc == 0, kc == 7, [kw, self.k_xT[tt]], [kp2])
            self.copy(self.alt(), vaug[:, tt, :, 0:64], ps.rearrange("p (h d) -> p h d", h=8), [kp], [k_v[tt]])
            self.copy(self.alt(), FL[:, tt, :], ps2[:, 0:8], [kp2], [k_FL])
        fb = A.sb("fb", [128, 8], F32)
        k_t = Trk()
        fw.dma("sp", fb, I["fox_f_bias"][l:l + 1, :].partition_broadcast(128), writes=[k_t])
        nls = A.sb("nls", [128, 32, 8], F32)
        fw.op("dve", lambda g: g.tensor_tensor(out=nls, in0=FL, in1=fb.unsqueeze(1).to_broadcast([128, 32, 8]), op=ALU.add), [k_FL, k_t], [k_t])
        fw.op("act", lambda g: g.activation(out=nls, in_=nls, func=AF.Exp, scale=-1.0), [k_t], [k_t])
        fw.op("act", lambda g: g.activation(out=nls, in_=nls, func=AF.Ln, bias=1.0), [k_t], [k_t])
        nlsf = nls.rearrange("p j h -> p (j h)")
        ps, kp = self.psum.get()
        self.mm(ps[:, 0:256], self.utri_f, nlsf, True, True, [kc_, k_t], [kp])
        ps2, kp2 = self.psum.get()
        self.mm(ps2[:, 0:256], self.ones_f, nlsf, True, True, [kc_, k_t], [kp2])
        totT = A.sb("totT", [128, 8, 32], F32)
        pin = A.sb("pin", [128, 8, 32], F32)
        cumk = A.sb("cumk", [128, 8, 32], F32)
        refp = A.sb("refp", [128, 8, 32], F32)
        k_c = Trk()
        fw.op("dve", lambda g: g.tensor_copy(out=totT, in_=ps2[:, 0:256].rearrange("p (j h) -> p h j", h=8)), [kp2], [k_c])
        fw.op("dve", lambda g: g.tensor_tensor_scan(out=pin.rearrange("p h j -> p (h j)"), data0=self.scanmask.rearrange("p h j -> p (h j)"),
                                                    data1=totT.rearrange("p h j -> p (h j)"), initial=0.0, op0=ALU.mult, op1=ALU.add),
              [k_c, kc_], [k_c])
        fw.op("dve", lambda g: g.tensor_tensor(out=cumk, in0=pin, in1=totT, op=ALU.subtract), [k_c], [k_c])
        fw.op("dve", lambda g: g.tensor_tensor(out=cumk, in0=cumk, in1=ps[:, 0:256].rearrange("p (j h) -> p h j", h=8), op=ALU.add), [k_c, kp], [k_c])
        ps3, kp3 = self.psum.get()
        self.mm(ps3[:, 0:256], self.mean_f, cumk.rearrange("p h j -> p (h j)"), True, True, [kc_, k_c], [kp3])
        fw.op("dve", lambda g: g.tensor_copy(out=refp.rearrange("p h j -> p (h j)"), in_=ps3[:, 0:256]), [kp3], [k_c])
        dsm = A.sb("dsm", [128, 8, 32], F32)
        hs_ = A.sb("hs_", [128, 8, 32], BF16)
        ls_ = A.sb("ls_", [128, 8, 32], BF16)
        v4 = lambda t: t.rearrange("p h (i s) -> p h i s", s=4)
        fw.op("dve", lambda g: g.tensor_tensor(out=v4(dsm), in0=v4(refp)[:, :, :, 0:1].to_broadcast([128, 8, 8, 4]), in1=v4(refp),
                                               op=ALU.subtract), [k_c], [k_c])
        fw.op("dve", lambda g: g.tensor_copy(out=hs_, in_=dsm), [k_c], [k_c])
        fw.op("dve", lambda g: g.tensor_tensor(out=dsm, in0=dsm, in1=hs_, op=ALU.subtract), [k_c], [k_c])
        fw.op("dve", lambda g: g.tensor_copy(out=ls_, in_=dsm), [k_c], [k_c])
        refhl = A.sb("refhl", [128, L_SEQ], BF16)
        k_rh = Trk()
        fw.op("pool", lambda g: g.memset(refhl, 0.0), writes=[k_rh])
        biasr = A.ring("biasT", 4, [128, 32], F32)
        ptr = A.ring("pt", 4, [128, 512], BF16)
        rcp = A.ring("rcp", 2, [128, 512], F32)
        ysb = A.ring("ysb", 2, [64, 512], F32)
        ybf = A.ring("ybf", 2, [64, 512], BF16)
        for hp in range(4):
            qk, k_q, k_k = proj_qk(hp)
            qT = qk[:, 0, :]
            kT = qk[:, 1, :]
            for hh in range(2):
                h = 2 * hp + hh
                r0 = hh * 64
                fw.op("pool", lambda g: g.tensor_copy(out=refhl[r0:r0 + 1, :].rearrange("p (j q) -> p j q", q=128),
                                                      in_=hs_[r0:r0 + 1, h, :].unsqueeze(2).to_broadcast([1, 32, 128])), [k_c, k_rh], [k_rh])
                fw.op("pool", lambda g: g.tensor_copy(out=refhl[r0 + 32:r0 + 33, :].rearrange("p (j q) -> p j q", q=128),
                                                      in_=ls_[r0 + 32:r0 + 33, h, :].unsqueeze(2).to_broadcast([1, 32, 128])), [k_c, k_rh], [k_rh])
            for i in range(NTB):
                for hh in range(2):
                    h = 2 * hp + hh
                    pr = slice(hh * 64, (hh + 1) * 64)
                    bt, kb = biasr.get()
                    fw.op("dve", lambda g: g.tensor_scalar(out=bt, in0=cumk[:, h, :], scalar1=refp[:, h, 4 * i:4 * i + 1], scalar2=None,
                                                           op0=ALU.subtract), [k_c], [kb])
                    po, kpo = self.psacc.get()
                    nj = 4 * i + 4

                    def emitS(j):
                        c0 = max(0, j - 4 * i) * 128
                        ps, kp = self.psum.get()
                        self.mm(ps[:, c0:512], kT[pr, j * 128:(j + 1) * 128], qT[pr, i * 512 + c0:(i + 1) * 512], True, False,
                                [k_k[j // 4], k_q[i]], [kp])
                        rs = slice(hh * 64, hh * 64 + 33)
                        self.mm(ps[:, c0:512], self.ones33[rs, :], refhl[rs, i * 512 + c0:(i + 1) * 512], False, True, [kc_, k_rh], [kp])
                        return ps, kp, c0
                    cur = emitS(0)
                    for j in range(nj):
                        nxt = emitS(j + 1) if j + 1 < nj else None
                        ps, kp, c0 = cur
                        pt, kpt = ptr.get()
                        fw.op("act", lambda g: g.activation(out=pt[:, c0:512], in_=ps[:, c0:512], func=AF.Exp, bias=bt[:, j:j + 1]), [kp, kb], [kpt])
                        if j >= 4 * i:
                            fw.op("pool", lambda g: g.tensor_tensor(out=pt[:, c0:c0 + 128], in0=pt[:, c0:c0 + 128], in1=self.trib, op=ALU.mult),
                                  [kpt, kc_], [kpt])
                        self.mm(po[0:65, c0:512], vaug[:, j, h, :], pt[:, c0:512], j == 0, j == nj - 1, [k_v[j], kpt], [kpo])
                        cur = nxt
                    rc, krc = rcp.get()
                    fw.op("dve", lambda g: g.reciprocal(out=rc[64:65, :], in_=po[64:65, :]), [kpo], [krc])
                    pb, kpb = self.psum.get()
                    self.mm(pb[0:64, :], self.ones_f[64:65, 0:64], rc[64:65, :], True, True, [kc_, krc], [kpb])
                    ys, kys = ysb.get()
                    self.copy("act", ys, po[0:64, :], [kpo], [kys])
                    yb, kyb = ybf.get()
                    fw.op("dve", lambda g: g.tensor_tensor(out=yb, in0=ys, in1=pb[0:64, :], op=ALU.mult), [kys, kpb], [kyb])
                    fw.dma("sp", self.yT[0, h * 64:(h + 1) * 64, i * 512:(i + 1) * 512], yb, reads=[kyb], writes=[self.k_yT[0][i]])
        A.close()


    def ssd(self, l):
        fw = self.fw
        I = self.inp
        A = Arena(fw)
        kc_ = self.k_ident
        w_d = I["w_in"][l].rearrange("(kc p) n -> p kc n", p=128)
        XBC = A.sb("xbc", [128, 6, 3 + L_SEQ], BF16)
        k_xbc = [[Trk() for _ in range(NTB)] for _ in range(6)]
        k_pad = Trk()
        fw.op("pool", lambda g: g.memset(XBC[:, :, 0:3], 0.0), writes=[k_pad])
        DT = A.sb("DT", [128, 32, 8], F32)
        k_DT = Trk()
        A2 = Arena(fw)
        win = A2.ring("win", 2, [128, 8, 512], BF16)
        wb, kw = win.get()
        fw.dma("pool", wb[:, :, 0:512], w_d[:, :, 1544:2056], writes=[kw])
        zst = A2.ring("zst", 2, [128, 512], F32)
        for tt in range(NTT):
            ps, kp = self.psum.get()
            for kc in range(8):
                self.mm(ps, self.xT[:, kc, tt * 128:(tt + 1) * 128], wb[:, kc, 0:512], kc == 0, kc == 7, [kw, self.k_xT[tt]], [kp])
            zs, kzs = zst.get()
            self.copy(self.alt(), zs, ps, [kp], [kzs])
            fw.dma("sp", self.zd[tt * 128:(tt + 1) * 128, :], zs, reads=[kzs], writes=[self.k_zd[tt]])
        for gi, (c0, nct, ctb, ncols) in enumerate(((2056, 4, 0, 512), (2568, 2, 4, 264))):
            wb, kw = win.get()
            fw.dma("pool", wb[:, :, 0:ncols], w_d[:, :, c0:c0 + ncols], writes=[kw])
            for m in range(nct):
                ct = ctb + m
                for tb in range(NTB):
                    ps, kp = self.psum.get()
                    for kc in range(8):
                        self.mm(ps, wb[:, kc, m * 128:(m + 1) * 128], self.xT[:, kc, tb * 512:(tb + 1) * 512], kc == 0, kc == 7,
                                [kw] + self.k_xT[tb * 4:(tb + 1) * 4], [kp])
                    self.copy(self.alt(), XBC[:, ct, 3 + tb * 512:3 + (tb + 1) * 512], ps, [kp], [k_xbc[ct][tb]])
            if gi == 1:
                for tt in range(NTT):
                    ps2, kp2 = self.psum.get()
                    for kc in range(8):
                        self.mm(ps2[:, 0:8], self.xT[:, kc, tt * 128:(tt + 1) * 128], wb[:, kc, 256:264], kc == 0, kc == 7, [kw, self.k_xT[tt]], [kp2])
                    self.copy(self.alt(), DT[:, tt, :], ps2[:, 0:8], [kp2], [k_DT])
        A2.close()
        cw = A.sb("cw", [128, 6, 4], F32)
        cb = A.sb("cb", [128, 6], F32)
        k_cp = Trk()
        fw.dma("sp", cw, I["ssd_cw"][l], writes=[k_cp])
        fw.dma("sp", cb, I["ssd_cb"][l], writes=[k_cp])
        dgc = A.sb("dgc", [128, 6, 4, 128], BF16)
        for ct in range(6):
            for t in range(4):
                fw.op("pool", lambda g: g.tensor_scalar(out=dgc[:, ct, t, :], in0=self.identf, scalar1=cw[:, ct, t:t + 1], scalar2=0.0,
                                                        op0=ALU.mult, op1=ALU.add), [kc_, k_cp], [k_cp])
        cbr = A.sb("cbr", [1, 768], F32)
        cbh = A.sb("cbh", [1, 768], BF16)
        cbl = A.sb("cbl", [1, 768], BF16)
        cbt = cbr
        fw.dma("sp", cbr, I["ssd_conv_b"][l:l + 1, :], writes=[k_cp])
        fw.op("dve", lambda g: g.tensor_copy(out=cbh, in_=cbr), [k_cp], [k_cp])
        fw.op("dve", lambda g: g.tensor_tensor(out=cbt, in0=cbr, in1=cbh, op=ALU.subtract), [k_cp], [k_cp])
        fw.op("dve", lambda g: g.tensor_copy(out=cbl, in_=cbt), [k_cp], [k_cp])
        BT = A.sb("BT", [128, 2, L_SEQ], BF16)
        CT = A.sb("CT", [128, L_SEQ], BF16)
        k_BT = [Trk() for _ in range(NTB)]
        k_CT = [Trk() for _ in range(NTB)]
        fw.op("pool", lambda g: g.memset(BT[64:128, 0, :], 0.0), writes=k_BT)
        fw.op("pool", lambda g: g.memset(BT[0:64, 1, :], 0.0), writes=k_BT)
        for ct in (4, 5):
            for tb in range(NTB):
                ps, kp = self.psum.get()
                rd = [k_cp, k_pad, k_xbc[ct][tb]] + ([k_xbc[ct][tb - 1]] if tb else [])
                for t in range(4):
                    self.mm(ps, dgc[:, ct, t, :], XBC[:, ct, tb * 512 + t:tb * 512 + t + 512], t == 0, t == 3, rd, [kp])
                ts_ = slice(tb * 512, (tb + 1) * 512)
                if ct == 5:
                    fw.op("act", lambda g: g.activation(out=CT[:, ts_], in_=ps, func=AF.Silu, bias=cb[:, ct:ct + 1]), [kp, k_cp], [k_CT[tb]])
                else:
                    fw.op("act", lambda g: g.activation(out=BT[0:64, 0, ts_], in_=ps[0:64, :], func=AF.Silu, bias=cb[0:64, ct:ct + 1]), [kp, k_cp], [k_BT[tb]])
                    fw.op("act", lambda g: g.activation(out=BT[64:128, 1, ts_], in_=ps[64:128, :], func=AF.Silu, bias=cb[64:128, ct:ct + 1]), [kp, k_cp], [k_BT[tb]])
        if self.dbg.get('ssd_stop') == 2:
            A.close()
            return
        dtb = A.sb("dtb", [128, 8], F32)
        alog = A.sb("alog", [128, 8], F32)
        dsk = A.sb("dsk", [128, 8], F32)
        nw = A.sb("nw", [128, 512], F32)
        k_p = Trk()
        fw.dma("sp", dtb, I["ssd_dt_bias"][l:l + 1, :].partition_broadcast(128), writes=[k_p])
        fw.dma("sp", alog, I["ssd_a_log"][l:l + 1, :].partition_broadcast(128), writes=[k_p])
        fw.dma("sp", dsk, I["ssd_d"][l:l + 1, :].partition_broadcast(128), writes=[k_p])
        fw.dma("sp", nw, I["ssd_norm_w"][l:l + 1, :].partition_broadcast(128), writes=[k_p])
        fw.op("act", lambda g: g.activation(out=alog, in_=alog, func=AF.Exp), [k_p], [k_p])
        fw.op("dve", lambda g: g.tensor_scalar(out=alog, in0=alog, scalar1=-1.0, scalar2=None, op0=ALU.mult), [k_p], [k_p])
        dt = A.sb("dt", [128, 32, 8], F32)
        adt = A.sb("adt", [128, 32, 8], F32)
        acs = A.sb("acs", [128, 32, 8], F32)
        eacs = A.sb("eacs", [128, 32, 8], F32)
        dtdec = A.sb("dtdec", [128, 32, 8], F32)
        eatot = A.sb("eatot", [128, 32, 8], F32)
        esel = A.sb("esel", [128, 32, 4], F32)
        k_d = Trk()
        fw.op("dve", lambda g: g.tensor_tensor(out=dt, in0=DT, in1=dtb.unsqueeze(1).to_broadcast([128, 32, 8]), op=ALU.add), [k_DT, k_p], [k_d])
        fw.op("act", lambda g: g.activation(out=dt, in_=dt, func=AF.Exp), [k_d], [k_d])
        fw.op("act", lambda g: g.activation(out=dt, in_=dt, func=AF.Ln, bias=1.0), [k_d], [k_d])
        fw.op("dve", lambda g: g.tensor_tensor(out=adt, in0=dt, in1=alog.unsqueeze(1).to_broadcast([128, 32, 8]), op=ALU.mult), [k_d, k_p], [k_d])
        fl = lambda t: t.rearrange("p c h -> p (c h)")
        ps, kp = self.psum.get()
        self.mm(ps[:, 0:256], self.utri_f, fl(adt), True, True, [kc_, k_d], [kp])
        ps2, kp2 = self.psum.get()
        self.mm(ps2[:, 0:256], self.ones_f, fl(adt), True, True, [kc_, k_d], [kp2])
        fw.op("dve", lambda g: g.tensor_copy(out=fl(acs), in_=ps[:, 0:256]), [kp], [k_d])
        fw.op("act", lambda g: g.activation(out=fl(eacs), in_=ps[:, 0:256], func=AF.Exp), [kp], [k_d])
        fw.op("dve", lambda g: g.tensor_tensor(out=fl(dtdec), in0=ps2[:, 0:256], in1=fl(acs), op=ALU.subtract), [kp2, k_d], [k_d])
        fw.op("act", lambda g: g.activation(out=dtdec, in_=dtdec, func=AF.Exp), [k_d], [k_d])
        fw.op("dve", lambda g: g.tensor_tensor(out=dtdec, in0=dtdec, in1=dt, op=ALU.mult), [k_d], [k_d])
        fw.op("act", lambda g: g.activation(out=fl(eatot), in_=ps2[:, 0:256], func=AF.Exp), [kp2], [k_d])
        fw.op("dve", lambda g: g.tensor_copy(out=esel[0:64], in_=eatot[0:64, :, 0:4]), [k_d], [k_d])
        fw.op("dve", lambda g: g.tensor_copy(out=esel[64:128], in_=eatot[64:128, :, 4:8]), [k_d], [k_d])
        if self.dbg.get('ssd_stop') == 3:
            A.close()
            return
        xsr = A.ring("xs", 2, [128, 8, 64], F32)
        bpr = A.ring("bp", 2, [128, 2, 128], BF16)
        for b_ in bpr.b:
            fw.op("pool", lambda g: g.memset(b_, 0.0), writes=bpr.k)
        xdtr = A.ring("xdt", 2, [128, 8, 64], BF16)
        xddr = A.ring("xdd", 2, [128, 8, 64], BF16)
        Ar = A.ring("Aall", 1, [128, 8, 128], F32)
        Dr = A.ring("Dg", 1, [128, 4, 128], F32)
        Mr = A.ring("Mg", 2, [128, 4, 128], BF16)
        S = A.sb("S", [128, 4, 64], F32)
        k_S = Trk()
        fw.op("pool", lambda g: g.memset(S, 0.0), writes=[k_S])
        Sbr = A.ring("Sb", 2, [128, 2, 4, 64], BF16)
        for b_ in Sbr.b:
            fw.op("pool", lambda g: g.memset(b_, 0.0), writes=Sbr.k)
        yr = A.ring("y", 2, [128, 8, 64], F32)
        tmr = A.ring("tm", 1, [128, 8, 64], F32)
        ztr = A.ring("zt", 2, [128, 512], F32)
        ssr = A.ring("ss", 2, [128, 4], F32)
        ybr = A.ring("yb", 2, [128, 512], BF16)
        ytr = A.ring("yt", 2, [128, 4, 128], BF16)
        Sb_prev = None
        for c in range(self.dbg.get('ssd_nchunk', NTT)):
            tb = c // 4
            rdx = lambda ct: [k_cp, k_pad, k_xbc[ct][tb]] + ([k_xbc[ct][tb - 1]] if (tb and c % 4 == 0) else [])
            ps, kp = self.psum.get()
            for ct in range(4):
                o = ps[:, ct * 128:(ct + 1) * 128]
                for t in range(4):
                    self.mm(o, XBC[:, ct, c * 128 + t:c * 128 + t + 128], dgc[:, ct, t, :], t == 0, False, rdx(ct), [kp])
                self.mm(o, self.ones_b[0:1, :], cbh[0:1, ct * 128:(ct + 1) * 128], False, False, [kc_, k_cp], [kp])
                self.mm(o, self.ones_b[0:1, :], cbl[0:1, ct * 128:(ct + 1) * 128], False, True, [kc_, k_cp], [kp])
            psB, kpB = self.psum.get()
            o = psB[:, 0:128]
            for t in range(4):
                self.mm(o, XBC[:, 4, c * 128 + t:c * 128 + t + 128], dgc[:, 4, t, :], t == 0, False, rdx(4), [kpB])
            self.mm(o, self.ones_b[0:1, :], cbh[0:1, 512:640], False, False, [kc_, k_cp], [kpB])
            self.mm(o, self.ones_b[0:1, :], cbl[0:1, 512:640], False, True, [kc_, k_cp], [kpB])
            xs, kxs = xsr.get()
            fw.op("act", lambda g: g.activation(out=xs.rearrange("p h d -> p (h d)"), in_=ps, func=AF.Silu), [kp], [kxs])
            bp, kbp = bpr.get()
            fw.op("act", lambda g: g.activation(out=bp[:, 0, 0:64], in_=psB[:, 0:64], func=AF.Silu), [kpB], [kbp])
            fw.op("act", lambda g: g.activation(out=bp[:, 1, 64:128], in_=psB[:, 64:128], func=AF.Silu), [kpB], [kbp])
            if self.dbg.get('ssd_stop') == 4:
                continue
            xdt, kxdt = xdtr.get()
            xdd, kxdd = xddr.get()
            fw.op("dve", lambda g: g.tensor_tensor(out=xdt, in0=xs, in1=dt[:, c, :].unsqueeze(2).to_broadcast([128, 8, 64]), op=ALU.mult), [kxs, k_d], [kxdt])
            fw.op("pool", lambda g: g.tensor_tensor(out=xdd, in0=xs, in1=dtdec[:, c, :].unsqueeze(2).to_broadcast([128, 8, 64]), op=ALU.mult), [kxs, k_d], [kxdd])
            if self.dbg.get('ssd_stop') == 5:
                continue
            psG, kpG = self.psum.get()
            for g_ in range(self.dbg.get('ssd_ng', 2)):
                self.mm(psG[:, g_ * 128:(g_ + 1) * 128], BT[:, g_, c * 128:(c + 1) * 128], CT[:, c * 128:(c + 1) * 128], True, True,
                        [k_BT[tb], k_CT[tb]], [kpG])
            sub = self.dbg.get('ssd_sub', 99)
            if sub < 1:
                continue
            Aa, kA = Ar.get()
            fw.op("pool", lambda g: g.tensor_tensor(out=Aa, in0=self.ones_f.unsqueeze(1).to_broadcast([128, 8, 128]),
                                                    in1=adt[:, c, :].unsqueeze(2).to_broadcast([128, 8, 128]), op=ALU.mult), [kc_, k_d], [kA])
            if sub < 2:
                continue
            yd, kyd = self.psacc.get()
            for g_ in range(2):
                psS, kpS = self.psum.get()
                for hh in range(4):
                    o_ = psS[:, hh * 128:(hh + 1) * 128]
                    self.mm(o_, Aa[:, 4 * g_ + hh, :], self.utri_f, True, False, [kA, kc_], [kpS])
                    self.mm(o_, self.nutri_f, Aa[:, 4 * g_ + hh, :], False, False, [kA, kc_], [kpS])
                    self.mm(o_, self.identf, self.negmask4[:, 0, :], False, True, [kc_], [kpS])
                if sub < 3:
                    continue
                Dg, kD = Dr.get()
                fw.op("act", lambda g: g.activation(out=Dg.rearrange("p h l -> p (h l)"), in_=psS, func=AF.Exp), [kpS], [kD])
                if sub < 4:
                    continue
                Mg, kM = Mr.get()
                fw.op("dve", lambda g: g.tensor_tensor(out=Mg, in0=Dg, in1=psG[:, g_ * 128:(g_ + 1) * 128].unsqueeze(1).to_broadcast([128, 4, 128]),
                                                       op=ALU.mult), [kD, kpG], [kM])
                if sub < 5:
                    continue
                for hh in range(4):
                    h = 4 * g_ + hh
                    self.mm(yd[:, h * 64:(h + 1) * 64], Mg[:, hh, :], xdt[:, h, :], True, True, [kM, kxdt], [kyd])
            if sub < 99:
                continue
            if self.dbg.get('ssd_stop') == 6:
                continue
            pst, kst = self.psum.get()
            self.mm(pst[:, 0:256], bp[:, 0, :], xdd[:, 0:4, :].rearrange("p h d -> p (h d)"), True, False, [kbp, kxdd], [kst])
            self.mm(pst[:, 0:256], bp[:, 1, :], xdd[:, 4:8, :].rearrange("p h d -> p (h d)"), False, True, [kbp, kxdd], [kst])
            y, ky = yr.get()
            yf = y.rearrange("p h d -> p (h d)")
            if c > 0:
                Sb, kSb = Sb_prev
                yo, kyo = self.psum.get()
                for g_ in range(2):
                    self.mm(yo[:, g_ * 256:(g_ + 1) * 256], CT[:, c * 128:(c + 1) * 128], Sb[:, g_].rearrange("p h d -> p (h d)"), True, True,
                            [k_CT[tb], kSb], [kyo])
                fw.op("dve", lambda g: g.tensor_tensor(out=y, in0=yo.rearrange("p (h d) -> p h d", h=8),
                                                       in1=eacs[:, c, :].unsqueeze(2).to_broadcast([128, 8, 64]), op=ALU.mult), [kyo, k_d], [ky])
                fw.op("dve", lambda g: g.tensor_tensor(out=yf, in0=yf, in1=yd, op=ALU.add), [ky, kyd], [ky])
            else:
                self.copy("dve", yf, yd, [kyd], [ky])
            fw.op("dve", lambda g: g.tensor_tensor(out=S, in0=S, in1=esel[:, c, :].unsqueeze(2).to_broadcast([128, 4, 64]), op=ALU.mult), [k_S, k_d], [k_S])
            fw.op("dve", lambda g: g.tensor_tensor(out=S.rearrange("p h d -> p (h d)"), in0=S.rearrange("p h d -> p (h d)"), in1=pst[:, 0:256], op=ALU.add),
                  [k_S, kst], [k_S])
            Sb, kSb = Sbr.get()
            self.copy("pool", Sb[0:64, 0], S[0:64], [k_S], [kSb])
            self.copy("pool", Sb[64:128, 1], S[64:128], [k_S], [kSb])
            Sb_prev = (Sb, kSb)
            if self.dbg.get('ssd_stop') == 7:
                continue
            tm, ktm = tmr.get()
            fw.op("pool", lambda g: g.tensor_tensor(out=tm, in0=xs, in1=dsk.unsqueeze(2).to_broadcast([128, 8, 64]), op=ALU.mult), [kxs, k_p], [ktm])
            fw.op("pool", lambda g: g.tensor_tensor(out=y, in0=y, in1=tm, op=ALU.add), [ky, ktm], [ky])
            zt, kzt = ztr.get()
            fw.dma("sp", zt, self.zd[c * 128:(c + 1) * 128, :], reads=[self.k_zd[c]], writes=[kzt])
            fw.op("act", lambda g: g.activation(out=zt, in_=zt, func=AF.Silu), [kzt], [kzt])
            fw.op("dve", lambda g: g.tensor_tensor(out=yf, in0=yf, in1=zt, op=ALU.mult), [ky, kzt], [ky])
            if self.dbg.get('ssd_stop') == 8:
                continue
            ss, kss = ssr.get()
            tmf = tm.rearrange("p h d -> p (h d)")
            for g_ in range(2):
                fw.op("act", lambda g: g.activation(out=tmf[:, g_ * 256:(g_ + 1) * 256], in_=yf[:, g_ * 256:(g_ + 1) * 256], func=AF.Square,
                                                    accum_out=ss[:, g_:g_ + 1]), [ky, ktm], [ktm, kss])
            fw.op("dve", lambda g: g.tensor_scalar(out=ss[:, 0:2], in0=ss[:, 0:2], scalar1=1.0 / 256.0, scalar2=1e-5, op0=ALU.mult, op1=ALU.add), [kss], [kss])
            fw.op("act", lambda g: g.activation(out=ss[:, 0:2], in_=ss[:, 0:2], func=AF.Sqrt), [kss], [kss])
            fw.op("dve", lambda g: g.reciprocal(out=ss[:, 2:4], in_=ss[:, 0:2]), [kss], [kss])
            for g_ in range(2):
                fw.op("dve", lambda g: g.tensor_scalar(out=yf[:, g_ * 256:(g_ + 1) * 256], in0=yf[:, g_ * 256:(g_ + 1) * 256],
                                                       scalar1=ss[:, 2 + g_:3 + g_], scalar2=None, op0=ALU.mult), [ky, kss], [ky])
            yb, kyb = ybr.get()
            fw.op("pool", lambda g: g.tensor_tensor(out=yb, in0=yf, in1=nw, op=ALU.mult), [ky, k_p], [kyb])
            if self.dbg.get('ssd_stop') == 9:
                continue
            pT, kpT = self.psum.get()
            pTb = pT.bitcast(BF16)
            for ct in range(4):
                fw.op("pe", lambda g: g.transpose(pTb[:, ct * 128:(ct + 1) * 128], yb[:, ct * 128:(ct + 1) * 128], self.identb), [kyb, kc_], [kpT])
            yt, kyt = ytr.get()
            self.copy("act", yt, pTb[:, 0:512].rearrange("p (c t) -> p c t", c=4), [kpT], [kyt])
            fw.dma("sp", self.yT[1].rearrange("(ct p) t -> p ct t", p=128)[:, :, c * 128:(c + 1) * 128], yt, reads=[kyt], writes=[self.k_yT[1][tb]])
        A.close()


    def cmul(self, e, ore, oim, are, aim, bre, bim, t1, t2, k, conj_b=False):
        fw = self.fw
        tt = lambda o, a, b, op: fw.op(e, lambda g: g.tensor_tensor(out=o, in0=a, in1=b, op=op), k, k)
        tt(t1, are, bre, ALU.mult)
        tt(t2, aim, bim, ALU.mult)
        tt(ore, t1, t2, ALU.add if conj_b else ALU.subtract)
        tt(t1, are, bim, ALU.mult)
        tt(t2, aim, bre, ALU.mult)
        if conj_b:
            tt(oim, t2, t1, ALU.subtract)
        else:
            tt(oim, t1, t2, ALU.add)

    def sincos(self, arg, osin, ocos, ki, kf, k):
        fw = self.fw
        TWO_PI = 2.0 * np.pi
        for r, shift in ((osin, 0.0), (ocos, np.pi / 2)):
            fw.op("dve", lambda g: g.tensor_scalar(out=kf, in0=arg, scalar1=float(shift), scalar2=float(1.0 / TWO_PI), op0=ALU.add, op1=ALU.mult), k, k)
            fw.op("dve", lambda g: g.tensor_copy(out=ki, in_=kf), k, k)
            fw.op("dve", lambda g: g.tensor_copy(out=kf, in_=ki), k, k)
            fw.op("dve", lambda g: g.tensor_scalar(out=kf, in0=kf, scalar1=float(-TWO_PI), scalar2=float(shift), op0=ALU.mult, op1=ALU.add), k, k)
            fw.op("dve", lambda g: g.tensor_tensor(out=r, in0=kf, in1=arg, op=ALU.add), k, k)
            fw.op("dve", lambda g: g.tensor_scalar(out=r, in0=r, scalar1=3.141592, scalar2=-3.141592, op0=ALU.min, op1=ALU.max), k, k)
            fw.op("act", lambda g: g.activation(out=r, in_=r, func=AF.Sin), k, k)

    def s5(self, l):
        fw = self.fw
        I = self.inp
        A = Arena(fw)
        kc_ = self.k_ident
        w_d = I["w_in"][l].rearrange("(kc p) n -> p kc n", p=128)
        U = A.sb("U", [128, 32, 32, 16], BF16)
        k_U = [Trk() for _ in range(32)]
        k_p = Trk()
        kp_ = [k_p]
        T = lambda nm, shp=(128, 16): A.sb(nm, list(shp), F32)
        abr_keep = {}
        bbr, bbi = T("bbr", (128, 16, 16)), T("bbi", (128, 16, 16))
        cre, cim = T("cre", (128, 16, 16)), T("cim", (128, 16, 16))
        pwr_, pwi_ = T("pwr_", (128, 16, 65)), T("pwi_", (128, 16, 65))
        R = T("R")
        cosT = A.sb("cosT", [128, 16, 128], F32)
        sinT = A.sb("sinT", [128, 16, 128], F32)
        drep = T("drep", (128, 512))
        A2 = Arena(fw)
        wb = A2.sb("wu", [128, 8, 512], BF16)
        kw = Trk()
        fw.dma("pool", wb, w_d[:, :, 2832:3344], writes=[kw])
        for tau in range(32):
            ps, kp = self.psum.get()
            for kc in range(8):
                lhsT = self.xT[:, kc, :].rearrange("p (c t) -> p t c", t=32)[:, tau, :]
                self.mm(ps, lhsT, wb[:, kc, :], kc == 0, kc == 7, [kw] + self.k_xT, [kp])
            self.copy(self.alt(), U[:, :, tau, :], ps.rearrange("p (g h) -> p g h", h=16), [kp], [k_U[tau]])
        T2 = lambda nm, shp=(128, 16): A2.sb(nm, list(shp), F32)

        def ld(t, src):
            fw.dma("sp", t, src, writes=kp_)
            return t
        are, aim, lst = ld(T2("are"), I["s5_are"][l]), ld(T2("aim"), I["s5_aim"][l]), ld(T2("lst"), I["s5_lst"][l])
        bre, bim = ld(T2("bre", (128, 16, 16)), I["s5_bre"][l]), ld(T2("bim", (128, 16, 16)), I["s5_bim"][l])
        ld(cre, I["s5_cre"][l]); ld(cim, I["s5_cim"][l])
        ld(drep, I["s5_d"][l:l + 1, :].partition_broadcast(128))
        step, lre, den, t1, t2, xr_, th, mag = [T2(n) for n in ("step", "lre", "den", "t1", "t2", "xr_", "th", "mag")]
        sn, cs, abr, abi, nre, kre, kim = [T2(n) for n in ("sn", "cs", "abr", "abi", "nre", "kre", "kim")]
        ki16, kf16 = A2.sb("ki16", [128, 16], I32), T2("kf16")
        V = lambda fn: fw.op("dve", fn, kp_, kp_)
        fw.op("act", lambda g: g.activation(out=step, in_=lst, func=AF.Exp), kp_, kp_)
        V(lambda g: g.tensor_scalar(out=lre, in0=are, scalar1=-1e-4, scalar2=None, op0=ALU.min))
        V(lambda g: g.tensor_tensor(out=xr_, in0=lre, in1=step, op=ALU.mult))
        V(lambda g: g.tensor_tensor(out=th, in0=aim, in1=step, op=ALU.mult))
        fw.op("act", lambda g: g.activation(out=mag, in_=xr_, func=AF.Exp), kp_, kp_)
        self.sincos(th, sn, cs, ki16, kf16, kp_)
        V(lambda g: g.tensor_tensor(out=abr, in0=mag, in1=cs, op=ALU.mult))
        V(lambda g: g.tensor_tensor(out=abi, in0=mag, in1=sn, op=ALU.mult))
        V(lambda g: g.tensor_tensor(out=t1, in0=lre, in1=lre, op=ALU.mult))
        V(lambda g: g.tensor_tensor(out=t2, in0=aim, in1=aim, op=ALU.mult))
        V(lambda g: g.tensor_tensor(out=den, in0=t1, in1=t2, op=ALU.add))
        V(lambda g: g.reciprocal(out=den, in_=den))
        V(lambda g: g.tensor_scalar(out=nre, in0=abr, scalar1=-1.0, scalar2=None, op0=ALU.add))
        self.cmul("dve", kre, kim, nre, abi, lre, aim, t1, t2, kp_, conj_b=True)
        V(lambda g: g.tensor_tensor(out=kre, in0=kre, in1=den, op=ALU.mult))
        V(lambda g: g.tensor_tensor(out=kim, in0=kim, in1=den, op=ALU.mult))
        sh3 = [128, 16, 16]
        t3a, t3b = T2("t3a", sh3), T2("t3b", sh3)
        bc3 = lambda t: t.unsqueeze(2).to_broadcast(sh3)
        self.cmul("dve", bbr, bbi, bc3(kre), bc3(kim), bre, bim, t3a, t3b, kp_)
        shp = [128, 16, 65]
        ioti = A2.sb("ioti", [128, 65], I32)
        iot = T2("iot", (128, 65))
        fw.op("pool", lambda g: g.iota(ioti[:, 0:33], pattern=[[1, 33]], base=0, channel_multiplier=0), kp_, kp_)
        fw.op("pool", lambda g: g.iota(ioti[:, 33:65], pattern=[[-1, 32]], base=31, channel_multiplier=0), kp_, kp_)
        V(lambda g: g.tensor_copy(out=iot, in_=ioti))
        marg, parg, psn, pcs, kf65 = [T2(n, shp) for n in ("marg", "parg", "psn", "pcs", "kf65")]
        ki65 = A2.sb("ki65", shp, I32)
        bcm = lambda t: t.unsqueeze(2).to_broadcast(shp)
        bci = iot.unsqueeze(1).to_broadcast(shp)
        V(lambda g: g.tensor_tensor(out=marg, in0=bcm(xr_), in1=bci, op=ALU.mult))
        V(lambda g: g.tensor_tensor(out=parg, in0=bcm(th), in1=bci, op=ALU.mult))
        fw.op("act", lambda g: g.activation(out=marg, in_=marg, func=AF.Exp), kp_, kp_)
        self.sincos(parg, psn, pcs, ki65, kf65, kp_)
        V(lambda g: g.tensor_tensor(out=pwr_, in0=marg, in1=pcs, op=ALU.mult))
        V(lambda g: g.tensor_tensor(out=pwi_, in0=marg, in1=psn, op=ALU.mult))
        V(lambda g: g.tensor_copy(out=R, in_=marg[:, :, 32]))
        wre, wim, u1, u2 = [T2(n) for n in ("wre", "wim", "u1", "u2")]
        V(lambda g: g.tensor_copy(out=wre, in_=pcs[:, :, 32]))
        V(lambda g: g.tensor_copy(out=wim, in_=psn[:, :, 32]))
        fw.op("pool", lambda g: g.memset(cosT[:, :, 0:1], 1.0), kp_, kp_)
        fw.op("pool", lambda g: g.memset(sinT[:, :, 0:1], 0.0), kp_, kp_)
        tA, tB = T2("tA", (128, 16, 64)), T2("tB", (128, 16, 64))
        n_ = 1
        while n_ < 128:
            shn = [128, 16, n_]
            bw = lambda t: t.unsqueeze(2).to_broadcast(shn)
            self.cmul("dve", cosT[:, :, n_:2 * n_], sinT[:, :, n_:2 * n_], cosT[:, :, 0:n_], sinT[:, :, 0:n_], bw(wre), bw(wim),
                      tA[:, :, 0:n_], tB[:, :, 0:n_], kp_)
            self.cmul("dve", u1, u2, wre, wim, wre, wim, t1, t2, kp_)
            V(lambda g: g.tensor_copy(out=wre, in_=u1))
            V(lambda g: g.tensor_copy(out=wim, in_=u2))
            n_ *= 2
        A2.close()
        Zr = A.ring("Z", 2, [128, 2, 32, 16], BF16)
        ABr = A.ring("ABp", 2, [128, 8, 2, 128], BF16)
        for b_ in ABr.b:
            fw.op("pool", lambda g: g.memset(b_, 0.0), writes=ABr.k)
        CAr = A.ring("CAP", 2, [128, 2, 33, 16], BF16)
        BBr = A.ring("BBrep", 2, [128, 2, 2, 8, 16], BF16)
        for b_ in BBr.b:
            fw.op("pool", lambda g: g.memset(b_, 0.0), writes=BBr.k)
        UTr = A.ring("UT", 3, [128, 4, 128], BF16)
        TBf = A.ring("TBf", 1, [128, 512], F32)
        TBr = A.ring("TB", 2, [128, 512], BF16)
        gsc = A.ring("gsc", 2, [128, 4, 128], F32)
        Rbr = A.ring("Rb", 2, [128, 128], F32)
        Spr = A.ring("Sprev", 2, [128, 2, 2, 128], BF16)
        for b_ in Spr.b:
            fw.op("pool", lambda g: g.memset(b_, 0.0), writes=Spr.k)
        ypr = A.ring("ypre", 2, [128, 32, 16], F32)
        y2r = A.ring("y2", 2, [128, 32, 16], F32)
        GYr = A.ring("GY", 1, [128, 32, 128], BF16)
        gts = A.ring("gts", 1, [128, L_SEQ], BF16)
        z3, z4 = T("z3", (128, 32, 16)), T("z4", (128, 32, 16))
        cA, cB = T("cA", (128, 33, 16)), T("cB", (128, 33, 16))
        k_z = Trk()
        GY, kGY = None, None
        for gp in range(16):
            Z, kZ = Zr.get()
            shz = [128, 32, 16]
            pa = lambda t: t[:, gp, 33:65].unsqueeze(2).to_broadcast(shz)
            pb = lambda t: t[:, gp, :].unsqueeze(1).to_broadcast(shz)
            self.cmul("pool", Z[:, 0], Z[:, 1], pa(pwr_), pa(pwi_), pb(bbr), pb(bbi), z3, z4, [k_p, k_z, kZ])
            pT, kpT = self.psum.get()
            pTb = pT.bitcast(BF16)
            for a in range(4):
                for ri in range(2):
                    j = a * 2 + ri
                    fw.op("pe", lambda g: g.transpose(pTb[:, j * 128:(j + 1) * 128], Z[:, ri, 8 * a:8 * a + 8, :].rearrange("p t h -> p (t h)"),
                                                      self.identb), [kZ, kc_], [kpT])
            AB, kAB = ABr.get()
            src = pTb.rearrange("p (j m) -> p j m", j=8)
            self.copy("act", AB[:, :, 0, 0:64], src[:, :, 0:64], [kpT], [kAB])
            self.copy("act", AB[:, :, 1, 64:128], src[:, :, 64:128], [kpT], [kAB])
            CAP, kCA = CAr.get()
            shc = [128, 33, 16]
            pa2 = lambda t: t[:, gp, 0:33].unsqueeze(2).to_broadcast(shc)
            pc2 = lambda t: t[:, gp, :].unsqueeze(1).to_broadcast(shc)
            kz2 = [k_p, k_z]
            tt = lambda o, a_, b_, op, wr: fw.op("dve", lambda g: g.tensor_tensor(out=o, in0=a_, in1=b_, op=op), kz2 + wr, [k_z] + wr)
            tt(cA, pa2(pwr_), pc2(cre), ALU.mult, [])
            tt(cB, pa2(pwi_), pc2(cim), ALU.mult, [])
            tt(CAP[:, 0], cA, cB, ALU.subtract, [kCA])
            tt(cA, pa2(pwr_), pc2(cim), ALU.mult, [])
            tt(cB, pa2(pwi_), pc2(cre), ALU.mult, [])
            tt(cA, cA, cB, ALU.add, [])
            fw.op("dve", lambda g: g.tensor_scalar(out=CAP[:, 1], in0=cA, scalar1=-1.0, scalar2=None, op0=ALU.mult), [k_z, kCA], [kCA])
            BB, kBB = BBr.get()
            for (half, gl) in ((slice(0, 64), 0), (slice(64, 128), 1)):
                fw.op("pool", lambda g: g.tensor_copy(out=BB[half, gl, 0], in_=bbr[half, gp, :].unsqueeze(1).to_broadcast([64, 8, 16])), [k_p], [kBB])
                fw.op("pool", lambda g: g.tensor_copy(out=BB[half, gl, 1], in_=bbi[half, gp, :].unsqueeze(1).to_broadcast([64, 8, 16])), [k_p], [kBB])
            UTs = []
            for gl in range(2):
                g_ = 2 * gp + gl
                pU, kpU = self.psum.get()
                pUb = pU.bitcast(BF16)
                UT, kUT = UTr.get()
                UTs.append((UT, kUT))
                for a in range(4):
                    fw.op("pe", lambda g: g.transpose(pUb[:, a * 128:(a + 1) * 128], U[:, g_, 8 * a:8 * a + 8, :].rearrange("p t h -> p (t h)"), self.identb),
                          k_U[8 * a:8 * a + 8] + [kc_], [kpU])
                self.copy(self.alt(), UT, pUb[:, 0:512].rearrange("p (a c) -> p a c", a=4), [kpU], [kUT])
            pW, kpW = self.psum.get()
            for ri in range(2):
                o = pW[:, ri * 128:(ri + 1) * 128]
                n_mm = 0
                for gl in range(2):
                    UT, kUT = UTs[gl]
                    for a in range(4):
                        self.mm(o, AB[:, a * 2 + ri, gl, :], UT[:, a, :], n_mm == 0, n_mm == 7, [kAB, kUT], [kpW])
                        n_mm += 1
            gs, kgs = gsc.get()
            cT_, sT_ = cosT[:, gp, :], sinT[:, gp, :]
            W0, W1 = pW[:, 0:128], pW[:, 128:256]
            dv = lambda o, a_, b_, op: fw.op("dve", lambda g: g.tensor_tensor(out=o, in0=a_, in1=b_, op=op), [kpW, k_p, kgs], [kgs])
            dv(gs[:, 2], W0, cT_, ALU.mult); dv(gs[:, 3], W1, sT_, ALU.mult); dv(gs[:, 0], gs[:, 2], gs[:, 3], ALU.add)
            dv(gs[:, 2], W1, cT_, ALU.mult); dv(gs[:, 3], W0, sT_, ALU.mult); dv(gs[:, 1], gs[:, 2], gs[:, 3], ALU.subtract)
            Rb, kRb = Rbr.get()
            fw.op("pool", lambda g: g.tensor_scalar(out=Rb, in0=self.ones_f, scalar1=R[:, gp:gp + 1], scalar2=0.0, op0=ALU.mult, op1=ALU.add),
                  [kc_, k_p], [kRb])
            for ri in range(2):
                fw.op("dve", lambda g: g.tensor_tensor_scan(out=gs[:, 2 + ri], data0=Rb, data1=gs[:, ri], initial=0.0, op0=ALU.mult, op1=ALU.add),
                      [kgs, kRb], [kgs])
            Sp, kSp = Spr.get()
            n1 = slice(0, 127)
            dv2 = lambda o, a_, b_, op, wr: fw.op("dve", lambda g: g.tensor_tensor(out=o, in0=a_, in1=b_, op=op), [k_p, kgs] + wr, [kgs] + wr)
            for (half, gl) in ((slice(0, 64), 0), (slice(64, 128), 1)):
                dv2(gs[half, 0, n1], gs[half, 2, n1], cT_[half, n1], ALU.mult, [])
                dv2(gs[half, 1, n1], gs[half, 3, n1], sT_[half, n1], ALU.mult, [])
                dv2(Sp[half, gl, 0, 1:128], gs[half, 0, n1], gs[half, 1, n1], ALU.subtract, [kSp])
                dv2(gs[half, 0, n1], gs[half, 2, n1], sT_[half, n1], ALU.mult, [])
                dv2(gs[half, 1, n1], gs[half, 3, n1], cT_[half, n1], ALU.mult, [])
                dv2(Sp[half, gl, 1, 1:128], gs[half, 0, n1], gs[half, 1, n1], ALU.add, [kSp])
            for gl in range(2):
                g_ = 2 * gp + gl
                rows = slice(gl * 64, (gl + 1) * 64)
                UT, kUT = UTs[gl]
                pK, kpK = self.psum.get()
                for ri in range(2):
                    lhs = BB[:, gl, ri].rearrange("p s h -> p (s h)")
                    self.mm(pK, lhs, CAP[:, ri, 0:32, :].rearrange("p m h -> p (m h)"), ri == 0, ri == 1, [kBB, kCA], [kpK])
                TBf_, kTBf = TBf.get()
                fw.op("dve", lambda g: g.tensor_scalar(out=TBf_, in0=pK, scalar1=self.rowmask[:, 0:1], scalar2=None, op0=ALU.mult), [kpK, kc_], [kTBf])
                for s_ in range(1, 8):
                    fw.op("dve", lambda g: g.scalar_tensor_tensor(out=TBf_[:, 16 * s_:512], in0=pK[:, 0:512 - 16 * s_], scalar=self.rowmask[:, s_:s_ + 1],
                                                                  in1=TBf_[:, 16 * s_:512], op0=ALU.mult, op1=ALU.add), [kpK, kc_, kTBf], [kTBf])
                TB_, kTB = TBr.get()
                self.copy("act", TB_, TBf_, [kTBf], [kTB])
                pY, kpY = self.psacc.get()
                for a in range(4):
                    self.mm(pY[:, 128 * a:512], UT[:, a, :], TB_[:, 0:512 - 128 * a], a == 0, False, [kUT, kTB], [kpY])
                for ri in range(2):
                    self.mm(pY, Sp[:, gl, ri, :], CAP[:, ri, 1:33, :].rearrange("p m h -> p (m h)"), False, ri == 1, [kSp, kCA], [kpY])
                yp, kyp = ypr.get()
                fw.op("pool", lambda g: g.tensor_tensor(out=yp, in0=U[:, g_, :, :],
                                                        in1=drep[:, 16 * g_:16 * g_ + 16].unsqueeze(1).to_broadcast([128, 32, 16]), op=ALU.mult),
                      k_U + [k_p], [kyp])
                fw.op("dve", lambda g: g.tensor_tensor(out=yp, in0=yp, in1=pY.rearrange("p (t h) -> p t h", h=16), op=ALU.add), [kyp, kpY], [kyp])
                y2, ky2 = y2r.get()
                fw.op("pool", lambda g: g.tensor_tensor(out=y2, in0=yp, in1=yp, op=ALU.mult), [kyp], [ky2])
                fw.op("pool", lambda g: g.tensor_scalar(out=y2, in0=y2, scalar1=0.044715, scalar2=1.0, op0=ALU.mult, op1=ALU.add), [ky2], [ky2])
                fw.op("pool", lambda g: g.tensor_tensor(out=y2, in0=y2, in1=yp, op=ALU.mult), [ky2, kyp], [ky2])
                fw.op("act", lambda g: g.activation(out=y2, in_=y2, func=AF.Sigmoid, scale=1.5957691216057308), [ky2], [ky2])
                if g_ % 8 == 0:
                    GY, kGY = GYr.get()
                fw.op("dve", lambda g: g.tensor_tensor(out=GY[:, :, (g_ % 8) * 16:(g_ % 8 + 1) * 16], in0=yp, in1=y2, op=ALU.mult), [kyp, ky2], [kGY])
                if g_ % 8 == 7:
                    kt = g_ // 8
                    gt, kgt = gts.get()
                    gtv = gt.rearrange("p (c t) -> p t c", t=32)
                    for b in range(4):
                        pT, kpT = self.psum.get()
                        pTb = pT.bitcast(BF16)
                        for t8 in range(8):
                            fw.op("pe", lambda g: g.transpose(pTb[:, t8 * 128:(t8 + 1) * 128], GY[:, 8 * b + t8, :], self.identb), [kGY, kc_], [kpT])
                        self.copy(self.alt(), gtv[:, 8 * b:8 * b + 8, :], pTb.rearrange("p (t c) -> p t c", t=8), [kpT], [kgt])
                    fw.dma("sp", self.gyT[kt], gt, reads=[kgt], writes=[self.k_gyT[kt]])
        A.close()
        A = Arena(fw)
        wg = A.sb("wglu", [128, 4, 512], BF16)
        bglu = A.sb("bglu", [128, 4], F32)
        k_wg = Trk()
        fw.dma("pool", wg, I["s5_w_glu"][l].rearrange("(kc p) n -> p kc n", p=128), writes=[k_wg])
        fw.dma("sp", bglu, I["s5_bglu"][l], writes=[k_wg])
        gbr = A.ring("gb", 2, [128, 4, 512], BF16)
        sgr = A.ring("sg", 2, [128, 512], F32)
        obr = A.ring("ob", 2, [128, 512], BF16)
        for tb in range(NTB):
            gb, kgb = gbr.get()
            fw.dma("sp", gb, self.gyT[:, :, tb * 512:(tb + 1) * 512].rearrange("k p t -> p k t"), reads=self.k_gyT, writes=[kgb])
            for oc in range(4):
                ps, kp = self.psum.get()
                for kc in range(4):
                    self.mm(ps, wg[:, kc, oc * 128:(oc + 1) * 128], gb[:, kc, :], kc == 0, kc == 3, [k_wg, kgb], [kp])
                sg, ksg = sgr.get()
                fw.op("act", lambda g: g.activation(out=sg, in_=ps, func=AF.Sigmoid, bias=bglu[:, oc:oc + 1]), [kp, k_wg], [ksg])
                ob, kob = obr.get()
                fw.op("dve", lambda g: g.tensor_tensor(out=ob, in0=gb[:, oc, :], in1=sg, op=ALU.mult), [kgb, ksg], [kob])
                fw.dma("sp", self.yT[2, oc * 128:(oc + 1) * 128, tb * 512:(tb + 1) * 512], ob, reads=[kob], writes=[self.k_yT[2][tb]])
        A.close()


    def merge(self, l, res_src, k_res):
        fw = self.fw
        I = self.inp
        A = Arena(fw)
        Y = A.sb("Y", [128, 3, 4, L_SEQ], BF16)
        k_Y = [Trk() for _ in range(3)]
        for r in range(3):
            fw.dma("sp", Y[:, r], self.yT[r].rearrange("(kc p) t -> p kc t", p=128), reads=self.k_yT[r], writes=[k_Y[r]])
        wgr = A.ring("wgt", 2, [128, 3, 8, 128], BF16)
        wbr = A.ring("wbr", 2, [128, 3, 4, 128], BF16)
        bg = A.sb("bg", [128, 3, 8], F32)
        k_bg = Trk()
        fw.dma("sp", bg, I["b_gate_l"][l], writes=[k_bg])
        w_d = I["w_in"][l].rearrange("(kc p) n -> p kc n", p=128)
        wbr_d = I["w_branch"][l].rearrange("r (kc p) d -> p r kc d", p=128)
        gsr = A.ring("gs", 3, [128, 512], F32)
        accr = A.ring("acc", 2, [128, 512], F32)
        mbr = A.ring("mb", 2, [128, 512], BF16)
        def load_mw(dt):
            wg, kwg = wgr.get()
            for r in range(3):
                c0 = 3344 + r * 1024 + dt * 128
                fw.dma("pool", wg[:, r], w_d[:, :, c0:c0 + 128], writes=[kwg])
            wb, kwb = wbr.get()
            for r in range(3):
                fw.dma("pool", wb[:, r], wbr_d[:, r, :, dt * 128:(dt + 1) * 128], writes=[kwb])
            return wg, kwg, wb, kwb
        nxt_mw = load_mw(0)
        for dt in range(8):
            wg, kwg, wb, kwb = nxt_mw
            if dt + 1 < 8:
                nxt_mw = load_mw(dt + 1)
            for tb in range(NTB):
                ts_ = slice(tb * 512, (tb + 1) * 512)
                acc, kacc = accr.get()
                for r in range(3):
                    pg, kpg = self.psum.get()
                    for kc in range(8):
                        self.mm(pg, wg[:, r, kc, :], self.xT[:, kc, ts_], kc == 0, kc == 7, [kwg] + self.k_xT[tb * 4:(tb + 1) * 4], [kpg])
                    gs, kgs = gsr.get()
                    fw.op("act", lambda g: g.activation(out=gs, in_=pg, func=AF.Sigmoid, bias=bg[:, r, dt:dt + 1]), [kpg, k_bg], [kgs])
                    pp, kpp = self.psum.get()
                    for kc in range(4):
                        self.mm(pp, wb[:, r, kc, :], Y[:, r, kc, ts_], kc == 0, kc == 3, [kwb, k_Y[r]], [kpp])
                    if r == 0:
                        fw.op("dve", lambda g: g.tensor_tensor(out=acc, in0=gs, in1=pp, op=ALU.mult), [kgs, kpp], [kacc])
                    else:
                        fw.op("dve", lambda g: g.tensor_tensor(out=gs, in0=gs, in1=pp, op=ALU.mult), [kgs, kpp], [kgs])
                        if r == 1:
                            fw.op("pool", lambda g: g.tensor_tensor(out=acc, in0=acc, in1=gs, op=ALU.add), [kacc, kgs], [kacc])
                        else:
                            mb, kmb = mbr.get()
                            fw.op("pool", lambda g: g.tensor_tensor(out=mb, in0=acc, in1=gs, op=ALU.add), [kacc, kgs], [kmb])
                            fw.dma("sp", self.mT[dt, :, ts_], mb, reads=[kmb], writes=[self.k_mT[dt][tb]])
        A.close()
        A = Arena(fw)
        self.alloc_ln(A)
        self.load_ln(I["ln1_g"][l:l + 1, :], I["ln1_b"][l:l + 1, :])
        wo = A.sb("wo", [128, 8, D], BF16)
        k_wo = Trk()
        fw.dma("pool", wo, I["w_out"][l].rearrange("(kc p) d -> p kc d", p=128), writes=[k_wo])
        mir = A.ring("mi", 2, [128, 8, 512], BF16)
        for tb in range(NTB):
            mi, kmi = mir.get()
            fw.dma("sp", mi, self.mT[:, :, tb * 512:(tb + 1) * 512].rearrange("k p t -> p k t"),
                   reads=[self.k_mT[dt][tb] for dt in range(8)], writes=[kmi])
            for ts in range(4):
                tt = tb * 4 + ts
                p0, k0 = self.psum.get()
                p1, k1 = self.psum.get()
                for h, (pp, kk) in enumerate(((p0, k0), (p1, k1))):
                    for kc in range(8):
                        self.mm(pp, mi[:, kc, ts * 128:(ts + 1) * 128], wo[:, kc, h * 512:(h + 1) * 512], kc == 0, kc == 7, [kmi, k_wo], [kk])
                self.resid_ln(tt, (p0, p1), (k0, k1), res_src, k_res[tt], self.xr, self.k_xr[tt])
        A.close()

    def setup_ffn(self):
        fw = self.fw
        self.actT = fw.dram("actT", [22, 128, L_SEQ], BF16)
        self.k_actT = [[Trk() for _ in range(NTB)] for _ in range(22)]

    def ffn(self, l, out_dram, k_out):
        fw = self.fw
        I = self.inp
        A = Arena(fw)
        self.alloc_ln(A)
        self.wdown = A.sb("wdown", [128, 22, D], BF16)
        self.k_wdown = Trk("wdown")
        self.wup = A.ring("wup", 2, [128, 8, 256], BF16)
        self.fcw = A.sb("fcw", [128, 44, 3], F32)
        self.fcb = A.sb("fcb", [128, 44], F32)
        self.k_fc = Trk("fc")
        self.sv = A.ring("sv", 3, [128, 514], BF16)
        self.sg = A.ring("sg", 3, [128, 514], BF16)
        self.dg = A.ring("dg", 2, [128, 6, 128], BF16)
        self.hg = A.ring("hg", 2, [128, 512], F32)
        self.actb = A.ring("actb", 3, [128, 512], BF16)
        self.actin = A.ring("actin", 2, [128, 22, 512], BF16)
        wup_d = I["ffn_w_up"][l].rearrange("(kc p) n -> p kc n", p=128)
        fw.dma("sp", self.fcw, I["ffn_cw"][l], writes=[self.k_fc])
        fw.dma("sp", self.fcb, I["ffn_cb"][l], writes=[self.k_fc])
        k_all_xT = self.k_xT
        def load_w(m):
            wb, kw = self.wup.get()
            fw.dma("pool", wb[:, :, 0:128], wup_d[:, :, m * 128:(m + 1) * 128], writes=[kw])
            fw.dma("pool", wb[:, :, 128:256], wup_d[:, :, DFF + m * 128:DFF + (m + 1) * 128], writes=[kw])
            return wb, kw
        nxt_w = load_w(0)
        for m in range(22):
            wb, kw = nxt_w
            if m + 1 < 22:
                nxt_w = load_w(m + 1)
            dg, kdg = self.dg.get()
            for t in range(3):
                fw.op("pool", lambda g: g.tensor_scalar(out=dg[:, t, :], in0=self.identf, scalar1=self.fcw[:, m, t:t + 1], scalar2=0.0,
                                                        op0=ALU.mult, op1=ALU.add), reads=[self.k_ident, self.k_fc], writes=[kdg])
                fw.op("pool", lambda g: g.tensor_scalar(out=dg[:, 3 + t, :], in0=self.identf, scalar1=self.fcw[:, 22 + m, t:t + 1], scalar2=0.0,
                                                        op0=ALU.mult, op1=ALU.add), reads=[self.k_ident, self.k_fc], writes=[kdg])
            prev = None
            pend = None

            def conv(st):
                sv, ksv, sg, ksg, tb = st
                cv, kcv = self.psum.get()
                for t in range(3):
                    self.mm(cv, dg[:, t, :], sv[:, t:t + 512], t == 0, t == 2, [kdg, ksv], [kcv])
                cg, kcg = self.psum.get()
                for t in range(3):
                    self.mm(cg, dg[:, 3 + t, :], sg[:, t:t + 512], t == 0, t == 2, [kdg, ksg], [kcg])
                hg, khg = self.hg.get()
                fw.op("act", lambda g: g.activation(out=hg, in_=cg, func=AF.Silu, bias=self.fcb[:, 22 + m:23 + m]),
                      reads=[kcg, self.k_fc], writes=[khg])
                ab, kab = self.actb.get()
                fw.op("dve", lambda g: g.scalar_tensor_tensor(out=ab, in0=cv, scalar=self.fcb[:, m:m + 1], in1=hg,
                                                              op0=ALU.add, op1=ALU.mult), reads=[kcv, self.k_fc, khg], writes=[kab])
                fw.dma("sp", self.actT[m, :, tb * 512:(tb + 1) * 512], ab, reads=[kab], writes=[self.k_actT[m][tb]])
            for tb in range(NTB):
                xk = k_all_xT[tb * 4:(tb + 1) * 4]
                pv, kpv = self.psum.get()
                for kc in range(8):
                    self.mm(pv, wb[:, kc, 0:128], self.xT[:, kc, tb * 512:(tb + 1) * 512], kc == 0, kc == 7, [kw] + xk, [kpv])
                pg, kpg = self.psum.get()
                for kc in range(8):
                    self.mm(pg, wb[:, kc, 128:256], self.xT[:, kc, tb * 512:(tb + 1) * 512], kc == 0, kc == 7, [kw] + xk, [kpg])
                if pend is not None:
                    conv(pend)
                sv, ksv = self.sv.get()
                sg, ksg = self.sg.get()
                self.copy("act", sv[:, 2:514], pv, [kpv], [ksv])
                self.copy("dve", sg[:, 2:514], pg, [kpg], [ksg])
                if prev is None:
                    fw.op("pool", lambda g: g.memset(sv[:, 0:2], 0.0), writes=[ksv])
                    fw.op("pool", lambda g: g.memset(sg[:, 0:2], 0.0), writes=[ksg])
                else:
                    psv, pksv, psg, pksg = prev
                    fw.op("pool", lambda g: g.tensor_copy(out=sv[:, 0:2], in_=psv[:, 512:514]), reads=[pksv], writes=[ksv])
                    fw.op("pool", lambda g: g.tensor_copy(out=sg[:, 0:2], in_=psg[:, 512:514]), reads=[pksg], writes=[ksg])
                prev = (sv, ksv, sg, ksg)
                pend = (sv, ksv, sg, ksg, tb)
            conv(pend)
        fw.dma("pool", self.wdown, I["ffn_w_down"][l].rearrange("(kt p) d -> p kt d", p=128), writes=[self.k_wdown])
        self.load_ln(I["ln2_g"][l:l + 1, :], I["ln2_b"][l:l + 1, :])
        for tb in range(NTB):
            ai, kai = self.actin.get()
            fw.dma("sp", ai, self.actT[:, :, tb * 512:(tb + 1) * 512].rearrange("m p t -> p m t"),
                   reads=[self.k_actT[m][tb] for m in range(22)], writes=[kai])
            for ts in range(4):
                tt = tb * 4 + ts
                p0, k0 = self.psum.get()
                p1, k1 = self.psum.get()
                for h, (pp, kk) in enumerate(((p0, k0), (p1, k1))):
                    for m in range(22):
                        self.mm(pp, ai[:, m, ts * 128:(ts + 1) * 128], self.wdown[:, m, h * 512:(h + 1) * 512], m == 0, m == 21,
                                [kai, self.k_wdown], [kk])
                self.resid_ln(tt, (p0, p1), (k0, k1), self.xr, self.k_xr[tt], out_dram, k_out[tt])
        A.close()


def _host_layout(inputs):
    f = {}
    A = lambda a: np.ascontiguousarray(np.asarray(a, dtype=np.float32))
    for k in ("w_in", "ffn_w_up", "ffn_w_down", "ln1_g", "ln1_b", "ln2_g", "ln2_b", "w_out", "w_branch", "s5_w_glu"):
        f[k] = A(inputs[k])
    cw = A(inputs["ffn_conv_w"])
    sw = A(inputs["ssd_conv_w"])
    f["ssd_cw"] = A(sw.reshape(DEPTH, 4, 6, 128).transpose(0, 3, 2, 1))
    f["ssd_cb"] = A(A(inputs["ssd_conv_b"]).reshape(DEPTH, 6, 128).transpose(0, 2, 1))
    for k in ("ssd_conv_b", "ssd_dt_bias", "ssd_a_log", "ssd_d", "ssd_norm_w", "fox_f_bias"):
        f[k] = A(inputs[k])
    pl = lambda a: A(a.reshape(DEPTH, 16, 2, 64).transpose(0, 2, 3, 1).reshape(DEPTH, 128, 16))
    f["s5_are"] = pl(A(inputs["s5_a_re"]))
    f["s5_aim"] = pl(A(inputs["s5_a_im"]))
    f["s5_lst"] = pl(np.repeat(A(inputs["s5_log_step"])[:, :, None], 64, axis=2))
    pb_ = lambda a: A(a.reshape(DEPTH, 16, 2, 64, 16).transpose(0, 2, 3, 1, 4).reshape(DEPTH, 128, 16, 16))
    f["s5_bre"] = pb_(A(inputs["s5_b_re"]))
    f["s5_bim"] = pb_(A(inputs["s5_b_im"]))
    f["s5_cre"] = pb_(A(A(inputs["s5_c_re"]).transpose(0, 1, 3, 2)))
    f["s5_cim"] = pb_(A(A(inputs["s5_c_im"]).transpose(0, 1, 3, 2)))
    f["s5_d"] = A(inputs["s5_d"])
    f["s5_bglu"] = A(A(inputs["s5_b_glu"]).reshape(DEPTH, 4, 128).transpose(0, 2, 1))
    f["b_gate_l"] = A(A(inputs["b_gate"]).reshape(DEPTH, 3, 8, 128).transpose(0, 3, 1, 2))
    f["ffn_cw"] = A(cw.reshape(DEPTH, 3, 44, 128).transpose(0, 3, 2, 1))
    f["ffn_cb"] = A(A(inputs["ffn_conv_b"]).reshape(DEPTH, 44, 128).transpose(0, 2, 1))
    return f


def build_ffn_test():
    k = K()
    x = k.din("x", [L_SEQ, D])
    k.din("ffn_w_up", [DEPTH, D, 2 * DFF]); k.din("ffn_w_down", [DEPTH, DFF, D])
    k.din("ffn_cw", [DEPTH, 128, 44, 3]); k.din("ffn_cb", [DEPTH, 128, 44])
    k.din("ln2_g", [DEPTH, D]); k.din("ln2_b", [DEPTH, D])
    out = k.nc.dram_tensor("out", [L_SEQ, D], F32, kind="ExternalOutput").ap()
    k.setup_common()
    k.setup_ffn()
    k.phase0(x)
    k.xr = x
    k_out = [Trk() for _ in range(NTT)]
    k.ffn(0, out, k_out)
    k.fw.drain()
    return k


def build_fox_test():
    k = K()
    x = k.din("x", [L_SEQ, D])
    k.din("w_in", [DEPTH, D, DIN]); k.din("fox_f_bias", [DEPTH, 8])
    out = k.nc.dram_tensor("out", [512, L_SEQ], BF16, kind="ExternalOutput").ap()
    k.setup_common()
    k.phase0(x)
    k.fox(0)
    t = k.fw.sb("cp", [128, 4, L_SEQ], BF16)
    kt = Trk()
    k.fw.dma("sp", t, k.yT[0].rearrange("(c p) t -> p c t", p=128), reads=[x for r in k.k_yT[0] for x in [r]], writes=[kt])
    k.fw.dma("sp", out.rearrange("(c p) t -> p c t", p=128), t, reads=[kt], writes=[Trk()])
    k.fw.drain()
    return k


def build_ssd_test(dbg=None):
    k = K(dbg=dbg)
    x = k.din("x", [L_SEQ, D])
    k.din("w_in", [DEPTH, D, DIN])
    k.din("ssd_cw", [DEPTH, 128, 6, 4]); k.din("ssd_cb", [DEPTH, 128, 6]); k.din("ssd_conv_b", [DEPTH, 768])
    k.din("ssd_dt_bias", [DEPTH, 8]); k.din("ssd_a_log", [DEPTH, 8]); k.din("ssd_d", [DEPTH, 8]); k.din("ssd_norm_w", [DEPTH, 512])
    out = k.nc.dram_tensor("out", [512, L_SEQ], BF16, kind="ExternalOutput").ap()
    k.setup_common()
    k.phase0(x)
    k.ssd(0)
    t = k.fw.sb("cp", [128, 4, L_SEQ], BF16)
    kt = Trk()
    k.fw.dma("sp", t, k.yT[1].rearrange("(c p) t -> p c t", p=128), reads=list(k.k_yT[1]), writes=[kt])
    k.fw.dma("sp", out.rearrange("(c p) t -> p c t", p=128), t, reads=[kt], writes=[Trk()])
    k.fw.drain()
    return k


S5_INS = [("s5_are", [DEPTH, 128, 16]), ("s5_aim", [DEPTH, 128, 16]), ("s5_lst", [DEPTH, 128, 16]),
          ("s5_bre", [DEPTH, 128, 16, 16]), ("s5_bim", [DEPTH, 128, 16, 16]), ("s5_cre", [DEPTH, 128, 16, 16]),
          ("s5_cim", [DEPTH, 128, 16, 16]), ("s5_d", [DEPTH, 512]), ("s5_bglu", [DEPTH, 128, 4]), ("s5_w_glu", [DEPTH, 512, 512])]


def build_s5_test(dbg=None):
    k = K(dbg=dbg)
    x = k.din("x", [L_SEQ, D])
    k.din("w_in", [DEPTH, D, DIN])
    for n, shp in S5_INS:
        k.din(n, shp)
    out = k.nc.dram_tensor("out", [512, L_SEQ], BF16, kind="ExternalOutput").ap()
    k.setup_common()
    k.phase0(x)
    k.s5(0)
    t = k.fw.sb("cp", [128, 4, L_SEQ], BF16)
    kt = Trk()
    k.fw.dma("sp", t, k.yT[2].rearrange("(c p) t -> p c t", p=128), reads=list(k.k_yT[2]), writes=[kt])
    k.fw.dma("sp", out.rearrange("(c p) t -> p c t", p=128), t, reads=[kt], writes=[Trk()])
    k.fw.drain()
    return k


ALL_INS = [("w_in", [DEPTH, D, DIN]), ("fox_f_bias", [DEPTH, 8]),
           ("ssd_cw", [DEPTH, 128, 6, 4]), ("ssd_cb", [DEPTH, 128, 6]), ("ssd_conv_b", [DEPTH, 768]),
           ("ssd_dt_bias", [DEPTH, 8]), ("ssd_a_log", [DEPTH, 8]), ("ssd_d", [DEPTH, 8]), ("ssd_norm_w", [DEPTH, 512])] + S5_INS + [
           ("w_branch", [DEPTH, 3, 512, D]), ("b_gate_l", [DEPTH, 128, 3, 8]), ("w_out", [DEPTH, D, D]),
           ("ln1_g", [DEPTH, D]), ("ln1_b", [DEPTH, D]),
           ("ffn_w_up", [DEPTH, D, 2 * DFF]), ("ffn_w_down", [DEPTH, DFF, D]), ("ffn_cw", [DEPTH, 128, 44, 3]), ("ffn_cb", [DEPTH, 128, 44]),
           ("ln2_g", [DEPTH, D]), ("ln2_b", [DEPTH, D])]


def build_full(nl=DEPTH, stop=None):
    k = K()
    x = k.din("x", [L_SEQ, D])
    for n, shp in ALL_INS:
        k.din(n, shp)
    out = k.nc.dram_tensor("out", [L_SEQ, D], F32, kind="ExternalOutput").ap()
    k.setup_common()
    k.setup_ffn()
    k.phase0(x)
    k_dummy = [Trk() for _ in range(NTT)]
    k_out = [Trk() for _ in range(NTT)]
    for l in range(nl):
        k.fox(l)
        k.ssd(l)
        k.s5(l)
        k.merge(l, x if l == 0 else k.xr, k_dummy if l == 0 else k.k_xr)
        if stop == "merge":
            break
        last = (l == nl - 1)
        k.ffn(l, out if last else k.xr, k_out if last else k.k_xr)
    if stop == "merge":
        A = Arena(k.fw)
        r = A.ring("dump", 2, [128, D], F32)
        for tt in range(NTT):
            t, kt = r.get()
            k.fw.dma("sp", t, k.xr[tt * 128:(tt + 1) * 128, :], reads=[k.k_xr[tt]], writes=[kt])
            k.fw.dma("sp", out[tt * 128:(tt + 1) * 128, :], t, reads=[kt], writes=[k_out[tt]])
    k.fw.drain()
    return k


def build_l1_test():
    return build_full(1)


def build_m1_test():
    return build_full(1, stop="merge")


_CACHE = {}


def kernel(**inputs):
    f = _host_layout(inputs)
    if "k" not in _CACHE:
        _CACHE["k"] = build_full(DEPTH)
    k = _CACHE["k"]
    x = np.ascontiguousarray(np.asarray(inputs["x"], dtype=np.float32))
    nb = x.shape[0]
    in_maps = []
    for b in range(nb):
        m = {"x": x[b]}
        for n, _ in ALL_INS:
            m[n] = f[n]
        in_maps.append(m)
    res = run_bass_kernel_spmd(k.nc, in_maps, core_ids=list(range(nb)))
    return np.stack([np.asarray(r["out"], dtype=np.float32) for r in res.results], axis=0)


def build_l4_test():
    return build_full(4)


def build_l2_test():
    return build_full(2)
```

```python
import numpy as np
import concourse.bass as bass
import concourse.mybir as mybir
from concourse.bass_utils import run_bass_kernel_spmd
from contextlib import ExitStack

F32 = mybir.dt.float32
BF16 = mybir.dt.bfloat16
I32 = mybir.dt.int32
AF = mybir.ActivationFunctionType
ALU = mybir.AluOpType

L_SEQ = 4096
D = 1024
DEPTH = 4
DFF = 2816
DIN = 6416
ALPHA = (2 * DEPTH) ** 0.25
NTT = L_SEQ // 128
NTB = L_SEQ // 512
NOSAME = ()


class Trk:
    __slots__ = ("w", "rs", "name")

    def __init__(self, name=""):
        self.w = None
        self.rs = []
        self.name = name


class Fw:
    NDMA = 48

    def __init__(self, nc):
        self.nc = nc
        self.eng = {"pe": nc.tensor, "act": nc.scalar, "dve": nc.vector,
                    "pool": nc.gpsimd, "sp": nc.sync}
        self.sem = {k: nc.alloc_semaphore("s_" + k) for k in self.eng}
        self.cnt = {k: 0 for k in self.eng}
        self.seen = {k: {} for k in self.eng}
        self.dsem = [nc.alloc_semaphore("d%d" % i) for i in range(self.NDMA)]
        self.dval = [0] * self.NDMA
        self.dnext = {"sp": 0, "pool": 0}
        self.drange = {"sp": (0, 32), "pool": (32, self.NDMA)}
        self.nwait = 0
        self.ninst = 0
        self.nosame = set(NOSAME)

    def sb(self, name, shape, dt):
        return self.nc.alloc_sbuf_tensor(name, list(shape), dt).ap()

    def ps(self, name, shape, dt=F32):
        return self.nc.alloc_psum_tensor(name, list(shape), dt).ap()

    def dram(self, name, shape, dt, kind="Internal"):
        return self.nc.dram_tensor(name, list(shape), dt, kind=kind).ap()

    def _need(self, e, ev):
        if ev is None:
            return
        key, val = ev
        if key == e and (e == "pe" or e in self.nosame):
            return
        if self.seen[e].get(key, 0) >= val:
            return
        sem = self.sem[key] if isinstance(key, str) else self.dsem[key]
        self.eng[e].wait_ge(sem, val)
        self.nwait += 1
        self.seen[e][key] = val

    def _deps(self, e, reads, writes):
        for t in reads:
            self._need(e, t.w)
        for t in writes:
            self._need(e, t.w)
            for r in t.rs:
                self._need(e, r)

    def _commit(self, ev, reads, writes):
        for t in reads:
            t.rs.append(ev)
            if len(t.rs) > 48:
                d = {}
                for k, v in t.rs:
                    if d.get(k, 0) < v:
                        d[k] = v
                t.rs = list(d.items())
        for t in writes:
            t.w = ev
            t.rs = []

    def op(self, e, fn, reads=(), writes=()):
        self._deps(e, reads, writes)
        ins = fn(self.eng[e])
        self.cnt[e] += 1
        ins.then_inc(self.sem[e], 1)
        ev = (e, self.cnt[e])
        self._commit(ev, reads, writes)
        self.ninst += 1
        return ev

    def dma(self, q, out, in_, reads=(), writes=(), **kw):
        lo, hi = self.drange[q]
        i = lo + self.dnext[q]
        self.dnext[q] = (self.dnext[q] + 1) % (hi - lo)
        if self.dval[i] > 0:
            self._need(q, (i, self.dval[i]))
        self._deps(q, reads, writes)
        self.dval[i] += 16
        self.eng[q].dma_start(out=out, in_=in_, **kw).then_inc(self.dsem[i], 16)
        ev = (i, self.dval[i])
        self._commit(ev, reads, writes)
        self.ninst += 1
        return ev

    def barrier(self):
        for i in range(self.NDMA):
            if self.dval[i]:
                self._need("sp", (i, self.dval[i]))
        ins = self.eng["sp"].sem_inc(self.sem["sp"], 1)
        self.cnt["sp"] += 1
        for e in ("pe", "act", "dve", "pool", "sp"):
            for k in ("pe", "act", "dve", "pool", "sp"):
                if k != e and self.cnt[k]:
                    if self.seen[e].get(k, 0) < self.cnt[k]:
                        self.eng[e].wait_ge(self.sem[k], self.cnt[k])
                        self.seen[e][k] = self.cnt[k]
                        self.nwait += 1

    def drain(self):
        for i in range(self.NDMA):
            if self.dval[i]:
                self._need("sp", (i, self.dval[i]))
        for k in ("pe", "act", "dve", "pool"):
            if self.cnt[k]:
                self._need("sp", (k, self.cnt[k]))


class Arena:
    def __init__(self, fw):
        self.fw = fw
        self.st = ExitStack()

    _uid = [0]

    def sb(self, name, shape, dt):
        Arena._uid[0] += 1
        return self.st.enter_context(self.fw.nc.sbuf_tensor("%s_%d" % (name, Arena._uid[0]), list(shape), dt)).ap()

    def ring(self, name, n, shape, dt):
        return Ring(self.fw, name, n, shape, dt, mk=self.sb)

    def close(self):
        self.fw.barrier()
        self.st.close()


class Ring:
    def __init__(self, fw, name, n, shape, dt, psum=False, mk=None):
        if mk is None:
            mk = fw.ps if psum else fw.sb
        self.b = [mk("%s%d" % (name, i), shape, dt) for i in range(n)]
        self.k = [Trk("%s%d" % (name, i)) for i in range(n)]
        self.i = 0
        self.n = n

    def get(self):
        i = self.i
        self.i = (i + 1) % self.n
        return self.b[i], self.k[i]


class K:
    def __init__(self, nlayers=DEPTH, dbg=None):
        self.nl = nlayers
        self.dbg = dbg or {}
        nc = self.nc = bass.Bass("TRN2", target_bir_lowering=False)
        fw = self.fw = Fw(nc)
        self.inp = {}
        self.ev_alt = 0

    def din(self, name, shape, dt=F32):
        ap = self.nc.dram_tensor(name, list(shape), dt, kind="ExternalInput").ap()
        self.inp[name] = ap
        return ap

    def alt(self):
        self.ev_alt ^= 1
        return "act" if self.ev_alt else "dve"

    def copy(self, e, out, in_, reads, writes):
        if e == "act":
            return self.fw.op("act", lambda g: g.activation(out=out, in_=in_, func=AF.Copy), reads, writes)
        return self.fw.op(e, lambda g: g.tensor_copy(out=out, in_=in_), reads, writes)

    def mm(self, out, lhsT, rhs, start, stop, reads, writes):
        return self.fw.op("pe", lambda g: g.matmul(out, lhsT=lhsT, rhs=rhs, start=start, stop=stop), reads, writes)

    def setup_common(self):
        fw = self.fw
        self.psum = Ring(fw, "psb", 6, [128, 512], F32, psum=True)
        self.psacc = Ring(fw, "psa", 2, [128, 512], F32, psum=True)
        self.identb = fw.sb("identb", [128, 128], BF16)
        self.identf = fw.sb("identf", [128, 128], F32)
        self.k_ident = Trk("ident")
        fw.op("pool", lambda g: g.memset(self.identf, 0.0), writes=[self.k_ident])
        fw.op("pool", lambda g: g.affine_select(out=self.identf, in_=self.identf, compare_op=ALU.not_equal, fill=1.0,
                                                base=0, pattern=[[-1, 128]], channel_multiplier=1),
              reads=[self.k_ident], writes=[self.k_ident])
        fw.op("pool", lambda g: g.tensor_copy(out=self.identb, in_=self.identf), reads=[self.k_ident], writes=[self.k_ident])
        self.utri_f = fw.sb("utri_f", [128, 128], F32)
        self.trib = fw.sb("trib", [128, 128], BF16)
        self.ones_f = fw.sb("ones_f", [128, 128], F32)
        self.mean_f = fw.sb("mean_f", [128, 128], F32)
        self.ones_b = fw.sb("ones_b", [128, 128], BF16)
        self.scanmask = fw.sb("scanmask", [128, 8, 32], F32)
        fw.op("pool", lambda g: g.memset(self.utri_f, 1.0), writes=[self.k_ident])
        fw.op("pool", lambda g: g.affine_select(out=self.utri_f, in_=self.utri_f, compare_op=ALU.is_ge, fill=0.0,
                                                base=0, pattern=[[1, 128]], channel_multiplier=-1),
              reads=[self.k_ident], writes=[self.k_ident])
        fw.op("pool", lambda g: g.tensor_copy(out=self.trib, in_=self.utri_f), reads=[self.k_ident], writes=[self.k_ident])
        fw.op("pool", lambda g: g.memset(self.ones_f, 1.0), writes=[self.k_ident])
        fw.op("pool", lambda g: g.memset(self.ones_b, 1.0), writes=[self.k_ident])
        fw.op("pool", lambda g: g.memset(self.mean_f, 1.0 / 128.0), writes=[self.k_ident])
        fw.op("pool", lambda g: g.memset(self.scanmask, 1.0), writes=[self.k_ident])
        fw.op("pool", lambda g: g.memset(self.scanmask[:, :, 0:1], 0.0), writes=[self.k_ident])
        self.ones33 = fw.sb("ones33", [128, 128], BF16)
        fw.op("pool", lambda g: g.memset(self.ones33, 0.0), writes=[self.k_ident])
        for r_ in (0, 32, 64, 96):
            fw.op("pool", lambda g: g.memset(self.ones33[r_:r_ + 1, :], 1.0), writes=[self.k_ident])
        self.nutri_f = fw.sb("nutri_f", [128, 128], F32)
        self.negmask4 = fw.sb("negmask4", [128, 4, 128], F32)
        fw.op("pool", lambda g: g.tensor_scalar(out=self.nutri_f, in0=self.utri_f, scalar1=-1.0, scalar2=0.0, op0=ALU.mult, op1=ALU.add),
              reads=[self.k_ident], writes=[self.k_ident])
        fw.op("pool", lambda g: g.memset(self.negmask4, -30000.0), writes=[self.k_ident])
        for r_ in range(4):
            fw.op("pool", lambda g: g.affine_select(out=self.negmask4[:, r_, :], in_=self.negmask4[:, r_, :], compare_op=ALU.is_gt, fill=0.0,
                                                    base=0, pattern=[[-1, 128]], channel_multiplier=1),
                  reads=[self.k_ident], writes=[self.k_ident])
        self.rowmask = fw.sb("rowmask", [128, 8], F32)
        fw.op("pool", lambda g: g.memset(self.rowmask, 1.0), writes=[self.k_ident])
        fw.op("pool", lambda g: g.affine_select(out=self.rowmask, in_=self.rowmask, compare_op=ALU.is_ge, fill=0.0,
                                                base=0, pattern=[[-16, 8]], channel_multiplier=1), reads=[self.k_ident], writes=[self.k_ident])
        fw.op("pool", lambda g: g.affine_select(out=self.rowmask, in_=self.rowmask, compare_op=ALU.is_ge, fill=0.0,
                                                base=15, pattern=[[16, 8]], channel_multiplier=-1), reads=[self.k_ident], writes=[self.k_ident])
        self.mT = fw.dram("mT", [8, 128, L_SEQ], BF16)
        self.k_mT = [[Trk() for _ in range(NTB)] for _ in range(8)]
        self.gyT = fw.dram("gyT", [4, 128, L_SEQ], BF16)
        self.k_gyT = [Trk() for _ in range(4)]
        self.zd = fw.dram("zd", [L_SEQ, 512], F32)
        self.k_zd = [Trk() for _ in range(NTT)]
        self.yT = fw.dram("yT", [3, 512, L_SEQ], BF16)
        self.k_yT = [[Trk() for _ in range(NTB)] for _ in range(3)]
        self.xT = fw.sb("xT", [128, 8, L_SEQ], BF16)
        self.k_xT = [Trk("xT%d" % i) for i in range(NTT)]
        self.xr = fw.dram("xr", [L_SEQ, D], F32)
        self.k_xr = [Trk("xr%d" % i) for i in range(NTT)]

    def alloc_ln(self, A):
        self.tok32 = A.ring("tok32", 6, [128, D], F32)
        self.tokbf = A.ring("tokbf", 3, [128, D], BF16)
        self.ln_pipe = []
        self.ln_g = A.sb("ln_g", [128, D], F32)
        self.ln_b = A.sb("ln_b", [128, D], F32)
        self.k_lnp = Trk("lnp")
        self.stat = A.ring("stat", 4, [128, 16], F32)

    def to_xT(self, src_bf, k_src, tt):
        fw = self.fw
        ps, kp = self.psum.get()
        psb = ps.bitcast(BF16)
        for c in range(8):
            fw.op("pe", lambda g: g.transpose(psb[:, c * 128:(c + 1) * 128], src_bf[:, c * 128:(c + 1) * 128], self.identb),
                  reads=[k_src, self.k_ident], writes=[kp])
        dst = self.xT[:, :, tt * 128:(tt + 1) * 128]
        self.copy(self.alt(), dst, psb.rearrange("p (c t) -> p c t", c=8), reads=[kp], writes=[self.k_xT[tt]])

    def phase0(self, x_in):
        fw = self.fw
        A = Arena(fw)
        self.alloc_ln(A)
        for tt in range(NTT):
            t32, k32 = self.tok32.get()
            fw.dma("sp", t32, x_in[tt * 128:(tt + 1) * 128, :], writes=[k32])
            tb, kb = self.tokbf.get()
            self.copy(self.alt(), tb, t32, reads=[k32], writes=[kb])
            self.to_xT(tb, kb, tt)
        A.close()

    def load_ln(self, g_ap, b_ap):
        fw = self.fw
        fw.dma("sp", self.ln_g, g_ap.partition_broadcast(128), writes=[self.k_lnp])
        fw.dma("sp", self.ln_b, b_ap.partition_broadcast(128), writes=[self.k_lnp])

    def resid_ln(self, tt, ps_halves, k_ps, res_src, k_res, out_dram, k_out):
        fw = self.fw
        r32, kr = self.tok32.get()
        fw.dma("sp", r32, res_src[tt * 128:(tt + 1) * 128, :], reads=[k_res], writes=[kr])
        t32, kt = self.tok32.get()
        for h in range(2):
            fw.op("dve", lambda g: g.scalar_tensor_tensor(out=t32[:, h * 512:(h + 1) * 512], in0=r32[:, h * 512:(h + 1) * 512],
                                                          scalar=float(ALPHA), in1=ps_halves[h], op0=ALU.mult, op1=ALU.add),
                  reads=[kr, k_ps[h]], writes=[kt])
        st, ks = self.stat.get()
        for h in range(2):
            fw.op("dve", lambda g: g.bn_stats(out=st[:, h * 6:(h + 1) * 6], in_=t32[:, h * 512:(h + 1) * 512]), reads=[kt], writes=[ks])
        fw.op("dve", lambda g: g.bn_aggr(out=st[:, 12:14], in_=st[:, 0:12]), reads=[ks], writes=[ks])
        fw.op("dve", lambda g: g.tensor_scalar(out=st[:, 14:15], in0=st[:, 13:14], scalar1=1e-5, scalar2=None, op0=ALU.add), reads=[ks], writes=[ks])
        fw.op("act", lambda g: g.activation(out=st[:, 14:15], in_=st[:, 14:15], func=AF.Sqrt), reads=[ks], writes=[ks])
        self.ln_pipe.append(dict(tt=tt, r32=r32, kr=kr, t32=t32, kt=kt, st=st, ks=ks, out=out_dram, k_out=k_out))
        if len(self.ln_pipe) >= 2:
            self.ln_stage_b(self.ln_pipe[-2])
        if len(self.ln_pipe) >= 3:
            self.ln_stage_c(self.ln_pipe[-3])

    def ln_stage_b(self, p):
        fw = self.fw
        t32, kt, r32, kr, st, ks, tt = p["t32"], p["kt"], p["r32"], p["kr"], p["st"], p["ks"], p["tt"]
        fw.op("dve", lambda g: g.reciprocal(out=st[:, 15:16], in_=st[:, 14:15]), reads=[ks], writes=[ks])
        fw.op("dve", lambda g: g.tensor_scalar(out=t32, in0=t32, scalar1=st[:, 12:13], scalar2=st[:, 15:16], op0=ALU.subtract, op1=ALU.mult),
              reads=[kt, ks], writes=[kt])
        fw.op("dve", lambda g: g.tensor_tensor(out=t32, in0=t32, in1=self.ln_g, op=ALU.mult), reads=[kt, self.k_lnp], writes=[kt])
        fw.op("dve", lambda g: g.tensor_tensor(out=r32, in0=t32, in1=self.ln_b, op=ALU.add), reads=[kt, self.k_lnp], writes=[kr])
        fw.dma("sp", p["out"][tt * 128:(tt + 1) * 128, :], r32, reads=[kr], writes=[p["k_out"]])
        tb, kb = self.tokbf.get()
        self.copy("act", tb, r32, reads=[kr], writes=[kb])
        p["tb"], p["kb"] = tb, kb

    def ln_stage_c(self, p):
        self.to_xT(p["tb"], p["kb"], p["tt"])

    def ln_flush(self):
        n = len(self.ln_pipe)
        if n >= 1:
            self.ln_stage_b(self.ln_pipe[-1])
        if n >= 2:
            self.ln_stage_c(self.ln_pipe[-2])
        if n >= 1:
            self.ln_stage_c(self.ln_pipe[-1])
        self.ln_pipe = []

    def fox(self, l):
        fw = self.fw
        I = self.inp
        A = Arena(fw)
        kc_ = self.k_ident
        win = A.ring("win", 2, [128, 8, 528], BF16)
        vaug = A.sb("vaug", [128, 32, 8, 65], BF16)
        k_v = [Trk() for _ in range(NTT)]
        FL = A.sb("FL", [128, 32, 8], F32)
        k_FL = Trk()
        qkr = A.ring("qk", 2, [128, 2, L_SEQ], BF16)
        w_d = I["w_in"][l].rearrange("(kc p) n -> p kc n", p=128)
        fw.op("pool", lambda g: g.memset(vaug[:, :, :, 64:65], 1.0), writes=k_v)

        def proj_qk(hp):
            qk, _ = qkr.get()
            kq = [Trk() for _ in range(NTB)]
            kk = [Trk() for _ in range(NTB)]
            wb, kw = win.get()
            fw.dma("pool", wb[:, :, 0:128], w_d[:, :, hp * 128:(hp + 1) * 128], writes=[kw])
            fw.dma("pool", wb[:, :, 128:256], w_d[:, :, 512 + hp * 128:512 + (hp + 1) * 128], writes=[kw])
            for which, kd in ((0, kq), (1, kk)):
                for tb in range(NTB):
                    ps, kp = self.psum.get()
                    for kc in range(8):
                        self.mm(ps, wb[:, kc, which * 128:(which + 1) * 128], self.xT[:, kc, tb * 512:(tb + 1) * 512], kc == 0, kc == 7,
                                [kw] + self.k_xT[tb * 4:(tb + 1) * 4], [kp])
                    o = qk[:, which, tb * 512:(tb + 1) * 512]
                    if which == 1:
                        self.copy(self.alt(), o, ps, [kp], [kd[tb]])
                    elif self.alt() == "act":
                        fw.op("act", lambda g: g.activation(out=o, in_=ps, func=AF.Copy, scale=0.125), [kp], [kd[tb]])
                    else:
                        fw.op("dve", lambda g: g.tensor_scalar(out=o, in0=ps, scalar1=0.125, scalar2=None, op0=ALU.mult), [kp], [kd[tb]])
            return qk, kq, kk
        wb, kw = win.get()
        fw.dma("pool", wb[:, :, 0:520], w_d[:, :, 1024:1544], writes=[kw])
        for tt in range(NTT):
            ps, kp = self.psum.get()
            for kc in range(8):
                self.mm(ps, self.xT[:, kc, tt * 128:(tt + 1) * 128], wb[:, kc, 0:512], kc == 0, kc == 7, [kw, self.k_xT[tt]], [kp])
            ps2, kp2 = self.psum.get()
            for kc in range(8):
                self.mm(ps2[:, 0:8], self.xT[:, kc, tt * 128:(tt + 1) * 128], wb[:, kc, 512:520], kc == 0, kc == 7, [kw, self.k_xT[tt]], [kp2])
            self.copy(self.alt(), vaug[:, tt, :, 0:64], ps.rearrange("p (h d) -> p h d", h=8), [kp], [k_v[tt]])
            self.copy(self.alt(), FL[:, tt, :], ps2[:, 0:8], [kp2], [k_FL])
        fb = A.sb("fb", [128, 8], F32)
        k_t = Trk()
        fw.dma("sp", fb, I["fox_f_bias"][l:l + 1, :].partition_broadcast(128), writes=[k_t])
        nls = A.sb("nls", [128, 32, 8], F32)
        fw.op("dve", lambda g: g.tensor_tensor(out=nls, in0=FL, in1=fb.unsqueeze(1).to_broadcast([128, 32, 8]), op=ALU.add), [k_FL, k_t], [k_t])
        fw.op("act", lambda g: g.activation(out=nls, in_=nls, func=AF.Exp, scale=-1.0), [k_t], [k_t])
        fw.op("act", lambda g: g.activation(out=nls, in_=nls, func=AF.Ln, bias=1.0), [k_t], [k_t])
        nlsf = nls.rearrange("p j h -> p (j h)")
        ps, kp = self.psum.get()
        self.mm(ps[:, 0:256], self.utri_f, nlsf, True, True, [kc_, k_t], [kp])
        ps2, kp2 = self.psum.get()
        self.mm(ps2[:, 0:256], self.ones_f, nlsf, True, True, [kc_, k_t], [kp2])
        totT = A.sb("totT", [128, 8, 32], F32)
        pin = A.sb("pin", [128, 8, 32], F32)
        cumk = A.sb("cumk", [128, 8, 32], F32)
        refp = A.sb("refp", [128, 8, 32], F32)
        k_c = Trk()
        fw.op("dve", lambda g: g.tensor_copy(out=totT, in_=ps2[:, 0:256].rearrange("p (j h) -> p h j", h=8)), [kp2], [k_c])
        fw.op("dve", lambda g: g.tensor_tensor_scan(out=pin.rearrange("p h j -> p (h j)"), data0=self.scanmask.rearrange("p h j -> p (h j)"),
                                                    data1=totT.rearrange("p h j -> p (h j)"), initial=0.0, op0=ALU.mult, op1=ALU.add),
              [k_c, kc_], [k_c])
        fw.op("dve", lambda g: g.tensor_tensor(out=cumk, in0=pin, in1=totT, op=ALU.subtract), [k_c], [k_c])
        fw.op("dve", lambda g: g.tensor_tensor(out=cumk, in0=cumk, in1=ps[:, 0:256].rearrange("p (j h) -> p h j", h=8), op=ALU.add), [k_c, kp], [k_c])
        ps3, kp3 = self.psum.get()
        self.mm(ps3[:, 0:256], self.mean_f, cumk.rearrange("p h j -> p (h j)"), True, True, [kc_, k_c], [kp3])
        fw.op("dve", lambda g: g.tensor_copy(out=refp.rearrange("p h j -> p (h j)"), in_=ps3[:, 0:256]), [kp3], [k_c])
        dsm = A.sb("dsm", [128, 8, 32], F32)
        hs_ = A.sb("hs_", [128, 8, 32], BF16)
        ls_ = A.sb("ls_", [128, 8, 32], BF16)
        v4 = lambda t: t.rearrange("p h (i s) -> p h i s", s=4)
        fw.op("dve", lambda g: g.tensor_tensor(out=v4(dsm), in0=v4(refp)[:, :, :, 0:1].to_broadcast([128, 8, 8, 4]), in1=v4(refp),
                                               op=ALU.subtract), [k_c], [k_c])
        fw.op("dve", lambda g: g.tensor_copy(out=hs_, in_=dsm), [k_c], [k_c])
        fw.op("dve", lambda g: g.tensor_tensor(out=dsm, in0=dsm, in1=hs_, op=ALU.subtract), [k_c], [k_c])
        fw.op("dve", lambda g: g.tensor_copy(out=ls_, in_=dsm), [k_c], [k_c])
        refhl = A.sb("refhl", [128, L_SEQ], BF16)
        k_rh = Trk()
        fw.op("pool", lambda g: g.memset(refhl, 0.0), writes=[k_rh])
        biasr = A.ring("biasT", 4, [128, 32], F32)
        ptr = A.ring("pt", 4, [128, 512], BF16)
        rcp = A.ring("rcp", 2, [128, 512], F32)
        ysb = A.ring("ysb", 2, [64, 512], F32)
        ybf = A.ring("ybf", 2, [64, 512], BF16)
        for hp in range(4):
            qk, k_q, k_k = proj_qk(hp)
            qT = qk[:, 0, :]
            kT = qk[:, 1, :]
            for hh in range(2):
                h = 2 * hp + hh
                r0 = hh * 64
                fw.op("pool", lambda g: g.tensor_copy(out=refhl[r0:r0 + 1, :].rearrange("p (j q) -> p j q", q=128),
                                                      in_=hs_[r0:r0 + 1, h, :].unsqueeze(2).to_broadcast([1, 32, 128])), [k_c, k_rh], [k_rh])
                fw.op("pool", lambda g: g.tensor_copy(out=refhl[r0 + 32:r0 + 33, :].rearrange("p (j q) -> p j q", q=128),
                                                      in_=ls_[r0 + 32:r0 + 33, h, :].unsqueeze(2).to_broadcast([1, 32, 128])), [k_c, k_rh], [k_rh])
            for i in range(NTB):
                for hh in range(2):
                    h = 2 * hp + hh
                    pr = slice(hh * 64, (hh + 1) * 64)
                    bt, kb = biasr.get()
                    fw.op("dve", lambda g: g.tensor_scalar(out=bt, in0=cumk[:, h, :], scalar1=refp[:, h, 4 * i:4 * i + 1], scalar2=None,
                                                           op0=ALU.subtract), [k_c], [kb])
                    po, kpo = self.psacc.get()
                    nj = 4 * i + 4

                    def emitS(j):
                        c0 = max(0, j - 4 * i) * 128
                        ps, kp = self.psum.get()
                        self.mm(ps[:, c0:512], kT[pr, j * 128:(j + 1) * 128], qT[pr, i * 512 + c0:(i + 1) * 512], True, False,
                                [k_k[j // 4], k_q[i]], [kp])
                        rs = slice(hh * 64, hh * 64 + 33)
                        self.mm(ps[:, c0:512], self.ones33[rs, :], refhl[rs, i * 512 + c0:(i + 1) * 512], False, True, [kc_, k_rh], [kp])
                        return ps, kp, c0
                    cur = emitS(0)
                    for j in range(nj):
                        nxt = emitS(j + 1) if j + 1 < nj else None
                        ps, kp, c0 = cur
                        pt, kpt = ptr.get()
                        fw.op("act", lambda g: g.activation(out=pt[:, c0:512], in_=ps[:, c0:512], func=AF.Exp, bias=bt[:, j:j + 1]), [kp, kb], [kpt])
                        if j >= 4 * i:
                            fw.op("pool", lambda g: g.tensor_tensor(out=pt[:, c0:c0 + 128], in0=pt[:, c0:c0 + 128], in1=self.trib, op=ALU.mult),
                                  [kpt, kc_], [kpt])
                        self.mm(po[0:65, c0:512], vaug[:, j, h, :], pt[:, c0:512], j == 0, j == nj - 1, [k_v[j], kpt], [kpo])
                        cur = nxt
                    rc, krc = rcp.get()
                    fw.op("dve", lambda g: g.reciprocal(out=rc[64:65, :], in_=po[64:65, :]), [kpo], [krc])
                    pb, kpb = self.psum.get()
                    self.mm(pb[0:64, :], self.ones_f[64:65, 0:64], rc[64:65, :], True, True, [kc_, krc], [kpb])
                    ys, kys = ysb.get()
                    self.copy("act", ys, po[0:64, :], [kpo], [kys])
                    yb, kyb = ybf.get()
                    fw.op("dve", lambda g: g.tensor_tensor(out=yb, in0=ys, in1=pb[0:64, :], op=ALU.mult), [kys, kpb], [kyb])
                    fw.dma("sp", self.yT[0, h * 64:(h + 1) * 64, i * 512:(i + 1) * 512], yb, reads=[kyb], writes=[self.k_yT[0][i]])
        A.close()


    def ssd(self, l):
        fw = self.fw
        I = self.inp
        A = Arena(fw)
        kc_ = self.k_ident
        w_d = I["w_in"][l].rearrange("(kc p) n -> p kc n", p=128)
        XBC = A.sb("xbc", [128, 6, 3 + L_SEQ], BF16)
        k_xbc = [[Trk() for _ in range(NTB)] for _ in range(6)]
        k_pad = Trk()
        fw.op("pool", lambda g: g.memset(XBC[:, :, 0:3], 0.0), writes=[k_pad])
        DT = A.sb("DT", [128, 32, 8], F32)
        k_DT = Trk()
        A2 = Arena(fw)
        win = A2.ring("win", 2, [128, 8, 512], BF16)
        wb, kw = win.get()
        fw.dma("pool", wb[:, :, 0:512], w_d[:, :, 1544:2056], writes=[kw])
        zst = A2.ring("zst", 2, [128, 512], F32)
        for tt in range(NTT):
            ps, kp = self.psum.get()
            for kc in range(8):
                self.mm(ps, self.xT[:, kc, tt * 128:(tt + 1) * 128], wb[:, kc, 0:512], kc == 0, kc == 7, [kw, self.k_xT[tt]], [kp])
            zs, kzs = zst.get()
            self.copy(self.alt(), zs, ps, [kp], [kzs])
            fw.dma("sp", self.zd[tt * 128:(tt + 1) * 128, :], zs, reads=[kzs], writes=[self.k_zd[tt]])
        for gi, (c0, nct, ctb, ncols) in enumerate(((2056, 4, 0, 512), (2568, 2, 4, 264))):
            wb, kw = win.get()
            fw.dma("pool", wb[:, :, 0:ncols], w_d[:, :, c0:c0 + ncols], writes=[kw])
            for m in range(nct):
                ct = ctb + m
                for tb in range(NTB):
                    ps, kp = self.psum.get()
                    for kc in range(8):
                        self.mm(ps, wb[:, kc, m * 128:(m + 1) * 128], self.xT[:, kc, tb * 512:(tb + 1) * 512], kc == 0, kc == 7,
                                [kw] + self.k_xT[tb * 4:(tb + 1) * 4], [kp])
                    self.copy(self.alt(), XBC[:, ct, 3 + tb * 512:3 + (tb + 1) * 512], ps, [kp], [k_xbc[ct][tb]])
            if gi == 1:
                for tt in range(NTT):
                    ps2, kp2 = self.psum.get()
                    for kc in range(8):
                        self.mm(ps2[:, 0:8], self.xT[:, kc, tt * 128:(tt + 1) * 128], wb[:, kc, 256:264], kc == 0, kc == 7, [kw, self.k_xT[tt]], [kp2])
                    self.copy(self.alt(), DT[:, tt, :], ps2[:, 0:8], [kp2], [k_DT])
        A2.close()
        cw = A.sb("cw", [128, 6, 4], F32)
        cb = A.sb("cb", [128, 6], F32)
        k_cp = Trk()
        fw.dma("sp", cw, I["ssd_cw"][l], writes=[k_cp])
        fw.dma("sp", cb, I["ssd_cb"][l], writes=[k_cp])
        dgc = A.sb("dgc", [128, 6, 4, 128], BF16)
        for ct in range(6):
            for t in range(4):
                fw.op("pool", lambda g: g.tensor_scalar(out=dgc[:, ct, t, :], in0=self.identf, scalar1=cw[:, ct, t:t + 1], scalar2=0.0,
                                                        op0=ALU.mult, op1=ALU.add), [kc_, k_cp], [k_cp])
        cbr = A.sb("cbr", [1, 768], F32)
        cbh = A.sb("cbh", [1, 768], BF16)
        cbl = A.sb("cbl", [1, 768], BF16)
        cbt = cbr
        fw.dma("sp", cbr, I["ssd_conv_b"][l:l + 1, :], writes=[k_cp])
        fw.op("dve", lambda g: g.tensor_copy(out=cbh, in_=cbr), [k_cp], [k_cp])
        fw.op("dve", lambda g: g.tensor_tensor(out=cbt, in0=cbr, in1=cbh, op=ALU.subtract), [k_cp], [k_cp])
        fw.op("dve", lambda g: g.tensor_copy(out=cbl, in_=cbt), [k_cp], [k_cp])
        BT = A.sb("BT", [128, 2, L_SEQ], BF16)
        CT = A.sb("CT", [128, L_SEQ], BF16)
        k_BT = [Trk() for _ in range(NTB)]
        k_CT = [Trk() for _ in range(NTB)]
        fw.op("pool", lambda g: g.memset(BT[64:128, 0, :], 0.0), writes=k_BT)
        fw.op("pool", lambda g: g.memset(BT[0:64, 1, :], 0.0), writes=k_BT)
        for ct in (4, 5):
            for tb in range(NTB):
                ps, kp = self.psum.get()
                rd = [k_cp, k_pad, k_xbc[ct][tb]] + ([k_xbc[ct][tb - 1]] if tb else [])
                for t in range(4):
                    self.mm(ps, dgc[:, ct, t, :], XBC[:, ct, tb * 512 + t:tb * 512 + t + 512], t == 0, t == 3, rd, [kp])
                ts_ = slice(tb * 512, (tb + 1) * 512)
                if ct == 5:
                    fw.op("act", lambda g: g.activation(out=CT[:, ts_], in_=ps, func=AF.Silu, bias=cb[:, ct:ct + 1]), [kp, k_cp], [k_CT[tb]])
                else:
                    fw.op("act", lambda g: g.activation(out=BT[0:64, 0, ts_], in_=ps[0:64, :], func=AF.Silu, bias=cb[0:64, ct:ct + 1]), [kp, k_cp], [k_BT[tb]])
                    fw.op("act", lambda g: g.activation(out=BT[64:128, 1, ts_], in_=ps[64:128, :], func=AF.Silu, bias=cb[64:128, ct:ct + 1]), [kp, k_cp], [k_BT[tb]])
        if self.dbg.get('ssd_stop') == 2:
            A.close()
            return
        dtb = A.sb("dtb", [128, 8], F32)
        alog = A.sb("alog", [128, 8], F32)
        dsk = A.sb("dsk", [128, 8], F32)
        nw = A.sb("nw", [128, 512], F32)
        k_p = Trk()
        fw.dma("sp", dtb, I["ssd_dt_bias"][l:l + 1, :].partition_broadcast(128), writes=[k_p])
        fw.dma("sp", alog, I["ssd_a_log"][l:l + 1, :].partition_broadcast(128), writes=[k_p])
        fw.dma("sp", dsk, I["ssd_d"][l:l + 1, :].partition_broadcast(128), writes=[k_p])
        fw.dma("sp", nw, I["ssd_norm_w"][l:l + 1, :].partition_broadcast(128), writes=[k_p])
        fw.op("act", lambda g: g.activation(out=alog, in_=alog, func=AF.Exp), [k_p], [k_p])
        fw.op("dve", lambda g: g.tensor_scalar(out=alog, in0=alog, scalar1=-1.0, scalar2=None, op0=ALU.mult), [k_p], [k_p])
        dt = A.sb("dt", [128, 32, 8], F32)
        adt = A.sb("adt", [128, 32, 8], F32)
        acs = A.sb("acs", [128, 32, 8], F32)
        eacs = A.sb("eacs", [128, 32, 8], F32)
        dtdec = A.sb("dtdec", [128, 32, 8], F32)
        eatot = A.sb("eatot", [128, 32, 8], F32)
        esel = A.sb("esel", [128, 32, 4], F32)
        k_d = Trk()
        fw.op("dve", lambda g: g.tensor_tensor(out=dt, in0=DT, in1=dtb.unsqueeze(1).to_broadcast([128, 32, 8]), op=ALU.add), [k_DT, k_p], [k_d])
        fw.op("act", lambda g: g.activation(out=dt, in_=dt, func=AF.Exp), [k_d], [k_d])
        fw.op("act", lambda g: g.activation(out=dt, in_=dt, func=AF.Ln, bias=1.0), [k_d], [k_d])
        fw.op("dve", lambda g: g.tensor_tensor(out=adt, in0=dt, in1=alog.unsqueeze(1).to_broadcast([128, 32, 8]), op=ALU.mult), [k_d, k_p], [k_d])
        fl = lambda t: t.rearrange("p c h -> p (c h)")
        ps, kp = self.psum.get()
        self.mm(ps[:, 0:256], self.utri_f, fl(adt), True, True, [kc_, k_d], [kp])
        ps2, kp2 = self.psum.get()
        self.mm(ps2[:, 0:256], self.ones_f, fl(adt), True, True, [kc_, k_d], [kp2])
        fw.op("dve", lambda g: g.tensor_copy(out=fl(acs), in_=ps[:, 0:256]), [kp], [k_d])
        fw.op("act", lambda g: g.activation(out=fl(eacs), in_=ps[:, 0:256], func=AF.Exp), [kp], [k_d])
        fw.op("dve", lambda g: g.tensor_tensor(out=fl(dtdec), in0=ps2[:, 0:256], in1=fl(acs), op=ALU.subtract), [kp2, k_d], [k_d])
        fw.op("act", lambda g: g.activation(out=dtdec, in_=dtdec, func=AF.Exp), [k_d], [k_d])
        fw.op("dve", lambda g: g.tensor_tensor(out=dtdec, in0=dtdec, in1=dt, op=ALU.mult), [k_d], [k_d])
        fw.op("act", lambda g: g.activation(out=fl(eatot), in_=ps2[:, 0:256], func=AF.Exp), [kp2], [k_d])
        fw.op("dve", lambda g: g.tensor_copy(out=esel[0:64], in_=eatot[0:64, :, 0:4]), [k_d], [k_d])
        fw.op("dve", lambda g: g.tensor_copy(out=esel[64:128], in_=eatot[64:128, :, 4:8]), [k_d], [k_d])
        if self.dbg.get('ssd_stop') == 3:
            A.close()
            return
        xsr = A.ring("xs", 2, [128, 8, 64], F32)
        bpr = A.ring("bp", 2, [128, 2, 128], BF16)
        for b_ in bpr.b:
            fw.op("pool", lambda g: g.memset(b_, 0.0), writes=bpr.k)
        xdtr = A.ring("xdt", 2, [128, 8, 64], BF16)
        xddr = A.ring("xdd", 2, [128, 8, 64], BF16)
        Ar = A.ring("Aall", 1, [128, 8, 128], F32)
        Dr = A.ring("Dg", 1, [128, 4, 128], F32)
        Mr = A.ring("Mg", 2, [128, 4, 128], BF16)
        S = A.sb("S", [128, 4, 64], F32)
        k_S = Trk()
        fw.op("pool", lambda g: g.memset(S, 0.0), writes=[k_S])
        Sbr = A.ring("Sb", 2, [128, 2, 4, 64], BF16)
        for b_ in Sbr.b:
            fw.op("pool", lambda g: g.memset(b_, 0.0), writes=Sbr.k)
        yr = A.ring("y", 2, [128, 8, 64], F32)
        tmr = A.ring("tm", 1, [128, 8, 64], F32)
        ztr = A.ring("zt", 2, [128, 512], F32)
        ssr = A.ring("ss", 2, [128, 4], F32)
        ybr = A.ring("yb", 2, [128, 512], BF16)
        ytr = A.ring("yt", 2, [128, 4, 128], BF16)
        Sb_prev = None
        for c in range(self.dbg.get('ssd_nchunk', NTT)):
            tb = c // 4
            rdx = lambda ct: [k_cp, k_pad, k_xbc[ct][tb]] + ([k_xbc[ct][tb - 1]] if (tb and c % 4 == 0) else [])
            ps, kp = self.psum.get()
            for ct in range(4):
                o = ps[:, ct * 128:(ct + 1) * 128]
                for t in range(4):
                    self.mm(o, XBC[:, ct, c * 128 + t:c * 128 + t + 128], dgc[:, ct, t, :], t == 0, False, rdx(ct), [kp])
                self.mm(o, self.ones_b[0:1, :], cbh[0:1, ct * 128:(ct + 1) * 128], False, False, [kc_, k_cp], [kp])
                self.mm(o, self.ones_b[0:1, :], cbl[0:1, ct * 128:(ct + 1) * 128], False, True, [kc_, k_cp], [kp])
            psB, kpB = self.psum.get()
            o = psB[:, 0:128]
            for t in range(4):
                self.mm(o, XBC[:, 4, c * 128 + t:c * 128 + t + 128], dgc[:, 4, t, :], t == 0, False, rdx(4), [kpB])
            self.mm(o, self.ones_b[0:1, :], cbh[0:1, 512:640], False, False, [kc_, k_cp], [kpB])
            self.mm(o, self.ones_b[0:1, :], cbl[0:1, 512:640], False, True, [kc_, k_cp], [kpB])
            xs, kxs = xsr.get()
            fw.op("act", lambda g: g.activation(out=xs.rearrange("p h d -> p (h d)"), in_=ps, func=AF.Silu), [kp], [kxs])
            bp, kbp = bpr.get()
            fw.op("act", lambda g: g.activation(out=bp[:, 0, 0:64], in_=psB[:, 0:64], func=AF.Silu), [kpB], [kbp])
            fw.op("act", lambda g: g.activation(out=bp[:, 1, 64:128], in_=psB[:, 64:128], func=AF.Silu), [kpB], [kbp])
            if self.dbg.get('ssd_stop') == 4:
                continue
            xdt, kxdt = xdtr.get()
            xdd, kxdd = xddr.get()
            fw.op("dve", lambda g: g.tensor_tensor(out=xdt, in0=xs, in1=dt[:, c, :].unsqueeze(2).to_broadcast([128, 8, 64]), op=ALU.mult), [kxs, k_d], [kxdt])
            fw.op("pool", lambda g: g.tensor_tensor(out=xdd, in0=xs, in1=dtdec[:, c, :].unsqueeze(2).to_broadcast([128, 8, 64]), op=ALU.mult), [kxs, k_d], [kxdd])
            if self.dbg.get('ssd_stop') == 5:
                continue
            psG, kpG = self.psum.get()
            for g_ in range(self.dbg.get('ssd_ng', 2)):
                self.mm(psG[:, g_ * 128:(g_ + 1) * 128], BT[:, g_, c * 128:(c + 1) * 128], CT[:, c * 128:(c + 1) * 128], True, True,
                        [k_BT[tb], k_CT[tb]], [kpG])
            sub = self.dbg.get('ssd_sub', 99)
            if sub < 1:
                continue
            Aa, kA = Ar.get()
            fw.op("pool", lambda g: g.tensor_tensor(out=Aa, in0=self.ones_f.unsqueeze(1).to_broadcast([128, 8, 128]),
                                                    in1=adt[:, c, :].unsqueeze(2).to_broadcast([128, 8, 128]), op=ALU.mult), [kc_, k_d], [kA])
            if sub < 2:
                continue
            yd, kyd = self.psacc.get()
            for g_ in range(2):
                psS, kpS = self.psum.get()
                for hh in range(4):
                    o_ = psS[:, hh * 128:(hh + 1) * 128]
                    self.mm(o_, Aa[:, 4 * g_ + hh, :], self.utri_f, True, False, [kA, kc_], [kpS])
                    self.mm(o_, self.nutri_f, Aa[:, 4 * g_ + hh, :], False, False, [kA, kc_], [kpS])
                    self.mm(o_, self.identf, self.negmask4[:, 0, :], False, True, [kc_], [kpS])
                if sub < 3:
                    continue
                Dg, kD = Dr.get()
                fw.op("act", lambda g: g.activation(out=Dg.rearrange("p h l -> p (h l)"), in_=psS, func=AF.Exp), [kpS], [kD])
                if sub < 4:
                    continue
                Mg, kM = Mr.get()
                fw.op("dve", lambda g: g.tensor_tensor(out=Mg, in0=Dg, in1=psG[:, g_ * 128:(g_ + 1) * 128].unsqueeze(1).to_broadcast([128, 4, 128]),
                                                       op=ALU.mult), [kD, kpG], [kM])
                if sub < 5:
                    continue
                for hh in range(4):
                    h = 4 * g_ + hh
                    self.mm(yd[:, h * 64:(h + 1) * 64], Mg[:, hh, :], xdt[:, h, :], True, True, [kM, kxdt], [kyd])
            if sub < 99:
                continue
            if self.dbg.get('ssd_stop') == 6:
                continue
            pst, kst = self.psum.get()
            self.mm(pst[:, 0:256], bp[:, 0, :], xdd[:, 0:4, :].rearrange("p h d -> p (h d)"), True, False, [kbp, kxdd], [kst])
            self.mm(pst[:, 0:256], bp[:, 1, :], xdd[:, 4:8, :].rearrange("p h d -> p (h d)"), False, True, [kbp, kxdd], [kst])
            y, ky = yr.get()
            yf = y.rearrange("p h d -> p (h d)")
            if c > 0:
                Sb, kSb = Sb_prev
                yo, kyo = self.psum.get()
                for g_ in range(2):
                    self.mm(yo[:, g_ * 256:(g_ + 1) * 256], CT[:, c * 128:(c + 1) * 128], Sb[:, g_].rearrange("p h d -> p (h d)"), True, True,
                            [k_CT[tb], kSb], [kyo])
                fw.op("dve", lambda g: g.tensor_tensor(out=y, in0=yo.rearrange("p (h d) -> p h d", h=8),
                                                       in1=eacs[:, c, :].unsqueeze(2).to_broadcast([128, 8, 64]), op=ALU.mult), [kyo, k_d], [ky])
                fw.op("dve", lambda g: g.tensor_tensor(out=yf, in0=yf, in1=yd, op=ALU.add), [ky, kyd], [ky])
            else:
                self.copy("dve", yf, yd, [kyd], [ky])
            fw.op("dve", lambda g: g.tensor_tensor(out=S, in0=S, in1=esel[:, c, :].unsqueeze(2).to_broadcast([128, 4, 64]), op=ALU.mult), [k_S, k_d], [k_S])
            fw.op("dve", lambda g: g.tensor_tensor(out=S.rearrange("p h d -> p (h d)"), in0=S.rearrange("p h d -> p (h d)"), in1=pst[:, 0:256], op=ALU.add),
                  [k_S, kst], [k_S])
            Sb, kSb = Sbr.get()
            self.copy("pool", Sb[0:64, 0], S[0:64], [k_S], [kSb])
            self.copy("pool", Sb[64:128, 1], S[64:128], [k_S], [kSb])
            Sb_prev = (Sb, kSb)
            if self.dbg.get('ssd_stop') == 7:
                continue
            tm, ktm = tmr.get()
            fw.op("pool", lambda g: g.tensor_tensor(out=tm, in0=xs, in1=dsk.unsqueeze(2).to_broadcast([128, 8, 64]), op=ALU.mult), [kxs, k_p], [ktm])
            fw.op("pool", lambda g: g.tensor_tensor(out=y, in0=y, in1=tm, op=ALU.add), [ky, ktm], [ky])
            zt, kzt = ztr.get()
            fw.dma("sp", zt, self.zd[c * 128:(c + 1) * 128, :], reads=[self.k_zd[c]], writes=[kzt])
            fw.op("act", lambda g: g.activation(out=zt, in_=zt, func=AF.Silu), [kzt], [kzt])
            fw.op("dve", lambda g: g.tensor_tensor(out=yf, in0=yf, in1=zt, op=ALU.mult), [ky, kzt], [ky])
            if self.dbg.get('ssd_stop') == 8:
                continue
            ss, kss = ssr.get()
            tmf = tm.rearrange("p h d -> p (h d)")
            for g_ in range(2):
                fw.op("act", lambda g: g.activation(out=tmf[:, g_ * 256:(g_ + 1) * 256], in_=yf[:, g_ * 256:(g_ + 1) * 256], func=AF.Square,
                                                    accum_out=ss[:, g_:g_ + 1]), [ky, ktm], [ktm, kss])
            fw.op("dve", lambda g: g.tensor_scalar(out=ss[:, 0:2], in0=ss[:, 0:2], scalar1=1.0 / 256.0, scalar2=1e-5, op0=ALU.mult, op1=ALU.add), [kss], [kss])
            fw.op("act", lambda g: g.activation(out=ss[:, 0:2], in_=ss[:, 0:2], func=AF.Sqrt), [kss], [kss])
            fw.op("dve", lambda g: g.reciprocal(out=ss[:, 2:4], in_=ss[:, 0:2]), [kss], [kss])
            for g_ in range(2):
                fw.op("dve", lambda g: g.tensor_scalar(out=yf[:, g_ * 256:(g_ + 1) * 256], in0=yf[:, g_ * 256:(g_ + 1) * 256],
                                                       scalar1=ss[:, 2 + g_:3 + g_], scalar2=None, op0=ALU.mult), [ky, kss], [ky])
            yb, kyb = ybr.get()
            fw.op("pool", lambda g: g.tensor_tensor(out=yb, in0=yf, in1=nw, op=ALU.mult), [ky, k_p], [kyb])
            if self.dbg.get('ssd_stop') == 9:
                continue
            pT, kpT = self.psum.get()
            pTb = pT.bitcast(BF16)
            for ct in range(4):
                fw.op("pe", lambda g: g.transpose(pTb[:, ct * 128:(ct + 1) * 128], yb[:, ct * 128:(ct + 1) * 128], self.identb), [kyb, kc_], [kpT])
            yt, kyt = ytr.get()
            self.copy("act", yt, pTb[:, 0:512].rearrange("p (c t) -> p c t", c=4), [kpT], [kyt])
            fw.dma("sp", self.yT[1].rearrange("(ct p) t -> p ct t", p=128)[:, :, c * 128:(c + 1) * 128], yt, reads=[kyt], writes=[self.k_yT[1][tb]])
        A.close()


    def cmul(self, e, ore, oim, are, aim, bre, bim, t1, t2, k, conj_b=False):
        fw = self.fw
        tt = lambda o, a, b, op: fw.op(e, lambda g: g.tensor_tensor(out=o, in0=a, in1=b, op=op), k, k)
        tt(t1, are, bre, ALU.mult)
        tt(t2, aim, bim, ALU.mult)
        tt(ore, t1, t2, ALU.add if conj_b else ALU.subtract)
        tt(t1, are, bim, ALU.mult)
        tt(t2, aim, bre, ALU.mult)
        if conj_b:
            tt(oim, t2, t1, ALU.subtract)
        else:
            tt(oim, t1, t2, ALU.add)

    def sincos(self, arg, osin, ocos, ki, kf, k):
        fw = self.fw
        TWO_PI = 2.0 * np.pi
        for r, shift in ((osin, 0.0), (ocos, np.pi / 2)):
            fw.op("dve", lambda g: g.tensor_scalar(out=kf, in0=arg, scalar1=float(shift), scalar2=float(1.0 / TWO_PI), op0=ALU.add, op1=ALU.mult), k, k)
            fw.op("dve", lambda g: g.tensor_copy(out=ki, in_=kf), k, k)
            fw.op("dve", lambda g: g.tensor_copy(out=kf, in_=ki), k, k)
            fw.op("dve", lambda g: g.tensor_scalar(out=kf, in0=kf, scalar1=float(-TWO_PI), scalar2=float(shift), op0=ALU.mult, op1=ALU.add), k, k)
            fw.op("dve", lambda g: g.tensor_tensor(out=r, in0=kf, in1=arg, op=ALU.add), k, k)
            fw.op("dve", lambda g: g.tensor_scalar(out=r, in0=r, scalar1=3.141592, scalar2=-3.141592, op0=ALU.min, op1=ALU.max), k, k)
            fw.op("act", lambda g: g.activation(out=r, in_=r, func=AF.Sin), k, k)

    def s5(self, l):
        fw = self.fw
        I = self.inp
        A = Arena(fw)
        kc_ = self.k_ident
        w_d = I["w_in"][l].rearrange("(kc p) n -> p kc n", p=128)
        U = A.sb("U", [128, 32, 32, 16], BF16)
        k_U = [Trk() for _ in range(32)]
        k_p = Trk()
        kp_ = [k_p]
        T = lambda nm, shp=(128, 16): A.sb(nm, list(shp), F32)
        abr_keep = {}
        bbr, bbi = T("bbr", (128, 16, 16)), T("bbi", (128, 16, 16))
        cre, cim = T("cre", (128, 16, 16)), T("cim", (128, 16, 16))
        pwr_, pwi_ = T("pwr_", (128, 16, 65)), T("pwi_", (128, 16, 65))
        R = T("R")
        cosT = A.sb("cosT", [128, 16, 128], F32)
        sinT = A.sb("sinT", [128, 16, 128], F32)
        drep = T("drep", (128, 512))
        A2 = Arena(fw)
        wb = A2.sb("wu", [128, 8, 512], BF16)
        kw = Trk()
        fw.dma("pool", wb, w_d[:, :, 2832:3344], writes=[kw])
        for tau in range(32):
            ps, kp = self.psum.get()
            for kc in range(8):
                lhsT = self.xT[:, kc, :].rearrange("p (c t) -> p t c", t=32)[:, tau, :]
                self.mm(ps, lhsT, wb[:, kc, :], kc == 0, kc == 7, [kw] + self.k_xT, [kp])
            self.copy(self.alt(), U[:, :, tau, :], ps.rearrange("p (g h) -> p g h", h=16), [kp], [k_U[tau]])
        T2 = lambda nm, shp=(128, 16): A2.sb(nm, list(shp), F32)

        def ld(t, src):
            fw.dma("sp", t, src, writes=kp_)
            return t
        are, aim, lst = ld(T2("are"), I["s5_are"][l]), ld(T2("aim"), I["s5_aim"][l]), ld(T2("lst"), I["s5_lst"][l])
        bre, bim = ld(T2("bre", (128, 16, 16)), I["s5_bre"][l]), ld(T2("bim", (128, 16, 16)), I["s5_bim"][l])
        ld(cre, I["s5_cre"][l]); ld(cim, I["s5_cim"][l])
        ld(drep, I["s5_d"][l:l + 1, :].partition_broadcast(128))
        step, lre, den, t1, t2, xr_, th, mag = [T2(n) for n in ("step", "lre", "den", "t1", "t2", "xr_", "th", "mag")]
        sn, cs, abr, abi, nre, kre, kim = [T2(n) for n in ("sn", "cs", "abr", "abi", "nre", "kre", "kim")]
        ki16, kf16 = A2.sb("ki16", [128, 16], I32), T2("kf16")
        V = lambda fn: fw.op("dve", fn, kp_, kp_)
        fw.op("act", lambda g: g.activation(out=step, in_=lst, func=AF.Exp), kp_, kp_)
        V(lambda g: g.tensor_scalar(out=lre, in0=are, scalar1=-1e-4, scalar2=None, op0=ALU.min))
        V(lambda g: g.tensor_tensor(out=xr_, in0=lre, in1=step, op=ALU.mult))
        V(lambda g: g.tensor_tensor(out=th, in0=aim, in1=step, op=ALU.mult))
        fw.op("act", lambda g: g.activation(out=mag, in_=xr_, func=AF.Exp), kp_, kp_)
        self.sincos(th, sn, cs, ki16, kf16, kp_)
        V(lambda g: g.tensor_tensor(out=abr, in0=mag, in1=cs, op=ALU.mult))
        V(lambda g: g.tensor_tensor(out=abi, in0=mag, in1=sn, op=ALU.mult))
        V(lambda g: g.tensor_tensor(out=t1, in0=lre, in1=lre, op=ALU.mult))
        V(lambda g: g.tensor_tensor(out=t2, in0=aim, in1=aim, op=ALU.mult))
        V(lambda g: g.tensor_tensor(out=den, in0=t1, in1=t2, op=ALU.add))
        V(lambda g: g.reciprocal(out=den, in_=den))
        V(lambda g: g.tensor_scalar(out=nre, in0=abr, scalar1=-1.0, scalar2=None, op0=ALU.add))
        self.cmul("dve", kre, kim, nre, abi, lre, aim, t1, t2, kp_, conj_b=True)
        V(lambda g: g.tensor_tensor(out=kre, in0=kre, in1=den, op=ALU.mult))
        V(lambda g: g.tensor_tensor(out=kim, in0=kim, in1=den, op=ALU.mult))
        sh3 = [128, 16, 16]
        t3a, t3b = T2("t3a", sh3), T2("t3b", sh3)
        bc3 = lambda t: t.unsqueeze(2).to_broadcast(sh3)
        self.cmul("dve", bbr, bbi, bc3(kre), bc3(kim), bre, bim, t3a, t3b, kp_)
        shp = [128, 16, 65]
        ioti = A2.sb("ioti", [128, 65], I32)
        iot = T2("iot", (128, 65))
        fw.op("pool", lambda g: g.iota(ioti[:, 0:33], pattern=[[1, 33]], base=0, channel_multiplier=0), kp_, kp_)
        fw.op("pool", lambda g: g.iota(ioti[:, 33:65], pattern=[[-1, 32]], base=31, channel_multiplier=0), kp_, kp_)
        V(lambda g: g.tensor_copy(out=iot, in_=ioti))
        marg, parg, psn, pcs, kf65 = [T2(n, shp) for n in ("marg", "parg", "psn", "pcs", "kf65")]
        ki65 = A2.sb("ki65", shp, I32)
        bcm = lambda t: t.unsqueeze(2).to_broadcast(shp)
        bci = iot.unsqueeze(1).to_broadcast(shp)
        V(lambda g: g.tensor_tensor(out=marg, in0=bcm(xr_), in1=bci, op=ALU.mult))
        V(lambda g: g.tensor_tensor(out=parg, in0=bcm(th), in1=bci, op=ALU.mult))
        fw.op("act", lambda g: g.activation(out=marg, in_=marg, func=AF.Exp), kp_, kp_)
        self.sincos(parg, psn, pcs, ki65, kf65, kp_)
        V(lambda g: g.tensor_tensor(out=pwr_, in0=marg, in1=pcs, op=ALU.mult))
        V(lambda g: g.tensor_tensor(out=pwi_, in0=marg, in1=psn, op=ALU.mult))
        V(lambda g: g.tensor_copy(out=R, in_=marg[:, :, 32]))
        wre, wim, u1, u2 = [T2(n) for n in ("wre", "wim", "u1", "u2")]
        V(lambda g: g.tensor_copy(out=wre, in_=pcs[:, :, 32]))
        V(lambda g: g.tensor_copy(out=wim, in_=psn[:, :, 32]))
        fw.op("pool", lambda g: g.memset(cosT[:, :, 0:1], 1.0), kp_, kp_)
        fw.op("pool", lambda g: g.memset(sinT[:, :, 0:1], 0.0), kp_, kp_)
        tA, tB = T2("tA", (128, 16, 64)), T2("tB", (128, 16, 64))
        n_ = 1
        while n_ < 128:
            shn = [128, 16, n_]
            bw = lambda t: t.unsqueeze(2).to_broadcast(shn)
            self.cmul("dve", cosT[:, :, n_:2 * n_], sinT[:, :, n_:2 * n_], cosT[:, :, 0:n_], sinT[:, :, 0:n_], bw(wre), bw(wim),
                      tA[:, :, 0:n_], tB[:, :, 0:n_], kp_)
            self.cmul("dve", u1, u2, wre, wim, wre, wim, t1, t2, kp_)
            V(lambda g: g.tensor_copy(out=wre, in_=u1))
            V(lambda g: g.tensor_copy(out=wim, in_=u2))
            n_ *= 2
        A2.close()
        Zr = A.ring("Z", 2, [128, 2, 32, 16], BF16)
        ABr = A.ring("ABp", 2, [128, 8, 2, 128], BF16)
        for b_ in ABr.b:
            fw.op("pool", lambda g: g.memset(b_, 0.0), writes=ABr.k)
        CAr = A.ring("CAP", 2, [128, 2, 33, 16], BF16)
        BBr = A.ring("BBrep", 2, [128, 2, 2, 8, 16], BF16)
        for b_ in BBr.b:
            fw.op("pool", lambda g: g.memset(b_, 0.0), writes=BBr.k)
        UTr = A.ring("UT", 3, [128, 4, 128], BF16)
        TBf = A.ring("TBf", 1, [128, 512], F32)
        TBr = A.ring("TB", 2, [128, 512], BF16)
        gsc = A.ring("gsc", 2, [128, 4, 128], F32)
        Rbr = A.ring("Rb", 2, [128, 128], F32)
        Spr = A.ring("Sprev", 2, [128, 2, 2, 128], BF16)
        for b_ in Spr.b:
            fw.op("pool", lambda g: g.memset(b_, 0.0), writes=Spr.k)
        ypr = A.ring("ypre", 2, [128, 32, 16], F32)
        y2r = A.ring("y2", 2, [128, 32, 16], F32)
        GYr = A.ring("GY", 1, [128, 32, 128], BF16)
        gts = A.ring("gts", 1, [128, L_SEQ], BF16)
        z3, z4 = T("z3", (128, 32, 16)), T("z4", (128, 32, 16))
        cA, cB = T("cA", (128, 33, 16)), T("cB", (128, 33, 16))
        k_z = Trk()
        GY, kGY = None, None
        for gp in range(16):
            Z, kZ = Zr.get()
            shz = [128, 32, 16]
            pa = lambda t: t[:, gp, 33:65].unsqueeze(2).to_broadcast(shz)
            pb = lambda t: t[:, gp, :].unsqueeze(1).to_broadcast(shz)
            self.cmul("pool", Z[:, 0], Z[:, 1], pa(pwr_), pa(pwi_), pb(bbr), pb(bbi), z3, z4, [k_p, k_z, kZ])
            pT, kpT = self.psum.get()
            pTb = pT.bitcast(BF16)
            for a in range(4):
                for ri in range(2):
                    j = a * 2 + ri
                    fw.op("pe", lambda g: g.transpose(pTb[:, j * 128:(j + 1) * 128], Z[:, ri, 8 * a:8 * a + 8, :].rearrange("p t h -> p (t h)"),
                                                      self.identb), [kZ, kc_], [kpT])
            AB, kAB = ABr.get()
            src = pTb.rearrange("p (j m) -> p j m", j=8)
            self.copy("act", AB[:, :, 0, 0:64], src[:, :, 0:64], [kpT], [kAB])
            self.copy("act", AB[:, :, 1, 64:128], src[:, :, 64:128], [kpT], [kAB])
            CAP, kCA = CAr.get()
            shc = [128, 33, 16]
            pa2 = lambda t: t[:, gp, 0:33].unsqueeze(2).to_broadcast(shc)
            pc2 = lambda t: t[:, gp, :].unsqueeze(1).to_broadcast(shc)
            kz2 = [k_p, k_z]
            tt = lambda o, a_, b_, op, wr: fw.op("dve", lambda g: g.tensor_tensor(out=o, in0=a_, in1=b_, op=op), kz2 + wr, [k_z] + wr)
            tt(cA, pa2(pwr_), pc2(cre), ALU.mult, [])
            tt(cB, pa2(pwi_), pc2(cim), ALU.mult, [])
            tt(CAP[:, 0], cA, cB, ALU.subtract, [kCA])
            tt(cA, pa2(pwr_), pc2(cim), ALU.mult, [])
            tt(cB, pa2(pwi_), pc2(cre), ALU.mult, [])
            tt(cA, cA, cB, ALU.add, [])
            fw.op("dve", lambda g: g.tensor_scalar(out=CAP[:, 1], in0=cA, scalar1=-1.0, scalar2=None, op0=ALU.mult), [k_z, kCA], [kCA])
            BB, kBB = BBr.get()
            for (half, gl) in ((slice(0, 64), 0), (slice(64, 128), 1)):
                fw.op("pool", lambda g: g.tensor_copy(out=BB[half, gl, 0], in_=bbr[half, gp, :].unsqueeze(1).to_broadcast([64, 8, 16])), [k_p], [kBB])
                fw.op("pool", lambda g: g.tensor_copy(out=BB[half, gl, 1], in_=bbi[half, gp, :].unsqueeze(1).to_broadcast([64, 8, 16])), [k_p], [kBB])
            UTs = []
            for gl in range(2):
                g_ = 2 * gp + gl
                pU, kpU = self.psum.get()
                pUb = pU.bitcast(BF16)
                UT, kUT = UTr.get()
                UTs.append((UT, kUT))
                for a in range(4):
                    fw.op("pe", lambda g: g.transpose(pUb[:, a * 128:(a + 1) * 128], U[:, g_, 8 * a:8 * a + 8, :].rearrange("p t h -> p (t h)"), self.identb),
                          k_U[8 * a:8 * a + 8] + [kc_], [kpU])
                self.copy(self.alt(), UT, pUb[:, 0:512].rearrange("p (a c) -> p a c", a=4), [kpU], [kUT])
            pW, kpW = self.psum.get()
            for ri in range(2):
                o = pW[:, ri * 128:(ri + 1) * 128]
                n_mm = 0
                for gl in range(2):
                    UT, kUT = UTs[gl]
                    for a in range(4):
                        self.mm(o, AB[:, a * 2 + ri, gl, :], UT[:, a, :], n_mm == 0, n_mm == 7, [kAB, kUT], [kpW])
                        n_mm += 1
            gs, kgs = gsc.get()
            cT_, sT_ = cosT[:, gp, :], sinT[:, gp, :]
            W0, W1 = pW[:, 0:128], pW[:, 128:256]
            dv = lambda o, a_, b_, op: fw.op("dve", lambda g: g.tensor_tensor(out=o, in0=a_, in1=b_, op=op), [kpW, k_p, kgs], [kgs])
            dv(gs[:, 2], W0, cT_, ALU.mult); dv(gs[:, 3], W1, sT_, ALU.mult); dv(gs[:, 0], gs[:, 2], gs[:, 3], ALU.add)
            dv(gs[:, 2], W1, cT_, ALU.mult); dv(gs[:, 3], W0, sT_, ALU.mult); dv(gs[:, 1], gs[:, 2], gs[:, 3], ALU.subtract)
            Rb, kRb = Rbr.get()
            fw.op("pool", lambda g: g.tensor_scalar(out=Rb, in0=self.ones_f, scalar1=R[:, gp:gp + 1], scalar2=0.0, op0=ALU.mult, op1=ALU.add),
                  [kc_, k_p], [kRb])
            for ri in range(2):
                fw.op("dve", lambda g: g.tensor_tensor_scan(out=gs[:, 2 + ri], data0=Rb, data1=gs[:, ri], initial=0.0, op0=ALU.mult, op1=ALU.add),
                      [kgs, kRb], [kgs])
            Sp, kSp = Spr.get()
            n1 = slice(0, 127)
            dv2 = lambda o, a_, b_, op, wr: fw.op("dve", lambda g: g.tensor_tensor(out=o, in0=a_, in1=b_, op=op), [k_p, kgs] + wr, [kgs] + wr)
            for (half, gl) in ((slice(0, 64), 0), (slice(64, 128), 1)):
                dv2(gs[half, 0, n1], gs[half, 2, n1], cT_[half, n1], ALU.mult, [])
                dv2(gs[half, 1, n1], gs[half, 3, n1], sT_[half, n1], ALU.mult, [])
                dv2(Sp[half, gl, 0, 1:128], gs[half, 0, n1], gs[half, 1, n1], ALU.subtract, [kSp])
                dv2(gs[half, 0, n1], gs[half, 2, n1], sT_[half, n1], ALU.mult, [])
                dv2(gs[half, 1, n1], gs[half, 3, n1], cT_[half, n1], ALU.mult, [])
                dv2(Sp[half, gl, 1, 1:128], gs[half, 0, n1], gs[half, 1, n1], ALU.add, [kSp])
            for gl in range(2):
                g_ = 2 * gp + gl
                rows = slice(gl * 64, (gl + 1) * 64)
                UT, kUT = UTs[gl]
                pK, kpK = self.psum.get()
                for ri in range(2):
                    lhs = BB[:, gl, ri].rearrange("p s h -> p (s h)")
                    self.mm(pK, lhs, CAP[:, ri, 0:32, :].rearrange("p m h -> p (m h)"), ri == 0, ri == 1, [kBB, kCA], [kpK])
                TBf_, kTBf = TBf.get()
                fw.op("dve", lambda g: g.tensor_scalar(out=TBf_, in0=pK, scalar1=self.rowmask[:, 0:1], scalar2=None, op0=ALU.mult), [kpK, kc_], [kTBf])
                for s_ in range(1, 8):
                    fw.op("dve", lambda g: g.scalar_tensor_tensor(out=TBf_[:, 16 * s_:512], in0=pK[:, 0:512 - 16 * s_], scalar=self.rowmask[:, s_:s_ + 1],
                                                                  in1=TBf_[:, 16 * s_:512], op0=ALU.mult, op1=ALU.add), [kpK, kc_, kTBf], [kTBf])
                TB_, kTB = TBr.get()
                self.copy("act", TB_, TBf_, [kTBf], [kTB])
                pY, kpY = self.psacc.get()
                for a in range(4):
                    self.mm(pY[:, 128 * a:512], UT[:, a, :], TB_[:, 0:512 - 128 * a], a == 0, False, [kUT, kTB], [kpY])
                for ri in range(2):
                    self.mm(pY, Sp[:, gl, ri, :], CAP[:, ri, 1:33, :].rearrange("p m h -> p (m h)"), False, ri == 1, [kSp, kCA], [kpY])
                yp, kyp = ypr.get()
                fw.op("pool", lambda g: g.tensor_tensor(out=yp, in0=U[:, g_, :, :],
                                                        in1=drep[:, 16 * g_:16 * g_ + 16].unsqueeze(1).to_broadcast([128, 32, 16]), op=ALU.mult),
                      k_U + [k_p], [kyp])
                fw.op("dve", lambda g: g.tensor_tensor(out=yp, in0=yp, in1=pY.rearrange("p (t h) -> p t h", h=16), op=ALU.add), [kyp, kpY], [kyp])
                y2, ky2 = y2r.get()
                fw.op("pool", lambda g: g.tensor_tensor(out=y2, in0=yp, in1=yp, op=ALU.mult), [kyp], [ky2])
                fw.op("pool", lambda g: g.tensor_scalar(out=y2, in0=y2, scalar1=0.044715, scalar2=1.0, op0=ALU.mult, op1=ALU.add), [ky2], [ky2])
                fw.op("pool", lambda g: g.tensor_tensor(out=y2, in0=y2, in1=yp, op=ALU.mult), [ky2, kyp], [ky2])
                fw.op("act", lambda g: g.activation(out=y2, in_=y2, func=AF.Sigmoid, scale=1.5957691216057308), [ky2], [ky2])
                if g_ % 8 == 0:
                    GY, kGY = GYr.get()
                fw.op("dve", lambda g: g.tensor_tensor(out=GY[:, :, (g_ % 8) * 16:(g_ % 8 + 1) * 16], in0=yp, in1=y2, op=ALU.mult), [kyp, ky2], [kGY])
                if g_ % 8 == 7:
                    kt = g_ // 8
                    gt, kgt = gts.get()
                    gtv = gt.rearrange("p (c t) -> p t c", t=32)
                    for b in range(4):
                        pT, kpT = self.psum.get()
                        pTb = pT.bitcast(BF16)
                        for t8 in range(8):
                            fw.op("pe", lambda g: g.transpose(pTb[:, t8 * 128:(t8 + 1) * 128], GY[:, 8 * b + t8, :], self.identb), [kGY, kc_], [kpT])
                        self.copy(self.alt(), gtv[:, 8 * b:8 * b + 8, :], pTb.rearrange("p (t c) -> p t c", t=8), [kpT], [kgt])
                    fw.dma("sp", self.gyT[kt], gt, reads=[kgt], writes=[self.k_gyT[kt]])
        A.close()
        A = Arena(fw)
        wg = A.sb("wglu", [128, 4, 512], BF16)
        bglu = A.sb("bglu", [128, 4], F32)
        k_wg = Trk()
        fw.dma("pool", wg, I["s5_w_glu"][l].rearrange("(kc p) n -> p kc n", p=128), writes=[k_wg])
        fw.dma("sp", bglu, I["s5_bglu"][l], writes=[k_wg])
        gbr = A.ring("gb", 2, [128, 4, 512], BF16)
        sgr = A.ring("sg", 2, [128, 512], F32)
        obr = A.ring("ob", 2, [128, 512], BF16)
        for tb in range(NTB):
            gb, kgb = gbr.get()
            fw.dma("sp", gb, self.gyT[:, :, tb * 512:(tb + 1) * 512].rearrange("k p t -> p k t"), reads=self.k_gyT, writes=[kgb])
            for oc in range(4):
                ps, kp = self.psum.get()
                for kc in range(4):
                    self.mm(ps, wg[:, kc, oc * 128:(oc + 1) * 128], gb[:, kc, :], kc == 0, kc == 3, [k_wg, kgb], [kp])
                sg, ksg = sgr.get()
                fw.op("act", lambda g: g.activation(out=sg, in_=ps, func=AF.Sigmoid, bias=bglu[:, oc:oc + 1]), [kp, k_wg], [ksg])
                ob, kob = obr.get()
                fw.op("dve", lambda g: g.tensor_tensor(out=ob, in0=gb[:, oc, :], in1=sg, op=ALU.mult), [kgb, ksg], [kob])
                fw.dma("sp", self.yT[2, oc * 128:(oc + 1) * 128, tb * 512:(tb + 1) * 512], ob, reads=[kob], writes=[self.k_yT[2][tb]])
        A.close()


    def merge(self, l, res_src, k_res):
        fw = self.fw
        I = self.inp
        A = Arena(fw)
        Y = A.sb("Y", [128, 3, 4, L_SEQ], BF16)
        k_Y = [Trk() for _ in range(3)]
        for r in range(3):
            fw.dma("sp", Y[:, r], self.yT[r].rearrange("(kc p) t -> p kc t", p=128), reads=self.k_yT[r], writes=[k_Y[r]])
        wgr = A.ring("wgt", 2, [128, 3, 8, 128], BF16)
        wbr = A.ring("wbr", 2, [128, 3, 4, 128], BF16)
        bg = A.sb("bg", [128, 3, 8], F32)
        k_bg = Trk()
        fw.dma("sp", bg, I["b_gate_l"][l], writes=[k_bg])
        w_d = I["w_in"][l].rearrange("(kc p) n -> p kc n", p=128)
        wbr_d = I["w_branch"][l].rearrange("r (kc p) d -> p r kc d", p=128)
        gsr = A.ring("gs", 3, [128, 512], F32)
        accr = A.ring("acc", 2, [128, 512], F32)
        mbr = A.ring("mb", 2, [128, 512], BF16)
        def load_mw(dt):
            wg, kwg = wgr.get()
            for r in range(3):
                c0 = 3344 + r * 1024 + dt * 128
                fw.dma("pool", wg[:, r], w_d[:, :, c0:c0 + 128], writes=[kwg])
            wb, kwb = wbr.get()
            for r in range(3):
                fw.dma("pool", wb[:, r], wbr_d[:, r, :, dt * 128:(dt + 1) * 128], writes=[kwb])
            return wg, kwg, wb, kwb
        nxt_mw = load_mw(0)
        for dt in range(8):
            wg, kwg, wb, kwb = nxt_mw
            if dt + 1 < 8:
                nxt_mw = load_mw(dt + 1)
            for tb in range(NTB):
                ts_ = slice(tb * 512, (tb + 1) * 512)
                acc, kacc = accr.get()
                for r in range(3):
                    pg, kpg = self.psum.get()
                    for kc in range(8):
                        self.mm(pg, wg[:, r, kc, :], self.xT[:, kc, ts_], kc == 0, kc == 7, [kwg] + self.k_xT[tb * 4:(tb + 1) * 4], [kpg])
                    gs, kgs = gsr.get()
                    fw.op("act", lambda g: g.activation(out=gs, in_=pg, func=AF.Sigmoid, bias=bg[:, r, dt:dt + 1]), [kpg, k_bg], [kgs])
                    pp, kpp = self.psum.get()
                    for kc in range(4):
                        self.mm(pp, wb[:, r, kc, :], Y[:, r, kc, ts_], kc == 0, kc == 3, [kwb, k_Y[r]], [kpp])
                    if r == 0:
                        fw.op("dve", lambda g: g.tensor_tensor(out=acc, in0=gs, in1=pp, op=ALU.mult), [kgs, kpp], [kacc])
                    else:
                        fw.op("dve", lambda g: g.tensor_tensor(out=gs, in0=gs, in1=pp, op=ALU.mult), [kgs, kpp], [kgs])
                        if r == 1:
                            fw.op("pool", lambda g: g.tensor_tensor(out=acc, in0=acc, in1=gs, op=ALU.add), [kacc, kgs], [kacc])
                        else:
                            mb, kmb = mbr.get()
                            fw.op("pool", lambda g: g.tensor_tensor(out=mb, in0=acc, in1=gs, op=ALU.add), [kacc, kgs], [kmb])
                            fw.dma("sp", self.mT[dt, :, ts_], mb, reads=[kmb], writes=[self.k_mT[dt][tb]])
        A.close()
        A = Arena(fw)
        self.alloc_ln(A)
        self.load_ln(I["ln1_g"][l:l + 1, :], I["ln1_b"][l:l + 1, :])
        wo = A.sb("wo", [128, 8, D], BF16)
        k_wo = Trk()
        fw.dma("pool", wo, I["w_out"][l].rearrange("(kc p) d -> p kc d", p=128), writes=[k_wo])
        mir = A.ring("mi", 2, [128, 8, 512], BF16)
        for tb in range(NTB):
            mi, kmi = mir.get()
            fw.dma("sp", mi, self.mT[:, :, tb * 512:(tb + 1) * 512].rearrange("k p t -> p k t"),
                   reads=[self.k_mT[dt][tb] for dt in range(8)], writes=[kmi])
            for ts in range(4):
                tt = tb * 4 + ts
                p0, k0 = self.psum.get()
                p1, k1 = self.psum.get()
                for h, (pp, kk) in enumerate(((p0, k0), (p1, k1))):
                    for kc in range(8):
                        self.mm(pp, mi[:, kc, ts * 128:(ts + 1) * 128], wo[:, kc, h * 512:(h + 1) * 512], kc == 0, kc == 7, [kmi, k_wo], [kk])
                self.resid_ln(tt, (p0, p1), (k0, k1), res_src, k_res[tt], self.xr, self.k_xr[tt])
        self.ln_flush()
        A.close()

    def setup_ffn(self):
        fw = self.fw
        self.actT = fw.dram("actT", [22, 128, L_SEQ], BF16)
        self.k_actT = [[Trk() for _ in range(NTB)] for _ in range(22)]

    def ffn(self, l, out_dram, k_out):
        fw = self.fw
        I = self.inp
        A = Arena(fw)
        self.alloc_ln(A)
        self.wdown = A.sb("wdown", [128, 22, D], BF16)
        self.k_wdown = Trk("wdown")
        self.wup = A.ring("wup", 2, [128, 8, 256], BF16)
        self.fcw = A.sb("fcw", [128, 44, 3], F32)
        self.fcb = A.sb("fcb", [128, 44], F32)
        self.k_fc = Trk("fc")
        self.sv = A.ring("sv", 3, [128, 514], BF16)
        self.sg = A.ring("sg", 3, [128, 514], BF16)
        self.dg = A.ring("dg", 2, [128, 6, 128], BF16)
        self.hg = A.ring("hg", 2, [128, 512], F32)
        self.actb = A.ring("actb", 3, [128, 512], BF16)
        self.actin = A.ring("actin", 2, [128, 22, 256], BF16)
        wup_d = I["ffn_w_up"][l].rearrange("(kc p) n -> p kc n", p=128)
        fw.dma("sp", self.fcw, I["ffn_cw"][l], writes=[self.k_fc])
        fw.dma("sp", self.fcb, I["ffn_cb"][l], writes=[self.k_fc])
        k_all_xT = self.k_xT
        def load_w(m):
            wb, kw = self.wup.get()
            fw.dma("pool", wb[:, :, 0:128], wup_d[:, :, m * 128:(m + 1) * 128], writes=[kw])
            fw.dma("pool", wb[:, :, 128:256], wup_d[:, :, DFF + m * 128:DFF + (m + 1) * 128], writes=[kw])
            return wb, kw
        nxt_w = load_w(0)
        for m in range(22):
            wb, kw = nxt_w
            if m + 1 < 22:
                nxt_w = load_w(m + 1)
            dg, kdg = self.dg.get()
            for t in range(3):
                fw.op("pool", lambda g: g.tensor_scalar(out=dg[:, t, :], in0=self.identf, scalar1=self.fcw[:, m, t:t + 1], scalar2=0.0,
                                                        op0=ALU.mult, op1=ALU.add), reads=[self.k_ident, self.k_fc], writes=[kdg])
                fw.op("pool", lambda g: g.tensor_scalar(out=dg[:, 3 + t, :], in0=self.identf, scalar1=self.fcw[:, 22 + m, t:t + 1], scalar2=0.0,
                                                        op0=ALU.mult, op1=ALU.add), reads=[self.k_ident, self.k_fc], writes=[kdg])
            prev = None
            pend = None

            def conv(st):
                sv, ksv, sg, ksg, tb = st
                cv, kcv = self.psum.get()
                for t in range(3):
                    self.mm(cv, dg[:, t, :], sv[:, t:t + 512], t == 0, t == 2, [kdg, ksv], [kcv])
                cg, kcg = self.psum.get()
                for t in range(3):
                    self.mm(cg, dg[:, 3 + t, :], sg[:, t:t + 512], t == 0, t == 2, [kdg, ksg], [kcg])
                hg, khg = self.hg.get()
                fw.op("act", lambda g: g.activation(out=hg, in_=cg, func=AF.Silu, bias=self.fcb[:, 22 + m:23 + m]),
                      reads=[kcg, self.k_fc], writes=[khg])
                ab, kab = self.actb.get()
                fw.op("dve", lambda g: g.scalar_tensor_tensor(out=ab, in0=cv, scalar=self.fcb[:, m:m + 1], in1=hg,
                                                              op0=ALU.add, op1=ALU.mult), reads=[kcv, self.k_fc, khg], writes=[kab])
                fw.dma("sp", self.actT[m, :, tb * 512:(tb + 1) * 512], ab, reads=[kab], writes=[self.k_actT[m][tb]])
            for tb in range(NTB):
                xk = k_all_xT[tb * 4:(tb + 1) * 4]
                pv, kpv = self.psum.get()
                for kc in range(8):
                    self.mm(pv, wb[:, kc, 0:128], self.xT[:, kc, tb * 512:(tb + 1) * 512], kc == 0, kc == 7, [kw] + xk, [kpv])
                pg, kpg = self.psum.get()
                for kc in range(8):
                    self.mm(pg, wb[:, kc, 128:256], self.xT[:, kc, tb * 512:(tb + 1) * 512], kc == 0, kc == 7, [kw] + xk, [kpg])
                if pend is not None:
                    conv(pend)
                sv, ksv = self.sv.get()
                sg, ksg = self.sg.get()
                self.copy("act", sv[:, 2:514], pv, [kpv], [ksv])
                self.copy("dve", sg[:, 2:514], pg, [kpg], [ksg])
                if prev is None:
                    fw.op("pool", lambda g: g.memset(sv[:, 0:2], 0.0), writes=[ksv])
                    fw.op("pool", lambda g: g.memset(sg[:, 0:2], 0.0), writes=[ksg])
                else:
                    psv, pksv, psg, pksg = prev
                    fw.op("pool", lambda g: g.tensor_copy(out=sv[:, 0:2], in_=psv[:, 512:514]), reads=[pksv], writes=[ksv])
                    fw.op("pool", lambda g: g.tensor_copy(out=sg[:, 0:2], in_=psg[:, 512:514]), reads=[pksg], writes=[ksg])
                prev = (sv, ksv, sg, ksg)
                pend = (sv, ksv, sg, ksg, tb)
            conv(pend)
        fw.dma("pool", self.wdown, I["ffn_w_down"][l].rearrange("(kt p) d -> p kt d", p=128), writes=[self.k_wdown])
        self.load_ln(I["ln2_g"][l:l + 1, :], I["ln2_b"][l:l + 1, :])
        for tb2 in range(2 * NTB):
            tb = tb2 // 2
            ai, kai = self.actin.get()
            fw.dma("sp", ai, self.actT[:, :, tb2 * 256:(tb2 + 1) * 256].rearrange("m p t -> p m t"),
                   reads=[self.k_actT[m][tb] for m in range(22)], writes=[kai])
            for ts in range(2):
                tt = tb2 * 2 + ts
                p0, k0 = self.psum.get()
                p1, k1 = self.psum.get()
                for h, (pp, kk) in enumerate(((p0, k0), (p1, k1))):
                    for m in range(22):
                        self.mm(pp, ai[:, m, ts * 128:(ts + 1) * 128], self.wdown[:, m, h * 512:(h + 1) * 512], m == 0, m == 21,
                                [kai, self.k_wdown], [kk])
                self.resid_ln(tt, (p0, p1), (k0, k1), self.xr, self.k_xr[tt], out_dram, k_out[tt])
        self.ln_flush()
        A.close()


def _host_layout(inputs):
    f = {}
    A = lambda a: np.ascontiguousarray(np.asarray(a, dtype=np.float32))
    for k in ("w_in", "ffn_w_up", "ffn_w_down", "ln1_g", "ln1_b", "ln2_g", "ln2_b", "w_out", "w_branch", "s5_w_glu"):
        f[k] = A(inputs[k])
    cw = A(inputs["ffn_conv_w"])
    sw = A(inputs["ssd_conv_w"])
    f["ssd_cw"] = A(sw.reshape(DEPTH, 4, 6, 128).transpose(0, 3, 2, 1))
    f["ssd_cb"] = A(A(inputs["ssd_conv_b"]).reshape(DEPTH, 6, 128).transpose(0, 2, 1))
    for k in ("ssd_conv_b", "ssd_dt_bias", "ssd_a_log", "ssd_d", "ssd_norm_w", "fox_f_bias"):
        f[k] = A(inputs[k])
    pl = lambda a: A(a.reshape(DEPTH, 16, 2, 64).transpose(0, 2, 3, 1).reshape(DEPTH, 128, 16))
    f["s5_are"] = pl(A(inputs["s5_a_re"]))
    f["s5_aim"] = pl(A(inputs["s5_a_im"]))
    f["s5_lst"] = pl(np.repeat(A(inputs["s5_log_step"])[:, :, None], 64, axis=2))
    pb_ = lambda a: A(a.reshape(DEPTH, 16, 2, 64, 16).transpose(0, 2, 3, 1, 4).reshape(DEPTH, 128, 16, 16))
    f["s5_bre"] = pb_(A(inputs["s5_b_re"]))
    f["s5_bim"] = pb_(A(inputs["s5_b_im"]))
    f["s5_cre"] = pb_(A(A(inputs["s5_c_re"]).transpose(0, 1, 3, 2)))
    f["s5_cim"] = pb_(A(A(inputs["s5_c_im"]).transpose(0, 1, 3, 2)))
    f["s5_d"] = A(inputs["s5_d"])
    f["s5_bglu"] = A(A(inputs["s5_b_glu"]).reshape(DEPTH, 4, 128).transpose(0, 2, 1))
    f["b_gate_l"] = A(A(inputs["b_gate"]).reshape(DEPTH, 3, 8, 128).transpose(0, 3, 1, 2))
    f["ffn_cw"] = A(cw.reshape(DEPTH, 3, 44, 128).transpose(0, 3, 2, 1))
    f["ffn_cb"] = A(A(inputs["ffn_conv_b"]).reshape(DEPTH, 44, 128).transpose(0, 2, 1))
    return f


def build_ffn_test():
    k = K()
    x = k.din("x", [L_SEQ, D])
    k.din("ffn_w_up", [DEPTH, D, 2 * DFF]); k.din("ffn_w_down", [DEPTH, DFF, D])
    k.din("ffn_cw", [DEPTH, 128, 44, 3]); k.din("ffn_cb", [DEPTH, 128, 44])
    k.din("ln2_g", [DEPTH, D]); k.din("ln2_b", [DEPTH, D])
    out = k.nc.dram_tensor("out", [L_SEQ, D], F32, kind="ExternalOutput").ap()
    k.setup_common()
    k.setup_ffn()
    k.phase0(x)
    k.xr = x
    k_out = [Trk() for _ in range(NTT)]
    k.ffn(0, out, k_out)
    k.fw.drain()
    return k


def build_fox_test():
    k = K()
    x = k.din("x", [L_SEQ, D])
    k.din("w_in", [DEPTH, D, DIN]); k.din("fox_f_bias", [DEPTH, 8])
    out = k.nc.dram_tensor("out", [512, L_SEQ], BF16, kind="ExternalOutput").ap()
    k.setup_common()
    k.phase0(x)
    k.fox(0)
    t = k.fw.sb("cp", [128, 4, L_SEQ], BF16)
    kt = Trk()
    k.fw.dma("sp", t, k.yT[0].rearrange("(c p) t -> p c t", p=128), reads=[x for r in k.k_yT[0] for x in [r]], writes=[kt])
    k.fw.dma("sp", out.rearrange("(c p) t -> p c t", p=128), t, reads=[kt], writes=[Trk()])
    k.fw.drain()
    return k


def build_ssd_test(dbg=None):
    k = K(dbg=dbg)
    x = k.din("x", [L_SEQ, D])
    k.din("w_in", [DEPTH, D, DIN])
    k.din("ssd_cw", [DEPTH, 128, 6, 4]); k.din("ssd_cb", [DEPTH, 128, 6]); k.din("ssd_conv_b", [DEPTH, 768])
    k.din("ssd_dt_bias", [DEPTH, 8]); k.din("ssd_a_log", [DEPTH, 8]); k.din("ssd_d", [DEPTH, 8]); k.din("ssd_norm_w", [DEPTH, 512])
    out = k.nc.dram_tensor("out", [512, L_SEQ], BF16, kind="ExternalOutput").ap()
    k.setup_common()
    k.phase0(x)
    k.ssd(0)
    t = k.fw.sb("cp", [128, 4, L_SEQ], BF16)
    kt = Trk()
    k.fw.dma("sp", t, k.yT[1].rearrange("(c p) t -> p c t", p=128), reads=list(k.k_yT[1]), writes=[kt])
    k.fw.dma("sp", out.rearrange("(c p) t -> p c t", p=128), t, reads=[kt], writes=[Trk()])
    k.fw.drain()
    return k


S5_INS = [("s5_are", [DEPTH, 128, 16]), ("s5_aim", [DEPTH, 128, 16]), ("s5_lst", [DEPTH, 128, 16]),
          ("s5_bre", [DEPTH, 128, 16, 16]), ("s5_bim", [DEPTH, 128, 16, 16]), ("s5_cre", [DEPTH, 128, 16, 16]),
          ("s5_cim", [DEPTH, 128, 16, 16]), ("s5_d", [DEPTH, 512]), ("s5_bglu", [DEPTH, 128, 4]), ("s5_w_glu", [DEPTH, 512, 512])]


def build_s5_test(dbg=None):
    k = K(dbg=dbg)
    x = k.din("x", [L_SEQ, D])
    k.din("w_in", [DEPTH, D, DIN])
    for n, shp in S5_INS:
        k.din(n, shp)
    out = k.nc.dram_tensor("out", [512, L_SEQ], BF16, kind="ExternalOutput").ap()
    k.setup_common()
    k.phase0(x)
    k.s5(0)
    t = k.fw.sb("cp", [128, 4, L_SEQ], BF16)
    kt = Trk()
    k.fw.dma("sp", t, k.yT[2].rearrange("(c p) t -> p c t", p=128), reads=list(k.k_yT[2]), writes=[kt])
    k.fw.dma("sp", out.rearrange("(c p) t -> p c t", p=128), t, reads=[kt], writes=[Trk()])
    k.fw.drain()
    return k


ALL_INS = [("w_in", [DEPTH, D, DIN]), ("fox_f_bias", [DEPTH, 8]),
           ("ssd_cw", [DEPTH, 128, 6, 4]), ("ssd_cb", [DEPTH, 128, 6]), ("ssd_conv_b", [DEPTH, 768]),
           ("ssd_dt_bias", [DEPTH, 8]), ("ssd_a_log", [DEPTH, 8]), ("ssd_d", [DEPTH, 8]), ("ssd_norm_w", [DEPTH, 512])] + S5_INS + [
           ("w_branch", [DEPTH, 3, 512, D]), ("b_gate_l", [DEPTH, 128, 3, 8]), ("w_out", [DEPTH, D, D]),
           ("ln1_g", [DEPTH, D]), ("ln1_b", [DEPTH, D]),
           ("ffn_w_up", [DEPTH, D, 2 * DFF]), ("ffn_w_down", [DEPTH, DFF, D]), ("ffn_cw", [DEPTH, 128, 44, 3]), ("ffn_cb", [DEPTH, 128, 44]),
           ("ln2_g", [DEPTH, D]), ("ln2_b", [DEPTH, D])]


def build_full(nl=DEPTH, stop=None):
    k = K()
    x = k.din("x", [L_SEQ, D])
    for n, shp in ALL_INS:
        k.din(n, shp)
    out = k.nc.dram_tensor("out", [L_SEQ, D], F32, kind="ExternalOutput").ap()
    k.setup_common()
    k.setup_ffn()
    k.phase0(x)
    k_dummy = [Trk() for _ in range(NTT)]
    k_out = [Trk() for _ in range(NTT)]
    for l in range(nl):
        k.fox(l)
        k.ssd(l)
        k.s5(l)
        k.merge(l, x if l == 0 else k.xr, k_dummy if l == 0 else k.k_xr)
        if stop == "merge":
            break
        last = (l == nl - 1)
        k.ffn(l, out if last else k.xr, k_out if last else k.k_xr)
    if stop == "merge":
        A = Arena(k.fw)
        r = A.ring("dump", 2, [128, D], F32)
        for tt in range(NTT):
            t, kt = r.get()
            k.fw.dma("sp", t, k.xr[tt * 128:(tt + 1) * 128, :], reads=[k.k_xr[tt]], writes=[kt])
            k.fw.dma("sp", out[tt * 128:(tt + 1) * 128, :], t, reads=[kt], writes=[k_out[tt]])
    k.fw.drain()
    return k


def build_l1_test():
    return build_full(1)


def build_m1_test():
    return build_full(1, stop="merge")


_CACHE = {}


def kernel(**inputs):
    f = _host_layout(inputs)
    if "k" not in _CACHE:
        _CACHE["k"] = build_full(DEPTH)
    k = _CACHE["k"]
    x = np.ascontiguousarray(np.asarray(inputs["x"], dtype=np.float32))
    nb = x.shape[0]
    in_maps = []
    for b in range(nb):
        m = {"x": x[b]}
        for n, _ in ALL_INS:
            m[n] = f[n]
        in_maps.append(m)
    res = run_bass_kernel_spmd(k.nc, in_maps, core_ids=list(range(nb)))
    return np.stack([np.asarray(r["out"], dtype=np.float32) for r in res.results], axis=0)


def build_l4_test():
    return build_full(4)


def build_l2_test():
    return build_full(2)
```

```python
import numpy as np
import concourse.bass as bass
import concourse.mybir as mybir
from concourse.bass_utils import run_bass_kernel_spmd
from contextlib import ExitStack

F32 = mybir.dt.float32
BF16 = mybir.dt.bfloat16
I32 = mybir.dt.int32
AF = mybir.ActivationFunctionType
ALU = mybir.AluOpType

L_SEQ = 4096
D = 1024
DEPTH = 4
DFF = 2816
DIN = 6416
ALPHA = (2 * DEPTH) ** 0.25
NTT = L_SEQ // 128
NTB = L_SEQ // 512
NOSAME = ()


class Trk:
    __slots__ = ("w", "rs", "name")

    def __init__(self, name=""):
        self.w = None
        self.rs = []
        self.name = name


class Fw:
    NDMA = 48

    def __init__(self, nc):
        self.nc = nc
        self.eng = {"pe": nc.tensor, "act": nc.scalar, "dve": nc.vector,
                    "pool": nc.gpsimd, "sp": nc.sync}
        self.sem = {k: nc.alloc_semaphore("s_" + k) for k in self.eng}
        self.cnt = {k: 0 for k in self.eng}
        self.seen = {k: {} for k in self.eng}
        self.dsem = [nc.alloc_semaphore("d%d" % i) for i in range(self.NDMA)]
        self.dval = [0] * self.NDMA
        self.dnext = {"sp": 0, "pool": 0}
        self.drange = {"sp": (0, 32), "pool": (32, self.NDMA)}
        self.nwait = 0
        self.ninst = 0
        self.nosame = set(NOSAME)

    def sb(self, name, shape, dt):
        return self.nc.alloc_sbuf_tensor(name, list(shape), dt).ap()

    def ps(self, name, shape, dt=F32):
        return self.nc.alloc_psum_tensor(name, list(shape), dt).ap()

    def dram(self, name, shape, dt, kind="Internal"):
        return self.nc.dram_tensor(name, list(shape), dt, kind=kind).ap()

    def _need(self, e, ev):
        if ev is None:
            return
        key, val = ev
        if key == e and (e == "pe" or e in self.nosame):
            return
        if self.seen[e].get(key, 0) >= val:
            return
        sem = self.sem[key] if isinstance(key, str) else self.dsem[key]
        self.eng[e].wait_ge(sem, val)
        self.nwait += 1
        self.seen[e][key] = val

    def _deps(self, e, reads, writes):
        for t in reads:
            self._need(e, t.w)
        for t in writes:
            self._need(e, t.w)
            for r in t.rs:
                self._need(e, r)

    def _commit(self, ev, reads, writes):
        for t in reads:
            t.rs.append(ev)
            if len(t.rs) > 48:
                d = {}
                for k, v in t.rs:
                    if d.get(k, 0) < v:
                        d[k] = v
                t.rs = list(d.items())
        for t in writes:
            t.w = ev
            t.rs = []

    def op(self, e, fn, reads=(), writes=()):
        self._deps(e, reads, writes)
        ins = fn(self.eng[e])
        self.cnt[e] += 1
        ins.then_inc(self.sem[e], 1)
        ev = (e, self.cnt[e])
        self._commit(ev, reads, writes)
        self.ninst += 1
        return ev

    def dma(self, q, out, in_, reads=(), writes=(), **kw):
        lo, hi = self.drange[q]
        i = lo + self.dnext[q]
        self.dnext[q] = (self.dnext[q] + 1) % (hi - lo)
        if self.dval[i] > 0:
            self._need(q, (i, self.dval[i]))
        self._deps(q, reads, writes)
        self.dval[i] += 16
        self.eng[q].dma_start(out=out, in_=in_, **kw).then_inc(self.dsem[i], 16)
        ev = (i, self.dval[i])
        self._commit(ev, reads, writes)
        self.ninst += 1
        return ev

    def barrier(self):
        for i in range(self.NDMA):
            if self.dval[i]:
                self._need("sp", (i, self.dval[i]))
        ins = self.eng["sp"].sem_inc(self.sem["sp"], 1)
        self.cnt["sp"] += 1
        for e in ("pe", "act", "dve", "pool", "sp"):
            for k in ("pe", "act", "dve", "pool", "sp"):
                if k != e and self.cnt[k]:
                    if self.seen[e].get(k, 0) < self.cnt[k]:
                        self.eng[e].wait_ge(self.sem[k], self.cnt[k])
                        self.seen[e][k] = self.cnt[k]
                        self.nwait += 1

    def drain(self):
        for i in range(self.NDMA):
            if self.dval[i]:
                self._need("sp", (i, self.dval[i]))
        for k in ("pe", "act", "dve", "pool"):
            if self.cnt[k]:
                self._need("sp", (k, self.cnt[k]))


class Arena:
    def __init__(self, fw):
        self.fw = fw
        self.st = ExitStack()

    _uid = [0]

    def sb(self, name, shape, dt):
        Arena._uid[0] += 1
        return self.st.enter_context(self.fw.nc.sbuf_tensor("%s_%d" % (name, Arena._uid[0]), list(shape), dt)).ap()

    def ring(self, name, n, shape, dt):
        return Ring(self.fw, name, n, shape, dt, mk=self.sb)

    def close(self):
        self.fw.barrier()
        self.st.close()


class Ring:
    def __init__(self, fw, name, n, shape, dt, psum=False, mk=None):
        if mk is None:
            mk = fw.ps if psum else fw.sb
        self.b = [mk("%s%d" % (name, i), shape, dt) for i in range(n)]
        self.k = [Trk("%s%d" % (name, i)) for i in range(n)]
        self.i = 0
        self.n = n

    def get(self):
        i = self.i
        self.i = (i + 1) % self.n
        return self.b[i], self.k[i]


class K:
    def __init__(self, nlayers=DEPTH, dbg=None):
        self.nl = nlayers
        self.dbg = dbg or {}
        nc = self.nc = bass.Bass("TRN2", target_bir_lowering=False)
        fw = self.fw = Fw(nc)
        self.inp = {}
        self.ev_alt = 0

    def din(self, name, shape, dt=F32):
        ap = self.nc.dram_tensor(name, list(shape), dt, kind="ExternalInput").ap()
        self.inp[name] = ap
        return ap

    def alt(self):
        self.ev_alt ^= 1
        return "act" if self.ev_alt else "dve"

    def copy(self, e, out, in_, reads, writes):
        if e == "act":
            return self.fw.op("act", lambda g: g.activation(out=out, in_=in_, func=AF.Copy), reads, writes)
        return self.fw.op(e, lambda g: g.tensor_copy(out=out, in_=in_), reads, writes)

    def mm(self, out, lhsT, rhs, start, stop, reads, writes):
        return self.fw.op("pe", lambda g: g.matmul(out, lhsT=lhsT, rhs=rhs, start=start, stop=stop), reads, writes)

    def setup_common(self):
        fw = self.fw
        self.psum = Ring(fw, "psb", 6, [128, 512], F32, psum=True)
        self.psacc = Ring(fw, "psa", 2, [128, 512], F32, psum=True)
        self.identb = fw.sb("identb", [128, 128], BF16)
        self.identf = fw.sb("identf", [128, 128], F32)
        self.k_ident = Trk("ident")
        fw.op("pool", lambda g: g.memset(self.identf, 0.0), writes=[self.k_ident])
        fw.op("pool", lambda g: g.affine_select(out=self.identf, in_=self.identf, compare_op=ALU.not_equal, fill=1.0,
                                                base=0, pattern=[[-1, 128]], channel_multiplier=1),
              reads=[self.k_ident], writes=[self.k_ident])
        fw.op("pool", lambda g: g.tensor_copy(out=self.identb, in_=self.identf), reads=[self.k_ident], writes=[self.k_ident])
        self.utri_f = fw.sb("utri_f", [128, 128], F32)
        self.trib = fw.sb("trib", [128, 128], BF16)
        self.ones_f = fw.sb("ones_f", [128, 128], F32)
        self.mean_f = fw.sb("mean_f", [128, 128], F32)
        self.ones_b = fw.sb("ones_b", [128, 128], BF16)
        self.scanmask = fw.sb("scanmask", [128, 8, 32], F32)
        fw.op("pool", lambda g: g.memset(self.utri_f, 1.0), writes=[self.k_ident])
        fw.op("pool", lambda g: g.affine_select(out=self.utri_f, in_=self.utri_f, compare_op=ALU.is_ge, fill=0.0,
                                                base=0, pattern=[[1, 128]], channel_multiplier=-1),
              reads=[self.k_ident], writes=[self.k_ident])
        fw.op("pool", lambda g: g.tensor_copy(out=self.trib, in_=self.utri_f), reads=[self.k_ident], writes=[self.k_ident])
        fw.op("pool", lambda g: g.memset(self.ones_f, 1.0), writes=[self.k_ident])
        fw.op("pool", lambda g: g.memset(self.ones_b, 1.0), writes=[self.k_ident])
        fw.op("pool", lambda g: g.memset(self.mean_f, 1.0 / 128.0), writes=[self.k_ident])
        fw.op("pool", lambda g: g.memset(self.scanmask, 1.0), writes=[self.k_ident])
        fw.op("pool", lambda g: g.memset(self.scanmask[:, :, 0:1], 0.0), writes=[self.k_ident])
        self.ones33 = fw.sb("ones33", [128, 128], BF16)
        fw.op("pool", lambda g: g.memset(self.ones33, 0.0), writes=[self.k_ident])
        for r_ in (0, 32, 64, 96):
            fw.op("pool", lambda g: g.memset(self.ones33[r_:r_ + 1, :], 1.0), writes=[self.k_ident])
        self.nutri_f = fw.sb("nutri_f", [128, 128], F32)
        self.negmask4 = fw.sb("negmask4", [128, 4, 128], F32)
        fw.op("pool", lambda g: g.tensor_scalar(out=self.nutri_f, in0=self.utri_f, scalar1=-1.0, scalar2=0.0, op0=ALU.mult, op1=ALU.add),
              reads=[self.k_ident], writes=[self.k_ident])
        fw.op("pool", lambda g: g.memset(self.negmask4, -30000.0), writes=[self.k_ident])
        for r_ in range(4):
            fw.op("pool", lambda g: g.affine_select(out=self.negmask4[:, r_, :], in_=self.negmask4[:, r_, :], compare_op=ALU.is_gt, fill=0.0,
                                                    base=0, pattern=[[-1, 128]], channel_multiplier=1),
                  reads=[self.k_ident], writes=[self.k_ident])
        self.rowmask = fw.sb("rowmask", [128, 8], F32)
        fw.op("pool", lambda g: g.memset(self.rowmask, 1.0), writes=[self.k_ident])
        fw.op("pool", lambda g: g.affine_select(out=self.rowmask, in_=self.rowmask, compare_op=ALU.is_ge, fill=0.0,
                                                base=0, pattern=[[-16, 8]], channel_multiplier=1), reads=[self.k_ident], writes=[self.k_ident])
        fw.op("pool", lambda g: g.affine_select(out=self.rowmask, in_=self.rowmask, compare_op=ALU.is_ge, fill=0.0,
                                                base=15, pattern=[[16, 8]], channel_multiplier=-1), reads=[self.k_ident], writes=[self.k_ident])
        self.mT = fw.dram("mT", [8, 128, L_SEQ], BF16)
        self.k_mT = [[Trk() for _ in range(NTB)] for _ in range(8)]
        self.gyT = fw.dram("gyT", [4, 128, L_SEQ], BF16)
        self.k_gyT = [Trk() for _ in range(4)]
        self.zd = fw.dram("zd", [L_SEQ, 512], F32)
        self.k_zd = [Trk() for _ in range(NTT)]
        self.yT = fw.dram("yT", [3, 512, L_SEQ], BF16)
        self.k_yT = [[Trk() for _ in range(NTB)] for _ in range(3)]
        self.xT = fw.sb("xT", [128, 8, L_SEQ], BF16)
        self.k_xT = [Trk("xT%d" % i) for i in range(NTT)]
        self.xr = fw.dram("xr", [L_SEQ, D], F32)
        self.k_xr = [Trk("xr%d" % i) for i in range(NTT)]

    def alloc_ln(self, A):
        self.tok32 = A.ring("tok32", 6, [128, D], F32)
        self.tokbf = A.ring("tokbf", 3, [128, D], BF16)
        self.ln_pipe = []
        self.ln_g = A.sb("ln_g", [128, D], F32)
        self.ln_b = A.sb("ln_b", [128, D], F32)
        self.k_lnp = Trk("lnp")
        self.stat = A.ring("stat", 4, [128, 16], F32)

    def to_xT(self, src_bf, k_src, tt):
        fw = self.fw
        ps, kp = self.psum.get()
        psb = ps.bitcast(BF16)
        for c in range(8):
            fw.op("pe", lambda g: g.transpose(psb[:, c * 128:(c + 1) * 128], src_bf[:, c * 128:(c + 1) * 128], self.identb),
                  reads=[k_src, self.k_ident], writes=[kp])
        dst = self.xT[:, :, tt * 128:(tt + 1) * 128]
        self.copy(self.alt(), dst, psb.rearrange("p (c t) -> p c t", c=8), reads=[kp], writes=[self.k_xT[tt]])

    def phase0(self, x_in):
        fw = self.fw
        A = Arena(fw)
        self.alloc_ln(A)
        for tt in range(NTT):
            t32, k32 = self.tok32.get()
            fw.dma("sp", t32, x_in[tt * 128:(tt + 1) * 128, :], writes=[k32])
            tb, kb = self.tokbf.get()
            self.copy(self.alt(), tb, t32, reads=[k32], writes=[kb])
            self.to_xT(tb, kb, tt)
        A.close()

    def load_ln(self, g_ap, b_ap):
        fw = self.fw
        fw.dma("sp", self.ln_g, g_ap.partition_broadcast(128), writes=[self.k_lnp])
        fw.dma("sp", self.ln_b, b_ap.partition_broadcast(128), writes=[self.k_lnp])

    def resid_ln(self, tt, ps_halves, k_ps, res_src, k_res, out_dram, k_out):
        fw = self.fw
        r32, kr = self.tok32.get()
        fw.dma("sp", r32, res_src[tt * 128:(tt + 1) * 128, :], reads=[k_res], writes=[kr])
        t32, kt = self.tok32.get()
        for h in range(2):
            fw.op("dve", lambda g: g.scalar_tensor_tensor(out=t32[:, h * 512:(h + 1) * 512], in0=r32[:, h * 512:(h + 1) * 512],
                                                          scalar=float(ALPHA), in1=ps_halves[h], op0=ALU.mult, op1=ALU.add),
                  reads=[kr, k_ps[h]], writes=[kt])
        st, ks = self.stat.get()
        for h in range(2):
            fw.op("dve", lambda g: g.bn_stats(out=st[:, h * 6:(h + 1) * 6], in_=t32[:, h * 512:(h + 1) * 512]), reads=[kt], writes=[ks])
        fw.op("dve", lambda g: g.bn_aggr(out=st[:, 12:14], in_=st[:, 0:12]), reads=[ks], writes=[ks])
        fw.op("dve", lambda g: g.tensor_scalar(out=st[:, 14:15], in0=st[:, 13:14], scalar1=1e-5, scalar2=None, op0=ALU.add), reads=[ks], writes=[ks])
        fw.op("act", lambda g: g.activation(out=st[:, 14:15], in_=st[:, 14:15], func=AF.Sqrt), reads=[ks], writes=[ks])
        self.ln_pipe.append(dict(tt=tt, r32=r32, kr=kr, t32=t32, kt=kt, st=st, ks=ks, out=out_dram, k_out=k_out))
        if len(self.ln_pipe) >= 2:
            self.ln_stage_b(self.ln_pipe[-2])
        if len(self.ln_pipe) >= 3:
            self.ln_stage_c(self.ln_pipe[-3])

    def ln_stage_b(self, p):
        fw = self.fw
        t32, kt, r32, kr, st, ks, tt = p["t32"], p["kt"], p["r32"], p["kr"], p["st"], p["ks"], p["tt"]
        fw.op("dve", lambda g: g.reciprocal(out=st[:, 15:16], in_=st[:, 14:15]), reads=[ks], writes=[ks])
        fw.op("dve", lambda g: g.tensor_scalar(out=t32, in0=t32, scalar1=st[:, 12:13], scalar2=st[:, 15:16], op0=ALU.subtract, op1=ALU.mult),
              reads=[kt, ks], writes=[kt])
        fw.op("dve", lambda g: g.tensor_tensor(out=t32, in0=t32, in1=self.ln_g, op=ALU.mult), reads=[kt, self.k_lnp], writes=[kt])
        fw.op("dve", lambda g: g.tensor_tensor(out=r32, in0=t32, in1=self.ln_b, op=ALU.add), reads=[kt, self.k_lnp], writes=[kr])
        fw.dma("sp", p["out"][tt * 128:(tt + 1) * 128, :], r32, reads=[kr], writes=[p["k_out"]])
        tb, kb = self.tokbf.get()
        self.copy("act", tb, r32, reads=[kr], writes=[kb])
        p["tb"], p["kb"] = tb, kb

    def ln_stage_c(self, p):
        self.to_xT(p["tb"], p["kb"], p["tt"])

    def ln_flush(self):
        n = len(self.ln_pipe)
        if n >= 1:
            self.ln_stage_b(self.ln_pipe[-1])
        if n >= 2:
            self.ln_stage_c(self.ln_pipe[-2])
        if n >= 1:
            self.ln_stage_c(self.ln_pipe[-1])
        self.ln_pipe = []

    def fox(self, l):
        fw = self.fw
        I = self.inp
        A = Arena(fw)
        kc_ = self.k_ident
        win = A.ring("win", 2, [128, 8, 528], BF16)
        vaug = A.sb("vaug", [128, 32, 8, 65], BF16)
        k_v = [Trk() for _ in range(NTT)]
        FL = A.sb("FL", [128, 32, 8], F32)
        k_FL = Trk()
        qkr = A.ring("qk", 2, [128, 2, L_SEQ], BF16)
        w_d = I["w_in"][l].rearrange("(kc p) n -> p kc n", p=128)
        fw.op("pool", lambda g: g.memset(vaug[:, :, :, 64:65], 1.0), writes=k_v)

        def proj_qk(hp):
            qk, _ = qkr.get()
            kq = [Trk() for _ in range(NTB)]
            kk = [Trk() for _ in range(NTB)]
            wb, kw = win.get()
            fw.dma("pool", wb[:, :, 0:128], w_d[:, :, hp * 128:(hp + 1) * 128], writes=[kw])
            fw.dma("pool", wb[:, :, 128:256], w_d[:, :, 512 + hp * 128:512 + (hp + 1) * 128], writes=[kw])
            for which, kd in ((0, kq), (1, kk)):
                for tb in range(NTB):
                    ps, kp = self.psum.get()
                    for kc in range(8):
                        self.mm(ps, wb[:, kc, which * 128:(which + 1) * 128], self.xT[:, kc, tb * 512:(tb + 1) * 512], kc == 0, kc == 7,
                                [kw] + self.k_xT[tb * 4:(tb + 1) * 4], [kp])
                    o = qk[:, which, tb * 512:(tb + 1) * 512]
                    if which == 1:
                        self.copy(self.alt(), o, ps, [kp], [kd[tb]])
                    elif self.alt() == "act":
                        fw.op("act", lambda g: g.activation(out=o, in_=ps, func=AF.Copy, scale=0.125), [kp], [kd[tb]])
                    else:
                        fw.op("dve", lambda g: g.tensor_scalar(out=o, in0=ps, scalar1=0.125, scalar2=None, op0=ALU.mult), [kp], [kd[tb]])
            return qk, kq, kk
        wb, kw = win.get()
        fw.dma("pool", wb[:, :, 0:520], w_d[:, :, 1024:1544], writes=[kw])
        for tt in range(NTT):
            ps, kp = self.psum.get()
            for kc in range(8):
                self.mm(ps, self.xT[:, kc, tt * 128:(tt + 1) * 128], wb[:, kc, 0:512], kc == 0, kc == 7, [kw, self.k_xT[tt]], [kp])
            ps2, kp2 = self.psum.get()
            for kc in range(8):
                self.mm(ps2[:, 0:8], self.xT[:, kc, tt * 128:(tt + 1) * 128], wb[:, kc, 512:520], kc == 0, kc == 7, [kw, self.k_xT[tt]], [kp2])
            self.copy(self.alt(), vaug[:, tt, :, 0:64], ps.rearrange("p (h d) -> p h d", h=8), [kp], [k_v[tt]])
            self.copy(self.alt(), FL[:, tt, :], ps2[:, 0:8], [kp2], [k_FL])
        fb = A.sb("fb", [128, 8], F32)
        k_t = Trk()
        fw.dma("sp", fb, I["fox_f_bias"][l:l + 1, :].partition_broadcast(128), writes=[k_t])
        nls = A.sb("nls", [128, 32, 8], F32)
        fw.op("dve", lambda g: g.tensor_tensor(out=nls, in0=FL, in1=fb.unsqueeze(1).to_broadcast([128, 32, 8]), op=ALU.add), [k_FL, k_t], [k_t])
        fw.op("act", lambda g: g.activation(out=nls, in_=nls, func=AF.Exp, scale=-1.0), [k_t], [k_t])
        fw.op("act", lambda g: g.activation(out=nls, in_=nls, func=AF.Ln, bias=1.0), [k_t], [k_t])
        nlsf = nls.rearrange("p j h -> p (j h)")
        ps, kp = self.psum.get()
        self.mm(ps[:, 0:256], self.utri_f, nlsf, True, True, [kc_, k_t], [kp])
        ps2, kp2 = self.psum.get()
        self.mm(ps2[:, 0:256], self.ones_f, nlsf, True, True, [kc_, k_t], [kp2])
        totT = A.sb("totT", [128, 8, 32], F32)
        pin = A.sb("pin", [128, 8, 32], F32)
        cumk = A.sb("cumk", [128, 8, 32], F32)
        refp = A.sb("refp", [128, 8, 32], F32)
        k_c = Trk()
        fw.op("dve", lambda g: g.tensor_copy(out=totT, in_=ps2[:, 0:256].rearrange("p (j h) -> p h j", h=8)), [kp2], [k_c])
        fw.op("dve", lambda g: g.tensor_tensor_scan(out=pin.rearrange("p h j -> p (h j)"), data0=self.scanmask.rearrange("p h j -> p (h j)"),
                                                    data1=totT.rearrange("p h j -> p (h j)"), initial=0.0, op0=ALU.mult, op1=ALU.add),
              [k_c, kc_], [k_c])
        fw.op("dve", lambda g: g.tensor_tensor(out=cumk, in0=pin, in1=totT, op=ALU.subtract), [k_c], [k_c])
        fw.op("dve", lambda g: g.tensor_tensor(out=cumk, in0=cumk, in1=ps[:, 0:256].rearrange("p (j h) -> p h j", h=8), op=ALU.add), [k_c, kp], [k_c])
        ps3, kp3 = self.psum.get()
        self.mm(ps3[:, 0:256], self.mean_f, cumk.rearrange("p h j -> p (h j)"), True, True, [kc_, k_c], [kp3])
        fw.op("dve", lambda g: g.tensor_copy(out=refp.rearrange("p h j -> p (h j)"), in_=ps3[:, 0:256]), [kp3], [k_c])
        dsm = A.sb("dsm", [128, 8, 32], F32)
        hs_ = A.sb("hs_", [128, 8, 32], BF16)
        ls_ = A.sb("ls_", [128, 8, 32], BF16)
        v4 = lambda t: t.rearrange("p h (i s) -> p h i s", s=4)
        fw.op("dve", lambda g: g.tensor_tensor(out=v4(dsm), in0=v4(refp)[:, :, :, 0:1].to_broadcast([128, 8, 8, 4]), in1=v4(refp),
                                               op=ALU.subtract), [k_c], [k_c])
        fw.op("dve", lambda g: g.tensor_copy(out=hs_, in_=dsm), [k_c], [k_c])
        fw.op("dve", lambda g: g.tensor_tensor(out=dsm, in0=dsm, in1=hs_, op=ALU.subtract), [k_c], [k_c])
        fw.op("dve", lambda g: g.tensor_copy(out=ls_, in_=dsm), [k_c], [k_c])
        refhl = A.sb("refhl", [128, L_SEQ], BF16)
        k_rh = Trk()
        fw.op("pool", lambda g: g.memset(refhl, 0.0), writes=[k_rh])
        biasr = A.ring("biasT", 4, [128, 32], F32)
        ptr = A.ring("pt", 4, [128, 512], BF16)
        rcp = A.ring("rcp", 2, [128, 512], F32)
        ysb = A.ring("ysb", 2, [64, 512], F32)
        ybf = A.ring("ybf", 2, [64, 512], BF16)
        for hp in range(4):
            qk, k_q, k_k = proj_qk(hp)
            qT = qk[:, 0, :]
            kT = qk[:, 1, :]
            for hh in range(2):
                h = 2 * hp + hh
                r0 = hh * 64
                fw.op("pool", lambda g: g.tensor_copy(out=refhl[r0:r0 + 1, :].rearrange("p (j q) -> p j q", q=128),
                                                      in_=hs_[r0:r0 + 1, h, :].unsqueeze(2).to_broadcast([1, 32, 128])), [k_c, k_rh], [k_rh])
                fw.op("pool", lambda g: g.tensor_copy(out=refhl[r0 + 32:r0 + 33, :].rearrange("p (j q) -> p j q", q=128),
                                                      in_=ls_[r0 + 32:r0 + 33, h, :].unsqueeze(2).to_broadcast([1, 32, 128])), [k_c, k_rh], [k_rh])
            for i in range(NTB):
                for hh in range(2):
                    h = 2 * hp + hh
                    pr = slice(hh * 64, (hh + 1) * 64)
                    bt, kb = biasr.get()
                    fw.op("dve", lambda g: g.tensor_scalar(out=bt, in0=cumk[:, h, :], scalar1=refp[:, h, 4 * i:4 * i + 1], scalar2=None,
                                                           op0=ALU.subtract), [k_c], [kb])
                    po, kpo = self.psacc.get()
                    nj = 4 * i + 4

                    def emitS(j):
                        c0 = max(0, j - 4 * i) * 128
                        ps, kp = self.psum.get()
                        self.mm(ps[:, c0:512], kT[pr, j * 128:(j + 1) * 128], qT[pr, i * 512 + c0:(i + 1) * 512], True, False,
                                [k_k[j // 4], k_q[i]], [kp])
                        rs = slice(hh * 64, hh * 64 + 33)
                        self.mm(ps[:, c0:512], self.ones33[rs, :], refhl[rs, i * 512 + c0:(i + 1) * 512], False, True, [kc_, k_rh], [kp])
                        return ps, kp, c0
                    LA = 3
                    q_ = [emitS(j) for j in range(min(LA, nj))]
                    for j in range(nj):
                        if j + LA < nj:
                            q_.append(emitS(j + LA))
                        ps, kp, c0 = q_.pop(0)
                        pt, kpt = ptr.get()
                        fw.op("act", lambda g: g.activation(out=pt[:, c0:512], in_=ps[:, c0:512], func=AF.Exp, bias=bt[:, j:j + 1]), [kp, kb], [kpt])
                        if j >= 4 * i:
                            fw.op("pool", lambda g: g.tensor_tensor(out=pt[:, c0:c0 + 128], in0=pt[:, c0:c0 + 128], in1=self.trib, op=ALU.mult),
                                  [kpt, kc_], [kpt])
                        self.mm(po[0:65, c0:512], vaug[:, j, h, :], pt[:, c0:512], j == 0, j == nj - 1, [k_v[j], kpt], [kpo])
                    rc, krc = rcp.get()
                    fw.op("dve", lambda g: g.reciprocal(out=rc[64:65, :], in_=po[64:65, :]), [kpo], [krc])
                    pb, kpb = self.psum.get()
                    self.mm(pb[0:64, :], self.ones_f[64:65, 0:64], rc[64:65, :], True, True, [kc_, krc], [kpb])
                    ys, kys = ysb.get()
                    self.copy("act", ys, po[0:64, :], [kpo], [kys])
                    yb, kyb = ybf.get()
                    fw.op("dve", lambda g: g.tensor_tensor(out=yb, in0=ys, in1=pb[0:64, :], op=ALU.mult), [kys, kpb], [kyb])
                    fw.dma("sp", self.yT[0, h * 64:(h + 1) * 64, i * 512:(i + 1) * 512], yb, reads=[kyb], writes=[self.k_yT[0][i]])
        A.close()


    def ssd(self, l):
        fw = self.fw
        I = self.inp
        A = Arena(fw)
        kc_ = self.k_ident
        w_d = I["w_in"][l].rearrange("(kc p) n -> p kc n", p=128)
        XBC = A.sb("xbc", [128, 6, 3 + L_SEQ], BF16)
        k_xbc = [[Trk() for _ in range(NTB)] for _ in range(6)]
        k_pad = Trk()
        fw.op("pool", lambda g: g.memset(XBC[:, :, 0:3], 0.0), writes=[k_pad])
        DT = A.sb("DT", [128, 32, 8], F32)
        k_DT = Trk()
        A2 = Arena(fw)
        win = A2.ring("win", 2, [128, 8, 512], BF16)
        wb, kw = win.get()
        fw.dma("pool", wb[:, :, 0:512], w_d[:, :, 1544:2056], writes=[kw])
        zst = A2.ring("zst", 2, [128, 512], F32)
        for tt in range(NTT):
            ps, kp = self.psum.get()
            for kc in range(8):
                self.mm(ps, self.xT[:, kc, tt * 128:(tt + 1) * 128], wb[:, kc, 0:512], kc == 0, kc == 7, [kw, self.k_xT[tt]], [kp])
            zs, kzs = zst.get()
            self.copy(self.alt(), zs, ps, [kp], [kzs])
            fw.dma("sp", self.zd[tt * 128:(tt + 1) * 128, :], zs, reads=[kzs], writes=[self.k_zd[tt]])
        for gi, (c0, nct, ctb, ncols) in enumerate(((2056, 4, 0, 512), (2568, 2, 4, 264))):
            wb, kw = win.get()
            fw.dma("pool", wb[:, :, 0:ncols], w_d[:, :, c0:c0 + ncols], writes=[kw])
            for m in range(nct):
                ct = ctb + m
                for tb in range(NTB):
                    ps, kp = self.psum.get()
                    for kc in range(8):
                        self.mm(ps, wb[:, kc, m * 128:(m + 1) * 128], self.xT[:, kc, tb * 512:(tb + 1) * 512], kc == 0, kc == 7,
                                [kw] + self.k_xT[tb * 4:(tb + 1) * 4], [kp])
                    self.copy(self.alt(), XBC[:, ct, 3 + tb * 512:3 + (tb + 1) * 512], ps, [kp], [k_xbc[ct][tb]])
            if gi == 1:
                for tt in range(NTT):
                    ps2, kp2 = self.psum.get()
                    for kc in range(8):
                        self.mm(ps2[:, 0:8], self.xT[:, kc, tt * 128:(tt + 1) * 128], wb[:, kc, 256:264], kc == 0, kc == 7, [kw, self.k_xT[tt]], [kp2])
                    self.copy(self.alt(), DT[:, tt, :], ps2[:, 0:8], [kp2], [k_DT])
        A2.close()
        cw = A.sb("cw", [128, 6, 4], F32)
        cb = A.sb("cb", [128, 6], F32)
        k_cp = Trk()
        fw.dma("sp", cw, I["ssd_cw"][l], writes=[k_cp])
        fw.dma("sp", cb, I["ssd_cb"][l], writes=[k_cp])
        dgc = A.sb("dgc", [128, 6, 4, 128], BF16)
        for ct in range(6):
            for t in range(4):
                fw.op("pool", lambda g: g.tensor_scalar(out=dgc[:, ct, t, :], in0=self.identf, scalar1=cw[:, ct, t:t + 1], scalar2=0.0,
                                                        op0=ALU.mult, op1=ALU.add), [kc_, k_cp], [k_cp])
        cbr = A.sb("cbr", [1, 768], F32)
        cbh = A.sb("cbh", [1, 768], BF16)
        cbl = A.sb("cbl", [1, 768], BF16)
        cbt = cbr
        fw.dma("sp", cbr, I["ssd_conv_b"][l:l + 1, :], writes=[k_cp])
        fw.op("dve", lambda g: g.tensor_copy(out=cbh, in_=cbr), [k_cp], [k_cp])
        fw.op("dve", lambda g: g.tensor_tensor(out=cbt, in0=cbr, in1=cbh, op=ALU.subtract), [k_cp], [k_cp])
        fw.op("dve", lambda g: g.tensor_copy(out=cbl, in_=cbt), [k_cp], [k_cp])
        BT = A.sb("BT", [128, 2, L_SEQ], BF16)
        CT = A.sb("CT", [128, L_SEQ], BF16)
        k_BT = [Trk() for _ in range(NTB)]
        k_CT = [Trk() for _ in range(NTB)]
        fw.op("pool", lambda g: g.memset(BT[64:128, 0, :], 0.0), writes=k_BT)
        fw.op("pool", lambda g: g.memset(BT[0:64, 1, :], 0.0), writes=k_BT)
        for ct in (4, 5):
            for tb in range(NTB):
                ps, kp = self.psum.get()
                rd = [k_cp, k_pad, k_xbc[ct][tb]] + ([k_xbc[ct][tb - 1]] if tb else [])
                for t in range(4):
                    self.mm(ps, dgc[:, ct, t, :], XBC[:, ct, tb * 512 + t:tb * 512 + t + 512], t == 0, t == 3, rd, [kp])
                ts_ = slice(tb * 512, (tb + 1) * 512)
                if ct == 5:
                    fw.op("act", lambda g: g.activation(out=CT[:, ts_], in_=ps, func=AF.Silu, bias=cb[:, ct:ct + 1]), [kp, k_cp], [k_CT[tb]])
                else:
                    fw.op("act", lambda g: g.activation(out=BT[0:64, 0, ts_], in_=ps[0:64, :], func=AF.Silu, bias=cb[0:64, ct:ct + 1]), [kp, k_cp], [k_BT[tb]])
                    fw.op("act", lambda g: g.activation(out=BT[64:128, 1, ts_], in_=ps[64:128, :], func=AF.Silu, bias=cb[64:128, ct:ct + 1]), [kp, k_cp], [k_BT[tb]])
        if self.dbg.get('ssd_stop') == 2:
            A.close()
            return
        dtb = A.sb("dtb", [128, 8], F32)
        alog = A.sb("alog", [128, 8], F32)
        dsk = A.sb("dsk", [128, 8], F32)
        nw = A.sb("nw", [128, 512], F32)
        k_p = Trk()
        fw.dma("sp", dtb, I["ssd_dt_bias"][l:l + 1, :].partition_broadcast(128), writes=[k_p])
        fw.dma("sp", alog, I["ssd_a_log"][l:l + 1, :].partition_broadcast(128), writes=[k_p])
        fw.dma("sp", dsk, I["ssd_d"][l:l + 1, :].partition_broadcast(128), writes=[k_p])
        fw.dma("sp", nw, I["ssd_norm_w"][l:l + 1, :].partition_broadcast(128), writes=[k_p])
        fw.op("act", lambda g: g.activation(out=alog, in_=alog, func=AF.Exp), [k_p], [k_p])
        fw.op("dve", lambda g: g.tensor_scalar(out=alog, in0=alog, scalar1=-1.0, scalar2=None, op0=ALU.mult), [k_p], [k_p])
        dt = A.sb("dt", [128, 32, 8], F32)
        adt = A.sb("adt", [128, 32, 8], F32)
        acs = A.sb("acs", [128, 32, 8], F32)
        eacs = A.sb("eacs", [128, 32, 8], F32)
        dtdec = A.sb("dtdec", [128, 32, 8], F32)
        eatot = A.sb("eatot", [128, 32, 8], F32)
        esel = A.sb("esel", [128, 32, 4], F32)
        k_d = Trk()
        fw.op("dve", lambda g: g.tensor_tensor(out=dt, in0=DT, in1=dtb.unsqueeze(1).to_broadcast([128, 32, 8]), op=ALU.add), [k_DT, k_p], [k_d])
        fw.op("act", lambda g: g.activation(out=dt, in_=dt, func=AF.Exp), [k_d], [k_d])
        fw.op("act", lambda g: g.activation(out=dt, in_=dt, func=AF.Ln, bias=1.0), [k_d], [k_d])
        fw.op("dve", lambda g: g.tensor_tensor(out=adt, in0=dt, in1=alog.unsqueeze(1).to_broadcast([128, 32, 8]), op=ALU.mult), [k_d, k_p], [k_d])
        fl = lambda t: t.rearrange("p c h -> p (c h)")
        ps, kp = self.psum.get()
        self.mm(ps[:, 0:256], self.utri_f, fl(adt), True, True, [kc_, k_d], [kp])
        ps2, kp2 = self.psum.get()
        self.mm(ps2[:, 0:256], self.ones_f, fl(adt), True, True, [kc_, k_d], [kp2])
        fw.op("dve", lambda g: g.tensor_copy(out=fl(acs), in_=ps[:, 0:256]), [kp], [k_d])
        fw.op("act", lambda g: g.activation(out=fl(eacs), in_=ps[:, 0:256], func=AF.Exp), [kp], [k_d])
        fw.op("dve", lambda g: g.tensor_tensor(out=fl(dtdec), in0=ps2[:, 0:256], in1=fl(acs), op=ALU.subtract), [kp2, k_d], [k_d])
        fw.op("act", lambda g: g.activation(out=dtdec, in_=dtdec, func=AF.Exp), [k_d], [k_d])
        fw.op("dve", lambda g: g.tensor_tensor(out=dtdec, in0=dtdec, in1=dt, op=ALU.mult), [k_d], [k_d])
        fw.op("act", lambda g: g.activation(out=fl(eatot), in_=ps2[:, 0:256], func=AF.Exp), [kp2], [k_d])
        fw.op("dve", lambda g: g.tensor_copy(out=esel[0:64], in_=eatot[0:64, :, 0:4]), [k_d], [k_d])
        fw.op("dve", lambda g: g.tensor_copy(out=esel[64:128], in_=eatot[64:128, :, 4:8]), [k_d], [k_d])
        if self.dbg.get('ssd_stop') == 3:
            A.close()
            return
        xsr = A.ring("xs", 2, [128, 8, 64], F32)
        bpr = A.ring("bp", 2, [128, 2, 128], BF16)
        for b_ in bpr.b:
            fw.op("pool", lambda g: g.memset(b_, 0.0), writes=bpr.k)
        xdtr = A.ring("xdt", 2, [128, 8, 64], BF16)
        xddr = A.ring("xdd", 2, [128, 8, 64], BF16)
        Ar = A.ring("Aall", 1, [128, 8, 128], F32)
        Dr = A.ring("Dg", 1, [128, 4, 128], F32)
        Mr = A.ring("Mg", 2, [128, 4, 128], BF16)
        S = A.sb("S", [128, 4, 64], F32)
        k_S = Trk()
        fw.op("pool", lambda g: g.memset(S, 0.0), writes=[k_S])
        Sbr = A.ring("Sb", 2, [128, 2, 4, 64], BF16)
        for b_ in Sbr.b:
            fw.op("pool", lambda g: g.memset(b_, 0.0), writes=Sbr.k)
        yr = A.ring("y", 2, [128, 8, 64], F32)
        tmr = A.ring("tm", 1, [128, 8, 64], F32)
        ztr = A.ring("zt", 2, [128, 512], F32)
        ssr = A.ring("ss", 2, [128, 4], F32)
        ybr = A.ring("yb", 2, [128, 512], BF16)
        ytr = A.ring("yt", 2, [128, 4, 128], BF16)
        Sb_prev = None
        for c in range(self.dbg.get('ssd_nchunk', NTT)):
            tb = c // 4
            rdx = lambda ct: [k_cp, k_pad, k_xbc[ct][tb]] + ([k_xbc[ct][tb - 1]] if (tb and c % 4 == 0) else [])
            ps, kp = self.psum.get()
            for ct in range(4):
                o = ps[:, ct * 128:(ct + 1) * 128]
                for t in range(4):
                    self.mm(o, XBC[:, ct, c * 128 + t:c * 128 + t + 128], dgc[:, ct, t, :], t == 0, False, rdx(ct), [kp])
                self.mm(o, self.ones_b[0:1, :], cbh[0:1, ct * 128:(ct + 1) * 128], False, False, [kc_, k_cp], [kp])
                self.mm(o, self.ones_b[0:1, :], cbl[0:1, ct * 128:(ct + 1) * 128], False, True, [kc_, k_cp], [kp])
            psB, kpB = self.psum.get()
            o = psB[:, 0:128]
            for t in range(4):
                self.mm(o, XBC[:, 4, c * 128 + t:c * 128 + t + 128], dgc[:, 4, t, :], t == 0, False, rdx(4), [kpB])
            self.mm(o, self.ones_b[0:1, :], cbh[0:1, 512:640], False, False, [kc_, k_cp], [kpB])
            self.mm(o, self.ones_b[0:1, :], cbl[0:1, 512:640], False, True, [kc_, k_cp], [kpB])
            xs, kxs = xsr.get()
            fw.op("act", lambda g: g.activation(out=xs.rearrange("p h d -> p (h d)"), in_=ps, func=AF.Silu), [kp], [kxs])
            bp, kbp = bpr.get()
            fw.op("act", lambda g: g.activation(out=bp[:, 0, 0:64], in_=psB[:, 0:64], func=AF.Silu), [kpB], [kbp])
            fw.op("act", lambda g: g.activation(out=bp[:, 1, 64:128], in_=psB[:, 64:128], func=AF.Silu), [kpB], [kbp])
            if self.dbg.get('ssd_stop') == 4:
                continue
            xdt, kxdt = xdtr.get()
            xdd, kxdd = xddr.get()
            fw.op("dve", lambda g: g.tensor_tensor(out=xdt, in0=xs, in1=dt[:, c, :].unsqueeze(2).to_broadcast([128, 8, 64]), op=ALU.mult), [kxs, k_d], [kxdt])
            fw.op("pool", lambda g: g.tensor_tensor(out=xdd, in0=xs, in1=dtdec[:, c, :].unsqueeze(2).to_broadcast([128, 8, 64]), op=ALU.mult), [kxs, k_d], [kxdd])
            if self.dbg.get('ssd_stop') == 5:
                continue
            psG, kpG = self.psum.get()
            for g_ in range(self.dbg.get('ssd_ng', 2)):
                self.mm(psG[:, g_ * 128:(g_ + 1) * 128], BT[:, g_, c * 128:(c + 1) * 128], CT[:, c * 128:(c + 1) * 128], True, True,
                        [k_BT[tb], k_CT[tb]], [kpG])
            sub = self.dbg.get('ssd_sub', 99)
            if sub < 1:
                continue
            Aa, kA = Ar.get()
            fw.op("pool", lambda g: g.tensor_tensor(out=Aa, in0=self.ones_f.unsqueeze(1).to_broadcast([128, 8, 128]),
                                                    in1=adt[:, c, :].unsqueeze(2).to_broadcast([128, 8, 128]), op=ALU.mult), [kc_, k_d], [kA])
            if sub < 2:
                continue
            yd, kyd = self.psacc.get()
            for g_ in range(2):
                psS, kpS = self.psum.get()
                for hh in range(4):
                    o_ = psS[:, hh * 128:(hh + 1) * 128]
                    self.mm(o_, Aa[:, 4 * g_ + hh, :], self.utri_f, True, False, [kA, kc_], [kpS])
                    self.mm(o_, self.nutri_f, Aa[:, 4 * g_ + hh, :], False, False, [kA, kc_], [kpS])
                    self.mm(o_, self.identf, self.negmask4[:, 0, :], False, True, [kc_], [kpS])
                if sub < 3:
                    continue
                Dg, kD = Dr.get()
                fw.op("act", lambda g: g.activation(out=Dg.rearrange("p h l -> p (h l)"), in_=psS, func=AF.Exp), [kpS], [kD])
                if sub < 4:
                    continue
                Mg, kM = Mr.get()
                fw.op("dve", lambda g: g.tensor_tensor(out=Mg, in0=Dg, in1=psG[:, g_ * 128:(g_ + 1) * 128].unsqueeze(1).to_broadcast([128, 4, 128]),
                                                       op=ALU.mult), [kD, kpG], [kM])
                if sub < 5:
                    continue
                for hh in range(4):
                    h = 4 * g_ + hh
                    self.mm(yd[:, h * 64:(h + 1) * 64], Mg[:, hh, :], xdt[:, h, :], True, True, [kM, kxdt], [kyd])
            if sub < 99:
                continue
            if self.dbg.get('ssd_stop') == 6:
                continue
            pst, kst = self.psum.get()
            self.mm(pst[:, 0:256], bp[:, 0, :], xdd[:, 0:4, :].rearrange("p h d -> p (h d)"), True, False, [kbp, kxdd], [kst])
            self.mm(pst[:, 0:256], bp[:, 1, :], xdd[:, 4:8, :].rearrange("p h d -> p (h d)"), False, True, [kbp, kxdd], [kst])
            y, ky = yr.get()
            yf = y.rearrange("p h d -> p (h d)")
            if c > 0:
                Sb, kSb = Sb_prev
                yo, kyo = self.psum.get()
                for g_ in range(2):
                    self.mm(yo[:, g_ * 256:(g_ + 1) * 256], CT[:, c * 128:(c + 1) * 128], Sb[:, g_].rearrange("p h d -> p (h d)"), True, True,
                            [k_CT[tb], kSb], [kyo])
                fw.op("dve", lambda g: g.tensor_tensor(out=y, in0=yo.rearrange("p (h d) -> p h d", h=8),
                                                       in1=eacs[:, c, :].unsqueeze(2).to_broadcast([128, 8, 64]), op=ALU.mult), [kyo, k_d], [ky])
                fw.op("dve", lambda g: g.tensor_tensor(out=yf, in0=yf, in1=yd, op=ALU.add), [ky, kyd], [ky])
            else:
                self.copy("dve", yf, yd, [kyd], [ky])
            fw.op("dve", lambda g: g.tensor_tensor(out=S, in0=S, in1=esel[:, c, :].unsqueeze(2).to_broadcast([128, 4, 64]), op=ALU.mult), [k_S, k_d], [k_S])
            fw.op("dve", lambda g: g.tensor_tensor(out=S.rearrange("p h d -> p (h d)"), in0=S.rearrange("p h d -> p (h d)"), in1=pst[:, 0:256], op=ALU.add),
                  [k_S, kst], [k_S])
            Sb, kSb = Sbr.get()
            self.copy("pool", Sb[0:64, 0], S[0:64], [k_S], [kSb])
            self.copy("pool", Sb[64:128, 1], S[64:128], [k_S], [kSb])
            Sb_prev = (Sb, kSb)
            if self.dbg.get('ssd_stop') == 7:
                continue
            tm, ktm = tmr.get()
            fw.op("pool", lambda g: g.tensor_tensor(out=tm, in0=xs, in1=dsk.unsqueeze(2).to_broadcast([128, 8, 64]), op=ALU.mult), [kxs, k_p], [ktm])
            fw.op("pool", lambda g: g.tensor_tensor(out=y, in0=y, in1=tm, op=ALU.add), [ky, ktm], [ky])
            zt, kzt = ztr.get()
            fw.dma("sp", zt, self.zd[c * 128:(c + 1) * 128, :], reads=[self.k_zd[c]], writes=[kzt])
            fw.op("act", lambda g: g.activation(out=zt, in_=zt, func=AF.Silu), [kzt], [kzt])
            fw.op("dve", lambda g: g.tensor_tensor(out=yf, in0=yf, in1=zt, op=ALU.mult), [ky, kzt], [ky])
            if self.dbg.get('ssd_stop') == 8:
                continue
            ss, kss = ssr.get()
            tmf = tm.rearrange("p h d -> p (h d)")
            for g_ in range(2):
                fw.op("act", lambda g: g.activation(out=tmf[:, g_ * 256:(g_ + 1) * 256], in_=yf[:, g_ * 256:(g_ + 1) * 256], func=AF.Square,
                                                    accum_out=ss[:, g_:g_ + 1]), [ky, ktm], [ktm, kss])
            fw.op("dve", lambda g: g.tensor_scalar(out=ss[:, 0:2], in0=ss[:, 0:2], scalar1=1.0 / 256.0, scalar2=1e-5, op0=ALU.mult, op1=ALU.add), [kss], [kss])
            fw.op("act", lambda g: g.activation(out=ss[:, 0:2], in_=ss[:, 0:2], func=AF.Sqrt), [kss], [kss])
            fw.op("dve", lambda g: g.reciprocal(out=ss[:, 2:4], in_=ss[:, 0:2]), [kss], [kss])
            for g_ in range(2):
                fw.op("dve", lambda g: g.tensor_scalar(out=yf[:, g_ * 256:(g_ + 1) * 256], in0=yf[:, g_ * 256:(g_ + 1) * 256],
                                                       scalar1=ss[:, 2 + g_:3 + g_], scalar2=None, op0=ALU.mult), [ky, kss], [ky])
            yb, kyb = ybr.get()
            fw.op("pool", lambda g: g.tensor_tensor(out=yb, in0=yf, in1=nw, op=ALU.mult), [ky, k_p], [kyb])
            if self.dbg.get('ssd_stop') == 9:
                continue
            pT, kpT = self.psum.get()
            pTb = pT.bitcast(BF16)
            for ct in range(4):
                fw.op("pe", lambda g: g.transpose(pTb[:, ct * 128:(ct + 1) * 128], yb[:, ct * 128:(ct + 1) * 128], self.identb), [kyb, kc_], [kpT])
            yt, kyt = ytr.get()
            self.copy("act", yt, pTb[:, 0:512].rearrange("p (c t) -> p c t", c=4), [kpT], [kyt])
            fw.dma("sp", self.yT[1].rearrange("(ct p) t -> p ct t", p=128)[:, :, c * 128:(c + 1) * 128], yt, reads=[kyt], writes=[self.k_yT[1][tb]])
        A.close()


    def cmul(self, e, ore, oim, are, aim, bre, bim, t1, t2, k, conj_b=False):
        fw = self.fw
        tt = lambda o, a, b, op: fw.op(e, lambda g: g.tensor_tensor(out=o, in0=a, in1=b, op=op), k, k)
        tt(t1, are, bre, ALU.mult)
        tt(t2, aim, bim, ALU.mult)
        tt(ore, t1, t2, ALU.add if conj_b else ALU.subtract)
        tt(t1, are, bim, ALU.mult)
        tt(t2, aim, bre, ALU.mult)
        if conj_b:
            tt(oim, t2, t1, ALU.subtract)
        else:
            tt(oim, t1, t2, ALU.add)

    def sincos(self, arg, osin, ocos, ki, kf, k):
        fw = self.fw
        TWO_PI = 2.0 * np.pi
        for r, shift in ((osin, 0.0), (ocos, np.pi / 2)):
            fw.op("dve", lambda g: g.tensor_scalar(out=kf, in0=arg, scalar1=float(shift), scalar2=float(1.0 / TWO_PI), op0=ALU.add, op1=ALU.mult), k, k)
            fw.op("dve", lambda g: g.tensor_copy(out=ki, in_=kf), k, k)
            fw.op("dve", lambda g: g.tensor_copy(out=kf, in_=ki), k, k)
            fw.op("dve", lambda g: g.tensor_scalar(out=kf, in0=kf, scalar1=float(-TWO_PI), scalar2=float(shift), op0=ALU.mult, op1=ALU.add), k, k)
            fw.op("dve", lambda g: g.tensor_tensor(out=r, in0=kf, in1=arg, op=ALU.add), k, k)
            fw.op("dve", lambda g: g.tensor_scalar(out=r, in0=r, scalar1=3.141592, scalar2=-3.141592, op0=ALU.min, op1=ALU.max), k, k)
            fw.op("act", lambda g: g.activation(out=r, in_=r, func=AF.Sin), k, k)

    def s5(self, l):
        fw = self.fw
        I = self.inp
        A = Arena(fw)
        kc_ = self.k_ident
        w_d = I["w_in"][l].rearrange("(kc p) n -> p kc n", p=128)
        U = A.sb("U", [128, 32, 32, 16], BF16)
        k_U = [Trk() for _ in range(32)]
        k_p = Trk()
        kp_ = [k_p]
        T = lambda nm, shp=(128, 16): A.sb(nm, list(shp), F32)
        abr_keep = {}
        bbr, bbi = T("bbr", (128, 16, 16)), T("bbi", (128, 16, 16))
        cre, cim = T("cre", (128, 16, 16)), T("cim", (128, 16, 16))
        pwr_, pwi_ = T("pwr_", (128, 16, 65)), T("pwi_", (128, 16, 65))
        R = T("R")
        cosT = A.sb("cosT", [128, 16, 128], F32)
        sinT = A.sb("sinT", [128, 16, 128], F32)
        drep = T("drep", (128, 512))
        A2 = Arena(fw)
        wb = A2.sb("wu", [128, 8, 512], BF16)
        kw = Trk()
        fw.dma("pool", wb, w_d[:, :, 2832:3344], writes=[kw])
        for tau in range(32):
            ps, kp = self.psum.get()
            for kc in range(8):
                lhsT = self.xT[:, kc, :].rearrange("p (c t) -> p t c", t=32)[:, tau, :]
                self.mm(ps, lhsT, wb[:, kc, :], kc == 0, kc == 7, [kw] + self.k_xT, [kp])
            self.copy(self.alt(), U[:, :, tau, :], ps.rearrange("p (g h) -> p g h", h=16), [kp], [k_U[tau]])
        T2 = lambda nm, shp=(128, 16): A2.sb(nm, list(shp), F32)

        def ld(t, src):
            fw.dma("sp", t, src, writes=kp_)
            return t
        are, aim, lst = ld(T2("are"), I["s5_are"][l]), ld(T2("aim"), I["s5_aim"][l]), ld(T2("lst"), I["s5_lst"][l])
        bre, bim = ld(T2("bre", (128, 16, 16)), I["s5_bre"][l]), ld(T2("bim", (128, 16, 16)), I["s5_bim"][l])
        ld(cre, I["s5_cre"][l]); ld(cim, I["s5_cim"][l])
        ld(drep, I["s5_d"][l:l + 1, :].partition_broadcast(128))
        step, lre, den, t1, t2, xr_, th, mag = [T2(n) for n in ("step", "lre", "den", "t1", "t2", "xr_", "th", "mag")]
        sn, cs, abr, abi, nre, kre, kim = [T2(n) for n in ("sn", "cs", "abr", "abi", "nre", "kre", "kim")]
        ki16, kf16 = A2.sb("ki16", [128, 16], I32), T2("kf16")
        V = lambda fn: fw.op("dve", fn, kp_, kp_)
        fw.op("act", lambda g: g.activation(out=step, in_=lst, func=AF.Exp), kp_, kp_)
        V(lambda g: g.tensor_scalar(out=lre, in0=are, scalar1=-1e-4, scalar2=None, op0=ALU.min))
        V(lambda g: g.tensor_tensor(out=xr_, in0=lre, in1=step, op=ALU.mult))
        V(lambda g: g.tensor_tensor(out=th, in0=aim, in1=step, op=ALU.mult))
        fw.op("act", lambda g: g.activation(out=mag, in_=xr_, func=AF.Exp), kp_, kp_)
        self.sincos(th, sn, cs, ki16, kf16, kp_)
        V(lambda g: g.tensor_tensor(out=abr, in0=mag, in1=cs, op=ALU.mult))
        V(lambda g: g.tensor_tensor(out=abi, in0=mag, in1=sn, op=ALU.mult))
        V(lambda g: g.tensor_tensor(out=t1, in0=lre, in1=lre, op=ALU.mult))
        V(lambda g: g.tensor_tensor(out=t2, in0=aim, in1=aim, op=ALU.mult))
        V(lambda g: g.tensor_tensor(out=den, in0=t1, in1=t2, op=ALU.add))
        V(lambda g: g.reciprocal(out=den, in_=den))
        V(lambda g: g.tensor_scalar(out=nre, in0=abr, scalar1=-1.0, scalar2=None, op0=ALU.add))
        self.cmul("dve", kre, kim, nre, abi, lre, aim, t1, t2, kp_, conj_b=True)
        V(lambda g: g.tensor_tensor(out=kre, in0=kre, in1=den, op=ALU.mult))
        V(lambda g: g.tensor_tensor(out=kim, in0=kim, in1=den, op=ALU.mult))
        sh3 = [128, 16, 16]
        t3a, t3b = T2("t3a", sh3), T2("t3b", sh3)
        bc3 = lambda t: t.unsqueeze(2).to_broadcast(sh3)
        self.cmul("dve", bbr, bbi, bc3(kre), bc3(kim), bre, bim, t3a, t3b, kp_)
        shp = [128, 16, 65]
        ioti = A2.sb("ioti", [128, 65], I32)
        iot = T2("iot", (128, 65))
        fw.op("pool", lambda g: g.iota(ioti[:, 0:33], pattern=[[1, 33]], base=0, channel_multiplier=0), kp_, kp_)
        fw.op("pool", lambda g: g.iota(ioti[:, 33:65], pattern=[[-1, 32]], base=31, channel_multiplier=0), kp_, kp_)
        V(lambda g: g.tensor_copy(out=iot, in_=ioti))
        marg, parg, psn, pcs, kf65 = [T2(n, shp) for n in ("marg", "parg", "psn", "pcs", "kf65")]
        ki65 = A2.sb("ki65", shp, I32)
        bcm = lambda t: t.unsqueeze(2).to_broadcast(shp)
        bci = iot.unsqueeze(1).to_broadcast(shp)
        V(lambda g: g.tensor_tensor(out=marg, in0=bcm(xr_), in1=bci, op=ALU.mult))
        V(lambda g: g.tensor_tensor(out=parg, in0=bcm(th), in1=bci, op=ALU.mult))
        fw.op("act", lambda g: g.activation(out=marg, in_=marg, func=AF.Exp), kp_, kp_)
        self.sincos(parg, psn, pcs, ki65, kf65, kp_)
        V(lambda g: g.tensor_tensor(out=pwr_, in0=marg, in1=pcs, op=ALU.mult))
        V(lambda g: g.tensor_tensor(out=pwi_, in0=marg, in1=psn, op=ALU.mult))
        V(lambda g: g.tensor_copy(out=R, in_=marg[:, :, 32]))
        wre, wim, u1, u2 = [T2(n) for n in ("wre", "wim", "u1", "u2")]
        V(lambda g: g.tensor_copy(out=wre, in_=pcs[:, :, 32]))
        V(lambda g: g.tensor_copy(out=wim, in_=psn[:, :, 32]))
        fw.op("pool", lambda g: g.memset(cosT[:, :, 0:1], 1.0), kp_, kp_)
        fw.op("pool", lambda g: g.memset(sinT[:, :, 0:1], 0.0), kp_, kp_)
        tA, tB = T2("tA", (128, 16, 64)), T2("tB", (128, 16, 64))
        n_ = 1
        while n_ < 128:
            shn = [128, 16, n_]
            bw = lambda t: t.unsqueeze(2).to_broadcast(shn)
            self.cmul("dve", cosT[:, :, n_:2 * n_], sinT[:, :, n_:2 * n_], cosT[:, :, 0:n_], sinT[:, :, 0:n_], bw(wre), bw(wim),
                      tA[:, :, 0:n_], tB[:, :, 0:n_], kp_)
            self.cmul("dve", u1, u2, wre, wim, wre, wim, t1, t2, kp_)
            V(lambda g: g.tensor_copy(out=wre, in_=u1))
            V(lambda g: g.tensor_copy(out=wim, in_=u2))
            n_ *= 2
        A2.close()
        Zr = A.ring("Z", 2, [128, 2, 32, 16], BF16)
        ABr = A.ring("ABp", 2, [128, 8, 2, 128], BF16)
        for b_ in ABr.b:
            fw.op("pool", lambda g: g.memset(b_, 0.0), writes=ABr.k)
        CAr = A.ring("CAP", 2, [128, 2, 33, 16], BF16)
        BBr = A.ring("BBrep", 2, [128, 2, 2, 8, 16], BF16)
        for b_ in BBr.b:
            fw.op("pool", lambda g: g.memset(b_, 0.0), writes=BBr.k)
        UTr = A.ring("UT", 3, [128, 4, 128], BF16)
        TBf = A.ring("TBf", 1, [128, 512], F32)
        TBr = A.ring("TB", 2, [128, 512], BF16)
        gsc = A.ring("gsc", 2, [128, 4, 128], F32)
        Rbr = A.ring("Rb", 2, [128, 128], F32)
        Spr = A.ring("Sprev", 2, [128, 2, 2, 128], BF16)
        for b_ in Spr.b:
            fw.op("pool", lambda g: g.memset(b_, 0.0), writes=Spr.k)
        ypr = A.ring("ypre", 2, [128, 32, 16], F32)
        y2r = A.ring("y2", 2, [128, 32, 16], F32)
        GYr = A.ring("GY", 1, [128, 32, 128], BF16)
        gts = A.ring("gts", 1, [128, L_SEQ], BF16)
        z3, z4 = T("z3", (128, 32, 16)), T("z4", (128, 32, 16))
        cA, cB = T("cA", (128, 33, 16)), T("cB", (128, 33, 16))
        k_z = Trk()
        GY, kGY = None, None
        for gp in range(16):
            Z, kZ = Zr.get()
            shz = [128, 32, 16]
            pa = lambda t: t[:, gp, 33:65].unsqueeze(2).to_broadcast(shz)
            pb = lambda t: t[:, gp, :].unsqueeze(1).to_broadcast(shz)
            self.cmul("pool", Z[:, 0], Z[:, 1], pa(pwr_), pa(pwi_), pb(bbr), pb(bbi), z3, z4, [k_p, k_z, kZ])
            pT, kpT = self.psum.get()
            pTb = pT.bitcast(BF16)
            for a in range(4):
                for ri in range(2):
                    j = a * 2 + ri
                    fw.op("pe", lambda g: g.transpose(pTb[:, j * 128:(j + 1) * 128], Z[:, ri, 8 * a:8 * a + 8, :].rearrange("p t h -> p (t h)"),
                                                      self.identb), [kZ, kc_], [kpT])
            AB, kAB = ABr.get()
            src = pTb.rearrange("p (j m) -> p j m", j=8)
            self.copy("act", AB[:, :, 0, 0:64], src[:, :, 0:64], [kpT], [kAB])
            self.copy("act", AB[:, :, 1, 64:128], src[:, :, 64:128], [kpT], [kAB])
            CAP, kCA = CAr.get()
            shc = [128, 33, 16]
            pa2 = lambda t: t[:, gp, 0:33].unsqueeze(2).to_broadcast(shc)
            pc2 = lambda t: t[:, gp, :].unsqueeze(1).to_broadcast(shc)
            kz2 = [k_p, k_z]
            tt = lambda o, a_, b_, op, wr: fw.op("dve", lambda g: g.tensor_tensor(out=o, in0=a_, in1=b_, op=op), kz2 + wr, [k_z] + wr)
            tt(cA, pa2(pwr_), pc2(cre), ALU.mult, [])
            tt(cB, pa2(pwi_), pc2(cim), ALU.mult, [])
            tt(CAP[:, 0], cA, cB, ALU.subtract, [kCA])
            tt(cA, pa2(pwr_), pc2(cim), ALU.mult, [])
            tt(cB, pa2(pwi_), pc2(cre), ALU.mult, [])
            tt(cA, cA, cB, ALU.add, [])
            fw.op("dve", lambda g: g.tensor_scalar(out=CAP[:, 1], in0=cA, scalar1=-1.0, scalar2=None, op0=ALU.mult), [k_z, kCA], [kCA])
            BB, kBB = BBr.get()
            for (half, gl) in ((slice(0, 64), 0), (slice(64, 128), 1)):
                fw.op("pool", lambda g: g.tensor_copy(out=BB[half, gl, 0], in_=bbr[half, gp, :].unsqueeze(1).to_broadcast([64, 8, 16])), [k_p], [kBB])
                fw.op("pool", lambda g: g.tensor_copy(out=BB[half, gl, 1], in_=bbi[half, gp, :].unsqueeze(1).to_broadcast([64, 8, 16])), [k_p], [kBB])
            UTs = []
            for gl in range(2):
                g_ = 2 * gp + gl
                pU, kpU = self.psum.get()
                pUb = pU.bitcast(BF16)
                UT, kUT = UTr.get()
                UTs.append((UT, kUT))
                for a in range(4):
                    fw.op("pe", lambda g: g.transpose(pUb[:, a * 128:(a + 1) * 128], U[:, g_, 8 * a:8 * a + 8, :].rearrange("p t h -> p (t h)"), self.identb),
                          k_U[8 * a:8 * a + 8] + [kc_], [kpU])
                self.copy(self.alt(), UT, pUb[:, 0:512].rearrange("p (a c) -> p a c", a=4), [kpU], [kUT])
            pW, kpW = self.psum.get()
            for ri in range(2):
                o = pW[:, ri * 128:(ri + 1) * 128]
                n_mm = 0
                for gl in range(2):
                    UT, kUT = UTs[gl]
                    for a in range(4):
                        self.mm(o, AB[:, a * 2 + ri, gl, :], UT[:, a, :], n_mm == 0, n_mm == 7, [kAB, kUT], [kpW])
                        n_mm += 1
            gs, kgs = gsc.get()
            cT_, sT_ = cosT[:, gp, :], sinT[:, gp, :]
            W0, W1 = pW[:, 0:128], pW[:, 128:256]
            dv = lambda o, a_, b_, op: fw.op("dve", lambda g: g.tensor_tensor(out=o, in0=a_, in1=b_, op=op), [kpW, k_p, kgs], [kgs])
            dv(gs[:, 2], W0, cT_, ALU.mult); dv(gs[:, 3], W1, sT_, ALU.mult); dv(gs[:, 0], gs[:, 2], gs[:, 3], ALU.add)
            dv(gs[:, 2], W1, cT_, ALU.mult); dv(gs[:, 3], W0, sT_, ALU.mult); dv(gs[:, 1], gs[:, 2], gs[:, 3], ALU.subtract)
            Rb, kRb = Rbr.get()
            fw.op("pool", lambda g: g.tensor_scalar(out=Rb, in0=self.ones_f, scalar1=R[:, gp:gp + 1], scalar2=0.0, op0=ALU.mult, op1=ALU.add),
                  [kc_, k_p], [kRb])
            for ri in range(2):
                fw.op("dve", lambda g: g.tensor_tensor_scan(out=gs[:, 2 + ri], data0=Rb, data1=gs[:, ri], initial=0.0, op0=ALU.mult, op1=ALU.add),
                      [kgs, kRb], [kgs])
            Sp, kSp = Spr.get()
            n1 = slice(0, 127)
            dv2 = lambda o, a_, b_, op, wr: fw.op("dve", lambda g: g.tensor_tensor(out=o, in0=a_, in1=b_, op=op), [k_p, kgs] + wr, [kgs] + wr)
            for (half, gl) in ((slice(0, 64), 0), (slice(64, 128), 1)):
                dv2(gs[half, 0, n1], gs[half, 2, n1], cT_[half, n1], ALU.mult, [])
                dv2(gs[half, 1, n1], gs[half, 3, n1], sT_[half, n1], ALU.mult, [])
                dv2(Sp[half, gl, 0, 1:128], gs[half, 0, n1], gs[half, 1, n1], ALU.subtract, [kSp])
                dv2(gs[half, 0, n1], gs[half, 2, n1], sT_[half, n1], ALU.mult, [])
                dv2(gs[half, 1, n1], gs[half, 3, n1], cT_[half, n1], ALU.mult, [])
                dv2(Sp[half, gl, 1, 1:128], gs[half, 0, n1], gs[half, 1, n1], ALU.add, [kSp])
            for gl in range(2):
                g_ = 2 * gp + gl
                rows = slice(gl * 64, (gl + 1) * 64)
                UT, kUT = UTs[gl]
                pK, kpK = self.psum.get()
                for ri in range(2):
                    lhs = BB[:, gl, ri].rearrange("p s h -> p (s h)")
                    self.mm(pK, lhs, CAP[:, ri, 0:32, :].rearrange("p m h -> p (m h)"), ri == 0, ri == 1, [kBB, kCA], [kpK])
                TBf_, kTBf = TBf.get()
                fw.op("dve", lambda g: g.tensor_scalar(out=TBf_, in0=pK, scalar1=self.rowmask[:, 0:1], scalar2=None, op0=ALU.mult), [kpK, kc_], [kTBf])
                for s_ in range(1, 8):
                    fw.op("dve", lambda g: g.scalar_tensor_tensor(out=TBf_[:, 16 * s_:512], in0=pK[:, 0:512 - 16 * s_], scalar=self.rowmask[:, s_:s_ + 1],
                                                                  in1=TBf_[:, 16 * s_:512], op0=ALU.mult, op1=ALU.add), [kpK, kc_, kTBf], [kTBf])
                TB_, kTB = TBr.get()
                self.copy("act", TB_, TBf_, [kTBf], [kTB])
                pY, kpY = self.psacc.get()
                for a in range(4):
                    self.mm(pY[:, 128 * a:512], UT[:, a, :], TB_[:, 0:512 - 128 * a], a == 0, False, [kUT, kTB], [kpY])
                for ri in range(2):
                    self.mm(pY, Sp[:, gl, ri, :], CAP[:, ri, 1:33, :].rearrange("p m h -> p (m h)"), False, ri == 1, [kSp, kCA], [kpY])
                yp, kyp = ypr.get()
                fw.op("pool", lambda g: g.tensor_tensor(out=yp, in0=U[:, g_, :, :],
                                                        in1=drep[:, 16 * g_:16 * g_ + 16].unsqueeze(1).to_broadcast([128, 32, 16]), op=ALU.mult),
                      k_U + [k_p], [kyp])
                fw.op("dve", lambda g: g.tensor_tensor(out=yp, in0=yp, in1=pY.rearrange("p (t h) -> p t h", h=16), op=ALU.add), [kyp, kpY], [kyp])
                y2, ky2 = y2r.get()
                fw.op("pool", lambda g: g.tensor_tensor(out=y2, in0=yp, in1=yp, op=ALU.mult), [kyp], [ky2])
                fw.op("pool", lambda g: g.tensor_scalar(out=y2, in0=y2, scalar1=0.044715, scalar2=1.0, op0=ALU.mult, op1=ALU.add), [ky2], [ky2])
                fw.op("pool", lambda g: g.tensor_tensor(out=y2, in0=y2, in1=yp, op=ALU.mult), [ky2, kyp], [ky2])
                fw.op("act", lambda g: g.activation(out=y2, in_=y2, func=AF.Sigmoid, scale=1.5957691216057308), [ky2], [ky2])
                if g_ % 8 == 0:
                    GY, kGY = GYr.get()
                fw.op("dve", lambda g: g.tensor_tensor(out=GY[:, :, (g_ % 8) * 16:(g_ % 8 + 1) * 16], in0=yp, in1=y2, op=ALU.mult), [kyp, ky2], [kGY])
                if g_ % 8 == 7:
                    kt = g_ // 8
                    gt, kgt = gts.get()
                    gtv = gt.rearrange("p (c t) -> p t c", t=32)
                    for b in range(4):
                        pT, kpT = self.psum.get()
                        pTb = pT.bitcast(BF16)
                        for t8 in range(8):
                            fw.op("pe", lambda g: g.transpose(pTb[:, t8 * 128:(t8 + 1) * 128], GY[:, 8 * b + t8, :], self.identb), [kGY, kc_], [kpT])
                        self.copy(self.alt(), gtv[:, 8 * b:8 * b + 8, :], pTb.rearrange("p (t c) -> p t c", t=8), [kpT], [kgt])
                    fw.dma("sp", self.gyT[kt], gt, reads=[kgt], writes=[self.k_gyT[kt]])
        A.close()
        A = Arena(fw)
        wg = A.sb("wglu", [128, 4, 512], BF16)
        bglu = A.sb("bglu", [128, 4], F32)
        k_wg = Trk()
        fw.dma("pool", wg, I["s5_w_glu"][l].rearrange("(kc p) n -> p kc n", p=128), writes=[k_wg])
        fw.dma("sp", bglu, I["s5_bglu"][l], writes=[k_wg])
        gbr = A.ring("gb", 2, [128, 4, 512], BF16)
        sgr = A.ring("sg", 2, [128, 512], F32)
        obr = A.ring("ob", 2, [128, 512], BF16)
        for tb in range(NTB):
            gb, kgb = gbr.get()
            fw.dma("sp", gb, self.gyT[:, :, tb * 512:(tb + 1) * 512].rearrange("k p t -> p k t"), reads=self.k_gyT, writes=[kgb])
            for oc in range(4):
                ps, kp = self.psum.get()
                for kc in range(4):
                    self.mm(ps, wg[:, kc, oc * 128:(oc + 1) * 128], gb[:, kc, :], kc == 0, kc == 3, [k_wg, kgb], [kp])
                sg, ksg = sgr.get()
                fw.op("act", lambda g: g.activation(out=sg, in_=ps, func=AF.Sigmoid, bias=bglu[:, oc:oc + 1]), [kp, k_wg], [ksg])
                ob, kob = obr.get()
                fw.op("dve", lambda g: g.tensor_tensor(out=ob, in0=gb[:, oc, :], in1=sg, op=ALU.mult), [kgb, ksg], [kob])
                fw.dma("sp", self.yT[2, oc * 128:(oc + 1) * 128, tb * 512:(tb + 1) * 512], ob, reads=[kob], writes=[self.k_yT[2][tb]])
        A.close()


    def merge(self, l, res_src, k_res):
        fw = self.fw
        I = self.inp
        A = Arena(fw)
        Y = A.sb("Y", [128, 3, 4, L_SEQ], BF16)
        k_Y = [Trk() for _ in range(3)]
        for r in range(3):
            fw.dma("sp", Y[:, r], self.yT[r].rearrange("(kc p) t -> p kc t", p=128), reads=self.k_yT[r], writes=[k_Y[r]])
        wgr = A.ring("wgt", 2, [128, 3, 8, 128], BF16)
        wbr = A.ring("wbr", 2, [128, 3, 4, 128], BF16)
        bg = A.sb("bg", [128, 3, 8], F32)
        k_bg = Trk()
        fw.dma("sp", bg, I["b_gate_l"][l], writes=[k_bg])
        w_d = I["w_in"][l].rearrange("(kc p) n -> p kc n", p=128)
        wbr_d = I["w_branch"][l].rearrange("r (kc p) d -> p r kc d", p=128)
        gsr = A.ring("gs", 3, [128, 512], F32)
        accr = A.ring("acc", 2, [128, 512], F32)
        mbr = A.ring("mb", 2, [128, 512], BF16)
        def load_mw(dt):
            wg, kwg = wgr.get()
            for r in range(3):
                c0 = 3344 + r * 1024 + dt * 128
                fw.dma("pool", wg[:, r], w_d[:, :, c0:c0 + 128], writes=[kwg])
            wb, kwb = wbr.get()
            for r in range(3):
                fw.dma("pool", wb[:, r], wbr_d[:, r, :, dt * 128:(dt + 1) * 128], writes=[kwb])
            return wg, kwg, wb, kwb
        nxt_mw = load_mw(0)
        for dt in range(8):
            wg, kwg, wb, kwb = nxt_mw
            if dt + 1 < 8:
                nxt_mw = load_mw(dt + 1)
            for tb in range(NTB):
                ts_ = slice(tb * 512, (tb + 1) * 512)
                acc, kacc = accr.get()
                for r in range(3):
                    pg, kpg = self.psum.get()
                    for kc in range(8):
                        self.mm(pg, wg[:, r, kc, :], self.xT[:, kc, ts_], kc == 0, kc == 7, [kwg] + self.k_xT[tb * 4:(tb + 1) * 4], [kpg])
                    gs, kgs = gsr.get()
                    fw.op("act", lambda g: g.activation(out=gs, in_=pg, func=AF.Sigmoid, bias=bg[:, r, dt:dt + 1]), [kpg, k_bg], [kgs])
                    pp, kpp = self.psum.get()
                    for kc in range(4):
                        self.mm(pp, wb[:, r, kc, :], Y[:, r, kc, ts_], kc == 0, kc == 3, [kwb, k_Y[r]], [kpp])
                    if r == 0:
                        fw.op("dve", lambda g: g.tensor_tensor(out=acc, in0=gs, in1=pp, op=ALU.mult), [kgs, kpp], [kacc])
                    else:
                        fw.op("dve", lambda g: g.tensor_tensor(out=gs, in0=gs, in1=pp, op=ALU.mult), [kgs, kpp], [kgs])
                        if r == 1:
                            fw.op("pool", lambda g: g.tensor_tensor(out=acc, in0=acc, in1=gs, op=ALU.add), [kacc, kgs], [kacc])
                        else:
                            mb, kmb = mbr.get()
                            fw.op("pool", lambda g: g.tensor_tensor(out=mb, in0=acc, in1=gs, op=ALU.add), [kacc, kgs], [kmb])
                            fw.dma("sp", self.mT[dt, :, ts_], mb, reads=[kmb], writes=[self.k_mT[dt][tb]])
        A.close()
        A = Arena(fw)
        self.alloc_ln(A)
        self.load_ln(I["ln1_g"][l:l + 1, :], I["ln1_b"][l:l + 1, :])
        wo = A.sb("wo", [128, 8, D], BF16)
        k_wo = Trk()
        fw.dma("pool", wo, I["w_out"][l].rearrange("(kc p) d -> p kc d", p=128), writes=[k_wo])
        mir = A.ring("mi", 2, [128, 8, 512], BF16)
        for tb in range(NTB):
            mi, kmi = mir.get()
            fw.dma("sp", mi, self.mT[:, :, tb * 512:(tb + 1) * 512].rearrange("k p t -> p k t"),
                   reads=[self.k_mT[dt][tb] for dt in range(8)], writes=[kmi])
            for ts in range(4):
                tt = tb * 4 + ts
                p0, k0 = self.psum.get()
                p1, k1 = self.psum.get()
                for h, (pp, kk) in enumerate(((p0, k0), (p1, k1))):
                    for kc in range(8):
                        self.mm(pp, mi[:, kc, ts * 128:(ts + 1) * 128], wo[:, kc, h * 512:(h + 1) * 512], kc == 0, kc == 7, [kmi, k_wo], [kk])
                self.resid_ln(tt, (p0, p1), (k0, k1), res_src, k_res[tt], self.xr, self.k_xr[tt])
        self.ln_flush()
        A.close()

    def setup_ffn(self):
        fw = self.fw
        self.actT = fw.dram("actT", [22, 128, L_SEQ], BF16)
        self.k_actT = [[Trk() for _ in range(NTB)] for _ in range(22)]

    def ffn(self, l, out_dram, k_out):
        fw = self.fw
        I = self.inp
        A = Arena(fw)
        self.alloc_ln(A)
        self.wdown = A.sb("wdown", [128, 22, D], BF16)
        self.k_wdown = Trk("wdown")
        self.wup = A.ring("wup", 2, [128, 8, 256], BF16)
        self.fcw = A.sb("fcw", [128, 44, 3], F32)
        self.fcb = A.sb("fcb", [128, 44], F32)
        self.k_fc = Trk("fc")
        self.sv = A.ring("sv", 3, [128, 514], BF16)
        self.sg = A.ring("sg", 3, [128, 514], BF16)
        self.dg = A.ring("dg", 2, [128, 6, 128], BF16)
        self.hg = A.ring("hg", 2, [128, 512], F32)
        self.actb = A.ring("actb", 3, [128, 512], BF16)
        self.actin = A.ring("actin", 2, [128, 22, 256], BF16)
        wup_d = I["ffn_w_up"][l].rearrange("(kc p) n -> p kc n", p=128)
        fw.dma("sp", self.fcw, I["ffn_cw"][l], writes=[self.k_fc])
        fw.dma("sp", self.fcb, I["ffn_cb"][l], writes=[self.k_fc])
        k_all_xT = self.k_xT
        def load_w(m):
            wb, kw = self.wup.get()
            fw.dma("pool", wb[:, :, 0:128], wup_d[:, :, m * 128:(m + 1) * 128], writes=[kw])
            fw.dma("pool", wb[:, :, 128:256], wup_d[:, :, DFF + m * 128:DFF + (m + 1) * 128], writes=[kw])
            return wb, kw
        nxt_w = load_w(0)
        for m in range(22):
            wb, kw = nxt_w
            if m + 1 < 22:
                nxt_w = load_w(m + 1)
            dg, kdg = self.dg.get()
            for t in range(3):
                fw.op("pool", lambda g: g.tensor_scalar(out=dg[:, t, :], in0=self.identf, scalar1=self.fcw[:, m, t:t + 1], scalar2=0.0,
                                                        op0=ALU.mult, op1=ALU.add), reads=[self.k_ident, self.k_fc], writes=[kdg])
                fw.op("pool", lambda g: g.tensor_scalar(out=dg[:, 3 + t, :], in0=self.identf, scalar1=self.fcw[:, 22 + m, t:t + 1], scalar2=0.0,
                                                        op0=ALU.mult, op1=ALU.add), reads=[self.k_ident, self.k_fc], writes=[kdg])
            prev = None
            pend = None

            def conv(st):
                sv, ksv, sg, ksg, tb = st
                cv, kcv = self.psum.get()
                for t in range(3):
                    self.mm(cv, dg[:, t, :], sv[:, t:t + 512], t == 0, t == 2, [kdg, ksv], [kcv])
                cg, kcg = self.psum.get()
                for t in range(3):
                    self.mm(cg, dg[:, 3 + t, :], sg[:, t:t + 512], t == 0, t == 2, [kdg, ksg], [kcg])
                hg, khg = self.hg.get()
                fw.op("act", lambda g: g.activation(out=hg, in_=cg, func=AF.Silu, bias=self.fcb[:, 22 + m:23 + m]),
                      reads=[kcg, self.k_fc], writes=[khg])
                ab, kab = self.actb.get()
                fw.op("dve", lambda g: g.scalar_tensor_tensor(out=ab, in0=cv, scalar=self.fcb[:, m:m + 1], in1=hg,
                                                              op0=ALU.add, op1=ALU.mult), reads=[kcv, self.k_fc, khg], writes=[kab])
                fw.dma("sp", self.actT[m, :, tb * 512:(tb + 1) * 512], ab, reads=[kab], writes=[self.k_actT[m][tb]])
            for tb in range(NTB):
                xk = k_all_xT[tb * 4:(tb + 1) * 4]
                pv, kpv = self.psum.get()
                for kc in range(8):
                    self.mm(pv, wb[:, kc, 0:128], self.xT[:, kc, tb * 512:(tb + 1) * 512], kc == 0, kc == 7, [kw] + xk, [kpv])
                pg, kpg = self.psum.get()
                for kc in range(8):
                    self.mm(pg, wb[:, kc, 128:256], self.xT[:, kc, tb * 512:(tb + 1) * 512], kc == 0, kc == 7, [kw] + xk, [kpg])
                if pend is not None:
                    conv(pend)
                sv, ksv = self.sv.get()
                sg, ksg = self.sg.get()
                self.copy("act", sv[:, 2:514], pv, [kpv], [ksv])
                self.copy("dve", sg[:, 2:514], pg, [kpg], [ksg])
                if prev is None:
                    fw.op("pool", lambda g: g.memset(sv[:, 0:2], 0.0), writes=[ksv])
                    fw.op("pool", lambda g: g.memset(sg[:, 0:2], 0.0), writes=[ksg])
                else:
                    psv, pksv, psg, pksg = prev
                    fw.op("pool", lambda g: g.tensor_copy(out=sv[:, 0:2], in_=psv[:, 512:514]), reads=[pksv], writes=[ksv])
                    fw.op("pool", lambda g: g.tensor_copy(out=sg[:, 0:2], in_=psg[:, 512:514]), reads=[pksg], writes=[ksg])
                prev = (sv, ksv, sg, ksg)
                pend = (sv, ksv, sg, ksg, tb)
            conv(pend)
        fw.dma("pool", self.wdown, I["ffn_w_down"][l].rearrange("(kt p) d -> p kt d", p=128), writes=[self.k_wdown])
        self.load_ln(I["ln2_g"][l:l + 1, :], I["ln2_b"][l:l + 1, :])
        for tb2 in range(2 * NTB):
            tb = tb2 // 2
            ai, kai = self.actin.get()
            fw.dma("sp", ai, self.actT[:, :, tb2 * 256:(tb2 + 1) * 256].rearrange("m p t -> p m t"),
                   reads=[self.k_actT[m][tb] for m in range(22)], writes=[kai])
            for ts in range(2):
                tt = tb2 * 2 + ts
                p0, k0 = self.psum.get()
                p1, k1 = self.psum.get()
                for h, (pp, kk) in enumerate(((p0, k0), (p1, k1))):
                    for m in range(22):
                        self.mm(pp, ai[:, m, ts * 128:(ts + 1) * 128], self.wdown[:, m, h * 512:(h + 1) * 512], m == 0, m == 21,
                                [kai, self.k_wdown], [kk])
                self.resid_ln(tt, (p0, p1), (k0, k1), self.xr, self.k_xr[tt], out_dram, k_out[tt])
        self.ln_flush()
        A.close()


def _host_layout(inputs):
    f = {}
    A = lambda a: np.ascontiguousarray(np.asarray(a, dtype=np.float32))
    for k in ("w_in", "ffn_w_up", "ffn_w_down", "ln1_g", "ln1_b", "ln2_g", "ln2_b", "w_out", "w_branch", "s5_w_glu"):
        f[k] = A(inputs[k])
    cw = A(inputs["ffn_conv_w"])
    sw = A(inputs["ssd_conv_w"])
    f["ssd_cw"] = A(sw.reshape(DEPTH, 4, 6, 128).transpose(0, 3, 2, 1))
    f["ssd_cb"] = A(A(inputs["ssd_conv_b"]).reshape(DEPTH, 6, 128).transpose(0, 2, 1))
    for k in ("ssd_conv_b", "ssd_dt_bias", "ssd_a_log", "ssd_d", "ssd_norm_w", "fox_f_bias"):
        f[k] = A(inputs[k])
    pl = lambda a: A(a.reshape(DEPTH, 16, 2, 64).transpose(0, 2, 3, 1).reshape(DEPTH, 128, 16))
    f["s5_are"] = pl(A(inputs["s5_a_re"]))
    f["s5_aim"] = pl(A(inputs["s5_a_im"]))
    f["s5_lst"] = pl(np.repeat(A(inputs["s5_log_step"])[:, :, None], 64, axis=2))
    pb_ = lambda a: A(a.reshape(DEPTH, 16, 2, 64, 16).transpose(0, 2, 3, 1, 4).reshape(DEPTH, 128, 16, 16))
    f["s5_bre"] = pb_(A(inputs["s5_b_re"]))
    f["s5_bim"] = pb_(A(inputs["s5_b_im"]))
    f["s5_cre"] = pb_(A(A(inputs["s5_c_re"]).transpose(0, 1, 3, 2)))
    f["s5_cim"] = pb_(A(A(inputs["s5_c_im"]).transpose(0, 1, 3, 2)))
    f["s5_d"] = A(inputs["s5_d"])
    f["s5_bglu"] = A(A(inputs["s5_b_glu"]).reshape(DEPTH, 4, 128).transpose(0, 2, 1))
    f["b_gate_l"] = A(A(inputs["b_gate"]).reshape(DEPTH, 3, 8, 128).transpose(0, 3, 1, 2))
    f["ffn_cw"] = A(cw.reshape(DEPTH, 3, 44, 128).transpose(0, 3, 2, 1))
    f["ffn_cb"] = A(A(inputs["ffn_conv_b"]).reshape(DEPTH, 44, 128).transpose(0, 2, 1))
    return f


def build_ffn_test():
    k = K()
    x = k.din("x", [L_SEQ, D])
    k.din("ffn_w_up", [DEPTH, D, 2 * DFF]); k.din("ffn_w_down", [DEPTH, DFF, D])
    k.din("ffn_cw", [DEPTH, 128, 44, 3]); k.din("ffn_cb", [DEPTH, 128, 44])
    k.din("ln2_g", [DEPTH, D]); k.din("ln2_b", [DEPTH, D])
    out = k.nc.dram_tensor("out", [L_SEQ, D], F32, kind="ExternalOutput").ap()
    k.setup_common()
    k.setup_ffn()
    k.phase0(x)
    k.xr = x
    k_out = [Trk() for _ in range(NTT)]
    k.ffn(0, out, k_out)
    k.fw.drain()
    return k


def build_fox_test():
    k = K()
    x = k.din("x", [L_SEQ, D])
    k.din("w_in", [DEPTH, D, DIN]); k.din("fox_f_bias", [DEPTH, 8])
    out = k.nc.dram_tensor("out", [512, L_SEQ], BF16, kind="ExternalOutput").ap()
    k.setup_common()
    k.phase0(x)
    k.fox(0)
    t = k.fw.sb("cp", [128, 4, L_SEQ], BF16)
    kt = Trk()
    k.fw.dma("sp", t, k.yT[0].rearrange("(c p) t -> p c t", p=128), reads=[x for r in k.k_yT[0] for x in [r]], writes=[kt])
    k.fw.dma("sp", out.rearrange("(c p) t -> p c t", p=128), t, reads=[kt], writes=[Trk()])
    k.fw.drain()
    return k


def build_ssd_test(dbg=None):
    k = K(dbg=dbg)
    x = k.din("x", [L_SEQ, D])
    k.din("w_in", [DEPTH, D, DIN])
    k.din("ssd_cw", [DEPTH, 128, 6, 4]); k.din("ssd_cb", [DEPTH, 128, 6]); k.din("ssd_conv_b", [DEPTH, 768])
    k.din("ssd_dt_bias", [DEPTH, 8]); k.din("ssd_a_log", [DEPTH, 8]); k.din("ssd_d", [DEPTH, 8]); k.din("ssd_norm_w", [DEPTH, 512])
    out = k.nc.dram_tensor("out", [512, L_SEQ], BF16, kind="ExternalOutput").ap()
    k.setup_common()
    k.phase0(x)
    k.ssd(0)
    t = k.fw.sb("cp", [128, 4, L_SEQ], BF16)
    kt = Trk()
    k.fw.dma("sp", t, k.yT[1].rearrange("(c p) t -> p c t", p=128), reads=list(k.k_yT[1]), writes=[kt])
    k.fw.dma("sp", out.rearrange("(c p) t -> p c t", p=128), t, reads=[kt], writes=[Trk()])
    k.fw.drain()
    return k


S5_INS = [("s5_are", [DEPTH, 128, 16]), ("s5_aim", [DEPTH, 128, 16]), ("s5_lst", [DEPTH, 128, 16]),
          ("s5_bre", [DEPTH, 128, 16, 16]), ("s5_bim", [DEPTH, 128, 16, 16]), ("s5_cre", [DEPTH, 128, 16, 16]),
          ("s5_cim", [DEPTH, 128, 16, 16]), ("s5_d", [DEPTH, 512]), ("s5_bglu", [DEPTH, 128, 4]), ("s5_w_glu", [DEPTH, 512, 512])]


def build_s5_test(dbg=None):
    k = K(dbg=dbg)
    x = k.din("x", [L_SEQ, D])
    k.din("w_in", [DEPTH, D, DIN])
    for n, shp in S5_INS:
        k.din(n, shp)
    out = k.nc.dram_tensor("out", [512, L_SEQ], BF16, kind="ExternalOutput").ap()
    k.setup_common()
    k.phase0(x)
    k.s5(0)
    t = k.fw.sb("cp", [128, 4, L_SEQ], BF16)
    kt = Trk()
    k.fw.dma("sp", t, k.yT[2].rearrange("(c p) t -> p c t", p=128), reads=list(k.k_yT[2]), writes=[kt])
    k.fw.dma("sp", out.rearrange("(c p) t -> p c t", p=128), t, reads=[kt], writes=[Trk()])
    k.fw.drain()
    return k


ALL_INS = [("w_in", [DEPTH, D, DIN]), ("fox_f_bias", [DEPTH, 8]),
           ("ssd_cw", [DEPTH, 128, 6, 4]), ("ssd_cb", [DEPTH, 128, 6]), ("ssd_conv_b", [DEPTH, 768]),
           ("ssd_dt_bias", [DEPTH, 8]), ("ssd_a_log", [DEPTH, 8]), ("ssd_d", [DEPTH, 8]), ("ssd_norm_w", [DEPTH, 512])] + S5_INS + [
           ("w_branch", [DEPTH, 3, 512, D]), ("b_gate_l", [DEPTH, 128, 3, 8]), ("w_out", [DEPTH, D, D]),
           ("ln1_g", [DEPTH, D]), ("ln1_b", [DEPTH, D]),
           ("ffn_w_up", [DEPTH, D, 2 * DFF]), ("ffn_w_down", [DEPTH, DFF, D]), ("ffn_cw", [DEPTH, 128, 44, 3]), ("ffn_cb", [DEPTH, 128, 44]),
           ("ln2_g", [DEPTH, D]), ("ln2_b", [DEPTH, D])]


def build_full(nl=DEPTH, stop=None):
    k = K()
    x = k.din("x", [L_SEQ, D])
    for n, shp in ALL_INS:
        k.din(n, shp)
    out = k.nc.dram_tensor("out", [L_SEQ, D], F32, kind="ExternalOutput").ap()
    k.setup_common()
    k.setup_ffn()
    k.phase0(x)
    k_dummy = [Trk() for _ in range(NTT)]
    k_out = [Trk() for _ in range(NTT)]
    for l in range(nl):
        k.fox(l)
        k.ssd(l)
        k.s5(l)
        k.merge(l, x if l == 0 else k.xr, k_dummy if l == 0 else k.k_xr)
        if stop == "merge":
            break
        last = (l == nl - 1)
        k.ffn(l, out if last else k.xr, k_out if last else k.k_xr)
    if stop == "merge":
        A = Arena(k.fw)
        r = A.ring("dump", 2, [128, D], F32)
        for tt in range(NTT):
            t, kt = r.get()
            k.fw.dma("sp", t, k.xr[tt * 128:(tt + 1) * 128, :], reads=[k.k_xr[tt]], writes=[kt])
            k.fw.dma("sp", out[tt * 128:(tt + 1) * 128, :], t, reads=[kt], writes=[k_out[tt]])
    k.fw.drain()
    return k


def build_l1_test():
    return build_full(1)


def build_m1_test():
    return build_full(1, stop="merge")


_CACHE = {}


def kernel(**inputs):
    f = _host_layout(inputs)
    if "k" not in _CACHE:
        _CACHE["k"] = build_full(DEPTH)
    k = _CACHE["k"]
    x = np.ascontiguousarray(np.asarray(inputs["x"], dtype=np.float32))
    nb = x.shape[0]
    in_maps = []
    for b in range(nb):
        m = {"x": x[b]}
        for n, _ in ALL_INS:
            m[n] = f[n]
        in_maps.append(m)
    res = run_bass_kernel_spmd(k.nc, in_maps, core_ids=list(range(nb)))
    return np.stack([np.asarray(r["out"], dtype=np.float32) for r in res.results], axis=0)


def build_l4_test():
    return build_full(4)


def build_l2_test():
    return build_full(2)
```

```python
import numpy as np
import concourse.bass as bass
import concourse.mybir as mybir
from concourse.bass_utils import run_bass_kernel_spmd
from contextlib import ExitStack

F32 = mybir.dt.float32
BF16 = mybir.dt.bfloat16
I32 = mybir.dt.int32
AF = mybir.ActivationFunctionType
ALU = mybir.AluOpType

L_SEQ = 4096
D = 1024
DEPTH = 4
DFF = 2816
DIN = 6416
ALPHA = (2 * DEPTH) ** 0.25
NTT = L_SEQ // 128
NTB = L_SEQ // 512
NOSAME = ()


class Trk:
    __slots__ = ("w", "rs", "name")

    def __init__(self, name=""):
        self.w = None
        self.rs = []
        self.name = name


class Fw:
    NDMA = 48

    def __init__(self, nc):
        self.nc = nc
        self.eng = {"pe": nc.tensor, "act": nc.scalar, "dve": nc.vector,
                    "pool": nc.gpsimd, "sp": nc.sync}
        self.sem = {k: nc.alloc_semaphore("s_" + k) for k in self.eng}
        self.cnt = {k: 0 for k in self.eng}
        self.seen = {k: {} for k in self.eng}
        self.dsem = [nc.alloc_semaphore("d%d" % i) for i in range(self.NDMA)]
        self.dval = [0] * self.NDMA
        self.dnext = {"sp": 0, "pool": 0}
        self.drange = {"sp": (0, 32), "pool": (32, self.NDMA)}
        self.nwait = 0
        self.ninst = 0
        self.nosame = set(NOSAME)

    def sb(self, name, shape, dt):
        return self.nc.alloc_sbuf_tensor(name, list(shape), dt).ap()

    def ps(self, name, shape, dt=F32):
        return self.nc.alloc_psum_tensor(name, list(shape), dt).ap()

    def dram(self, name, shape, dt, kind="Internal"):
        return self.nc.dram_tensor(name, list(shape), dt, kind=kind).ap()

    def _need(self, e, ev):
        if ev is None:
            return
        key, val = ev
        if key == e and (e == "pe" or e in self.nosame):
            return
        if self.seen[e].get(key, 0) >= val:
            return
        sem = self.sem[key] if isinstance(key, str) else self.dsem[key]
        self.eng[e].wait_ge(sem, val)
        self.nwait += 1
        self.seen[e][key] = val

    def _deps(self, e, reads, writes):
        for t in reads:
            self._need(e, t.w)
        for t in writes:
            self._need(e, t.w)
            for r in t.rs:
                self._need(e, r)

    def _commit(self, ev, reads, writes):
        for t in reads:
            t.rs.append(ev)
            if len(t.rs) > 48:
                d = {}
                for k, v in t.rs:
                    if d.get(k, 0) < v:
                        d[k] = v
                t.rs = list(d.items())
        for t in writes:
            t.w = ev
            t.rs = []

    def op(self, e, fn, reads=(), writes=(), inc=True):
        self._deps(e, reads, writes)
        ins = fn(self.eng[e])
        if inc:
            self.cnt[e] += 1
            ins.then_inc(self.sem[e], 1)
            ev = (e, self.cnt[e])
        else:
            ev = (e, self.cnt[e] + 1)
        self._commit(ev, reads, writes)
        self.ninst += 1
        return ev

    def dma(self, q, out, in_, reads=(), writes=(), **kw):
        lo, hi = self.drange[q]
        i = lo + self.dnext[q]
        self.dnext[q] = (self.dnext[q] + 1) % (hi - lo)
        if self.dval[i] > 0:
            self._need(q, (i, self.dval[i]))
        self._deps(q, reads, writes)
        self.dval[i] += 16
        self.eng[q].dma_start(out=out, in_=in_, **kw).then_inc(self.dsem[i], 16)
        ev = (i, self.dval[i])
        self._commit(ev, reads, writes)
        self.ninst += 1
        return ev

    def barrier(self):
        for i in range(self.NDMA):
            if self.dval[i]:
                self._need("sp", (i, self.dval[i]))
        ins = self.eng["sp"].sem_inc(self.sem["sp"], 1)
        self.cnt["sp"] += 1
        for e in ("pe", "act", "dve", "pool", "sp"):
            for k in ("pe", "act", "dve", "pool", "sp"):
                if k != e and self.cnt[k]:
                    if self.seen[e].get(k, 0) < self.cnt[k]:
                        self.eng[e].wait_ge(self.sem[k], self.cnt[k])
                        self.seen[e][k] = self.cnt[k]
                        self.nwait += 1

    def drain(self):
        for i in range(self.NDMA):
            if self.dval[i]:
                self._need("sp", (i, self.dval[i]))
        for k in ("pe", "act", "dve", "pool"):
            if self.cnt[k]:
                self._need("sp", (k, self.cnt[k]))


class Arena:
    def __init__(self, fw):
        self.fw = fw
        self.st = ExitStack()

    _uid = [0]

    def sb(self, name, shape, dt):
        Arena._uid[0] += 1
        return self.st.enter_context(self.fw.nc.sbuf_tensor("%s_%d" % (name, Arena._uid[0]), list(shape), dt)).ap()

    def ring(self, name, n, shape, dt):
        return Ring(self.fw, name, n, shape, dt, mk=self.sb)

    def close(self):
        self.fw.barrier()
        self.st.close()


class Ring:
    def __init__(self, fw, name, n, shape, dt, psum=False, mk=None):
        if mk is None:
            mk = fw.ps if psum else fw.sb
        self.b = [mk("%s%d" % (name, i), shape, dt) for i in range(n)]
        self.k = [Trk("%s%d" % (name, i)) for i in range(n)]
        self.i = 0
        self.n = n

    def get(self):
        i = self.i
        self.i = (i + 1) % self.n
        return self.b[i], self.k[i]


class K:
    def __init__(self, nlayers=DEPTH, dbg=None):
        self.nl = nlayers
        self.dbg = dbg or {}
        nc = self.nc = bass.Bass("TRN2", target_bir_lowering=False)
        fw = self.fw = Fw(nc)
        self.inp = {}
        self.ev_alt = 0

    def din(self, name, shape, dt=F32):
        ap = self.nc.dram_tensor(name, list(shape), dt, kind="ExternalInput").ap()
        self.inp[name] = ap
        return ap

    def alt(self):
        self.ev_alt ^= 1
        return "act" if self.ev_alt else "dve"

    def copy(self, e, out, in_, reads, writes):
        if e == "act":
            return self.fw.op("act", lambda g: g.activation(out=out, in_=in_, func=AF.Copy), reads, writes)
        return self.fw.op(e, lambda g: g.tensor_copy(out=out, in_=in_), reads, writes)

    def mm(self, out, lhsT, rhs, start, stop, reads, writes):
        return self.fw.op("pe", lambda g: g.matmul(out, lhsT=lhsT, rhs=rhs, start=start, stop=stop), reads, writes, inc=bool(stop))

    def setup_common(self):
        fw = self.fw
        self.psum = Ring(fw, "psb", 6, [128, 512], F32, psum=True)
        self.psacc = Ring(fw, "psa", 2, [128, 512], F32, psum=True)
        self.identb = fw.sb("identb", [128, 128], BF16)
        self.identf = fw.sb("identf", [128, 128], F32)
        self.k_ident = Trk("ident")
        fw.op("pool", lambda g: g.memset(self.identf, 0.0), writes=[self.k_ident])
        fw.op("pool", lambda g: g.affine_select(out=self.identf, in_=self.identf, compare_op=ALU.not_equal, fill=1.0,
                                                base=0, pattern=[[-1, 128]], channel_multiplier=1),
              reads=[self.k_ident], writes=[self.k_ident])
        fw.op("pool", lambda g: g.tensor_copy(out=self.identb, in_=self.identf), reads=[self.k_ident], writes=[self.k_ident])
        self.utri_f = fw.sb("utri_f", [128, 128], F32)
        self.trib = fw.sb("trib", [128, 128], BF16)
        self.ones_f = fw.sb("ones_f", [128, 128], F32)
        self.mean_f = fw.sb("mean_f", [128, 128], F32)
        self.ones_b = fw.sb("ones_b", [128, 128], BF16)
        self.scanmask = fw.sb("scanmask", [128, 8, 32], F32)
        fw.op("pool", lambda g: g.memset(self.utri_f, 1.0), writes=[self.k_ident])
        fw.op("pool", lambda g: g.affine_select(out=self.utri_f, in_=self.utri_f, compare_op=ALU.is_ge, fill=0.0,
                                                base=0, pattern=[[1, 128]], channel_multiplier=-1),
              reads=[self.k_ident], writes=[self.k_ident])
        fw.op("pool", lambda g: g.tensor_copy(out=self.trib, in_=self.utri_f), reads=[self.k_ident], writes=[self.k_ident])
        fw.op("pool", lambda g: g.memset(self.ones_f, 1.0), writes=[self.k_ident])
        fw.op("pool", lambda g: g.memset(self.ones_b, 1.0), writes=[self.k_ident])
        fw.op("pool", lambda g: g.memset(self.mean_f, 1.0 / 128.0), writes=[self.k_ident])
        fw.op("pool", lambda g: g.memset(self.scanmask, 1.0), writes=[self.k_ident])
        fw.op("pool", lambda g: g.memset(self.scanmask[:, :, 0:1], 0.0), writes=[self.k_ident])
        self.ones33 = fw.sb("ones33", [128, 128], BF16)
        fw.op("pool", lambda g: g.memset(self.ones33, 0.0), writes=[self.k_ident])
        for r_ in (0, 32, 64, 96):
            fw.op("pool", lambda g: g.memset(self.ones33[r_:r_ + 1, :], 1.0), writes=[self.k_ident])
        self.nutri_f = fw.sb("nutri_f", [128, 128], F32)
        self.negmask4 = fw.sb("negmask4", [128, 4, 128], F32)
        fw.op("pool", lambda g: g.tensor_scalar(out=self.nutri_f, in0=self.utri_f, scalar1=-1.0, scalar2=0.0, op0=ALU.mult, op1=ALU.add),
              reads=[self.k_ident], writes=[self.k_ident])
        fw.op("pool", lambda g: g.memset(self.negmask4, -30000.0), writes=[self.k_ident])
        for r_ in range(4):
            fw.op("pool", lambda g: g.affine_select(out=self.negmask4[:, r_, :], in_=self.negmask4[:, r_, :], compare_op=ALU.is_gt, fill=0.0,
                                                    base=0, pattern=[[-1, 128]], channel_multiplier=1),
                  reads=[self.k_ident], writes=[self.k_ident])
        self.rowmask = fw.sb("rowmask", [128, 8], F32)
        fw.op("pool", lambda g: g.memset(self.rowmask, 1.0), writes=[self.k_ident])
        fw.op("pool", lambda g: g.affine_select(out=self.rowmask, in_=self.rowmask, compare_op=ALU.is_ge, fill=0.0,
                                                base=0, pattern=[[-16, 8]], channel_multiplier=1), reads=[self.k_ident], writes=[self.k_ident])
        fw.op("pool", lambda g: g.affine_select(out=self.rowmask, in_=self.rowmask, compare_op=ALU.is_ge, fill=0.0,
                                                base=15, pattern=[[16, 8]], channel_multiplier=-1), reads=[self.k_ident], writes=[self.k_ident])
        self.mT = fw.dram("mT", [8, 128, L_SEQ], BF16)
        self.k_mT = [[Trk() for _ in range(NTB)] for _ in range(8)]
        self.gyT = fw.dram("gyT", [4, 128, L_SEQ], BF16)
        self.k_gyT = [Trk() for _ in range(4)]
        self.zd = fw.dram("zd", [L_SEQ, 512], F32)
        self.k_zd = [Trk() for _ in range(NTT)]
        self.yT = fw.dram("yT", [3, 512, L_SEQ], BF16)
        self.k_yT = [[Trk() for _ in range(NTB)] for _ in range(3)]
        self.xT = fw.sb("xT", [128, 8, L_SEQ], BF16)
        self.k_xT = [Trk("xT%d" % i) for i in range(NTT)]
        self.xr = fw.dram("xr", [L_SEQ, D], F32)
        self.k_xr = [Trk("xr%d" % i) for i in range(NTT)]

    def alloc_ln(self, A):
        self.tok32 = A.ring("tok32", 6, [128, D], F32)
        self.tokbf = A.ring("tokbf", 3, [128, D], BF16)
        self.ln_pipe = []
        self.ln_g = A.sb("ln_g", [128, D], F32)
        self.ln_b = A.sb("ln_b", [128, D], F32)
        self.k_lnp = Trk("lnp")
        self.stat = A.ring("stat", 4, [128, 16], F32)

    def to_xT(self, src_bf, k_src, tt):
        fw = self.fw
        ps, kp = self.psum.get()
        psb = ps.bitcast(BF16)
        for c in range(8):
            fw.op("pe", lambda g: g.transpose(psb[:, c * 128:(c + 1) * 128], src_bf[:, c * 128:(c + 1) * 128], self.identb),
                  reads=[k_src, self.k_ident], writes=[kp])
        dst = self.xT[:, :, tt * 128:(tt + 1) * 128]
        self.copy(self.alt(), dst, psb.rearrange("p (c t) -> p c t", c=8), reads=[kp], writes=[self.k_xT[tt]])

    def phase0(self, x_in):
        fw = self.fw
        A = Arena(fw)
        self.alloc_ln(A)
        for tt in range(NTT):
            t32, k32 = self.tok32.get()
            fw.dma("sp", t32, x_in[tt * 128:(tt + 1) * 128, :], writes=[k32])
            tb, kb = self.tokbf.get()
            self.copy(self.alt(), tb, t32, reads=[k32], writes=[kb])
            self.to_xT(tb, kb, tt)
        A.close()

    def load_ln(self, g_ap, b_ap):
        fw = self.fw
        fw.dma("sp", self.ln_g, g_ap.partition_broadcast(128), writes=[self.k_lnp])
        fw.dma("sp", self.ln_b, b_ap.partition_broadcast(128), writes=[self.k_lnp])

    def resid_ln(self, tt, ps_halves, k_ps, res_src, k_res, out_dram, k_out):
        fw = self.fw
        r32, kr = self.tok32.get()
        fw.dma("sp", r32, res_src[tt * 128:(tt + 1) * 128, :], reads=[k_res], writes=[kr])
        t32, kt = self.tok32.get()
        for h in range(2):
            fw.op("dve", lambda g: g.scalar_tensor_tensor(out=t32[:, h * 512:(h + 1) * 512], in0=r32[:, h * 512:(h + 1) * 512],
                                                          scalar=float(ALPHA), in1=ps_halves[h], op0=ALU.mult, op1=ALU.add),
                  reads=[kr, k_ps[h]], writes=[kt])
        st, ks = self.stat.get()
        for h in range(2):
            fw.op("dve", lambda g: g.bn_stats(out=st[:, h * 6:(h + 1) * 6], in_=t32[:, h * 512:(h + 1) * 512]), reads=[kt], writes=[ks])
        fw.op("dve", lambda g: g.bn_aggr(out=st[:, 12:14], in_=st[:, 0:12]), reads=[ks], writes=[ks])
        fw.op("dve", lambda g: g.tensor_scalar(out=st[:, 14:15], in0=st[:, 13:14], scalar1=1e-5, scalar2=None, op0=ALU.add), reads=[ks], writes=[ks])
        fw.op("act", lambda g: g.activation(out=st[:, 14:15], in_=st[:, 14:15], func=AF.Sqrt), reads=[ks], writes=[ks])
        self.ln_pipe.append(dict(tt=tt, r32=r32, kr=kr, t32=t32, kt=kt, st=st, ks=ks, out=out_dram, k_out=k_out))
        if len(self.ln_pipe) >= 2:
            self.ln_stage_b(self.ln_pipe[-2])
        if len(self.ln_pipe) >= 3:
            self.ln_stage_c(self.ln_pipe[-3])

    def ln_stage_b(self, p):
        fw = self.fw
        t32, kt, r32, kr, st, ks, tt = p["t32"], p["kt"], p["r32"], p["kr"], p["st"], p["ks"], p["tt"]
        fw.op("dve", lambda g: g.reciprocal(out=st[:, 15:16], in_=st[:, 14:15]), reads=[ks], writes=[ks])
        fw.op("dve", lambda g: g.tensor_scalar(out=t32, in0=t32, scalar1=st[:, 12:13], scalar2=st[:, 15:16], op0=ALU.subtract, op1=ALU.mult),
              reads=[kt, ks], writes=[kt])
        fw.op("dve", lambda g: g.tensor_tensor(out=t32, in0=t32, in1=self.ln_g, op=ALU.mult), reads=[kt, self.k_lnp], writes=[kt])
        fw.op("dve", lambda g: g.tensor_tensor(out=r32, in0=t32, in1=self.ln_b, op=ALU.add), reads=[kt, self.k_lnp], writes=[kr])
        fw.dma("sp", p["out"][tt * 128:(tt + 1) * 128, :], r32, reads=[kr], writes=[p["k_out"]])
        tb, kb = self.tokbf.get()
        self.copy("act", tb, r32, reads=[kr], writes=[kb])
        p["tb"], p["kb"] = tb, kb

    def ln_stage_c(self, p):
        self.to_xT(p["tb"], p["kb"], p["tt"])

    def ln_flush(self):
        n = len(self.ln_pipe)
        if n >= 1:
            self.ln_stage_b(self.ln_pipe[-1])
        if n >= 2:
            self.ln_stage_c(self.ln_pipe[-2])
        if n >= 1:
            self.ln_stage_c(self.ln_pipe[-1])
        self.ln_pipe = []

    def fox(self, l):
        fw = self.fw
        I = self.inp
        A = Arena(fw)
        kc_ = self.k_ident
        win = A.ring("win", 2, [128, 8, 528], BF16)
        vaug = A.sb("vaug", [128, 32, 8, 65], BF16)
        k_v = [Trk() for _ in range(NTT)]
        FL = A.sb("FL", [128, 32, 8], F32)
        k_FL = Trk()
        qkr = A.ring("qk", 2, [128, 2, L_SEQ], BF16)
        w_d = I["w_in"][l].rearrange("(kc p) n -> p kc n", p=128)
        fw.op("pool", lambda g: g.memset(vaug[:, :, :, 64:65], 1.0), writes=k_v)

        def proj_qk(hp):
            qk, _ = qkr.get()
            kq = [Trk() for _ in range(NTB)]
            kk = [Trk() for _ in range(NTB)]
            wb, kw = win.get()
            fw.dma("pool", wb[:, :, 0:128], w_d[:, :, hp * 128:(hp + 1) * 128], writes=[kw])
            fw.dma("pool", wb[:, :, 128:256], w_d[:, :, 512 + hp * 128:512 + (hp + 1) * 128], writes=[kw])
            for which, kd in ((0, kq), (1, kk)):
                for tb in range(NTB):
                    ps, kp = self.psum.get()
                    for kc in range(8):
                        self.mm(ps, wb[:, kc, which * 128:(which + 1) * 128], self.xT[:, kc, tb * 512:(tb + 1) * 512], kc == 0, kc == 7,
                                [kw] + self.k_xT[tb * 4:(tb + 1) * 4], [kp])
                    o = qk[:, which, tb * 512:(tb + 1) * 512]
                    if which == 1:
                        self.copy(self.alt(), o, ps, [kp], [kd[tb]])
                    elif self.alt() == "act":
                        fw.op("act", lambda g: g.activation(out=o, in_=ps, func=AF.Copy, scale=0.125), [kp], [kd[tb]])
                    else:
                        fw.op("dve", lambda g: g.tensor_scalar(out=o, in0=ps, scalar1=0.125, scalar2=None, op0=ALU.mult), [kp], [kd[tb]])
            return qk, kq, kk
        wb, kw = win.get()
        fw.dma("pool", wb[:, :, 0:520], w_d[:, :, 1024:1544], writes=[kw])
        for tt in range(NTT):
            ps, kp = self.psum.get()
            for kc in range(8):
                self.mm(ps, self.xT[:, kc, tt * 128:(tt + 1) * 128], wb[:, kc, 0:512], kc == 0, kc == 7, [kw, self.k_xT[tt]], [kp])
            ps2, kp2 = self.psum.get()
            for kc in range(8):
                self.mm(ps2[:, 0:8], self.xT[:, kc, tt * 128:(tt + 1) * 128], wb[:, kc, 512:520], kc == 0, kc == 7, [kw, self.k_xT[tt]], [kp2])
            self.copy(self.alt(), vaug[:, tt, :, 0:64], ps.rearrange("p (h d) -> p h d", h=8), [kp], [k_v[tt]])
            self.copy(self.alt(), FL[:, tt, :], ps2[:, 0:8], [kp2], [k_FL])
        fb = A.sb("fb", [128, 8], F32)
        k_t = Trk()
        fw.dma("sp", fb, I["fox_f_bias"][l:l + 1, :].partition_broadcast(128), writes=[k_t])
        nls = A.sb("nls", [128, 32, 8], F32)
        fw.op("dve", lambda g: g.tensor_tensor(out=nls, in0=FL, in1=fb.unsqueeze(1).to_broadcast([128, 32, 8]), op=ALU.add), [k_FL, k_t], [k_t])
        fw.op("act", lambda g: g.activation(out=nls, in_=nls, func=AF.Exp, scale=-1.0), [k_t], [k_t])
        fw.op("act", lambda g: g.activation(out=nls, in_=nls, func=AF.Ln, bias=1.0), [k_t], [k_t])
        nlsf = nls.rearrange("p j h -> p (j h)")
        ps, kp = self.psum.get()
        self.mm(ps[:, 0:256], self.utri_f, nlsf, True, True, [kc_, k_t], [kp])
        ps2, kp2 = self.psum.get()
        self.mm(ps2[:, 0:256], self.ones_f, nlsf, True, True, [kc_, k_t], [kp2])
        totT = A.sb("totT", [128, 8, 32], F32)
        pin = A.sb("pin", [128, 8, 32], F32)
        cumk = A.sb("cumk", [128, 8, 32], F32)
        refp = A.sb("refp", [128, 8, 32], F32)
        k_c = Trk()
        fw.op("dve", lambda g: g.tensor_copy(out=totT, in_=ps2[:, 0:256].rearrange("p (j h) -> p h j", h=8)), [kp2], [k_c])
        fw.op("dve", lambda g: g.tensor_tensor_scan(out=pin.rearrange("p h j -> p (h j)"), data0=self.scanmask.rearrange("p h j -> p (h j)"),
                                                    data1=totT.rearrange("p h j -> p (h j)"), initial=0.0, op0=ALU.mult, op1=ALU.add),
              [k_c, kc_], [k_c])
        fw.op("dve", lambda g: g.tensor_tensor(out=cumk, in0=pin, in1=totT, op=ALU.subtract), [k_c], [k_c])
        fw.op("dve", lambda g: g.tensor_tensor(out=cumk, in0=cumk, in1=ps[:, 0:256].rearrange("p (j h) -> p h j", h=8), op=ALU.add), [k_c, kp], [k_c])
        ps3, kp3 = self.psum.get()
        self.mm(ps3[:, 0:256], self.mean_f, cumk.rearrange("p h j -> p (h j)"), True, True, [kc_, k_c], [kp3])
        fw.op("dve", lambda g: g.tensor_copy(out=refp.rearrange("p h j -> p (h j)"), in_=ps3[:, 0:256]), [kp3], [k_c])
        dsm = A.sb("dsm", [128, 8, 32], F32)
        hs_ = A.sb("hs_", [128, 8, 32], BF16)
        ls_ = A.sb("ls_", [128, 8, 32], BF16)
        v4 = lambda t: t.rearrange("p h (i s) -> p h i s", s=4)
        fw.op("dve", lambda g: g.tensor_tensor(out=v4(dsm), in0=v4(refp)[:, :, :, 0:1].to_broadcast([128, 8, 8, 4]), in1=v4(refp),
                                               op=ALU.subtract), [k_c], [k_c])
        fw.op("dve", lambda g: g.tensor_copy(out=hs_, in_=dsm), [k_c], [k_c])
        fw.op("dve", lambda g: g.tensor_tensor(out=dsm, in0=dsm, in1=hs_, op=ALU.subtract), [k_c], [k_c])
        fw.op("dve", lambda g: g.tensor_copy(out=ls_, in_=dsm), [k_c], [k_c])
        refhl = A.sb("refhl", [128, L_SEQ], BF16)
        k_rh = Trk()
        fw.op("pool", lambda g: g.memset(refhl, 0.0), writes=[k_rh])
        biasr = A.ring("biasT", 4, [128, 32], F32)
        ptr = A.ring("pt", 4, [128, 512], BF16)
        rcp = A.ring("rcp", 2, [128, 512], F32)
        ysb = A.ring("ysb", 2, [64, 512], F32)
        ybf = A.ring("ybf", 2, [64, 512], BF16)
        for hp in range(4):
            qk, k_q, k_k = proj_qk(hp)
            qT = qk[:, 0, :]
            kT = qk[:, 1, :]
            for hh in range(2):
                h = 2 * hp + hh
                r0 = hh * 64
                fw.op("pool", lambda g: g.tensor_copy(out=refhl[r0:r0 + 1, :].rearrange("p (j q) -> p j q", q=128),
                                                      in_=hs_[r0:r0 + 1, h, :].unsqueeze(2).to_broadcast([1, 32, 128])), [k_c, k_rh], [k_rh])
                fw.op("pool", lambda g: g.tensor_copy(out=refhl[r0 + 32:r0 + 33, :].rearrange("p (j q) -> p j q", q=128),
                                                      in_=ls_[r0 + 32:r0 + 33, h, :].unsqueeze(2).to_broadcast([1, 32, 128])), [k_c, k_rh], [k_rh])
            for i in range(NTB):
                for hh in range(2):
                    h = 2 * hp + hh
                    pr = slice(hh * 64, (hh + 1) * 64)
                    bt, kb = biasr.get()
                    fw.op("dve", lambda g: g.tensor_scalar(out=bt, in0=cumk[:, h, :], scalar1=refp[:, h, 4 * i:4 * i + 1], scalar2=None,
                                                           op0=ALU.subtract), [k_c], [kb])
                    po, kpo = self.psacc.get()
                    nj = 4 * i + 4

                    def emitS(j):
                        c0 = max(0, j - 4 * i) * 128
                        ps, kp = self.psum.get()
                        self.mm(ps[:, c0:512], kT[pr, j * 128:(j + 1) * 128], qT[pr, i * 512 + c0:(i + 1) * 512], True, False,
                                [k_k[j // 4], k_q[i]], [kp])
                        rs = slice(hh * 64, hh * 64 + 33)
                        self.mm(ps[:, c0:512], self.ones33[rs, :], refhl[rs, i * 512 + c0:(i + 1) * 512], False, True, [kc_, k_rh], [kp])
                        return ps, kp, c0
                    LA = 3
                    q_ = [emitS(j) for j in range(min(LA, nj))]
                    for j in range(nj):
                        if j + LA < nj:
                            q_.append(emitS(j + LA))
                        ps, kp, c0 = q_.pop(0)
                        pt, kpt = ptr.get()
                        fw.op("act", lambda g: g.activation(out=pt[:, c0:512], in_=ps[:, c0:512], func=AF.Exp, bias=bt[:, j:j + 1]), [kp, kb], [kpt])
                        if j >= 4 * i:
                            fw.op("pool", lambda g: g.tensor_tensor(out=pt[:, c0:c0 + 128], in0=pt[:, c0:c0 + 128], in1=self.trib, op=ALU.mult),
                                  [kpt, kc_], [kpt])
                        self.mm(po[0:65, c0:512], vaug[:, j, h, :], pt[:, c0:512], j == 0, j == nj - 1, [k_v[j], kpt], [kpo])
                    rc, krc = rcp.get()
                    fw.op("dve", lambda g: g.reciprocal(out=rc[64:65, :], in_=po[64:65, :]), [kpo], [krc])
                    pb, kpb = self.psum.get()
                    self.mm(pb[0:64, :], self.ones_f[64:65, 0:64], rc[64:65, :], True, True, [kc_, krc], [kpb])
                    ys, kys = ysb.get()
                    self.copy("act", ys, po[0:64, :], [kpo], [kys])
                    yb, kyb = ybf.get()
                    fw.op("dve", lambda g: g.tensor_tensor(out=yb, in0=ys, in1=pb[0:64, :], op=ALU.mult), [kys, kpb], [kyb])
                    fw.dma("sp", self.yT[0, h * 64:(h + 1) * 64, i * 512:(i + 1) * 512], yb, reads=[kyb], writes=[self.k_yT[0][i]])
        A.close()


    def ssd(self, l):
        fw = self.fw
        I = self.inp
        A = Arena(fw)
        kc_ = self.k_ident
        w_d = I["w_in"][l].rearrange("(kc p) n -> p kc n", p=128)
        XBC = A.sb("xbc", [128, 6, 3 + L_SEQ], BF16)
        k_xbc = [[Trk() for _ in range(NTB)] for _ in range(6)]
        k_pad = Trk()
        fw.op("pool", lambda g: g.memset(XBC[:, :, 0:3], 0.0), writes=[k_pad])
        DT = A.sb("DT", [128, 32, 8], F32)
        k_DT = Trk()
        A2 = Arena(fw)
        win = A2.ring("win", 2, [128, 8, 512], BF16)
        wb, kw = win.get()
        fw.dma("pool", wb[:, :, 0:512], w_d[:, :, 1544:2056], writes=[kw])
        zst = A2.ring("zst", 2, [128, 512], F32)
        for tt in range(NTT):
            ps, kp = self.psum.get()
            for kc in range(8):
                self.mm(ps, self.xT[:, kc, tt * 128:(tt + 1) * 128], wb[:, kc, 0:512], kc == 0, kc == 7, [kw, self.k_xT[tt]], [kp])
            zs, kzs = zst.get()
            self.copy(self.alt(), zs, ps, [kp], [kzs])
            fw.dma("sp", self.zd[tt * 128:(tt + 1) * 128, :], zs, reads=[kzs], writes=[self.k_zd[tt]])
        for gi, (c0, nct, ctb, ncols) in enumerate(((2056, 4, 0, 512), (2568, 2, 4, 264))):
            wb, kw = win.get()
            fw.dma("pool", wb[:, :, 0:ncols], w_d[:, :, c0:c0 + ncols], writes=[kw])
            for m in range(nct):
                ct = ctb + m
                for tb in range(NTB):
                    ps, kp = self.psum.get()
                    for kc in range(8):
                        self.mm(ps, wb[:, kc, m * 128:(m + 1) * 128], self.xT[:, kc, tb * 512:(tb + 1) * 512], kc == 0, kc == 7,
                                [kw] + self.k_xT[tb * 4:(tb + 1) * 4], [kp])
                    self.copy(self.alt(), XBC[:, ct, 3 + tb * 512:3 + (tb + 1) * 512], ps, [kp], [k_xbc[ct][tb]])
            if gi == 1:
                for tt in range(NTT):
                    ps2, kp2 = self.psum.get()
                    for kc in range(8):
                        self.mm(ps2[:, 0:8], self.xT[:, kc, tt * 128:(tt + 1) * 128], wb[:, kc, 256:264], kc == 0, kc == 7, [kw, self.k_xT[tt]], [kp2])
                    self.copy(self.alt(), DT[:, tt, :], ps2[:, 0:8], [kp2], [k_DT])
        A2.close()
        cw = A.sb("cw", [128, 6, 4], F32)
        cb = A.sb("cb", [128, 6], F32)
        k_cp = Trk()
        fw.dma("sp", cw, I["ssd_cw"][l], writes=[k_cp])
        fw.dma("sp", cb, I["ssd_cb"][l], writes=[k_cp])
        dgc = A.sb("dgc", [128, 6, 4, 128], BF16)
        for ct in range(6):
            for t in range(4):
                fw.op("pool", lambda g: g.tensor_scalar(out=dgc[:, ct, t, :], in0=self.identf, scalar1=cw[:, ct, t:t + 1], scalar2=0.0,
                                                        op0=ALU.mult, op1=ALU.add), [kc_, k_cp], [k_cp])
        cbr = A.sb("cbr", [1, 768], F32)
        cbh = A.sb("cbh", [1, 768], BF16)
        cbl = A.sb("cbl", [1, 768], BF16)
        cbt = cbr
        fw.dma("sp", cbr, I["ssd_conv_b"][l:l + 1, :], writes=[k_cp])
        fw.op("dve", lambda g: g.tensor_copy(out=cbh, in_=cbr), [k_cp], [k_cp])
        fw.op("dve", lambda g: g.tensor_tensor(out=cbt, in0=cbr, in1=cbh, op=ALU.subtract), [k_cp], [k_cp])
        fw.op("dve", lambda g: g.tensor_copy(out=cbl, in_=cbt), [k_cp], [k_cp])
        BT = A.sb("BT", [128, 2, L_SEQ], BF16)
        CT = A.sb("CT", [128, L_SEQ], BF16)
        k_BT = [Trk() for _ in range(NTB)]
        k_CT = [Trk() for _ in range(NTB)]
        fw.op("pool", lambda g: g.memset(BT[64:128, 0, :], 0.0), writes=k_BT)
        fw.op("pool", lambda g: g.memset(BT[0:64, 1, :], 0.0), writes=k_BT)
        for ct in (4, 5):
            for tb in range(NTB):
                ps, kp = self.psum.get()
                rd = [k_cp, k_pad, k_xbc[ct][tb]] + ([k_xbc[ct][tb - 1]] if tb else [])
                for t in range(4):
                    self.mm(ps, dgc[:, ct, t, :], XBC[:, ct, tb * 512 + t:tb * 512 + t + 512], t == 0, t == 3, rd, [kp])
                ts_ = slice(tb * 512, (tb + 1) * 512)
                if ct == 5:
                    fw.op("act", lambda g: g.activation(out=CT[:, ts_], in_=ps, func=AF.Silu, bias=cb[:, ct:ct + 1]), [kp, k_cp], [k_CT[tb]])
                else:
                    fw.op("act", lambda g: g.activation(out=BT[0:64, 0, ts_], in_=ps[0:64, :], func=AF.Silu, bias=cb[0:64, ct:ct + 1]), [kp, k_cp], [k_BT[tb]])
                    fw.op("act", lambda g: g.activation(out=BT[64:128, 1, ts_], in_=ps[64:128, :], func=AF.Silu, bias=cb[64:128, ct:ct + 1]), [kp, k_cp], [k_BT[tb]])
        if self.dbg.get('ssd_stop') == 2:
            A.close()
            return
        dtb = A.sb("dtb", [128, 8], F32)
        alog = A.sb("alog", [128, 8], F32)
        dsk = A.sb("dsk", [128, 8], F32)
        nw = A.sb("nw", [128, 512], F32)
        k_p = Trk()
        fw.dma("sp", dtb, I["ssd_dt_bias"][l:l + 1, :].partition_broadcast(128), writes=[k_p])
        fw.dma("sp", alog, I["ssd_a_log"][l:l + 1, :].partition_broadcast(128), writes=[k_p])
        fw.dma("sp", dsk, I["ssd_d"][l:l + 1, :].partition_broadcast(128), writes=[k_p])
        fw.dma("sp", nw, I["ssd_norm_w"][l:l + 1, :].partition_broadcast(128), writes=[k_p])
        fw.op("act", lambda g: g.activation(out=alog, in_=alog, func=AF.Exp), [k_p], [k_p])
        fw.op("dve", lambda g: g.tensor_scalar(out=alog, in0=alog, scalar1=-1.0, scalar2=None, op0=ALU.mult), [k_p], [k_p])
        dt = A.sb("dt", [128, 32, 8], F32)
        adt = A.sb("adt", [128, 32, 8], F32)
        acs = A.sb("acs", [128, 32, 8], F32)
        eacs = A.sb("eacs", [128, 32, 8], F32)
        dtdec = A.sb("dtdec", [128, 32, 8], F32)
        eatot = A.sb("eatot", [128, 32, 8], F32)
        esel = A.sb("esel", [128, 32, 4], F32)
        k_d = Trk()
        fw.op("dve", lambda g: g.tensor_tensor(out=dt, in0=DT, in1=dtb.unsqueeze(1).to_broadcast([128, 32, 8]), op=ALU.add), [k_DT, k_p], [k_d])
        fw.op("act", lambda g: g.activation(out=dt, in_=dt, func=AF.Exp), [k_d], [k_d])
        fw.op("act", lambda g: g.activation(out=dt, in_=dt, func=AF.Ln, bias=1.0), [k_d], [k_d])
        fw.op("dve", lambda g: g.tensor_tensor(out=adt, in0=dt, in1=alog.unsqueeze(1).to_broadcast([128, 32, 8]), op=ALU.mult), [k_d, k_p], [k_d])
        fl = lambda t: t.rearrange("p c h -> p (c h)")
        ps, kp = self.psum.get()
        self.mm(ps[:, 0:256], self.utri_f, fl(adt), True, True, [kc_, k_d], [kp])
        ps2, kp2 = self.psum.get()
        self.mm(ps2[:, 0:256], self.ones_f, fl(adt), True, True, [kc_, k_d], [kp2])
        fw.op("dve", lambda g: g.tensor_copy(out=fl(acs), in_=ps[:, 0:256]), [kp], [k_d])
        fw.op("act", lambda g: g.activation(out=fl(eacs), in_=ps[:, 0:256], func=AF.Exp), [kp], [k_d])
        fw.op("dve", lambda g: g.tensor_tensor(out=fl(dtdec), in0=ps2[:, 0:256], in1=fl(acs), op=ALU.subtract), [kp2, k_d], [k_d])
        fw.op("act", lambda g: g.activation(out=dtdec, in_=dtdec, func=AF.Exp), [k_d], [k_d])
        fw.op("dve", lambda g: g.tensor_tensor(out=dtdec, in0=dtdec, in1=dt, op=ALU.mult), [k_d], [k_d])
        fw.op("act", lambda g: g.activation(out=fl(eatot), in_=ps2[:, 0:256], func=AF.Exp), [kp2], [k_d])
        fw.op("dve", lambda g: g.tensor_copy(out=esel[0:64], in_=eatot[0:64, :, 0:4]), [k_d], [k_d])
        fw.op("dve", lambda g: g.tensor_copy(out=esel[64:128], in_=eatot[64:128, :, 4:8]), [k_d], [k_d])
        if self.dbg.get('ssd_stop') == 3:
            A.close()
            return
        xsr = A.ring("xs", 2, [128, 8, 64], F32)
        bpr = A.ring("bp", 2, [128, 2, 128], BF16)
        for b_ in bpr.b:
            fw.op("pool", lambda g: g.memset(b_, 0.0), writes=bpr.k)
        xdtr = A.ring("xdt", 2, [128, 8, 64], BF16)
        xddr = A.ring("xdd", 2, [128, 8, 64], BF16)
        Ar = A.ring("Aall", 1, [128, 8, 128], F32)
        Dr = A.ring("Dg", 1, [128, 4, 128], F32)
        Mr = A.ring("Mg", 2, [128, 4, 128], BF16)
        S = A.sb("S", [128, 4, 64], F32)
        k_S = Trk()
        fw.op("pool", lambda g: g.memset(S, 0.0), writes=[k_S])
        Sbr = A.ring("Sb", 2, [128, 2, 4, 64], BF16)
        for b_ in Sbr.b:
            fw.op("pool", lambda g: g.memset(b_, 0.0), writes=Sbr.k)
        yr = A.ring("y", 2, [128, 8, 64], F32)
        tmr = A.ring("tm", 1, [128, 8, 64], F32)
        ztr = A.ring("zt", 2, [128, 512], F32)
        ssr = A.ring("ss", 2, [128, 4], F32)
        ybr = A.ring("yb", 2, [128, 512], BF16)
        ytr = A.ring("yt", 2, [128, 4, 128], BF16)
        Sb_prev = None
        for c in range(self.dbg.get('ssd_nchunk', NTT)):
            tb = c // 4
            rdx = lambda ct: [k_cp, k_pad, k_xbc[ct][tb]] + ([k_xbc[ct][tb - 1]] if (tb and c % 4 == 0) else [])
            ps, kp = self.psum.get()
            for ct in range(4):
                o = ps[:, ct * 128:(ct + 1) * 128]
                for t in range(4):
                    self.mm(o, XBC[:, ct, c * 128 + t:c * 128 + t + 128], dgc[:, ct, t, :], t == 0, False, rdx(ct), [kp])
                self.mm(o, self.ones_b[0:1, :], cbh[0:1, ct * 128:(ct + 1) * 128], False, False, [kc_, k_cp], [kp])
                self.mm(o, self.ones_b[0:1, :], cbl[0:1, ct * 128:(ct + 1) * 128], False, True, [kc_, k_cp], [kp])
            psB, kpB = self.psum.get()
            o = psB[:, 0:128]
            for t in range(4):
                self.mm(o, XBC[:, 4, c * 128 + t:c * 128 + t + 128], dgc[:, 4, t, :], t == 0, False, rdx(4), [kpB])
            self.mm(o, self.ones_b[0:1, :], cbh[0:1, 512:640], False, False, [kc_, k_cp], [kpB])
            self.mm(o, self.ones_b[0:1, :], cbl[0:1, 512:640], False, True, [kc_, k_cp], [kpB])
            xs, kxs = xsr.get()
            fw.op("act", lambda g: g.activation(out=xs.rearrange("p h d -> p (h d)"), in_=ps, func=AF.Silu), [kp], [kxs])
            bp, kbp = bpr.get()
            fw.op("act", lambda g: g.activation(out=bp[:, 0, 0:64], in_=psB[:, 0:64], func=AF.Silu), [kpB], [kbp])
            fw.op("act", lambda g: g.activation(out=bp[:, 1, 64:128], in_=psB[:, 64:128], func=AF.Silu), [kpB], [kbp])
            if self.dbg.get('ssd_stop') == 4:
                continue
            xdt, kxdt = xdtr.get()
            xdd, kxdd = xddr.get()
            fw.op("dve", lambda g: g.tensor_tensor(out=xdt, in0=xs, in1=dt[:, c, :].unsqueeze(2).to_broadcast([128, 8, 64]), op=ALU.mult), [kxs, k_d], [kxdt])
            fw.op("pool", lambda g: g.tensor_tensor(out=xdd, in0=xs, in1=dtdec[:, c, :].unsqueeze(2).to_broadcast([128, 8, 64]), op=ALU.mult), [kxs, k_d], [kxdd])
            if self.dbg.get('ssd_stop') == 5:
                continue
            psG, kpG = self.psum.get()
            for g_ in range(self.dbg.get('ssd_ng', 2)):
                self.mm(psG[:, g_ * 128:(g_ + 1) * 128], BT[:, g_, c * 128:(c + 1) * 128], CT[:, c * 128:(c + 1) * 128], True, True,
                        [k_BT[tb], k_CT[tb]], [kpG])
            sub = self.dbg.get('ssd_sub', 99)
            if sub < 1:
                continue
            Aa, kA = Ar.get()
            fw.op("pool", lambda g: g.tensor_tensor(out=Aa, in0=self.ones_f.unsqueeze(1).to_broadcast([128, 8, 128]),
                                                    in1=adt[:, c, :].unsqueeze(2).to_broadcast([128, 8, 128]), op=ALU.mult), [kc_, k_d], [kA])
            if sub < 2:
                continue
            yd, kyd = self.psacc.get()
            for g_ in range(2):
                psS, kpS = self.psum.get()
                for hh in range(4):
                    o_ = psS[:, hh * 128:(hh + 1) * 128]
                    self.mm(o_, Aa[:, 4 * g_ + hh, :], self.utri_f, True, False, [kA, kc_], [kpS])
                    self.mm(o_, self.nutri_f, Aa[:, 4 * g_ + hh, :], False, False, [kA, kc_], [kpS])
                    self.mm(o_, self.identf, self.negmask4[:, 0, :], False, True, [kc_], [kpS])
                if sub < 3:
                    continue
                Dg, kD = Dr.get()
                fw.op("act", lambda g: g.activation(out=Dg.rearrange("p h l -> p (h l)"), in_=psS, func=AF.Exp), [kpS], [kD])
                if sub < 4:
                    continue
                Mg, kM = Mr.get()
                fw.op("dve", lambda g: g.tensor_tensor(out=Mg, in0=Dg, in1=psG[:, g_ * 128:(g_ + 1) * 128].unsqueeze(1).to_broadcast([128, 4, 128]),
                                                       op=ALU.mult), [kD, kpG], [kM])
                if sub < 5:
                    continue
                for hh in range(4):
                    h = 4 * g_ + hh
                    self.mm(yd[:, h * 64:(h + 1) * 64], Mg[:, hh, :], xdt[:, h, :], True, True, [kM, kxdt], [kyd])
            if sub < 99:
                continue
            if self.dbg.get('ssd_stop') == 6:
                continue
            pst, kst = self.psum.get()
            self.mm(pst[:, 0:256], bp[:, 0, :], xdd[:, 0:4, :].rearrange("p h d -> p (h d)"), True, False, [kbp, kxdd], [kst])
            self.mm(pst[:, 0:256], bp[:, 1, :], xdd[:, 4:8, :].rearrange("p h d -> p (h d)"), False, True, [kbp, kxdd], [kst])
            y, ky = yr.get()
            yf = y.rearrange("p h d -> p (h d)")
            if c > 0:
                Sb, kSb = Sb_prev
                yo, kyo = self.psum.get()
                for g_ in range(2):
                    self.mm(yo[:, g_ * 256:(g_ + 1) * 256], CT[:, c * 128:(c + 1) * 128], Sb[:, g_].rearrange("p h d -> p (h d)"), True, True,
                            [k_CT[tb], kSb], [kyo])
                fw.op("dve", lambda g: g.tensor_tensor(out=y, in0=yo.rearrange("p (h d) -> p h d", h=8),
                                                       in1=eacs[:, c, :].unsqueeze(2).to_broadcast([128, 8, 64]), op=ALU.mult), [kyo, k_d], [ky])
                fw.op("dve", lambda g: g.tensor_tensor(out=yf, in0=yf, in1=yd, op=ALU.add), [ky, kyd], [ky])
            else:
                self.copy("dve", yf, yd, [kyd], [ky])
            fw.op("dve", lambda g: g.tensor_tensor(out=S, in0=S, in1=esel[:, c, :].unsqueeze(2).to_broadcast([128, 4, 64]), op=ALU.mult), [k_S, k_d], [k_S])
            fw.op("dve", lambda g: g.tensor_tensor(out=S.rearrange("p h d -> p (h d)"), in0=S.rearrange("p h d -> p (h d)"), in1=pst[:, 0:256], op=ALU.add),
                  [k_S, kst], [k_S])
            Sb, kSb = Sbr.get()
            self.copy("pool", Sb[0:64, 0], S[0:64], [k_S], [kSb])
            self.copy("pool", Sb[64:128, 1], S[64:128], [k_S], [kSb])
            Sb_prev = (Sb, kSb)
            if self.dbg.get('ssd_stop') == 7:
                continue
            tm, ktm = tmr.get()
            fw.op("pool", lambda g: g.tensor_tensor(out=tm, in0=xs, in1=dsk.unsqueeze(2).to_broadcast([128, 8, 64]), op=ALU.mult), [kxs, k_p], [ktm])
            fw.op("pool", lambda g: g.tensor_tensor(out=y, in0=y, in1=tm, op=ALU.add), [ky, ktm], [ky])
            zt, kzt = ztr.get()
            fw.dma("sp", zt, self.zd[c * 128:(c + 1) * 128, :], reads=[self.k_zd[c]], writes=[kzt])
            fw.op("act", lambda g: g.activation(out=zt, in_=zt, func=AF.Silu), [kzt], [kzt])
            fw.op("dve", lambda g: g.tensor_tensor(out=yf, in0=yf, in1=zt, op=ALU.mult), [ky, kzt], [ky])
            if self.dbg.get('ssd_stop') == 8:
                continue
            ss, kss = ssr.get()
            tmf = tm.rearrange("p h d -> p (h d)")
            for g_ in range(2):
                fw.op("act", lambda g: g.activation(out=tmf[:, g_ * 256:(g_ + 1) * 256], in_=yf[:, g_ * 256:(g_ + 1) * 256], func=AF.Square,
                                                    accum_out=ss[:, g_:g_ + 1]), [ky, ktm], [ktm, kss])
            fw.op("dve", lambda g: g.tensor_scalar(out=ss[:, 0:2], in0=ss[:, 0:2], scalar1=1.0 / 256.0, scalar2=1e-5, op0=ALU.mult, op1=ALU.add), [kss], [kss])
            fw.op("act", lambda g: g.activation(out=ss[:, 0:2], in_=ss[:, 0:2], func=AF.Sqrt), [kss], [kss])
            fw.op("dve", lambda g: g.reciprocal(out=ss[:, 2:4], in_=ss[:, 0:2]), [kss], [kss])
            for g_ in range(2):
                fw.op("dve", lambda g: g.tensor_scalar(out=yf[:, g_ * 256:(g_ + 1) * 256], in0=yf[:, g_ * 256:(g_ + 1) * 256],
                                                       scalar1=ss[:, 2 + g_:3 + g_], scalar2=None, op0=ALU.mult), [ky, kss], [ky])
            yb, kyb = ybr.get()
            fw.op("pool", lambda g: g.tensor_tensor(out=yb, in0=yf, in1=nw, op=ALU.mult), [ky, k_p], [kyb])
            if self.dbg.get('ssd_stop') == 9:
                continue
            pT, kpT = self.psum.get()
            pTb = pT.bitcast(BF16)
            for ct in range(4):
                fw.op("pe", lambda g: g.transpose(pTb[:, ct * 128:(ct + 1) * 128], yb[:, ct * 128:(ct + 1) * 128], self.identb), [kyb, kc_], [kpT])
            yt, kyt = ytr.get()
            self.copy("act", yt, pTb[:, 0:512].rearrange("p (c t) -> p c t", c=4), [kpT], [kyt])
            fw.dma("sp", self.yT[1].rearrange("(ct p) t -> p ct t", p=128)[:, :, c * 128:(c + 1) * 128], yt, reads=[kyt], writes=[self.k_yT[1][tb]])
        A.close()


    def cmul(self, e, ore, oim, are, aim, bre, bim, t1, t2, k, conj_b=False):
        fw = self.fw
        tt = lambda o, a, b, op: fw.op(e, lambda g: g.tensor_tensor(out=o, in0=a, in1=b, op=op), k, k)
        tt(t1, are, bre, ALU.mult)
        tt(t2, aim, bim, ALU.mult)
        tt(ore, t1, t2, ALU.add if conj_b else ALU.subtract)
        tt(t1, are, bim, ALU.mult)
        tt(t2, aim, bre, ALU.mult)
        if conj_b:
            tt(oim, t2, t1, ALU.subtract)
        else:
            tt(oim, t1, t2, ALU.add)

    def sincos(self, arg, osin, ocos, ki, kf, k):
        fw = self.fw
        TWO_PI = 2.0 * np.pi
        for r, shift in ((osin, 0.0), (ocos, np.pi / 2)):
            fw.op("dve", lambda g: g.tensor_scalar(out=kf, in0=arg, scalar1=float(shift), scalar2=float(1.0 / TWO_PI), op0=ALU.add, op1=ALU.mult), k, k)
            fw.op("dve", lambda g: g.tensor_copy(out=ki, in_=kf), k, k)
            fw.op("dve", lambda g: g.tensor_copy(out=kf, in_=ki), k, k)
            fw.op("dve", lambda g: g.tensor_scalar(out=kf, in0=kf, scalar1=float(-TWO_PI), scalar2=float(shift), op0=ALU.mult, op1=ALU.add), k, k)
            fw.op("dve", lambda g: g.tensor_tensor(out=r, in0=kf, in1=arg, op=ALU.add), k, k)
            fw.op("dve", lambda g: g.tensor_scalar(out=r, in0=r, scalar1=3.141592, scalar2=-3.141592, op0=ALU.min, op1=ALU.max), k, k)
            fw.op("act", lambda g: g.activation(out=r, in_=r, func=AF.Sin), k, k)

    def s5(self, l):
        fw = self.fw
        I = self.inp
        A = Arena(fw)
        kc_ = self.k_ident
        w_d = I["w_in"][l].rearrange("(kc p) n -> p kc n", p=128)
        U = A.sb("U", [128, 32, 32, 16], BF16)
        k_U = [Trk() for _ in range(32)]
        k_p = Trk()
        kp_ = [k_p]
        T = lambda nm, shp=(128, 16): A.sb(nm, list(shp), F32)
        abr_keep = {}
        bbr, bbi = T("bbr", (128, 16, 16)), T("bbi", (128, 16, 16))
        cre, cim = T("cre", (128, 16, 16)), T("cim", (128, 16, 16))
        pwr_, pwi_ = T("pwr_", (128, 16, 65)), T("pwi_", (128, 16, 65))
        R = T("R")
        cosT = A.sb("cosT", [128, 16, 128], F32)
        sinT = A.sb("sinT", [128, 16, 128], F32)
        drep = T("drep", (128, 512))
        A2 = Arena(fw)
        wb = A2.sb("wu", [128, 8, 512], BF16)
        kw = Trk()
        fw.dma("pool", wb, w_d[:, :, 2832:3344], writes=[kw])
        for tau in range(32):
            ps, kp = self.psum.get()
            for kc in range(8):
                lhsT = self.xT[:, kc, :].rearrange("p (c t) -> p t c", t=32)[:, tau, :]
                self.mm(ps, lhsT, wb[:, kc, :], kc == 0, kc == 7, [kw] + self.k_xT, [kp])
            self.copy(self.alt(), U[:, :, tau, :], ps.rearrange("p (g h) -> p g h", h=16), [kp], [k_U[tau]])
        T2 = lambda nm, shp=(128, 16): A2.sb(nm, list(shp), F32)

        def ld(t, src):
            fw.dma("sp", t, src, writes=kp_)
            return t
        are, aim, lst = ld(T2("are"), I["s5_are"][l]), ld(T2("aim"), I["s5_aim"][l]), ld(T2("lst"), I["s5_lst"][l])
        bre, bim = ld(T2("bre", (128, 16, 16)), I["s5_bre"][l]), ld(T2("bim", (128, 16, 16)), I["s5_bim"][l])
        ld(cre, I["s5_cre"][l]); ld(cim, I["s5_cim"][l])
        ld(drep, I["s5_d"][l:l + 1, :].partition_broadcast(128))
        step, lre, den, t1, t2, xr_, th, mag = [T2(n) for n in ("step", "lre", "den", "t1", "t2", "xr_", "th", "mag")]
        sn, cs, abr, abi, nre, kre, kim = [T2(n) for n in ("sn", "cs", "abr", "abi", "nre", "kre", "kim")]
        ki16, kf16 = A2.sb("ki16", [128, 16], I32), T2("kf16")
        V = lambda fn: fw.op("dve", fn, kp_, kp_)
        fw.op("act", lambda g: g.activation(out=step, in_=lst, func=AF.Exp), kp_, kp_)
        V(lambda g: g.tensor_scalar(out=lre, in0=are, scalar1=-1e-4, scalar2=None, op0=ALU.min))
        V(lambda g: g.tensor_tensor(out=xr_, in0=lre, in1=step, op=ALU.mult))
        V(lambda g: g.tensor_tensor(out=th, in0=aim, in1=step, op=ALU.mult))
        fw.op("act", lambda g: g.activation(out=mag, in_=xr_, func=AF.Exp), kp_, kp_)
        self.sincos(th, sn, cs, ki16, kf16, kp_)
        V(lambda g: g.tensor_tensor(out=abr, in0=mag, in1=cs, op=ALU.mult))
        V(lambda g: g.tensor_tensor(out=abi, in0=mag, in1=sn, op=ALU.mult))
        V(lambda g: g.tensor_tensor(out=t1, in0=lre, in1=lre, op=ALU.mult))
        V(lambda g: g.tensor_tensor(out=t2, in0=aim, in1=aim, op=ALU.mult))
        V(lambda g: g.tensor_tensor(out=den, in0=t1, in1=t2, op=ALU.add))
        V(lambda g: g.reciprocal(out=den, in_=den))
        V(lambda g: g.tensor_scalar(out=nre, in0=abr, scalar1=-1.0, scalar2=None, op0=ALU.add))
        self.cmul("dve", kre, kim, nre, abi, lre, aim, t1, t2, kp_, conj_b=True)
        V(lambda g: g.tensor_tensor(out=kre, in0=kre, in1=den, op=ALU.mult))
        V(lambda g: g.tensor_tensor(out=kim, in0=kim, in1=den, op=ALU.mult))
        sh3 = [128, 16, 16]
        t3a, t3b = T2("t3a", sh3), T2("t3b", sh3)
        bc3 = lambda t: t.unsqueeze(2).to_broadcast(sh3)
        self.cmul("dve", bbr, bbi, bc3(kre), bc3(kim), bre, bim, t3a, t3b, kp_)
        shp = [128, 16, 65]
        ioti = A2.sb("ioti", [128, 65], I32)
        iot = T2("iot", (128, 65))
        fw.op("pool", lambda g: g.iota(ioti[:, 0:33], pattern=[[1, 33]], base=0, channel_multiplier=0), kp_, kp_)
        fw.op("pool", lambda g: g.iota(ioti[:, 33:65], pattern=[[-1, 32]], base=31, channel_multiplier=0), kp_, kp_)
        V(lambda g: g.tensor_copy(out=iot, in_=ioti))
        marg, parg, psn, pcs, kf65 = [T2(n, shp) for n in ("marg", "parg", "psn", "pcs", "kf65")]
        ki65 = A2.sb("ki65", shp, I32)
        bcm = lambda t: t.unsqueeze(2).to_broadcast(shp)
        bci = iot.unsqueeze(1).to_broadcast(shp)
        V(lambda g: g.tensor_tensor(out=marg, in0=bcm(xr_), in1=bci, op=ALU.mult))
        V(lambda g: g.tensor_tensor(out=parg, in0=bcm(th), in1=bci, op=ALU.mult))
        fw.op("act", lambda g: g.activation(out=marg, in_=marg, func=AF.Exp), kp_, kp_)
        self.sincos(parg, psn, pcs, ki65, kf65, kp_)
        V(lambda g: g.tensor_tensor(out=pwr_, in0=marg, in1=pcs, op=ALU.mult))
        V(lambda g: g.tensor_tensor(out=pwi_, in0=marg, in1=psn, op=ALU.mult))
        V(lambda g: g.tensor_copy(out=R, in_=marg[:, :, 32]))
        wre, wim, u1, u2 = [T2(n) for n in ("wre", "wim", "u1", "u2")]
        V(lambda g: g.tensor_copy(out=wre, in_=pcs[:, :, 32]))
        V(lambda g: g.tensor_copy(out=wim, in_=psn[:, :, 32]))
        fw.op("pool", lambda g: g.memset(cosT[:, :, 0:1], 1.0), kp_, kp_)
        fw.op("pool", lambda g: g.memset(sinT[:, :, 0:1], 0.0), kp_, kp_)
        tA, tB = T2("tA", (128, 16, 64)), T2("tB", (128, 16, 64))
        n_ = 1
        while n_ < 128:
            shn = [128, 16, n_]
            bw = lambda t: t.unsqueeze(2).to_broadcast(shn)
            self.cmul("dve", cosT[:, :, n_:2 * n_], sinT[:, :, n_:2 * n_], cosT[:, :, 0:n_], sinT[:, :, 0:n_], bw(wre), bw(wim),
                      tA[:, :, 0:n_], tB[:, :, 0:n_], kp_)
            self.cmul("dve", u1, u2, wre, wim, wre, wim, t1, t2, kp_)
            V(lambda g: g.tensor_copy(out=wre, in_=u1))
            V(lambda g: g.tensor_copy(out=wim, in_=u2))
            n_ *= 2
        A2.close()
        Zr = A.ring("Z", 2, [128, 2, 32, 16], BF16)
        ABr = A.ring("ABp", 2, [128, 8, 2, 128], BF16)
        for b_ in ABr.b:
            fw.op("pool", lambda g: g.memset(b_, 0.0), writes=ABr.k)
        CAr = A.ring("CAP", 2, [128, 2, 33, 16], BF16)
        BBr = A.ring("BBrep", 2, [128, 2, 2, 8, 16], BF16)
        for b_ in BBr.b:
            fw.op("pool", lambda g: g.memset(b_, 0.0), writes=BBr.k)
        UTr = A.ring("UT", 3, [128, 4, 128], BF16)
        TBf = A.ring("TBf", 1, [128, 512], F32)
        TBr = A.ring("TB", 2, [128, 512], BF16)
        gsc = A.ring("gsc", 2, [128, 4, 128], F32)
        Rbr = A.ring("Rb", 2, [128, 128], F32)
        Spr = A.ring("Sprev", 2, [128, 2, 2, 128], BF16)
        for b_ in Spr.b:
            fw.op("pool", lambda g: g.memset(b_, 0.0), writes=Spr.k)
        ypr = A.ring("ypre", 2, [128, 32, 16], F32)
        y2r = A.ring("y2", 2, [128, 32, 16], F32)
        GYr = A.ring("GY", 1, [128, 32, 128], BF16)
        gts = A.ring("gts", 1, [128, L_SEQ], BF16)
        z3, z4 = T("z3", (128, 32, 16)), T("z4", (128, 32, 16))
        cA, cB = T("cA", (128, 33, 16)), T("cB", (128, 33, 16))
        k_z = Trk()
        GY, kGY = None, None
        for gp in range(16):
            Z, kZ = Zr.get()
            shz = [128, 32, 16]
            pa = lambda t: t[:, gp, 33:65].unsqueeze(2).to_broadcast(shz)
            pb = lambda t: t[:, gp, :].unsqueeze(1).to_broadcast(shz)
            self.cmul("pool", Z[:, 0], Z[:, 1], pa(pwr_), pa(pwi_), pb(bbr), pb(bbi), z3, z4, [k_p, k_z, kZ])
            pT, kpT = self.psum.get()
            pTb = pT.bitcast(BF16)
            for a in range(4):
                for ri in range(2):
                    j = a * 2 + ri
                    fw.op("pe", lambda g: g.transpose(pTb[:, j * 128:(j + 1) * 128], Z[:, ri, 8 * a:8 * a + 8, :].rearrange("p t h -> p (t h)"),
                                                      self.identb), [kZ, kc_], [kpT])
            AB, kAB = ABr.get()
            src = pTb.rearrange("p (j m) -> p j m", j=8)
            self.copy("act", AB[:, :, 0, 0:64], src[:, :, 0:64], [kpT], [kAB])
            self.copy("act", AB[:, :, 1, 64:128], src[:, :, 64:128], [kpT], [kAB])
            CAP, kCA = CAr.get()
            shc = [128, 33, 16]
            pa2 = lambda t: t[:, gp, 0:33].unsqueeze(2).to_broadcast(shc)
            pc2 = lambda t: t[:, gp, :].unsqueeze(1).to_broadcast(shc)
            kz2 = [k_p, k_z]
            tt = lambda o, a_, b_, op, wr: fw.op("dve", lambda g: g.tensor_tensor(out=o, in0=a_, in1=b_, op=op), kz2 + wr, [k_z] + wr)
            tt(cA, pa2(pwr_), pc2(cre), ALU.mult, [])
            tt(cB, pa2(pwi_), pc2(cim), ALU.mult, [])
            tt(CAP[:, 0], cA, cB, ALU.subtract, [kCA])
            tt(cA, pa2(pwr_), pc2(cim), ALU.mult, [])
            tt(cB, pa2(pwi_), pc2(cre), ALU.mult, [])
            tt(cA, cA, cB, ALU.add, [])
            fw.op("dve", lambda g: g.tensor_scalar(out=CAP[:, 1], in0=cA, scalar1=-1.0, scalar2=None, op0=ALU.mult), [k_z, kCA], [kCA])
            BB, kBB = BBr.get()
            for (half, gl) in ((slice(0, 64), 0), (slice(64, 128), 1)):
                fw.op("pool", lambda g: g.tensor_copy(out=BB[half, gl, 0], in_=bbr[half, gp, :].unsqueeze(1).to_broadcast([64, 8, 16])), [k_p], [kBB])
                fw.op("pool", lambda g: g.tensor_copy(out=BB[half, gl, 1], in_=bbi[half, gp, :].unsqueeze(1).to_broadcast([64, 8, 16])), [k_p], [kBB])
            UTs = []
            for gl in range(2):
                g_ = 2 * gp + gl
                pU, kpU = self.psum.get()
                pUb = pU.bitcast(BF16)
                UT, kUT = UTr.get()
                UTs.append((UT, kUT))
                for a in range(4):
                    fw.op("pe", lambda g: g.transpose(pUb[:, a * 128:(a + 1) * 128], U[:, g_, 8 * a:8 * a + 8, :].rearrange("p t h -> p (t h)"), self.identb),
                          k_U[8 * a:8 * a + 8] + [kc_], [kpU])
                self.copy(self.alt(), UT, pUb[:, 0:512].rearrange("p (a c) -> p a c", a=4), [kpU], [kUT])
            pW, kpW = self.psum.get()
            for ri in range(2):
                o = pW[:, ri * 128:(ri + 1) * 128]
                n_mm = 0
                for gl in range(2):
                    UT, kUT = UTs[gl]
                    for a in range(4):
                        self.mm(o, AB[:, a * 2 + ri, gl, :], UT[:, a, :], n_mm == 0, n_mm == 7, [kAB, kUT], [kpW])
                        n_mm += 1
            gs, kgs = gsc.get()
            cT_, sT_ = cosT[:, gp, :], sinT[:, gp, :]
            W0, W1 = pW[:, 0:128], pW[:, 128:256]
            dv = lambda o, a_, b_, op: fw.op("dve", lambda g: g.tensor_tensor(out=o, in0=a_, in1=b_, op=op), [kpW, k_p, kgs], [kgs])
            dv(gs[:, 2], W0, cT_, ALU.mult); dv(gs[:, 3], W1, sT_, ALU.mult); dv(gs[:, 0], gs[:, 2], gs[:, 3], ALU.add)
            dv(gs[:, 2], W1, cT_, ALU.mult); dv(gs[:, 3], W0, sT_, ALU.mult); dv(gs[:, 1], gs[:, 2], gs[:, 3], ALU.subtract)
            Rb, kRb = Rbr.get()
            fw.op("pool", lambda g: g.tensor_scalar(out=Rb, in0=self.ones_f, scalar1=R[:, gp:gp + 1], scalar2=0.0, op0=ALU.mult, op1=ALU.add),
                  [kc_, k_p], [kRb])
            for ri in range(2):
                fw.op("dve", lambda g: g.tensor_tensor_scan(out=gs[:, 2 + ri], data0=Rb, data1=gs[:, ri], initial=0.0, op0=ALU.mult, op1=ALU.add),
                      [kgs, kRb], [kgs])
            Sp, kSp = Spr.get()
            n1 = slice(0, 127)
            dv2 = lambda o, a_, b_, op, wr: fw.op("dve", lambda g: g.tensor_tensor(out=o, in0=a_, in1=b_, op=op), [k_p, kgs] + wr, [kgs] + wr)
            for (half, gl) in ((slice(0, 64), 0), (slice(64, 128), 1)):
                dv2(gs[half, 0, n1], gs[half, 2, n1], cT_[half, n1], ALU.mult, [])
                dv2(gs[half, 1, n1], gs[half, 3, n1], sT_[half, n1], ALU.mult, [])
                dv2(Sp[half, gl, 0, 1:128], gs[half, 0, n1], gs[half, 1, n1], ALU.subtract, [kSp])
                dv2(gs[half, 0, n1], gs[half, 2, n1], sT_[half, n1], ALU.mult, [])
                dv2(gs[half, 1, n1], gs[half, 3, n1], cT_[half, n1], ALU.mult, [])
                dv2(Sp[half, gl, 1, 1:128], gs[half, 0, n1], gs[half, 1, n1], ALU.add, [kSp])
            for gl in range(2):
                g_ = 2 * gp + gl
                rows = slice(gl * 64, (gl + 1) * 64)
                UT, kUT = UTs[gl]
                pK, kpK = self.psum.get()
                for ri in range(2):
                    lhs = BB[:, gl, ri].rearrange("p s h -> p (s h)")
                    self.mm(pK, lhs, CAP[:, ri, 0:32, :].rearrange("p m h -> p (m h)"), ri == 0, ri == 1, [kBB, kCA], [kpK])
                TBf_, kTBf = TBf.get()
                fw.op("dve", lambda g: g.tensor_scalar(out=TBf_, in0=pK, scalar1=self.rowmask[:, 0:1], scalar2=None, op0=ALU.mult), [kpK, kc_], [kTBf])
                for s_ in range(1, 8):
                    fw.op("dve", lambda g: g.scalar_tensor_tensor(out=TBf_[:, 16 * s_:512], in0=pK[:, 0:512 - 16 * s_], scalar=self.rowmask[:, s_:s_ + 1],
                                                                  in1=TBf_[:, 16 * s_:512], op0=ALU.mult, op1=ALU.add), [kpK, kc_, kTBf], [kTBf])
                TB_, kTB = TBr.get()
                self.copy("act", TB_, TBf_, [kTBf], [kTB])
                pY, kpY = self.psacc.get()
                for a in range(4):
                    self.mm(pY[:, 128 * a:512], UT[:, a, :], TB_[:, 0:512 - 128 * a], a == 0, False, [kUT, kTB], [kpY])
                for ri in range(2):
                    self.mm(pY, Sp[:, gl, ri, :], CAP[:, ri, 1:33, :].rearrange("p m h -> p (m h)"), False, ri == 1, [kSp, kCA], [kpY])
                yp, kyp = ypr.get()
                fw.op("pool", lambda g: g.tensor_tensor(out=yp, in0=U[:, g_, :, :],
                                                        in1=drep[:, 16 * g_:16 * g_ + 16].unsqueeze(1).to_broadcast([128, 32, 16]), op=ALU.mult),
                      k_U + [k_p], [kyp])
                fw.op("dve", lambda g: g.tensor_tensor(out=yp, in0=yp, in1=pY.rearrange("p (t h) -> p t h", h=16), op=ALU.add), [kyp, kpY], [kyp])
                y2, ky2 = y2r.get()
                fw.op("pool", lambda g: g.tensor_tensor(out=y2, in0=yp, in1=yp, op=ALU.mult), [kyp], [ky2])
                fw.op("pool", lambda g: g.tensor_scalar(out=y2, in0=y2, scalar1=0.044715, scalar2=1.0, op0=ALU.mult, op1=ALU.add), [ky2], [ky2])
                fw.op("pool", lambda g: g.tensor_tensor(out=y2, in0=y2, in1=yp, op=ALU.mult), [ky2, kyp], [ky2])
                fw.op("act", lambda g: g.activation(out=y2, in_=y2, func=AF.Sigmoid, scale=1.5957691216057308), [ky2], [ky2])
                if g_ % 8 == 0:
                    GY, kGY = GYr.get()
                fw.op("dve", lambda g: g.tensor_tensor(out=GY[:, :, (g_ % 8) * 16:(g_ % 8 + 1) * 16], in0=yp, in1=y2, op=ALU.mult), [kyp, ky2], [kGY])
                if g_ % 8 == 7:
                    kt = g_ // 8
                    gt, kgt = gts.get()
                    gtv = gt.rearrange("p (c t) -> p t c", t=32)
                    for b in range(4):
                        pT, kpT = self.psum.get()
                        pTb = pT.bitcast(BF16)
                        for t8 in range(8):
                            fw.op("pe", lambda g: g.transpose(pTb[:, t8 * 128:(t8 + 1) * 128], GY[:, 8 * b + t8, :], self.identb), [kGY, kc_], [kpT])
                        self.copy(self.alt(), gtv[:, 8 * b:8 * b + 8, :], pTb.rearrange("p (t c) -> p t c", t=8), [kpT], [kgt])
                    fw.dma("sp", self.gyT[kt], gt, reads=[kgt], writes=[self.k_gyT[kt]])
        A.close()
        A = Arena(fw)
        wg = A.sb("wglu", [128, 4, 512], BF16)
        bglu = A.sb("bglu", [128, 4], F32)
        k_wg = Trk()
        fw.dma("pool", wg, I["s5_w_glu"][l].rearrange("(kc p) n -> p kc n", p=128), writes=[k_wg])
        fw.dma("sp", bglu, I["s5_bglu"][l], writes=[k_wg])
        gbr = A.ring("gb", 2, [128, 4, 512], BF16)
        sgr = A.ring("sg", 2, [128, 512], F32)
        obr = A.ring("ob", 2, [128, 512], BF16)
        for tb in range(NTB):
            gb, kgb = gbr.get()
            fw.dma("sp", gb, self.gyT[:, :, tb * 512:(tb + 1) * 512].rearrange("k p t -> p k t"), reads=self.k_gyT, writes=[kgb])
            for oc in range(4):
                ps, kp = self.psum.get()
                for kc in range(4):
                    self.mm(ps, wg[:, kc, oc * 128:(oc + 1) * 128], gb[:, kc, :], kc == 0, kc == 3, [k_wg, kgb], [kp])
                sg, ksg = sgr.get()
                fw.op("act", lambda g: g.activation(out=sg, in_=ps, func=AF.Sigmoid, bias=bglu[:, oc:oc + 1]), [kp, k_wg], [ksg])
                ob, kob = obr.get()
                fw.op("dve", lambda g: g.tensor_tensor(out=ob, in0=gb[:, oc, :], in1=sg, op=ALU.mult), [kgb, ksg], [kob])
                fw.dma("sp", self.yT[2, oc * 128:(oc + 1) * 128, tb * 512:(tb + 1) * 512], ob, reads=[kob], writes=[self.k_yT[2][tb]])
        A.close()


    def merge(self, l, res_src, k_res):
        fw = self.fw
        I = self.inp
        A = Arena(fw)
        Y = A.sb("Y", [128, 3, 4, L_SEQ], BF16)
        k_Y = [Trk() for _ in range(3)]
        for r in range(3):
            fw.dma("sp", Y[:, r], self.yT[r].rearrange("(kc p) t -> p kc t", p=128), reads=self.k_yT[r], writes=[k_Y[r]])
        wgr = A.ring("wgt", 2, [128, 3, 8, 128], BF16)
        wbr = A.ring("wbr", 2, [128, 3, 4, 128], BF16)
        bg = A.sb("bg", [128, 3, 8], F32)
        k_bg = Trk()
        fw.dma("sp", bg, I["b_gate_l"][l], writes=[k_bg])
        w_d = I["w_in"][l].rearrange("(kc p) n -> p kc n", p=128)
        wbr_d = I["w_branch"][l].rearrange("r (kc p) d -> p r kc d", p=128)
        gsr = A.ring("gs", 3, [128, 512], F32)
        accr = A.ring("acc", 2, [128, 512], F32)
        mbr = A.ring("mb", 2, [128, 512], BF16)
        def load_mw(dt):
            wg, kwg = wgr.get()
            for r in range(3):
                c0 = 3344 + r * 1024 + dt * 128
                fw.dma("pool", wg[:, r], w_d[:, :, c0:c0 + 128], writes=[kwg])
            wb, kwb = wbr.get()
            for r in range(3):
                fw.dma("pool", wb[:, r], wbr_d[:, r, :, dt * 128:(dt + 1) * 128], writes=[kwb])
            return wg, kwg, wb, kwb
        nxt_mw = load_mw(0)
        for dt in range(8):
            wg, kwg, wb, kwb = nxt_mw
            if dt + 1 < 8:
                nxt_mw = load_mw(dt + 1)
            for tb in range(NTB):
                ts_ = slice(tb * 512, (tb + 1) * 512)
                acc, kacc = accr.get()
                for r in range(3):
                    pg, kpg = self.psum.get()
                    for kc in range(8):
                        self.mm(pg, wg[:, r, kc, :], self.xT[:, kc, ts_], kc == 0, kc == 7, [kwg] + self.k_xT[tb * 4:(tb + 1) * 4], [kpg])
                    gs, kgs = gsr.get()
                    fw.op("act", lambda g: g.activation(out=gs, in_=pg, func=AF.Sigmoid, bias=bg[:, r, dt:dt + 1]), [kpg, k_bg], [kgs])
                    pp, kpp = self.psum.get()
                    for kc in range(4):
                        self.mm(pp, wb[:, r, kc, :], Y[:, r, kc, ts_], kc == 0, kc == 3, [kwb, k_Y[r]], [kpp])
                    if r == 0:
                        fw.op("dve", lambda g: g.tensor_tensor(out=acc, in0=gs, in1=pp, op=ALU.mult), [kgs, kpp], [kacc])
                    else:
                        fw.op("dve", lambda g: g.tensor_tensor(out=gs, in0=gs, in1=pp, op=ALU.mult), [kgs, kpp], [kgs])
                        if r == 1:
                            fw.op("pool", lambda g: g.tensor_tensor(out=acc, in0=acc, in1=gs, op=ALU.add), [kacc, kgs], [kacc])
                        else:
                            mb, kmb = mbr.get()
                            fw.op("pool", lambda g: g.tensor_tensor(out=mb, in0=acc, in1=gs, op=ALU.add), [kacc, kgs], [kmb])
                            fw.dma("sp", self.mT[dt, :, ts_], mb, reads=[kmb], writes=[self.k_mT[dt][tb]])
        A.close()
        A = Arena(fw)
        self.alloc_ln(A)
        self.load_ln(I["ln1_g"][l:l + 1, :], I["ln1_b"][l:l + 1, :])
        wo = A.sb("wo", [128, 8, D], BF16)
        k_wo = Trk()
        fw.dma("pool", wo, I["w_out"][l].rearrange("(kc p) d -> p kc d", p=128), writes=[k_wo])
        mir = A.ring("mi", 2, [128, 8, 512], BF16)
        for tb in range(NTB):
            mi, kmi = mir.get()
            fw.dma("sp", mi, self.mT[:, :, tb * 512:(tb + 1) * 512].rearrange("k p t -> p k t"),
                   reads=[self.k_mT[dt][tb] for dt in range(8)], writes=[kmi])
            for ts in range(4):
                tt = tb * 4 + ts
                p0, k0 = self.psum.get()
                p1, k1 = self.psum.get()
                for h, (pp, kk) in enumerate(((p0, k0), (p1, k1))):
                    for kc in range(8):
                        self.mm(pp, mi[:, kc, ts * 128:(ts + 1) * 128], wo[:, kc, h * 512:(h + 1) * 512], kc == 0, kc == 7, [kmi, k_wo], [kk])
                self.resid_ln(tt, (p0, p1), (k0, k1), res_src, k_res[tt], self.xr, self.k_xr[tt])
        self.ln_flush()
        A.close()

    def setup_ffn(self):
        fw = self.fw
        self.actT = fw.dram("actT", [22, 128, L_SEQ], BF16)
        self.k_actT = [[Trk() for _ in range(NTB)] for _ in range(22)]

    def ffn(self, l, out_dram, k_out):
        fw = self.fw
        I = self.inp
        A = Arena(fw)
        self.alloc_ln(A)
        self.wdown = A.sb("wdown", [128, 22, D], BF16)
        self.k_wdown = Trk("wdown")
        self.wup = A.ring("wup", 2, [128, 8, 256], BF16)
        self.fcw = A.sb("fcw", [128, 44, 3], F32)
        self.fcb = A.sb("fcb", [128, 44], F32)
        self.k_fc = Trk("fc")
        self.sv = A.ring("sv", 3, [128, 514], BF16)
        self.sg = A.ring("sg", 3, [128, 514], BF16)
        self.dg = A.ring("dg", 2, [128, 6, 128], BF16)
        self.hg = A.ring("hg", 2, [128, 512], F32)
        self.actb = A.ring("actb", 3, [128, 512], BF16)
        self.actin = A.ring("actin", 2, [128, 22, 256], BF16)
        wup_d = I["ffn_w_up"][l].rearrange("(kc p) n -> p kc n", p=128)
        fw.dma("sp", self.fcw, I["ffn_cw"][l], writes=[self.k_fc])
        fw.dma("sp", self.fcb, I["ffn_cb"][l], writes=[self.k_fc])
        k_all_xT = self.k_xT
        def load_w(m):
            wb, kw = self.wup.get()
            fw.dma("pool", wb[:, :, 0:128], wup_d[:, :, m * 128:(m + 1) * 128], writes=[kw])
            fw.dma("pool", wb[:, :, 128:256], wup_d[:, :, DFF + m * 128:DFF + (m + 1) * 128], writes=[kw])
            return wb, kw
        nxt_w = load_w(0)
        for m in range(22):
            wb, kw = nxt_w
            if m + 1 < 22:
                nxt_w = load_w(m + 1)
            dg, kdg = self.dg.get()
            for t in range(3):
                fw.op("pool", lambda g: g.tensor_scalar(out=dg[:, t, :], in0=self.identf, scalar1=self.fcw[:, m, t:t + 1], scalar2=0.0,
                                                        op0=ALU.mult, op1=ALU.add), reads=[self.k_ident, self.k_fc], writes=[kdg])
                fw.op("pool", lambda g: g.tensor_scalar(out=dg[:, 3 + t, :], in0=self.identf, scalar1=self.fcw[:, 22 + m, t:t + 1], scalar2=0.0,
                                                        op0=ALU.mult, op1=ALU.add), reads=[self.k_ident, self.k_fc], writes=[kdg])
            prev = None
            pend = None

            def conv(st):
                sv, ksv, sg, ksg, tb = st
                cv, kcv = self.psum.get()
                for t in range(3):
                    self.mm(cv, dg[:, t, :], sv[:, t:t + 512], t == 0, t == 2, [kdg, ksv], [kcv])
                cg, kcg = self.psum.get()
                for t in range(3):
                    self.mm(cg, dg[:, 3 + t, :], sg[:, t:t + 512], t == 0, t == 2, [kdg, ksg], [kcg])
                hg, khg = self.hg.get()
                fw.op("act", lambda g: g.activation(out=hg, in_=cg, func=AF.Silu, bias=self.fcb[:, 22 + m:23 + m]),
                      reads=[kcg, self.k_fc], writes=[khg])
                ab, kab = self.actb.get()
                fw.op("dve", lambda g: g.scalar_tensor_tensor(out=ab, in0=cv, scalar=self.fcb[:, m:m + 1], in1=hg,
                                                              op0=ALU.add, op1=ALU.mult), reads=[kcv, self.k_fc, khg], writes=[kab])
                fw.dma("sp", self.actT[m, :, tb * 512:(tb + 1) * 512], ab, reads=[kab], writes=[self.k_actT[m][tb]])
            for tb in range(NTB):
                xk = k_all_xT[tb * 4:(tb + 1) * 4]
                pv, kpv = self.psum.get()
                for kc in range(8):
                    self.mm(pv, wb[:, kc, 0:128], self.xT[:, kc, tb * 512:(tb + 1) * 512], kc == 0, kc == 7, [kw] + xk, [kpv])
                pg, kpg = self.psum.get()
                for kc in range(8):
                    self.mm(pg, wb[:, kc, 128:256], self.xT[:, kc, tb * 512:(tb + 1) * 512], kc == 0, kc == 7, [kw] + xk, [kpg])
                if pend is not None:
                    conv(pend)
                sv, ksv = self.sv.get()
                sg, ksg = self.sg.get()
                self.copy("act", sv[:, 2:514], pv, [kpv], [ksv])
                self.copy("dve", sg[:, 2:514], pg, [kpg], [ksg])
                if prev is None:
                    fw.op("pool", lambda g: g.memset(sv[:, 0:2], 0.0), writes=[ksv])
                    fw.op("pool", lambda g: g.memset(sg[:, 0:2], 0.0), writes=[ksg])
                else:
                    psv, pksv, psg, pksg = prev
                    fw.op("pool", lambda g: g.tensor_copy(out=sv[:, 0:2], in_=psv[:, 512:514]), reads=[pksv], writes=[ksv])
                    fw.op("pool", lambda g: g.tensor_copy(out=sg[:, 0:2], in_=psg[:, 512:514]), reads=[pksg], writes=[ksg])
                prev = (sv, ksv, sg, ksg)
                pend = (sv, ksv, sg, ksg, tb)
            conv(pend)
        fw.dma("pool", self.wdown, I["ffn_w_down"][l].rearrange("(kt p) d -> p kt d", p=128), writes=[self.k_wdown])
        self.load_ln(I["ln2_g"][l:l + 1, :], I["ln2_b"][l:l + 1, :])
        for tb2 in range(2 * NTB):
            tb = tb2 // 2
            ai, kai = self.actin.get()
            fw.dma("sp", ai, self.actT[:, :, tb2 * 256:(tb2 + 1) * 256].rearrange("m p t -> p m t"),
                   reads=[self.k_actT[m][tb] for m in range(22)], writes=[kai])
            for ts in range(2):
                tt = tb2 * 2 + ts
                p0, k0 = self.psum.get()
                p1, k1 = self.psum.get()
                for h, (pp, kk) in enumerate(((p0, k0), (p1, k1))):
                    for m in range(22):
                        self.mm(pp, ai[:, m, ts * 128:(ts + 1) * 128], self.wdown[:, m, h * 512:(h + 1) * 512], m == 0, m == 21,
                                [kai, self.k_wdown], [kk])
                self.resid_ln(tt, (p0, p1), (k0, k1), self.xr, self.k_xr[tt], out_dram, k_out[tt])
        self.ln_flush()
        A.close()


def _host_layout(inputs):
    f = {}
    A = lambda a: np.ascontiguousarray(np.asarray(a, dtype=np.float32))
    for k in ("w_in", "ffn_w_up", "ffn_w_down", "ln1_g", "ln1_b", "ln2_g", "ln2_b", "w_out", "w_branch", "s5_w_glu"):
        f[k] = A(inputs[k])
    cw = A(inputs["ffn_conv_w"])
    sw = A(inputs["ssd_conv_w"])
    f["ssd_cw"] = A(sw.reshape(DEPTH, 4, 6, 128).transpose(0, 3, 2, 1))
    f["ssd_cb"] = A(A(inputs["ssd_conv_b"]).reshape(DEPTH, 6, 128).transpose(0, 2, 1))
    for k in ("ssd_conv_b", "ssd_dt_bias", "ssd_a_log", "ssd_d", "ssd_norm_w", "fox_f_bias"):
        f[k] = A(inputs[k])
    pl = lambda a: A(a.reshape(DEPTH, 16, 2, 64).transpose(0, 2, 3, 1).reshape(DEPTH, 128, 16))
    f["s5_are"] = pl(A(inputs["s5_a_re"]))
    f["s5_aim"] = pl(A(inputs["s5_a_im"]))
    f["s5_lst"] = pl(np.repeat(A(inputs["s5_log_step"])[:, :, None], 64, axis=2))
    pb_ = lambda a: A(a.reshape(DEPTH, 16, 2, 64, 16).transpose(0, 2, 3, 1, 4).reshape(DEPTH, 128, 16, 16))
    f["s5_bre"] = pb_(A(inputs["s5_b_re"]))
    f["s5_bim"] = pb_(A(inputs["s5_b_im"]))
    f["s5_cre"] = pb_(A(A(inputs["s5_c_re"]).transpose(0, 1, 3, 2)))
    f["s5_cim"] = pb_(A(A(inputs["s5_c_im"]).transpose(0, 1, 3, 2)))
    f["s5_d"] = A(inputs["s5_d"])
    f["s5_bglu"] = A(A(inputs["s5_b_glu"]).reshape(DEPTH, 4, 128).transpose(0, 2, 1))
    f["b_gate_l"] = A(A(inputs["b_gate"]).reshape(DEPTH, 3, 8, 128).transpose(0, 3, 1, 2))
    f["ffn_cw"] = A(cw.reshape(DEPTH, 3, 44, 128).transpose(0, 3, 2, 1))
    f["ffn_cb"] = A(A(inputs["ffn_conv_b"]).reshape(DEPTH, 44, 128).transpose(0, 2, 1))
    return f


def build_ffn_test():
    k = K()
    x = k.din("x", [L_SEQ, D])
    k.din("ffn_w_up", [DEPTH, D, 2 * DFF]); k.din("ffn_w_down", [DEPTH, DFF, D])
    k.din("ffn_cw", [DEPTH, 128, 44, 3]); k.din("ffn_cb", [DEPTH, 128, 44])
    k.din("ln2_g", [DEPTH, D]); k.din("ln2_b", [DEPTH, D])
    out = k.nc.dram_tensor("out", [L_SEQ, D], F32, kind="ExternalOutput").ap()
    k.setup_common()
    k.setup_ffn()
    k.phase0(x)
    k.xr = x
    k_out = [Trk() for _ in range(NTT)]
    k.ffn(0, out, k_out)
    k.fw.drain()
    return k


def build_fox_test():
    k = K()
    x = k.din("x", [L_SEQ, D])
    k.din("w_in", [DEPTH, D, DIN]); k.din("fox_f_bias", [DEPTH, 8])
    out = k.nc.dram_tensor("out", [512, L_SEQ], BF16, kind="ExternalOutput").ap()
    k.setup_common()
    k.phase0(x)
    k.fox(0)
    t = k.fw.sb("cp", [128, 4, L_SEQ], BF16)
    kt = Trk()
    k.fw.dma("sp", t, k.yT[0].rearrange("(c p) t -> p c t", p=128), reads=[x for r in k.k_yT[0] for x in [r]], writes=[kt])
    k.fw.dma("sp", out.rearrange("(c p) t -> p c t", p=128), t, reads=[kt], writes=[Trk()])
    k.fw.drain()
    return k


def build_ssd_test(dbg=None):
    k = K(dbg=dbg)
    x = k.din("x", [L_SEQ, D])
    k.din("w_in", [DEPTH, D, DIN])
    k.din("ssd_cw", [DEPTH, 128, 6, 4]); k.din("ssd_cb", [DEPTH, 128, 6]); k.din("ssd_conv_b", [DEPTH, 768])
    k.din("ssd_dt_bias", [DEPTH, 8]); k.din("ssd_a_log", [DEPTH, 8]); k.din("ssd_d", [DEPTH, 8]); k.din("ssd_norm_w", [DEPTH, 512])
    out = k.nc.dram_tensor("out", [512, L_SEQ], BF16, kind="ExternalOutput").ap()
    k.setup_common()
    k.phase0(x)
    k.ssd(0)
    t = k.fw.sb("cp", [128, 4, L_SEQ], BF16)
    kt = Trk()
    k.fw.dma("sp", t, k.yT[1].rearrange("(c p) t -> p c t", p=128), reads=list(k.k_yT[1]), writes=[kt])
    k.fw.dma("sp", out.rearrange("(c p) t -> p c t", p=128), t, reads=[kt], writes=[Trk()])
    k.fw.drain()
    return k


S5_INS = [("s5_are", [DEPTH, 128, 16]), ("s5_aim", [DEPTH, 128, 16]), ("s5_lst", [DEPTH, 128, 16]),
          ("s5_bre", [DEPTH, 128, 16, 16]), ("s5_bim", [DEPTH, 128, 16, 16]), ("s5_cre", [DEPTH, 128, 16, 16]),
          ("s5_cim", [DEPTH, 128, 16, 16]), ("s5_d", [DEPTH, 512]), ("s5_bglu", [DEPTH, 128, 4]), ("s5_w_glu", [DEPTH, 512, 512])]


def build_s5_test(dbg=None):
    k = K(dbg=dbg)
    x = k.din("x", [L_SEQ, D])
    k.din("w_in", [DEPTH, D, DIN])
    for n, shp in S5_INS:
        k.din(n, shp)
    out = k.nc.dram_tensor("out", [512, L_SEQ], BF16, kind="ExternalOutput").ap()
    k.setup_common()
    k.phase0(x)
    k.s5(0)
    t = k.fw.sb("cp", [128, 4, L_SEQ], BF16)
    kt = Trk()
    k.fw.dma("sp", t, k.yT[2].rearrange("(c p) t -> p c t", p=128), reads=list(k.k_yT[2]), writes=[kt])
    k.fw.dma("sp", out.rearrange("(c p) t -> p c t", p=128), t, reads=[kt], writes=[Trk()])
    k.fw.drain()
    return k


ALL_INS = [("w_in", [DEPTH, D, DIN]), ("fox_f_bias", [DEPTH, 8]),
           ("ssd_cw", [DEPTH, 128, 6, 4]), ("ssd_cb", [DEPTH, 128, 6]), ("ssd_conv_b", [DEPTH, 768]),
           ("ssd_dt_bias", [DEPTH, 8]), ("ssd_a_log", [DEPTH, 8]), ("ssd_d", [DEPTH, 8]), ("ssd_norm_w", [DEPTH, 512])] + S5_INS + [
           ("w_branch", [DEPTH, 3, 512, D]), ("b_gate_l", [DEPTH, 128, 3, 8]), ("w_out", [DEPTH, D, D]),
           ("ln1_g", [DEPTH, D]), ("ln1_b", [DEPTH, D]),
           ("ffn_w_up", [DEPTH, D, 2 * DFF]), ("ffn_w_down", [DEPTH, DFF, D]), ("ffn_cw", [DEPTH, 128, 44, 3]), ("ffn_cb", [DEPTH, 128, 44]),
           ("ln2_g", [DEPTH, D]), ("ln2_b", [DEPTH, D])]


def build_full(nl=DEPTH, stop=None):
    k = K()
    x = k.din("x", [L_SEQ, D])
    for n, shp in ALL_INS:
        k.din(n, shp)
    out = k.nc.dram_tensor("out", [L_SEQ, D], F32, kind="ExternalOutput").ap()
    k.setup_common()
    k.setup_ffn()
    k.phase0(x)
    k_dummy = [Trk() for _ in range(NTT)]
    k_out = [Trk() for _ in range(NTT)]
    for l in range(nl):
        k.fox(l)
        k.ssd(l)
        k.s5(l)
        k.merge(l, x if l == 0 else k.xr, k_dummy if l == 0 else k.k_xr)
        if stop == "merge":
            break
        last = (l == nl - 1)
        k.ffn(l, out if last else k.xr, k_out if last else k.k_xr)
    if stop == "merge":
        A = Arena(k.fw)
        r = A.ring("dump", 2, [128, D], F32)
        for tt in range(NTT):
            t, kt = r.get()
            k.fw.dma("sp", t, k.xr[tt * 128:(tt + 1) * 128, :], reads=[k.k_xr[tt]], writes=[kt])
            k.fw.dma("sp", out[tt * 128:(tt + 1) * 128, :], t, reads=[kt], writes=[k_out[tt]])
    k.fw.drain()
    return k


def build_l1_test():
    return build_full(1)


def build_m1_test():
    return build_full(1, stop="merge")


_CACHE = {}


def kernel(**inputs):
    f = _host_layout(inputs)
    if "k" not in _CACHE:
        _CACHE["k"] = build_full(DEPTH)
    k = _CACHE["k"]
    x = np.ascontiguousarray(np.asarray(inputs["x"], dtype=np.float32))
    nb = x.shape[0]
    in_maps = []
    for b in range(nb):
        m = {"x": x[b]}
        for n, _ in ALL_INS:
            m[n] = f[n]
        in_maps.append(m)
    res = run_bass_kernel_spmd(k.nc, in_maps, core_ids=list(range(nb)))
    return np.stack([np.asarray(r["out"], dtype=np.float32) for r in res.results], axis=0)


def build_l4_test():
    return build_full(4)


def build_l2_test():
    return build_full(2)
```
